# Optimizing a Trainium2 kernel written in Bass

```python
import math
import jax, jax.numpy as jnp
from jax import lax
import numpy as np

D_MODEL = 1024
BATCH = 16
SEQ = 2048
DEPTH = 2

N_MEM = 256
N_BRANCH = 5
E_BRANCH = D_MODEL // 2
E_A = E_BRANCH
E_B = E_BRANCH
E_C = E_BRANCH
E_D = E_BRANCH
E_M = E_BRANCH
CONF_WIDTH = 31
H_B = 8
DH_B = E_B // H_B
SB_BLOCK = 128
H_C = 4
DH_C = E_C // H_C
MLSTM_CHUNK = 64
SHORT_CONV = 4
NB_D = 8
BW_D = E_D // NB_D
LRU_C = 8.0
H_M = 4
DH_M = E_M // H_M
EPS = 1e-6
SECTION_SIZES = (E_A, E_A, E_A,
                 E_B, E_B, E_B, E_B,
                 E_C, E_C, E_C, H_C, H_C, E_C, E_C,
                 E_D, E_D,
                 E_M, E_M,
                 N_BRANCH * D_MODEL)
N_IN = sum(SECTION_SIZES)

kernel_name = 'hybrid_gated_parallel_mixers'


def rmsnorm(x, g):
    xf = x.astype(jnp.float32)
    y = xf * lax.rsqrt(jnp.mean(xf * xf, axis=-1, keepdims=True) + EPS)
    return (y * g.astype(jnp.float32)).astype(x.dtype)


def layernorm(x, g, b):
    xf = x.astype(jnp.float32)
    mu = jnp.mean(xf, axis=-1, keepdims=True)
    var = jnp.mean(jnp.square(xf - mu), axis=-1, keepdims=True)
    y = (xf - mu) * lax.rsqrt(var + EPS)
    return (y * g.astype(jnp.float32) + b.astype(jnp.float32)).astype(x.dtype)


def causal_dwconv(x, w, b):
    k = w.shape[0]
    c = x.shape[-1]
    y = lax.conv_general_dilated(x, w.astype(x.dtype)[:, None, :], window_strides=(1,),
                                 padding=((k - 1, 0),),
                                 dimension_numbers=('NWC', 'WIO', 'NWC'),
                                 feature_group_count=c)
    return y + b.astype(x.dtype)


def stick_breaking_attention(q, k, v):
    b, s, h, d = q.shape
    scale = d ** -0.5
    qf, kf, vf = (t.astype(jnp.float32).transpose(0, 2, 1, 3) for t in (q, k, v))
    outs = []
    for blk in range(s // SB_BLOCK):
        q0 = blk * SB_BLOCK
        tk = q0 + SB_BLOCK
        z = jnp.einsum('bhqd,bhkd->bhqk', qf[:, :, q0:tk], kf[:, :, :tk]) * scale
        t_pos = q0 + jnp.arange(SB_BLOCK)[:, None]
        s_pos = jnp.arange(tk)[None, :]
        strict = s_pos < t_pos
        log_keep = jnp.where(strict, jax.nn.log_sigmoid(-z), 0.0)
        later = lax.cumsum(log_keep, axis=3, reverse=True) - log_keep
        w = jnp.where(strict, jnp.exp(jax.nn.log_sigmoid(z) + later), 0.0)
        outs.append(jnp.einsum('bhqk,bhkd->bhqd', w, vf[:, :, :tk]))
    o = jnp.concatenate(outs, axis=2)
    return o.transpose(0, 2, 1, 3).astype(q.dtype)


def mlstm_chunkwise(q, k, v, i_pre, log_f):
    b, s, h, d = q.shape
    L = MLSTM_CHUNK
    nc = s // L

    def chunks(t):
        t = t.astype(jnp.float32).reshape(b, nc, L, h, *t.shape[3:])
        return jnp.moveaxis(t, (1, 3), (0, 2))

    qc, kc, vc = chunks(q), chunks(k) * (d ** -0.5), chunks(v)
    ic, fc = chunks(i_pre), chunks(log_f)
    causal = jnp.tril(jnp.ones((L, L), dtype=bool))

    def step(carry, inp):
        c_mat, n_vec, m_prev = carry
        qb, kb, vb, ib, fb = inp
        bcum = jnp.cumsum(fb, axis=-1)
        d_log = jnp.where(causal, bcum[..., :, None] - bcum[..., None, :] + ib[..., None, :], -jnp.inf)
        inter = bcum + m_prev[..., None]
        m_t = jnp.maximum(inter, jnp.max(d_log, axis=-1))
        w_intra = jnp.exp(d_log - m_t[..., None])
        w_inter = jnp.exp(inter - m_t)
        scores = jnp.einsum('bhtd,bhsd->bhts', qb, kb) * w_intra
        num = (jnp.einsum('bhts,bhsd->bhtd', scores, vb)
               + w_inter[..., None] * jnp.einsum('bhvk,bhtk->bhtv', c_mat, qb))
        den = jnp.sum(scores, axis=-1) + w_inter * jnp.einsum('bhk,bhtk->bht', n_vec, qb)
        h_t = num / jnp.maximum(jnp.abs(den), jnp.exp(-m_t))[..., None]
        b_last = bcum[..., -1]
        w_log = b_last[..., None] - bcum + ib
        m_new = jnp.maximum(b_last + m_prev, jnp.max(w_log, axis=-1))
        w_state = jnp.exp(w_log - m_new[..., None])
        decay = jnp.exp(b_last + m_prev - m_new)
        c_new = decay[..., None, None] * c_mat + jnp.einsum('bhs,bhsv,bhsk->bhvk', w_state, vb, kb)
        n_new = decay[..., None] * n_vec + jnp.einsum('bhs,bhsk->bhk', w_state, kb)
        return (c_new, n_new, m_new), h_t

    init = (jnp.zeros((b, h, d, d), jnp.float32), jnp.zeros((b, h, d), jnp.float32),
            jnp.zeros((b, h), jnp.float32))
    _, hs = lax.scan(step, init, (qc, kc, vc, ic, fc))
    return jnp.moveaxis(hs, (0, 2), (1, 3)).reshape(b, s, h, d)


def rg_lru(x, w_a, b_a, w_x, b_x, lam):
    b, s, e = x.shape
    xf = x.astype(jnp.float32)
    xb = xf.reshape(b, s, NB_D, BW_D)
    r = jax.nn.sigmoid(jnp.einsum('bsnc,ncd->bsnd', xb, w_a.astype(jnp.float32)).reshape(b, s, e) + b_a)
    i = jax.nn.sigmoid(jnp.einsum('bsnc,ncd->bsnd', xb, w_x.astype(jnp.float32)).reshape(b, s, e) + b_x)
    log_a = LRU_C * r * jax.nn.log_sigmoid(lam.astype(jnp.float32))
    a = jnp.exp(log_a)
    u = jnp.sqrt(-jnp.expm1(2.0 * log_a)) * (i * xf)

    def combine(c1, c2):
        return (c1[0] * c2[0], c2[0] * c1[1] + c2[1])

    _, hs = lax.associative_scan(combine, (a, u), axis=1)
    return hs.astype(x.dtype)


def memory_attention(q, mem_n, w_mkv):
    b, s, _ = q.shape
    kv = mem_n @ w_mkv
    mk, mv = jnp.split(kv, 2, axis=-1)
    qh = q.reshape(b, s, H_M, DH_M).astype(jnp.float32)
    kh = mk.reshape(b, -1, H_M, DH_M).astype(jnp.float32)
    vh = mv.reshape(b, -1, H_M, DH_M).astype(jnp.float32)
    p = jax.nn.softmax(jnp.einsum('bshd,bmhd->bhsm', qh, kh) * (DH_M ** -0.5), axis=-1)
    o = jnp.einsum('bhsm,bmhd->bshd', p, vh)
    return o.reshape(b, s, E_M).astype(q.dtype)


def hybrid_layer(x, mem, norm_g, w_in, b_in, a_conv_w, a_conv_b, a_ln_g, a_ln_b,
                 c_conv_w, c_conv_b, c_f_bias, c_hn_g, d_conv_w, d_conv_b,
                 d_wa, d_ba, d_wx, d_bx, d_lambda, mem_norm_g, w_mkv, w_up, w_out):
    b, s, _ = x.shape
    h = rmsnorm(x, norm_g)
    proj = h @ w_in + b_in
    split_points = [int(v) for v in np.cumsum(SECTION_SIZES)[:-1]]
    (a_val, a_glu, a_z,
     b_q, b_k, b_v, b_z,
     c_q, c_k, c_v, c_i, c_f, c_o, c_z,
     d_x, d_z,
     m_q, m_z,
     gate_logits) = jnp.split(proj, split_points, axis=-1)

    u = a_val * jax.nn.sigmoid(a_glu)
    u = layernorm(causal_dwconv(u, a_conv_w, a_conv_b), a_ln_g, a_ln_b)
    y_a = jax.nn.silu(u) * jax.nn.silu(a_z)

    y_b = stick_breaking_attention(b_q.reshape(b, s, H_B, DH_B), b_k.reshape(b, s, H_B, DH_B),
                                   b_v.reshape(b, s, H_B, DH_B)).reshape(b, s, E_B)
    y_b = y_b * jax.nn.silu(b_z)

    qk = jax.nn.silu(causal_dwconv(jnp.concatenate([c_q, c_k], axis=-1), c_conv_w, c_conv_b))
    cq, ck = jnp.split(qk, 2, axis=-1)
    log_f = jax.nn.log_sigmoid((c_f + c_f_bias).astype(jnp.float32))
    hc = mlstm_chunkwise(cq.reshape(b, s, H_C, DH_C), ck.reshape(b, s, H_C, DH_C),
                         c_v.reshape(b, s, H_C, DH_C), c_i, log_f)
    mu = jnp.mean(hc, axis=-1, keepdims=True)
    var = jnp.mean(jnp.square(hc - mu), axis=-1, keepdims=True)
    hc = ((hc - mu) * lax.rsqrt(var + EPS)).reshape(b, s, E_C) * c_hn_g.astype(jnp.float32)
    y_c = hc.astype(x.dtype) * jax.nn.sigmoid(c_o) * jax.nn.silu(c_z)

    y_d = rg_lru(causal_dwconv(d_x, d_conv_w, d_conv_b), d_wa, d_ba, d_wx, d_bx, d_lambda)
    y_d = y_d * jax.nn.silu(d_z)

    y_m = memory_attention(m_q, rmsnorm(mem, mem_norm_g), w_mkv) * jax.nn.silu(m_z)

    gates = jax.nn.sigmoid(gate_logits).reshape(b, s, N_BRANCH, D_MODEL)
    merged = jnp.zeros_like(x)
    for n, y in enumerate((y_a, y_b, y_c, y_d, y_m)):
        merged = merged + gates[:, :, n] * (y @ w_up[n])
    return x + merged @ w_out


def setup_inputs(seed: int = 0) -> dict:
    key = jax.random.key(seed)
    ks = jax.random.split(key, 32)
    f32 = jnp.float32

    def nrm(k, shape, scale):
        return jax.random.normal(k, shape, f32) * scale

    u = jax.random.uniform(ks[20], (DEPTH, E_D), f32, minval=0.9, maxval=0.999)
    p = u ** (1.0 / LRU_C)
    d_lambda = jnp.log(p) - jnp.log1p(-p)
    return {
        'x': nrm(ks[0], (BATCH, SEQ, D_MODEL), 1.0),
        'mem': nrm(ks[1], (BATCH, N_MEM, D_MODEL), 1.0),
        'norm_g': 1.0 + nrm(ks[2], (DEPTH, D_MODEL), 0.02),
        'w_in': nrm(ks[3], (DEPTH, D_MODEL, N_IN), D_MODEL ** -0.5),
        'b_in': nrm(ks[4], (DEPTH, N_IN), 0.02),
        'a_conv_w': nrm(ks[5], (DEPTH, CONF_WIDTH, E_A), CONF_WIDTH ** -0.5),
        'a_conv_b': nrm(ks[6], (DEPTH, E_A), 0.02),
        'a_ln_g': 1.0 + nrm(ks[7], (DEPTH, E_A), 0.02),
        'a_ln_b': nrm(ks[8], (DEPTH, E_A), 0.02),
        'c_conv_w': nrm(ks[9], (DEPTH, SHORT_CONV, 2 * E_C), SHORT_CONV ** -0.5),
        'c_conv_b': nrm(ks[10], (DEPTH, 2 * E_C), 0.02),
        'c_f_bias': jnp.linspace(3.0, 6.0, H_C, dtype=f32)[None, :] + nrm(ks[11], (DEPTH, H_C), 0.1),
        'c_hn_g': 1.0 + nrm(ks[12], (DEPTH, E_C), 0.02),
        'd_conv_w': nrm(ks[13], (DEPTH, SHORT_CONV, E_D), SHORT_CONV ** -0.5),
        'd_conv_b': nrm(ks[14], (DEPTH, E_D), 0.02),
        'd_wa': nrm(ks[15], (DEPTH, NB_D, BW_D, BW_D), BW_D ** -0.5),
        'd_ba': nrm(ks[16], (DEPTH, E_D), 0.02),
        'd_wx': nrm(ks[17], (DEPTH, NB_D, BW_D, BW_D), BW_D ** -0.5),
        'd_bx': nrm(ks[18], (DEPTH, E_D), 0.02),
        'd_lambda': d_lambda,
        'mem_norm_g': 1.0 + nrm(ks[21], (DEPTH, D_MODEL), 0.02),
        'w_mkv': nrm(ks[22], (DEPTH, D_MODEL, 2 * E_M), D_MODEL ** -0.5),
        'w_up': nrm(ks[23], (DEPTH, N_BRANCH, E_BRANCH, D_MODEL), E_BRANCH ** -0.5),
        'w_out': nrm(ks[24], (DEPTH, D_MODEL, D_MODEL), D_MODEL ** -0.5),
        'final_norm_g': 1.0 + nrm(ks[25], (D_MODEL,), 0.02),
    }


def reference(x, mem, norm_g, w_in, b_in, a_conv_w, a_conv_b, a_ln_g, a_ln_b,
              c_conv_w, c_conv_b, c_f_bias, c_hn_g, d_conv_w, d_conv_b,
              d_wa, d_ba, d_wx, d_bx, d_lambda, mem_norm_g, w_mkv, w_up, w_out,
              final_norm_g):
    for l in range(DEPTH):
        x = hybrid_layer(x, mem, norm_g[l], w_in[l], b_in[l], a_conv_w[l], a_conv_b[l],
                         a_ln_g[l], a_ln_b[l], c_conv_w[l], c_conv_b[l], c_f_bias[l], c_hn_g[l],
                         d_conv_w[l], d_conv_b[l], d_wa[l], d_ba[l], d_wx[l], d_bx[l], d_lambda[l],
                         mem_norm_g[l], w_mkv[l], w_up[l], w_out[l])
    return rmsnorm(x, final_norm_g)
```

```python
import math
from contextlib import ExitStack
import numpy as np
import concourse.bass as bass
import concourse.mybir as mybir
from concourse.bass_utils import run_bass_kernel_spmd

F32 = mybir.dt.float32
BF16 = mybir.dt.bfloat16
AF = mybir.ActivationFunctionType
ALU = mybir.AluOpType

ENGS = ("pe", "act", "dve", "pool", "sp")
EPOCH = 24000
NSLOT = {"sp": 8, "pool": 4}


class Sch:
    def __init__(self, nc, es):
        self.nc = nc
        self.es = es
        self.ops = {e: [] for e in ENGS}
        self.cnt = {e: 0 for e in ENGS}
        self.esem = {e: [] for e in ENGS}
        self.known = {e: {} for e in ENGS}
        self.lastw = {}
        self.readers = {}
        self.lasttok = {}
        self.dcnt = {q: 0 for q in NSLOT}
        self.dsem = {q: [self._newsem(f"d_{q}{i}") for i in range(n)] for q, n in NSLOT.items()}

    def _newsem(self, name):
        return self.es.enter_context(self.nc.semaphore(name))

    def _esem(self, e, ep):
        while len(self.esem[e]) <= ep:
            self.esem[e].append(self._newsem(f"e_{e}{len(self.esem[e])}"))
        return self.esem[e][ep]

    def _need(self, e, tok, waits):
        sem, val, src = tok
        if src == "pe" and e == "pe":
            return
        k = self.known[e]
        if k.get(id(sem), 0) >= val:
            return
        k[id(sem)] = val
        waits.append((sem, val))

    def _deps(self, e, r, w):
        waits = []
        for key in r:
            t = self.lastw.get(key)
            if t is not None:
                self._need(e, t, waits)
        for key in w:
            t = self.lastw.get(key)
            if t is not None:
                self._need(e, t, waits)
            for t in self.readers.get(key, ()):
                if t[2] == e:
                    continue
                self._need(e, t, waits)
        return waits

    def _commit(self, tok, r, w):
        for key in r:
            self.readers.setdefault(key, []).append(tok)
        for key in w:
            self.lastw[key] = tok
            self.readers[key] = []

    def op(self, e, fn, r=(), w=()):
        waits = self._deps(e, r, w)
        idx = self.cnt[e]
        self.cnt[e] += 1
        sem = self._esem(e, idx // EPOCH)
        val = idx % EPOCH + 1
        self.ops[e].append((waits, fn, (sem, 1)))
        tok = (sem, val, e)
        self.lasttok[e] = tok
        self._commit(tok, r, w)
        return tok

    def dma(self, q, out, in_, r=(), w=(), **kw):
        waits = self._deps(q, r, w)
        k = self.dcnt[q]
        self.dcnt[q] += 1
        n = NSLOT[q]
        sem = self.dsem[q][k % n]
        val = 16 * (k // n + 1)
        if k >= n:
            self._need(q, (sem, val - 16, "dma"), waits)
        fn = lambda eng: eng.dma_start(out=out, in_=in_, **kw)
        self.ops[q].append((waits, fn, (sem, 16)))
        tok = (sem, val, "dma")
        self._commit(tok, r, w)
        return tok

    def dma_tokens(self):
        toks = []
        for q, n in NSLOT.items():
            k = self.dcnt[q]
            for i in range(min(k, n)):
                j = ((k - 1 - i) // n) * n + i
                toks.append((self.dsem[q][i], 16 * (j // n + 1), "dma"))
        return toks

    def wait_all(self, e, toks):
        waits = []
        for t in toks:
            self._need(e, t, waits)
        if waits:
            self.ops[e].append((waits, None, None))

    def barrier(self):
        toks = list(self.lasttok.values()) + self.dma_tokens()
        for e in ENGS:
            self.wait_all(e, [t for t in toks if t[2] != e])

    def emit(self):
        nc = self.nc
        with nc.Block() as block:
            def run(e, eng):
                for waits, fn, inc in self.ops[e]:
                    for sem, val in waits:
                        eng.wait_ge(sem, val)
                    if fn is not None:
                        fn(eng).then_inc(inc[0], inc[1])

            @block.tensor
            def _(eng):
                run("pe", eng)

            @block.scalar
            def _(eng):
                run("act", eng)

            @block.vector
            def _(eng):
                run("dve", eng)

            @block.gpsimd
            def _(eng):
                run("pool", eng)

            @block.sync
            def _(eng):
                run("sp", eng)


D = 1024
E = 512
NMEM = 256
NIN = 13320
EPS = 1e-6
SEC = dict(a_val=0, a_glu=1, a_z=2, b_q=3, b_k=4, b_v=5, b_z=6, c_q=7, c_k=8, c_v=9,
           c_o=10, c_z=11, d_x=12, d_z=13, m_q=14, m_z=15)
GATE0 = 16 * 512
CIF0 = GATE0 + 5 * 1024


def col_layout():
    names = [("bin", 104), ("ng", 8), ("mng", 8), ("acw", 124), ("acb", 4), ("alg", 4), ("alb", 4),
             ("ccw", 32), ("ccb", 8), ("chg", 4), ("dcw", 16), ("dcb", 4), ("dba", 4), ("dbx", 4),
             ("dlam", 4), ("cib", 1), ("cfb", 1), ("cfb2", 1)]
    off = {}
    o = 0
    for n, w in names:
        off[n] = o
        o += w
    return off, o


COLS, NCOL = col_layout()
AR_BASE = 14336


class _Stop(Exception):
    pass


def build(S, NSEQ, DEPTH, dbg_names=(), stop=None):
    NT = S // 512
    NB = S // 128
    YW = 2 * S
    nc = bass.Bass("TRN2", target_bir_lowering=False)
    x_d = nc.dram_tensor("x", [NSEQ, S, D], F32, kind="ExternalInput").ap()
    mem_d = nc.dram_tensor("mem", [NSEQ, NMEM, D], F32, kind="ExternalInput").ap()
    win_d = nc.dram_tensor("w_in_r", [DEPTH, D, NIN], F32, kind="ExternalInput").ap()
    brow_d = nc.dram_tensor("brow", [DEPTH, 2, 512], F32, kind="ExternalInput").ap()
    cols_d = nc.dram_tensor("cols", [DEPTH, 128, NCOL], F32, kind="ExternalInput").ap()
    dw_d = nc.dram_tensor("dw", [DEPTH, 2, 4, 128, 128], F32, kind="ExternalInput").ap()
    wmkv_d = nc.dram_tensor("w_mkv", [DEPTH, D, 2 * E], F32, kind="ExternalInput").ap()
    wup_d = nc.dram_tensor("w_up", [DEPTH, 5, E, D], F32, kind="ExternalInput").ap()
    wout_d = nc.dram_tensor("w_out", [DEPTH, D, D], F32, kind="ExternalInput").ap()
    fcols_d = nc.dram_tensor("fcols", [128, 8], F32, kind="ExternalInput").ap()
    out_d = nc.dram_tensor("out", [NSEQ, S, D], F32, kind="ExternalOutput").ap()
    xs_d = nc.dram_tensor("xs", [NSEQ, 128, 8, S], F32, kind="Internal").ap()
    dbg_d = {}

    with ExitStack() as es:
        Sc = Sch(nc, es)

        def sb(name, shape, dt=F32):
            return es.enter_context(nc.sbuf_tensor("s_" + name, shape, dt))

        def ACT(out, in_, func, r, w, **kw):
            Sc.op("act", lambda e: e.activation(out=out, in_=in_, func=func, **kw), r, w)

        def TT(eng, out, in0, in1, op, r, w):
            Sc.op(eng, lambda e: e.tensor_tensor(out=out, in0=in0, in1=in1, op=op), r, w)

        def TS(eng, out, in0, s1, s2, op0, op1, r, w):
            if op1 is None:
                Sc.op(eng, lambda e: e.tensor_scalar(out=out, in0=in0, scalar1=s1, scalar2=None, op0=op0), r, w)
            else:
                Sc.op(eng, lambda e: e.tensor_scalar(out=out, in0=in0, scalar1=s1, scalar2=s2, op0=op0, op1=op1), r, w)

        def STT(out, in0, scalar, in1, op0, op1, r, w):
            Sc.op("dve", lambda e: e.scalar_tensor_tensor(out=out, in0=in0, scalar=scalar, in1=in1, op0=op0, op1=op1), r, w)

        def RECIP(out, in_, r, w):
            Sc.op("dve", lambda e: e.reciprocal(out=out, in_=in_), r, w)

        def MM(out, lhsT, rhs, start, stop, r, w):
            Sc.op("pe", lambda e: e.matmul(out, lhsT=lhsT, rhs=rhs, start=start, stop=stop), r, w)

        def TR(out, in_, idn, r, w):
            Sc.op("pe", lambda e: e.transpose(out, in_, idn), r, w)

        def CP(eng, out, in_, r, w):
            if eng == "act":
                Sc.op("act", lambda e: e.activation(out=out, in_=in_, func=AF.Copy), r, w)
            else:
                Sc.op(eng, lambda e: e.tensor_copy(out=out, in_=in_), r, w)

        def MEMSET(eng, ap, val, w):
            Sc.op(eng, lambda e: e.memset(ap, val), (), w)

        def ASEL(out, in_, pattern, cmp, cm, r, w, base=0):
            Sc.op("pool", lambda e: e.affine_select(out=out, in_=in_, pattern=pattern, compare_op=cmp, fill=0.0,
                                                    base=base, channel_multiplier=cm), r, w)

        def SCAN(out, d0, d1, init, op0, op1, r, w):
            Sc.op("dve", lambda e: e.tensor_tensor_scan(out=out, data0=d0, data1=d1, initial=init, op0=op0, op1=op1), r, w)

        def DMA(out, in_, r, w, q="sp", **kw):
            return Sc.dma(q, out, in_, r, w, **kw)

        def debug(name, ap, shape, key, dt=F32):
            if name not in dbg_names:
                return
            d = nc.dram_tensor("dbg_" + name, list(shape), dt, kind="ExternalOutput").ap()
            dbg_d[name] = d
            DMA(d, ap, [key] if not isinstance(key, list) else key, [("dbgd", name)])

        ident = sb("ident", [128, 128])
        identb = sb("identb", [128, 128], BF16)
        onesf = sb("onesf", [128, 128])
        onesD = sb("onesD", [128, 128])
        onesE = sb("onesE", [128, 128])
        sel4 = sb("sel4", [4, 4, 128])
        cols = sb("cols", [128, NCOL])
        fcols = sb("fcols", [128, 8])
        epsc = sb("epsc", [128, 1])
        hT = sb("hT", [128, 8, S], BF16)
        WST_N = 2
        wst = [sb(f"wst{i}", [128, 8, 256]) for i in range(WST_N)]
        AR = AR_BASE + 5 * YW
        arena = sb("arena", [128, AR])
        NPS = 6
        psb = [es.enter_context(nc.psum_tensor(f"ps{i}", [128, 512], F32)) for i in range(NPS)]
        psh = [es.enter_context(nc.psum_tensor(f"psh{i}", [128, 1024], BF16)) for i in range(2)]
        pskey = {id(t): ("ps", i) for i, t in enumerate(psb)}

        Y = [None] * 5
        for pos, n in enumerate((4, 3, 0, 1, 2)):
            a = AR_BASE + pos * YW
            Y[n] = arena[:, a:a + YW].bitcast(BF16).rearrange("p (a b) -> p a b", a=4)

        st = dict(ar=0, lim=AR_BASE, ps=0, psz=0, psh=0, wst=0, cast=0, ev=0, phase=0)
        tmpviews = {}

        def carve(name, free, dt=F32):
            n = int(np.prod(free))
            words = n if dt == F32 else (n + 1) // 2
            words = (words + 7) // 8 * 8
            a = st["ar"]
            assert a + words <= st["lim"], (name, a, words, st["lim"])
            st["ar"] = a + words
            v = arena[:, a:a + words]
            if dt != F32:
                v = v.bitcast(dt)
            v = v[:, 0:n]
            if len(free) == 2:
                v = v.rearrange("p (a b) -> p a b", a=free[0])
            elif len(free) == 3:
                v = v.rearrange("p (a b c) -> p a b c", a=free[0], b=free[1])
            return v

        def T(name, free, dt=F32):
            k = (name, st["phase"])
            if k not in tmpviews:
                tmpviews[k] = carve(name, free, dt)
            return tmpviews[k]

        def new_phase(extra=0):
            Sc.barrier()
            st["ar"] = 0
            st["lim"] = AR_BASE + extra * YW
            st["phase"] += 1

        def PS(pool="all"):
            if pool == "all":
                i = st["ps"] % NPS
                st["ps"] += 1
            else:
                i = st["psz"] % 4
                st["psz"] += 1
            return psb[i], ("ps", i)

        def PSH():
            i = st["psh"] % 2
            st["psh"] += 1
            return psh[i], ("psh", i)

        def ev_eng():
            st["ev"] += 1
            return "act" if st["ev"] % 2 else "dve"

        def onesrow(p, n):
            return onesf[0:p, 0:1].to_broadcast([p, n])

        MEMSET("pool", onesf[:], 1.0, ["onesf"])
        MEMSET("pool", onesD[:], 1.0 / D, ["onesD"])
        MEMSET("pool", onesE[:], 1.0 / E, ["onesE"])
        MEMSET("pool", epsc[:], EPS, ["epsc"])
        ASEL(ident[:], onesf[:], [[-1, 128]], ALU.is_equal, 1, ["onesf"], ["ident"])
        CP("pool", identb[:], ident[:], ["ident"], ["identb"])
        for h in range(4):
            ASEL(sel4[:, h, :], onesf[0:4, :], [[0, 128]], ALU.is_equal, 1, ["onesf"], ["sel4"], base=-h)
        DMA(fcols[:], fcols_d, [], ["fcols"])

        def col(name, i=0):
            o = COLS[name] + i
            return cols[:, o:o + 1]

        def load_w(src2d, K, ncols, dst, dkey):
            kc = K // 128
            c0 = 0
            while c0 < ncols:
                n = min(256, ncols - c0)
                i = st["wst"] % WST_N
                st["wst"] += 1
                stg = wst[i][:, 0:kc, 0:n]
                DMA(stg, src2d[:, c0:c0 + n].rearrange("(kc p) n -> p kc n", p=128), [], [("wst", i)])
                st["cast"] += 1
                eng = "pool" if st["cast"] % 2 else "act"
                CP(eng, dst[:, :, c0:c0 + n], stg, [("wst", i)], [dkey])
                c0 += n

        def rmsnorm_tile(xf, xkey, sq, sqkey, out_fn):
            ACT(sq, xf, AF.Square, [xkey], [sqkey])
            ps, pk = PS()
            for kc in range(8):
                MM(ps[:, 0:512], onesD[:], sq[:, kc, :], kc == 0, kc == 7, ["onesD", sqkey], [pk])
            rstd = T("rn_rstd", [512])
            ACT(rstd, ps[:, 0:512], AF.Sqrt, [pk, "epsc"], ["rn_rstd"], bias=epsc[:, 0:1])
            RECIP(rstd, rstd, ["rn_rstd"], ["rn_rstd"])
            for kc in range(8):
                out_fn(kc, rstd, "rn_rstd")

        def xskeys(seq, tt):
            return [("xs", seq, c, tt) for c in range(8)]

        def main_body():
          for seq in range(NSEQ):
            new_phase(5)
            for tb in range(NB):
                j = tb % 2
                xt = T(f"xin{j}", [D])
                xo = T(f"xout{j}", [8, 128])
                DMA(xt, x_d[seq, tb * 128:(tb + 1) * 128, :], [], [("xin", j)])
                for half in range(2):
                    ps, pk = PS()
                    for q in range(4):
                        kc = half * 4 + q
                        TR(ps[:, q * 128:(q + 1) * 128], xt[:, kc * 128:(kc + 1) * 128], ident[:], [("xin", j), "ident"], [pk])
                    CP(ev_eng(), xo[:, half * 4:(half + 1) * 4, :], ps[:, 0:512].rearrange("p (a b) -> p a b", a=4), [pk], [("xout", j)])
                DMA(xs_d[seq, :, :, tb * 128:(tb + 1) * 128], xo, [("xout", j)], xskeys(seq, tb // 4))

            for l in range(DEPTH):
                new_phase(5)
                DMA(cols[:], cols_d[l], [], ["cols"])
                for tt in range(NT):
                    j = tt % 2
                    xf = T(f"xf{j}", [8, 512])
                    sq = T("rn_sq", [8, 512])
                    DMA(xf, xs_d[seq, :, :, tt * 512:(tt + 1) * 512], xskeys(seq, tt), [("xf", j)])

                    def mk_h(kc, rstd, rkey, xf=xf, tt=tt, j=j):
                        STT(hT[:, kc, tt * 512:(tt + 1) * 512], xf[:, kc, :], col("ng", kc), rstd, ALU.mult, ALU.mult,
                            [("xf", j), "cols", rkey], ["hT"])
                    rmsnorm_tile(xf, ("xf", j), sq, "rn_sq", mk_h)
                debug(f"hT{l}", hT[:], [128, 8, S], "hT", BF16)
                if stop == "hT":
                    raise _Stop()

                def wsec(name):
                    return win_d[l, :, SEC[name] * 512:(SEC[name] + 1) * 512]

                def bcol(name, cc):
                    return col("bin", SEC[name] * 4 + cc)

                def proj(ps, wt, wkey, cc, tsl):
                    for kc in range(8):
                        MM(ps[:, 0:512], wt[:, kc, cc * 128:(cc + 1) * 128], hT[:, kc, tsl], kc == 0, kc == 7, [wkey, "hT"], [pskey[id(ps)]])

                new_phase(4)
                wC = [T(f"wC{i}", [8, 512], BF16) for i in range(2)]
                wci = T("wci", [8, 8], BF16)
                Vc = T("Vc", [NB, 4, 130], BF16)
                browc = T("browC", [512])
                Gt = T("Gt", [S])
                TSm = T("TSm", [NB, 96])
                expnm = T("expnm", [NB, 4])
                NGL = T("NGL", [4, NB + 1])
                PGL = T("PGL", [4, NB + 1])
                dec = T("decC", [4, NB])
                wint = T("wint", [NB, 4])
                wsta = T("wsta", [NB, 4])
                qC = T("qC", [4, S], BF16)
                kC = T("kC", [4, S], BF16)
                CTf = T("CTf", [4, 130])
                CTb = T("CTb", [4, 130], BF16)
                WT = [T(f"WTc{i}", [128]) for i in range(2)]
                STb = [T(f"STc{i}", [128], BF16) for i in range(2)]
                tmpi = [T(f"tmpiC{i}", [130]) for i in range(2)]
                nd = [T(f"ndC{i}", [130]) for i in range(2)]
                kw = [T(f"kwC{i}", [128], BF16) for i in range(2)]
                hn = [T(f"hnC{i}", [128]) for i in range(2)]
                sml = [T(f"smlC{i}", [16]) for i in range(2)]
                gtmp = [T(f"gtmpC{i}", [512]) for i in range(2)]
                mark = st["ar"]
                ibt = T("ibt", [S])
                Ft = T("Ft", [S])
                stk = T("stk", [S])

                DMA(browc, brow_d[l, 1, :].partition_broadcast(128), [], ["browC"])
                load_w(wsec("c_v"), D, 512, wC[0], "wC0")
                load_w(win_d[l, :, CIF0:CIF0 + 8], D, 8, wci, "wci")
                MEMSET("pool", Vc[:, :, :, 128:130], 1.0, ["Vc"])
                for tb in range(NB):
                    ps, pk = PS()
                    for kc in range(8):
                        MM(ps[:, 0:512], hT[:, kc, tb * 128:(tb + 1) * 128], wC[0][:, kc, :], kc == 0, kc == 7, ["wC0", "hT"], [pk])
                    TT("dve", Vc[:, tb, :, 0:128], ps[:, 0:512].rearrange("p (a b) -> p a b", a=4),
                       browc.rearrange("p (a b) -> p a b", a=4), ALU.add, [pk, "browC"], ["Vc"])
                MEMSET("pool", stk, 0.0, ["stk"])
                for tt in range(NT):
                    tsl = slice(tt * 512, (tt + 1) * 512)
                    ps, pk = PS()
                    for kc in range(8):
                        MM(ps[0:4, 0:512], wci[:, kc, 0:4], hT[:, kc, tsl], kc == 0, kc == 7, ["wci", "hT"], [pk])
                    ACT(ibt[0:4, tsl], ps[0:4, 0:512], AF.Identity, [pk, "cols"], ["ibt"], bias=cols[0:4, COLS["cib"]:COLS["cib"] + 1])
                    ps2, pk2 = PS()
                    for kc in range(8):
                        MM(ps2[0:4, 0:512], wci[:, kc, 4:8], hT[:, kc, tsl], kc == 0, kc == 7, ["wci", "hT"], [pk2])
                    ACT(Ft[0:4, tsl], ps2[0:4, 0:512], AF.Identity, [pk2, "cols"], ["Ft"], bias=cols[0:4, COLS["cfb"]:COLS["cfb"] + 1])
                    ACT(Ft[0:4, tsl], Ft[0:4, tsl], AF.Identity, ["Ft", "cols"], ["Ft"], bias=cols[0:4, COLS["cfb2"]:COLS["cfb2"] + 1])
                    ACT(Ft[0:4, tsl], Ft[0:4, tsl], AF.Exp, ["Ft"], ["Ft"], scale=-1.0)
                    ACT(Ft[0:4, tsl], Ft[0:4, tsl], AF.Ln, ["Ft"], ["Ft"], bias=1.0)
                SCAN(Gt[0:4, :], onesrow(4, S), Ft[0:4, :], 0.0, ALU.mult, ALU.subtract, ["Ft", "onesf"], ["Gt"])
                TT("dve", ibt[0:4, :], ibt[0:4, :], Gt[0:4, :], ALU.subtract, ["ibt", "Gt"], ["ibt"])
                SCAN(Ft[0:4, :], ibt[0:4, :], ibt[0:4, :], 0.0, ALU.max, ALU.max, ["ibt"], ["Ft"])
                TT("dve", Gt[0:4, :], Gt[0:4, :], Ft[0:4, :], ALU.add, ["Gt", "Ft"], ["Gt"])
                TS("dve", stk[32:36, :], Gt[0:4, :], -1.0, None, ALU.mult, None, ["Gt"], ["stk"])
                TS("dve", Gt[0:4, :], Ft[0:4, :], -1.0, None, ALU.mult, None, ["Ft"], ["Gt"])
                CP("dve", stk[64:68, :], Gt[0:4, :], ["Gt"], ["stk"])
                TS("dve", stk[0:4, :], ibt[0:4, :], math.log(128.0 ** -0.5), None, ALU.add, None, ["ibt"], ["stk"])
                for tb in range(NB):
                    ps, pk = PS()
                    TR(ps[:, 0:96], stk[0:96, tb * 128:(tb + 1) * 128], ident[0:96, 0:96], ["stk", "ident"], [pk])
                    CP(ev_eng(), TSm[:, tb, :], ps[:, 0:96], [pk], ["TSm"])
                ACT(expnm, TSm[:, :, 32:36], AF.Exp, ["TSm"], ["expnm"])
                MEMSET("pool", NGL, 0.0, ["NGL"])
                for h in range(4):
                    ps, pk = PS()
                    MM(ps[:, 0:NB], sel4[:, h, :], Gt[0:4, 127::128], True, True, ["sel4", "Gt"], [pk])
                    CP("dve", NGL[:, h, 1:NB + 1], ps[:, 0:NB], [pk], ["NGL"])
                TS("dve", PGL, NGL, -1.0, None, ALU.mult, None, ["NGL"], ["PGL"])
                TT("dve", dec, NGL[:, :, 1:NB + 1], NGL[:, :, 0:NB], ALU.subtract, ["NGL"], ["decC"])
                ACT(dec, dec, AF.Exp, ["decC"], ["decC"])
                for h in range(4):
                    for tb in range(NB):
                        ACT(wint[:, tb, h:h + 1], TSm[:, tb, 64 + h:65 + h], AF.Exp, ["TSm", "PGL"], ["wint"], bias=PGL[:, h, tb:tb + 1])
                        ACT(wsta[:, tb, h:h + 1], TSm[:, tb, h:h + 1], AF.Exp, ["TSm", "NGL"], ["wsta"], bias=NGL[:, h, tb + 1:tb + 2])
                debug(f"tsm{l}", TSm, [128, NB, 96], "TSm")
                debug(f"wint{l}", wint, [128, NB, 4], "wint")
                debug(f"wsta{l}", wsta, [128, NB, 4], "wsta")
                if stop == "Cprep":
                    raise _Stop()
                Sc.barrier()
                st["ar"] = mark
                cpad = T("cpad", [3 + S])
                cacc = [T(f"cacc{i}", [S]) for i in range(2)]
                load_w(wsec("c_q"), D, 512, wC[1], "wC1")
                load_w(wsec("c_k"), D, 512, wC[0], "wC0")
                MEMSET("pool", cpad[:, 0:3], 0.0, ["cpad"])
                it = 0
                for which, wt, wk, dst, sname in ((0, wC[1], "wC1", qC, "c_q"), (1, wC[0], "wC0", kC, "c_k")):
                    for h in range(4):
                        i = it % 2
                        it += 1
                        for tt in range(NT):
                            tsl = slice(tt * 512, (tt + 1) * 512)
                            ps, pk = PS()
                            proj(ps, wt, wk, h, tsl)
                            ACT(cpad[:, 3 + tt * 512:3 + (tt + 1) * 512], ps[:, 0:512], AF.Identity, [pk, "cols"], ["cpad"],
                                bias=bcol(sname, h))
                        ch = which * 4 + h
                        w0 = COLS["ccw"] + ch * 4
                        TS("dve", cacc[i], cpad[:, 0:S], cols[:, w0:w0 + 1], col("ccb", ch), ALU.mult, ALU.add,
                           ["cpad", "cols"], [("cacc", i)])
                        for jt in range(1, 4):
                            STT(cacc[i], cpad[:, jt:jt + S], cols[:, w0 + jt:w0 + jt + 1], cacc[i], ALU.mult, ALU.add,
                                ["cpad", "cols", ("cacc", i)], [("cacc", i)])
                        ACT(dst[:, h, :], cacc[i], AF.Silu, [("cacc", i)], [("qkC", which)])
                debug(f"qc{l}", qC, [128, 4, S], ("qkC", 0), BF16)
                debug(f"kc{l}", kC, [128, 4, S], ("qkC", 1), BF16)
                if stop == "Cqk":
                    raise _Stop()
                load_w(wsec("c_o"), D, 512, wC[1], "wC1")
                load_w(wsec("c_z"), D, 512, wC[0], "wC0")
                for h in range(4):
                    for tt in range(NT):
                        tsl = slice(tt * 512, (tt + 1) * 512)
                        j = tt % 2
                        ps, pk = PS()
                        proj(ps, wC[1], "wC1", h, tsl)
                        ACT(gtmp[j], ps[:, 0:512], AF.Sigmoid, [pk, "cols"], [("gtmpC", j)], bias=bcol("c_o", h))
                        ps2, pk2 = PS()
                        proj(ps2, wC[0], "wC0", h, tsl)
                        ACT(Y[2][:, h, tsl], ps2[:, 0:512], AF.Silu, [pk2, "cols"], [("Y", 2)], bias=bcol("c_z", h))
                        TT("dve", Y[2][:, h, tsl], Y[2][:, h, tsl], gtmp[j], ALU.mult, [("Y", 2), ("gtmpC", j)], [("Y", 2)])
                MEMSET("pool", CTf, 0.0, [("CTf", h) for h in range(4)])
                MEMSET("pool", CTb, 0.0, [("CT", h) for h in range(4)])
                it = 0
                for tb in range(NB):
                    bsl = slice(tb * 128, (tb + 1) * 128)
                    for h in range(4):
                        j = it % 2
                        it += 1
                        ck = ("CT", h)
                        ps, pk = PS()
                        MM(ps[:, 0:128], kC[:, h, bsl], qC[:, h, bsl], True, True, [("qkC", 0), ("qkC", 1)], [pk])
                        psg, pkg = PS()
                        MM(psg[:, 0:128], sel4[:, h, :], Gt[0:4, bsl], True, True, ["sel4", "Gt"], [pkg])
                        ACT(WT[j], psg[:, 0:128], AF.Exp, [pkg, "TSm"], [("WTc", j)], bias=TSm[:, tb, h:h + 1])
                        ASEL(WT[j], WT[j], [[1, 128]], ALU.is_ge, -1, [("WTc", j)], [("WTc", j)])
                        TT("dve", STb[j], ps[:, 0:128], WT[j], ALU.mult, [pk, ("WTc", j)], [("STc", j)])
                        psn, pkn = PS()
                        MM(psn[:, 0:129], STb[j], Vc[:, tb, h, 0:129], True, True, [("STc", j), "Vc"], [pkn])
                        psi, pki = PS()
                        MM(psi[:, 0:129], qC[:, h, bsl], CTb[:, h, 0:129], True, True, [("qkC", 0), ck], [pki])
                        ACT(tmpi[j][:, 0:129], psi[:, 0:129], AF.Copy, [pki, "wint"], [("tmpiC", j)], scale=wint[:, tb, h:h + 1])
                        TT("dve", nd[j][:, 0:129], psn[:, 0:129], tmpi[j][:, 0:129], ALU.add, [pkn, ("tmpiC", j)], [("ndC", j)])
                        ACT(sml[j][:, 12:13], nd[j][:, 128:129], AF.Abs, [("ndC", j)], [("smlC", j)])
                        TS("dve", sml[j][:, 0:1], sml[j][:, 12:13], expnm[:, tb, h:h + 1], None, ALU.max, None,
                           [("smlC", j), "expnm"], [("smlC", j)])
                        RECIP(sml[j][:, 1:2], sml[j][:, 0:1], [("smlC", j)], [("smlC", j)])
                        TS("dve", nd[j][:, 0:128], nd[j][:, 0:128], sml[j][:, 1:2], None, ALU.mult, None, [("ndC", j), ("smlC", j)], [("ndC", j)])
                        Sc.op("dve", lambda e, j=j: e.bn_stats(out=sml[j][:, 2:8], in_=nd[j][:, 0:128]), [("ndC", j)], [("smlC", j)])
                        Sc.op("dve", lambda e, j=j: e.bn_aggr(out=sml[j][:, 8:10], in_=sml[j][:, 2:8]), [("smlC", j)], [("smlC", j)])
                        ACT(sml[j][:, 10:11], sml[j][:, 9:10], AF.Sqrt, [("smlC", j), "epsc"], [("smlC", j)], bias=epsc[:, 0:1])
                        RECIP(sml[j][:, 11:12], sml[j][:, 10:11], [("smlC", j)], [("smlC", j)])
                        TS("dve", hn[j], nd[j][:, 0:128], sml[j][:, 8:9], sml[j][:, 11:12], ALU.subtract, ALU.mult,
                           [("ndC", j), ("smlC", j)], [("hnC", j)])
                        pst, pkt = PS()
                        TR(pst[:, 0:128], hn[j], ident[:], [("hnC", j), "ident"], [pkt])
                        STT(Y[2][:, h, bsl], pst[:, 0:128], col("chg", h), Y[2][:, h, bsl], ALU.mult, ALU.mult, [pkt, "cols", ("Y", 2)], [("Y", 2)])
                        if tb < NB - 1:
                            ph, phk = PSH()
                            TR(ph[:, 0:128], kC[:, h, bsl], identb[:], [("qkC", 1), "identb"], [phk])
                            TS("dve", kw[j], ph[:, 0:128], wsta[:, tb, h:h + 1], None, ALU.mult, None, [phk, "wsta"], [("kwC", j)])
                            psu, pku = PS()
                            MM(psu[:, 0:129], kw[j], Vc[:, tb, h, 0:129], True, True, [("kwC", j), "Vc"], [pku])
                            STT(CTf[:, h, 0:129], CTf[:, h, 0:129], dec[:, h, tb:tb + 1], psu[:, 0:129], ALU.mult, ALU.add,
                                [("CTf", h), "decC", pku], [("CTf", h)])
                            CP("act", CTb[:, h, 0:129], CTf[:, h, 0:129], [("CTf", h)], [ck])
                debug(f"yc{l}", Y[2], [128, 4, S], ("Y", 2), BF16)
                if stop == "C":
                    raise _Stop()

                new_phase(3)
                wB = [T(f"wB{i}", [8, 512], BF16) for i in range(2)]
                Vb = T("Vb", [NB, 512], BF16)
                brow = T("browB", [512])
                qB = T("qB", [S], BF16)
                kB = T("kB", [S], BF16)
                zs2 = [T(f"zsB{i}", [S]) for i in range(2)]
                spb2 = [T(f"spB{i}", [S + 1]) for i in range(2)]
                lat2 = [T(f"latB{i}", [S]) for i in range(2)]
                wbf2 = [T(f"wbfB{i}", [S], BF16) for i in range(2)]
                itb = 0
                wT = [T(f"wTB{i}", [4, 128], BF16) for i in range(2)]
                ob = T("obB", [128])
                DMA(brow, brow_d[l, 0, :].partition_broadcast(128), [], ["browB"])
                load_w(wsec("b_v"), D, 512, wB[0], "wB0")
                for tb in range(NB):
                    ps, pk = PS()
                    for kc in range(8):
                        MM(ps[:, 0:512], hT[:, kc, tb * 128:(tb + 1) * 128], wB[0][:, kc, :], kc == 0, kc == 7, ["wB0", "hT"], [pk])
                    TT("dve", Vb[:, tb, :], ps[:, 0:512], brow, ALU.add, [pk, "browB"], ["Vb"])
                load_w(wsec("b_q"), D, 512, wB[1], "wB1")
                load_w(wsec("b_k"), D, 512, wB[0], "wB0")
                sc_b = 64.0 ** -0.5
                for pr in range(4):
                    for tt in range(NT):
                        tsl = slice(tt * 512, (tt + 1) * 512)
                        ps, pk = PS()
                        proj(ps, wB[1], "wB1", pr, tsl)
                        ACT(qB[:, tsl], ps[:, 0:512], AF.Identity, [pk, "cols"], ["qB"], bias=bcol("b_q", pr))
                        ps2, pk2 = PS()
                        proj(ps2, wB[0], "wB0", pr, tsl)
                        ACT(kB[:, tsl], ps2[:, 0:512], AF.Identity, [pk2, "cols"], ["kB"], bias=bcol("b_k", pr))
                    for qb in range(NB):
                        L = (qb + 1) * 128
                        bsl = slice(qb * 128, (qb + 1) * 128)
                        pso, pko = psb[4 + qb % 2], ("ps", 4 + qb % 2)
                        for hh in range(2):
                            jb = itb % 2
                            itb += 1
                            zs, spb, lat, wbf = zs2[jb], spb2[jb], lat2[jb], wbf2[jb]
                            kz, ksp, kla, kwb = ("zsB", jb), ("spB", jb), ("latB", jb), ("wbfB", jb)
                            pl = slice(hh * 64, (hh + 1) * 64)
                            nk = (L + 511) // 512
                            zps = []
                            for ki in range(nk):
                                n = min(512, L - ki * 512)
                                ps, pk = PS("z")
                                MM(ps[:, 0:n], qB[pl, bsl], kB[pl, ki * 512:ki * 512 + n], True, True, ["qB", "kB"], [pk])
                                zps.append((ps, pk, n))
                            for ki, (ps, pk, n) in enumerate(zps):
                                ksl = slice(ki * 512, ki * 512 + n)
                                ACT(spb[:, ksl], ps[:, 0:n], AF.Exp, [pk], [ksp], scale=sc_b)
                                ACT(zs[:, ksl], ps[:, 0:n], AF.Copy, [pk], [kz], scale=sc_b)
                            ACT(spb[:, 0:L], spb[:, 0:L], AF.Ln, [ksp], [ksp], bias=1.0)
                            ASEL(spb[:, qb * 128:qb * 128 + 129], spb[:, qb * 128:qb * 128 + 129], [[-1, 129]], ALU.is_gt, 1, [ksp], [ksp])
                            SCAN(lat[:, 0:L][:, ::-1], onesrow(128, L), spb[:, 1:L + 1][:, ::-1], 0.0, ALU.mult, ALU.add,
                                 [ksp, "onesf"], [kla])
                            TT("pool", zs[:, 0:L], zs[:, 0:L], spb[:, 0:L], ALU.subtract, [kz, ksp], [kz])
                            TT("pool", zs[:, 0:L], zs[:, 0:L], lat[:, 0:L], ALU.subtract, [kz, kla], [kz])
                            ACT(wbf[:, 0:L], zs[:, 0:L], AF.Exp, [kz], [kwb])
                            ASEL(wbf[:, bsl], wbf[:, bsl], [[-1, 128]], ALU.is_gt, 1, [kwb], [kwb])
                            nkb = qb + 1
                            for g0 in range(0, nkb, 4):
                                gn = min(4, nkb - g0)
                                ph, phk = PSH()
                                j = (g0 // 4) % 2
                                for q in range(gn):
                                    kb = g0 + q
                                    TR(ph[:, q * 128:(q + 1) * 128], wbf[:, kb * 128:(kb + 1) * 128], identb[:], [kwb, "identb"], [phk])
                                CP(ev_eng(), wT[j][:, 0:gn, :], ph[:, 0:gn * 128].rearrange("p (a b) -> p a b", a=gn), [phk], [("wTB", j)])
                                for q in range(gn):
                                    kb = g0 + q
                                    hc = (pr * 2 + hh) * 64
                                    MM(pso[:, hh * 64:(hh + 1) * 64], wT[j][:, q, :], Vb[:, kb, hc:hc + 64],
                                       kb == 0, kb == nkb - 1, [("wTB", j), "Vb"], [pko])
                        CP("act", ob, pso[:, 0:128], [pko], ["obB"])
                        pst, pkt = PS("z")
                        TR(pst[:, 0:128], ob, ident[:], ["obB", "ident"], [pkt])
                        CP("dve", Y[1][:, pr, bsl], pst[:, 0:128], [pkt], [("Y", 1)])
                debug(f"yb_pre{l}", Y[1], [128, 4, S], ("Y", 1), BF16)
                load_w(wsec("b_z"), D, 512, wB[1], "wB1")
                for pr in range(4):
                    for tt in range(NT):
                        tsl = slice(tt * 512, (tt + 1) * 512)
                        ps, pk = PS()
                        proj(ps, wB[1], "wB1", pr, tsl)
                        ACT(qB[:, tsl], ps[:, 0:512], AF.Silu, [pk, "cols"], ["qB"], bias=bcol("b_z", pr))
                        TT("dve", Y[1][:, pr, tsl], Y[1][:, pr, tsl], qB[:, tsl], ALU.mult, [("Y", 1), "qB"], [("Y", 1)])
                debug(f"yb{l}", Y[1], [128, 4, S], ("Y", 1), BF16)
                if stop == "B":
                    raise _Stop()

                new_phase(2)
                wA = [T(f"wA{i}", [8, 512], BF16) for i in range(2)]
                yconv = T("yconv", [4, S])
                upad = [T("upad0", [30 + S])] * 2
                sg = [T(f"sgA{i}", [512]) for i in range(2)]
                ysq = T("ysq", [4, 512])
                mean_s = T("meanA", [512])
                rstd_s = T("rstdA", [512])
                tn = [T(f"tnA{i}", [512]) for i in range(2)]
                za = [T(f"zaA{i}", [512]) for i in range(2)]
                load_w(wsec("a_val"), D, 512, wA[0], "wA0")
                load_w(wsec("a_glu"), D, 512, wA[1], "wA1")
                MEMSET("pool", upad[0][:, 0:30], 0.0, [("upad", 0)])
                for cc in range(4):
                    i = 0
                    for tt in range(NT):
                        tsl = slice(tt * 512, (tt + 1) * 512)
                        j = tt % 2
                        ps, pk = PS()
                        proj(ps, wA[1], "wA1", cc, tsl)
                        ACT(sg[j], ps[:, 0:512], AF.Sigmoid, [pk, "cols"], [("sgA", j)], bias=bcol("a_glu", cc))
                        ps2, pk2 = PS()
                        proj(ps2, wA[0], "wA0", cc, tsl)
                        STT(upad[i][:, 30 + tt * 512:30 + (tt + 1) * 512], ps2[:, 0:512], bcol("a_val", cc), sg[j],
                            ALU.add, ALU.mult, [pk2, "cols", ("sgA", j)], [("upad", i)])
                    acw0 = COLS["acw"] + cc * 31
                    TS("dve", yconv[:, cc, :], upad[i][:, 0:S], cols[:, acw0:acw0 + 1], col("acb", cc), ALU.mult, ALU.add,
                       [("upad", i), "cols"], [("yconv", cc)])
                    for jt in range(1, 31):
                        STT(yconv[:, cc, :], upad[i][:, jt:jt + S], cols[:, acw0 + jt:acw0 + jt + 1], yconv[:, cc, :],
                            ALU.mult, ALU.add, [("upad", i), "cols", ("yconv", cc)], [("yconv", cc)])
                debug(f"yconv{l}", yconv, [128, 4, S], [("yconv", c) for c in range(4)])
                load_w(wsec("a_z"), D, 512, wA[0], "wA0")
                for tt in range(NT):
                    tsl = slice(tt * 512, (tt + 1) * 512)
                    ACT(ysq, yconv[:, :, tsl], AF.Square, [("yconv", c) for c in range(4)], ["ysq"])
                    psm, pkm = PS()
                    for cc in range(4):
                        MM(psm[:, 0:512], onesE[:], yconv[:, cc, tsl], cc == 0, cc == 3, ["onesE", ("yconv", cc)], [pkm])
                    pss, pks = PS()
                    for cc in range(4):
                        MM(pss[:, 0:512], onesE[:], ysq[:, cc, :], cc == 0, cc == 3, ["onesE", "ysq"], [pks])
                    CP("act", mean_s, psm[:, 0:512], [pkm], ["meanA"])
                    TT("dve", rstd_s, mean_s, mean_s, ALU.mult, ["meanA"], ["rstdA"])
                    TT("dve", rstd_s, pss[:, 0:512], rstd_s, ALU.subtract, [pks, "rstdA"], ["rstdA"])
                    ACT(rstd_s, rstd_s, AF.Sqrt, ["rstdA", "epsc"], ["rstdA"], bias=epsc[:, 0:1])
                    RECIP(rstd_s, rstd_s, ["rstdA"], ["rstdA"])
                    for cc in range(4):
                        j = cc % 2
                        TT("pool", tn[j], yconv[:, cc, tsl], mean_s, ALU.subtract, [("yconv", cc), "meanA"], [("tnA", j)])
                        TT("dve", tn[j], tn[j], rstd_s, ALU.mult, [("tnA", j), "rstdA"], [("tnA", j)])
                        ACT(tn[j], tn[j], AF.Silu, [("tnA", j), "cols"], [("tnA", j)], scale=col("alg", cc), bias=col("alb", cc))
                        ps, pk = PS()
                        proj(ps, wA[0], "wA0", cc, tsl)
                        ACT(za[j], ps[:, 0:512], AF.Silu, [pk, "cols"], [("zaA", j)], bias=bcol("a_z", cc))
                        TT("dve", Y[0][:, cc, tsl], tn[j], za[j], ALU.mult, [("tnA", j), ("zaA", j)], [("Y", 0)])
                debug(f"ya{l}", Y[0], [128, 4, S], ("Y", 0), BF16)
                if stop == "A":
                    raise _Stop()

                new_phase(1)
                wD = [T(f"wD{i}", [8, 512], BF16) for i in range(2)]
                dwall = T("dwall", [8, 128])
                c1 = T("c1", [4])
                dpad = T("dpad", [3 + S])
                xc = T("xcD", [S])
                av = T("avD", [S])
                uv = T("uvD", [S])
                gi = [T(f"giD{i}", [512]) for i in range(2)]
                zd = [T(f"zdD{i}", [512]) for i in range(2)]
                DMA(dwall, dw_d[l].rearrange("g c p d -> p (g c) d"), [], ["dwall"])
                load_w(wsec("d_x"), D, 512, wD[0], "wD0")
                load_w(wsec("d_z"), D, 512, wD[1], "wD1")
                ACT(c1, cols[:, COLS["dlam"]:COLS["dlam"] + 4], AF.Exp, ["cols"], ["c1"], scale=-1.0)
                ACT(c1, c1, AF.Ln, ["c1"], ["c1"], bias=1.0)
                TS("dve", c1, c1, -8.0, None, ALU.mult, None, ["c1"], ["c1"])
                MEMSET("pool", dpad[:, 0:3], 0.0, ["dpad"])
                for cc in range(4):
                    for tt in range(NT):
                        tsl = slice(tt * 512, (tt + 1) * 512)
                        ps, pk = PS()
                        proj(ps, wD[0], "wD0", cc, tsl)
                        ACT(dpad[:, 3 + tt * 512:3 + (tt + 1) * 512], ps[:, 0:512], AF.Identity, [pk, "cols"], ["dpad"],
                            bias=bcol("d_x", cc))
                    w0 = COLS["dcw"] + cc * 4
                    TS("dve", xc, dpad[:, 0:S], cols[:, w0:w0 + 1], col("dcb", cc), ALU.mult, ALU.add, ["dpad", "cols"], ["xcD"])
                    for jt in range(1, 4):
                        STT(xc, dpad[:, jt:jt + S], cols[:, w0 + jt:w0 + jt + 1], xc, ALU.mult, ALU.add, ["dpad", "cols", "xcD"], ["xcD"])
                    for tt in range(NT):
                        tsl = slice(tt * 512, (tt + 1) * 512)
                        j = tt % 2
                        psa, pka = PS()
                        MM(psa[:, 0:512], dwall[:, 0 * 4 + cc, :], xc[:, tsl], True, True, ["dwall", "xcD"], [pka])
                        psx, pkx = PS()
                        MM(psx[:, 0:512], dwall[:, 1 * 4 + cc, :], xc[:, tsl], True, True, ["dwall", "xcD"], [pkx])
                        ACT(av[:, tsl], psa[:, 0:512], AF.Sigmoid, [pka, "cols"], ["avD"], bias=col("dba", cc))
                        ACT(av[:, tsl], av[:, tsl], AF.Exp, ["avD", "c1"], ["avD"], scale=c1[:, cc:cc + 1])
                        ACT(gi[j], psx[:, 0:512], AF.Sigmoid, [pkx, "cols"], [("giD", j)], bias=col("dbx", cc))
                        TT("pool", uv[:, tsl], av[:, tsl], av[:, tsl], ALU.mult, ["avD"], ["uvD"])
                        ACT(uv[:, tsl], uv[:, tsl], AF.Sqrt, ["uvD"], ["uvD"], scale=-1.0, bias=1.0)
                        TT("pool", gi[j], gi[j], xc[:, tsl], ALU.mult, [("giD", j), "xcD"], [("giD", j)])
                        TT("dve", uv[:, tsl], uv[:, tsl], gi[j], ALU.mult, ["uvD", ("giD", j)], ["uvD"])
                    SCAN(xc, av, uv, 0.0, ALU.mult, ALU.add, ["avD", "uvD", "xcD"], ["xcD"])
                    for tt in range(NT):
                        tsl = slice(tt * 512, (tt + 1) * 512)
                        j = tt % 2
                        ps, pk = PS()
                        proj(ps, wD[1], "wD1", cc, tsl)
                        ACT(zd[j], ps[:, 0:512], AF.Silu, [pk, "cols"], [("zdD", j)], bias=bcol("d_z", cc))
                        TT("dve", Y[3][:, cc, tsl], xc[:, tsl], zd[j], ALU.mult, ["xcD", ("zdD", j)], [("Y", 3)])
                debug(f"yd{l}", Y[3], [128, 4, S], ("Y", 3), BF16)
                if stop == "D":
                    raise _Stop()

                new_phase(0)
                memT = T("memT", [8, NMEM], BF16)
                mkT = T("mkT", [4, NMEM], BF16)
                mv = T("mv", [2, 512], BF16)
                wM = [T(f"wM{i}", [8, 512], BF16) for i in range(2)]
                mrs = T("mrs", [2])
                mq = T("memsq", [D])
                mxs = [T(f"memx{mt}", [D]) for mt in range(2)]
                qm = T("qm", [S], BF16)
                zm = T("zm", [S], BF16)
                pbuf = [T(f"pm{i}", [NMEM], BF16) for i in range(2)]
                pT = [T(f"pTm{i}", [2, 128], BF16) for i in range(2)]
                on = [T(f"onm{i}", [128]) for i in range(2)]
                sm = [T(f"smm{i}", [4]) for i in range(2)]
                for mt in range(2):
                    mx = mxs[mt]
                    DMA(mx, mem_d[seq, mt * 128:(mt + 1) * 128, :], [], [("memx", mt)])
                    ACT(mq, mx, AF.Square, [("memx", mt)], ["memsq", ("mrs", mt)], accum_out=mrs[:, mt:mt + 1])
                    TS("dve", mrs[:, mt:mt + 1], mrs[:, mt:mt + 1], 1.0 / D, EPS, ALU.mult, ALU.add, [("mrs", mt)], [("mrs", mt)])
                    ACT(mrs[:, mt:mt + 1], mrs[:, mt:mt + 1], AF.Sqrt, [("mrs", mt)], [("mrs", mt)])
                    RECIP(mrs[:, mt:mt + 1], mrs[:, mt:mt + 1], [("mrs", mt)], [("mrs", mt)])
                    TS("dve", mx, mx, mrs[:, mt:mt + 1], None, ALU.mult, None, [("memx", mt), ("mrs", mt)], [("memx", mt)])
                    for half in range(2):
                        ps, pk = PS()
                        for q in range(4):
                            kc = half * 4 + q
                            TR(ps[:, q * 128:(q + 1) * 128], mx[:, kc * 128:(kc + 1) * 128], ident[:], [("memx", mt), "ident"], [pk])
                        for q in range(4):
                            kc = half * 4 + q
                            ACT(memT[:, kc, mt * 128:(mt + 1) * 128], ps[:, q * 128:(q + 1) * 128], AF.Copy, [pk, "cols"], ["memT"],
                                scale=col("mng", kc))
                load_w(wmkv_d[l, :, 0:512], D, 512, wM[0], "wM0")
                load_w(wmkv_d[l, :, 512:1024], D, 512, wM[1], "wM1")
                for h in range(4):
                    ps, pk = PS()
                    for kc in range(8):
                        MM(ps[:, 0:NMEM], wM[0][:, kc, h * 128:(h + 1) * 128], memT[:, kc, :], kc == 0, kc == 7, ["wM0", "memT"], [pk])
                    CP(ev_eng(), mkT[:, h, :], ps[:, 0:NMEM], [pk], ["mkT"])
                for mt in range(2):
                    ps, pk = PS()
                    for kc in range(8):
                        MM(ps[:, 0:512], memT[:, kc, mt * 128:(mt + 1) * 128], wM[1][:, kc, :], kc == 0, kc == 7, ["wM1", "memT"], [pk])
                    CP(ev_eng(), mv[:, mt, :], ps[:, 0:512], [pk], ["mv"])
                load_w(wsec("m_q"), D, 512, wM[0], "wM0")
                load_w(wsec("m_z"), D, 512, wM[1], "wM1")
                sc_m = 128.0 ** -0.5
                for h in range(4):
                    for tt in range(NT):
                        tsl = slice(tt * 512, (tt + 1) * 512)
                        ps, pk = PS()
                        proj(ps, wM[0], "wM0", h, tsl)
                        ACT(qm[:, tsl], ps[:, 0:512], AF.Identity, [pk, "cols"], ["qm"], bias=bcol("m_q", h))
                        ps2, pk2 = PS()
                        proj(ps2, wM[1], "wM1", h, tsl)
                        ACT(zm[:, tsl], ps2[:, 0:512], AF.Silu, [pk2, "cols"], ["zm"], bias=bcol("m_z", h))
                    for tb in range(NB):
                        bsl = slice(tb * 128, (tb + 1) * 128)
                        j = tb % 2
                        ps, pk = PS()
                        MM(ps[:, 0:NMEM], qm[:, bsl], mkT[:, h, :], True, True, ["qm", "mkT"], [pk])
                        Sc.op("dve", lambda e, ps=ps, j=j: e.reduce_max(out=sm[j][:, 0:1], in_=ps[:, 0:NMEM], axis=mybir.AxisListType.X),
                              [pk], [("smm", j)])
                        TS("dve", sm[j][:, 1:2], sm[j][:, 0:1], -sc_m, None, ALU.mult, None, [("smm", j)], [("smm", j)])
                        ACT(pbuf[j], ps[:, 0:NMEM], AF.Exp, [pk, ("smm", j)], [("pm", j), ("smm", j)], scale=sc_m, bias=sm[j][:, 1:2],
                            accum_out=sm[j][:, 2:3])
                        ph, phk = PSH()
                        for mt in range(2):
                            TR(ph[:, mt * 128:(mt + 1) * 128], pbuf[j][:, mt * 128:(mt + 1) * 128], identb[:], [("pm", j), "identb"], [phk])
                        CP(ev_eng(), pT[j], ph[:, 0:256].rearrange("p (a b) -> p a b", a=2), [phk], [("pTm", j)])
                        pso, pko = PS()
                        for mt in range(2):
                            MM(pso[:, 0:128], pT[j][:, mt, :], mv[:, mt, h * 128:(h + 1) * 128], mt == 0, mt == 1, [("pTm", j), "mv"], [pko])
                        RECIP(sm[j][:, 3:4], sm[j][:, 2:3], [("smm", j)], [("smm", j)])
                        TS("dve", on[j], pso[:, 0:128], sm[j][:, 3:4], None, ALU.mult, None, [pko, ("smm", j)], [("onm", j)])
                        pst, pkt = PS()
                        TR(pst[:, 0:128], on[j], ident[:], [("onm", j), "ident"], [pkt])
                        TT("dve", Y[4][:, h, bsl], pst[:, 0:128], zm[:, bsl], ALU.mult, [pkt, "zm"], [("Y", 4)])
                debug(f"ym{l}", Y[4], [128, 4, S], ("Y", 4), BF16)
                if stop == "M":
                    raise _Stop()

                new_phase(0)
                mg = T("mg", [8, S], BF16)
                mark = st["ar"]
                RING = 4
                wgn = [T(f"wgn{i}", [8, 128], BF16) for i in range(RING)]
                wun = [T(f"wun{i}", [4, 128], BF16) for i in range(RING)]
                sgm = [T(f"sgm{i}", [512]) for i in range(2)]
                acc = T("accm", [NT, 512])
                order = [(c, n) for c in range(8) for n in range(5)]

                def ldm(idx):
                    c, n = order[idx]
                    i = idx % RING
                    g0 = GATE0 + (c * 5 + n) * 128
                    load_w(win_d[l, :, g0:g0 + 128], D, 128, wgn[i], ("wgn", i))
                    load_w(wup_d[l, n, :, c * 128:(c + 1) * 128], E, 128, wun[i], ("wun", i))
                for idx in range(RING - 1):
                    ldm(idx)
                it = 0
                for idx, (c, n) in enumerate(order):
                    i = idx % RING
                    if idx + RING - 1 < len(order):
                        ldm(idx + RING - 1)
                    for tt in range(NT):
                        tsl = slice(tt * 512, (tt + 1) * 512)
                        j = it % 2
                        it += 1
                        psg, pkg = PS()
                        for kc in range(8):
                            MM(psg[:, 0:512], wgn[i][:, kc, :], hT[:, kc, tsl], kc == 0, kc == 7, [("wgn", i), "hT"], [pkg])
                        ACT(sgm[j], psg[:, 0:512], AF.Sigmoid, [pkg, "cols"], [("sgm", j)], bias=col("bin", 64 + c * 5 + n))
                        psu, pku = PS()
                        for kc in range(4):
                            MM(psu[:, 0:512], wun[i][:, kc, :], Y[n][:, kc, tsl], kc == 0, kc == 3, [("wun", i), ("Y", n)], [pku])
                        if n == 0:
                            TT("dve", acc[:, tt, :], sgm[j], psu[:, 0:512], ALU.mult, [("sgm", j), pku], [("accm", tt)])
                        else:
                            TT("dve", sgm[j], sgm[j], psu[:, 0:512], ALU.mult, [("sgm", j), pku], [("sgm", j)])
                            if n < 4:
                                TT("pool", acc[:, tt, :], acc[:, tt, :], sgm[j], ALU.add, [("accm", tt), ("sgm", j)], [("accm", tt)])
                            else:
                                TT("pool", mg[:, c, tsl], acc[:, tt, :], sgm[j], ALU.add, [("accm", tt), ("sgm", j)], ["mg"])
                debug(f"mg{l}", mg, [128, 8, S], "mg", BF16)
                if stop == "merge":
                    raise _Stop()
                Sc.barrier()
                st["ar"] = mark
                st["lim"] = AR_BASE + 5 * YW
                wo = [T(f"wo{i}", [8, 128], BF16) for i in range(2)]
                xbuf = T("xbuf", [8, S])
                allxs = [("xs", seq, c, tt) for c in range(8) for tt in range(NT)]
                DMA(xbuf, xs_d[seq], allxs, ["xbuf"])
                load_w(wout_d[l, :, 0:128], D, 128, wo[0], ("wo", 0))
                Sc.barrier()
                for c in range(8):
                    i = c % 2
                    if c + 1 < 8:
                        load_w(wout_d[l, :, (c + 1) * 128:(c + 2) * 128], D, 128, wo[(c + 1) % 2], ("wo", (c + 1) % 2))
                    for tt in range(NT):
                        tsl = slice(tt * 512, (tt + 1) * 512)
                        ps, pk = PS()
                        for kc in range(8):
                            MM(ps[:, 0:512], wo[i][:, kc, :], mg[:, kc, tsl], kc == 0, kc == 7, [("wo", i), "mg"], [pk])
                        TT("dve", xbuf[:, c, tsl], xbuf[:, c, tsl], ps[:, 0:512], ALU.add, ["xbuf", pk], ["xbuf"])
                Sc.barrier()
                DMA(xs_d[seq], xbuf, ["xbuf"], allxs)
                if stop == "resid":
                    raise _Stop()

            new_phase(5)
            ot = [T(f"ot{i}", [D]) for i in range(2)]
            it = 0
            for tt in range(NT):
                j = tt % 2
                xf = T(f"xf{j}", [8, 512])
                yo = T(f"yo{j}", [8, 512])
                DMA(xf, xs_d[seq, :, :, tt * 512:(tt + 1) * 512], xskeys(seq, tt), [("xf", j)])

                def mk_o(kc, rstd, rkey, xf=xf, yo=yo, j=j):
                    STT(yo[:, kc, :], xf[:, kc, :], fcols[:, kc:kc + 1], rstd, ALU.mult, ALU.mult,
                        [("xf", j), "fcols", rkey], [("yo", j)])
                rmsnorm_tile(xf, ("xf", j), yo, ("yo", j), mk_o)
                for q4 in range(4):
                    jo = it % 2
                    it += 1
                    for half in range(2):
                        ps, pk = PS()
                        for q in range(4):
                            kc = half * 4 + q
                            TR(ps[:, q * 128:(q + 1) * 128], yo[:, kc, q4 * 128:(q4 + 1) * 128], ident[:], [("yo", j), "ident"], [pk])
                        CP(ev_eng(), ot[jo][:, half * 512:(half + 1) * 512], ps[:, 0:512], [pk], [("ot", jo)])
                    tb = tt * 4 + q4
                    DMA(out_d[seq, tb * 128:(tb + 1) * 128, :], ot[jo], [("ot", jo)], [("outd", seq, tb)])

        try:
            main_body()
        except _Stop:
            pass
        Sc.barrier()
        Sc.emit()
    return nc, dbg_d


def prep_weights(inp):
    DEPTH = inp["w_in"].shape[0]
    perm = np.concatenate([np.arange(0, 5120), np.arange(5128, 8200)] +
                          [8200 + n * 1024 + c * 128 + np.arange(128) for c in range(8) for n in range(5)] +
                          [np.arange(5120, 5128)])
    w_in_r = np.ascontiguousarray(np.asarray(inp["w_in"], np.float32)[:, :, perm])
    b_in_r = np.asarray(inp["b_in"], np.float32)[:, perm]
    cols = np.zeros((DEPTH, 128, NCOL), np.float32)

    def put(l, name, arr):
        cols[l, :, COLS[name]:COLS[name] + arr.shape[1]] = arr

    def pc(v):
        return np.asarray(v, np.float32).reshape(-1, 128).T

    brow = np.zeros((DEPTH, 2, 512), np.float32)
    dw = np.zeros((DEPTH, 2, 4, 128, 128), np.float32)
    for l in range(DEPTH):
        put(l, "bin", pc(b_in_r[l, :13312]))
        put(l, "ng", pc(inp["norm_g"][l]))
        put(l, "mng", pc(inp["mem_norm_g"][l]))
        acw = np.asarray(inp["a_conv_w"][l], np.float32)
        put(l, "acw", acw.T.reshape(4, 128, 31).transpose(1, 0, 2).reshape(128, 124))
        put(l, "acb", pc(inp["a_conv_b"][l]))
        put(l, "alg", pc(inp["a_ln_g"][l]))
        put(l, "alb", pc(inp["a_ln_b"][l]))
        ccw = np.asarray(inp["c_conv_w"][l], np.float32)
        put(l, "ccw", ccw.T.reshape(8, 128, 4).transpose(1, 0, 2).reshape(128, 32))
        put(l, "ccb", pc(inp["c_conv_b"][l]))
        put(l, "chg", pc(inp["c_hn_g"][l]))
        dcw = np.asarray(inp["d_conv_w"][l], np.float32)
        put(l, "dcw", dcw.T.reshape(4, 128, 4).transpose(1, 0, 2).reshape(128, 16))
        put(l, "dcb", pc(inp["d_conv_b"][l]))
        put(l, "dba", pc(inp["d_ba"][l]))
        put(l, "dbx", pc(inp["d_bx"][l]))
        put(l, "dlam", pc(inp["d_lambda"][l]))
        cols[l, 0:4, COLS["cib"]] = b_in_r[l, 13312:13316]
        cols[l, 0:4, COLS["cfb"]] = b_in_r[l, 13316:13320]
        cols[l, 0:4, COLS["cfb2"]] = np.asarray(inp["c_f_bias"][l], np.float32)
        brow[l, 0] = b_in_r[l, SEC["b_v"] * 512:(SEC["b_v"] + 1) * 512]
        brow[l, 1] = b_in_r[l, SEC["c_v"] * 512:(SEC["c_v"] + 1) * 512]
        for g, nm in enumerate(("d_wa", "d_wx")):
            wgt = np.asarray(inp[nm][l], np.float32)
            for cc in range(4):
                dw[l, g, cc, 0:64, 0:64] = wgt[2 * cc]
                dw[l, g, cc, 64:128, 64:128] = wgt[2 * cc + 1]
    fcols = np.ascontiguousarray(pc(inp["final_norm_g"]))
    return dict(w_in_r=w_in_r, brow=brow, cols=cols, dw=dw,
                w_mkv=np.ascontiguousarray(np.asarray(inp["w_mkv"], np.float32)),
                w_up=np.ascontiguousarray(np.asarray(inp["w_up"], np.float32)),
                w_out=np.ascontiguousarray(np.asarray(inp["w_out"], np.float32)),
                fcols=fcols)


_NC_CACHE = {}


def kernel(**inputs):
    x = np.asarray(inputs["x"], np.float32)
    mem = np.asarray(inputs["mem"], np.float32)
    B, S, _ = x.shape
    DEPTH = inputs["w_in"].shape[0]
    ncores = 8
    nseq = B // ncores
    wts = prep_weights(inputs)
    key = (S, nseq, DEPTH)
    if key not in _NC_CACHE:
        _NC_CACHE[key] = build(S, nseq, DEPTH)[0]
    nc = _NC_CACHE[key]
    in_maps = []
    for c in range(ncores):
        m = dict(wts)
        m["x"] = np.ascontiguousarray(x[c * nseq:(c + 1) * nseq])
        m["mem"] = np.ascontiguousarray(mem[c * nseq:(c + 1) * nseq])
        in_maps.append(m)
    res = run_bass_kernel_spmd(nc, in_maps, core_ids=list(range(ncores)))
    out = np.concatenate([np.asarray(r["out"], np.float32) for r in res.results], axis=0)
    return out
```

```python
import math
from contextlib import ExitStack
import numpy as np
import concourse.bass as bass
import concourse.mybir as mybir
from concourse.bass_utils import run_bass_kernel_spmd

F32 = mybir.dt.float32
BF16 = mybir.dt.bfloat16
AF = mybir.ActivationFunctionType
ALU = mybir.AluOpType

ENGS = ("pe", "act", "dve", "pool", "sp")
EPOCH = 24000
NSLOT = {"sp": 8, "pool": 4}


class Sch:
    def __init__(self, nc, es):
        self.nc = nc
        self.es = es
        self.ops = {e: [] for e in ENGS}
        self.cnt = {e: 0 for e in ENGS}
        self.esem = {e: [] for e in ENGS}
        self.known = {e: {} for e in ENGS}
        self.lastw = {}
        self.readers = {}
        self.lasttok = {}
        self.dcnt = {q: 0 for q in NSLOT}
        self.dsem = {q: [self._newsem(f"d_{q}{i}") for i in range(n)] for q, n in NSLOT.items()}

    def _newsem(self, name):
        return self.es.enter_context(self.nc.semaphore(name))

    def _esem(self, e, ep):
        while len(self.esem[e]) <= ep:
            self.esem[e].append(self._newsem(f"e_{e}{len(self.esem[e])}"))
        return self.esem[e][ep]

    def _need(self, e, tok, waits):
        sem, val, src = tok
        if src == "pe" and e == "pe":
            return
        k = self.known[e]
        if k.get(id(sem), 0) >= val:
            return
        k[id(sem)] = val
        waits.append((sem, val))

    def _deps(self, e, r, w):
        waits = []
        for key in r:
            t = self.lastw.get(key)
            if t is not None:
                self._need(e, t, waits)
        for key in w:
            t = self.lastw.get(key)
            if t is not None:
                self._need(e, t, waits)
            for t in self.readers.get(key, ()):
                if t[2] == e:
                    continue
                self._need(e, t, waits)
        return waits

    def _commit(self, tok, r, w):
        for key in r:
            self.readers.setdefault(key, []).append(tok)
        for key in w:
            self.lastw[key] = tok
            self.readers[key] = []

    def op(self, e, fn, r=(), w=()):
        waits = self._deps(e, r, w)
        idx = self.cnt[e]
        self.cnt[e] += 1
        sem = self._esem(e, idx // EPOCH)
        val = idx % EPOCH + 1
        self.ops[e].append((waits, fn, (sem, 1)))
        tok = (sem, val, e)
        self.lasttok[e] = tok
        self._commit(tok, r, w)
        return tok

    def dma(self, q, out, in_, r=(), w=(), **kw):
        waits = self._deps(q, r, w)
        k = self.dcnt[q]
        self.dcnt[q] += 1
        n = NSLOT[q]
        sem = self.dsem[q][k % n]
        val = 16 * (k // n + 1)
        if k >= n:
            self._need(q, (sem, val - 16, "dma"), waits)
        fn = lambda eng: eng.dma_start(out=out, in_=in_, **kw)
        self.ops[q].append((waits, fn, (sem, 16)))
        tok = (sem, val, "dma")
        self._commit(tok, r, w)
        return tok

    def dma_tokens(self):
        toks = []
        for q, n in NSLOT.items():
            k = self.dcnt[q]
            for i in range(min(k, n)):
                j = ((k - 1 - i) // n) * n + i
                toks.append((self.dsem[q][i], 16 * (j // n + 1), "dma"))
        return toks

    def wait_all(self, e, toks):
        waits = []
        for t in toks:
            self._need(e, t, waits)
        if waits:
            self.ops[e].append((waits, None, None))

    def barrier(self):
        toks = list(self.lasttok.values()) + self.dma_tokens()
        for e in ENGS:
            self.wait_all(e, [t for t in toks if t[2] != e])

    def emit(self):
        nc = self.nc
        with nc.Block() as block:
            def run(e, eng):
                for waits, fn, inc in self.ops[e]:
                    for sem, val in waits:
                        eng.wait_ge(sem, val)
                    if fn is not None:
                        fn(eng).then_inc(inc[0], inc[1])

            @block.tensor
            def _(eng):
                run("pe", eng)

            @block.scalar
            def _(eng):
                run("act", eng)

            @block.vector
            def _(eng):
                run("dve", eng)

            @block.gpsimd
            def _(eng):
                run("pool", eng)

            @block.sync
            def _(eng):
                run("sp", eng)


D = 1024
E = 512
NMEM = 256
NIN = 13320
EPS = 1e-6
SEC = dict(a_val=0, a_glu=1, a_z=2, b_q=3, b_k=4, b_v=5, b_z=6, c_q=7, c_k=8, c_v=9,
           c_o=10, c_z=11, d_x=12, d_z=13, m_q=14, m_z=15)
GATE0 = 16 * 512
CIF0 = GATE0 + 5 * 1024


def col_layout():
    names = [("bin", 104), ("ng", 8), ("mng", 8), ("acw", 124), ("acb", 4), ("alg", 4), ("alb", 4),
             ("ccw", 32), ("ccb", 8), ("chg", 4), ("dcw", 16), ("dcb", 4), ("dba", 4), ("dbx", 4),
             ("dlam", 4), ("cib", 1), ("cfb", 1), ("cfb2", 1)]
    off = {}
    o = 0
    for n, w in names:
        off[n] = o
        o += w
    return off, o


COLS, NCOL = col_layout()
AR_BASE = 14336


class _Stop(Exception):
    pass


def build(S, NSEQ, DEPTH, dbg_names=(), stop=None):
    NT = S // 512
    NB = S // 128
    YW = 2 * S
    nc = bass.Bass("TRN2", target_bir_lowering=False)
    x_d = nc.dram_tensor("x", [NSEQ, S, D], F32, kind="ExternalInput").ap()
    mem_d = nc.dram_tensor("mem", [NSEQ, NMEM, D], F32, kind="ExternalInput").ap()
    win_d = nc.dram_tensor("w_in_r", [DEPTH, D, NIN], F32, kind="ExternalInput").ap()
    brow_d = nc.dram_tensor("brow", [DEPTH, 2, 512], F32, kind="ExternalInput").ap()
    cols_d = nc.dram_tensor("cols", [DEPTH, 128, NCOL], F32, kind="ExternalInput").ap()
    dw_d = nc.dram_tensor("dw", [DEPTH, 2, 4, 128, 128], F32, kind="ExternalInput").ap()
    wmkv_d = nc.dram_tensor("w_mkv", [DEPTH, D, 2 * E], F32, kind="ExternalInput").ap()
    wup_d = nc.dram_tensor("w_up", [DEPTH, 5, E, D], F32, kind="ExternalInput").ap()
    wout_d = nc.dram_tensor("w_out", [DEPTH, D, D], F32, kind="ExternalInput").ap()
    fcols_d = nc.dram_tensor("fcols", [128, 8], F32, kind="ExternalInput").ap()
    out_d = nc.dram_tensor("out", [NSEQ, S, D], F32, kind="ExternalOutput").ap()
    xs_d = nc.dram_tensor("xs", [NSEQ, 128, 8, S], F32, kind="Internal").ap()
    dbg_d = {}

    with ExitStack() as es:
        Sc = Sch(nc, es)

        def sb(name, shape, dt=F32):
            return es.enter_context(nc.sbuf_tensor("s_" + name, shape, dt))

        def ACT(out, in_, func, r, w, **kw):
            Sc.op("act", lambda e: e.activation(out=out, in_=in_, func=func, **kw), r, w)

        def TT(eng, out, in0, in1, op, r, w):
            Sc.op(eng, lambda e: e.tensor_tensor(out=out, in0=in0, in1=in1, op=op), r, w)

        def TS(eng, out, in0, s1, s2, op0, op1, r, w):
            if op1 is None:
                Sc.op(eng, lambda e: e.tensor_scalar(out=out, in0=in0, scalar1=s1, scalar2=None, op0=op0), r, w)
            else:
                Sc.op(eng, lambda e: e.tensor_scalar(out=out, in0=in0, scalar1=s1, scalar2=s2, op0=op0, op1=op1), r, w)

        def STT(out, in0, scalar, in1, op0, op1, r, w):
            Sc.op("dve", lambda e: e.scalar_tensor_tensor(out=out, in0=in0, scalar=scalar, in1=in1, op0=op0, op1=op1), r, w)

        def RECIP(out, in_, r, w):
            Sc.op("dve", lambda e: e.reciprocal(out=out, in_=in_), r, w)

        def MM(out, lhsT, rhs, start, stop, r, w):
            Sc.op("pe", lambda e: e.matmul(out, lhsT=lhsT, rhs=rhs, start=start, stop=stop), r, w)

        def TR(out, in_, idn, r, w):
            Sc.op("pe", lambda e: e.transpose(out, in_, idn), r, w)

        def CP(eng, out, in_, r, w):
            if eng == "act":
                Sc.op("act", lambda e: e.activation(out=out, in_=in_, func=AF.Copy), r, w)
            else:
                Sc.op(eng, lambda e: e.tensor_copy(out=out, in_=in_), r, w)

        def MEMSET(eng, ap, val, w):
            Sc.op(eng, lambda e: e.memset(ap, val), (), w)

        def ASEL(out, in_, pattern, cmp, cm, r, w, base=0):
            Sc.op("pool", lambda e: e.affine_select(out=out, in_=in_, pattern=pattern, compare_op=cmp, fill=0.0,
                                                    base=base, channel_multiplier=cm), r, w)

        def SCAN(out, d0, d1, init, op0, op1, r, w):
            Sc.op("dve", lambda e: e.tensor_tensor_scan(out=out, data0=d0, data1=d1, initial=init, op0=op0, op1=op1), r, w)

        def DMA(out, in_, r, w, q="sp", **kw):
            return Sc.dma(q, out, in_, r, w, **kw)

        def debug(name, ap, shape, key, dt=F32):
            if name not in dbg_names:
                return
            d = nc.dram_tensor("dbg_" + name, list(shape), dt, kind="ExternalOutput").ap()
            dbg_d[name] = d
            DMA(d, ap, [key] if not isinstance(key, list) else key, [("dbgd", name)])

        ident = sb("ident", [128, 128])
        identb = sb("identb", [128, 128], BF16)
        onesf = sb("onesf", [128, 128])
        onesD = sb("onesD", [128, 128])
        onesE = sb("onesE", [128, 128])
        sel4 = sb("sel4", [4, 4, 128])
        cols = sb("cols", [128, NCOL])
        fcols = sb("fcols", [128, 8])
        epsc = sb("epsc", [128, 1])
        hT = sb("hT", [128, 8, S], BF16)
        WST_N = 2
        wst = [sb(f"wst{i}", [128, 8, 256]) for i in range(WST_N)]
        AR = AR_BASE + 5 * YW
        arena = sb("arena", [128, AR])
        NPS = 6
        psb = [es.enter_context(nc.psum_tensor(f"ps{i}", [128, 512], F32)) for i in range(NPS)]
        psh = [es.enter_context(nc.psum_tensor(f"psh{i}", [128, 1024], BF16)) for i in range(2)]
        pskey = {id(t): ("ps", i) for i, t in enumerate(psb)}

        Y = [None] * 5
        for pos, n in enumerate((4, 3, 0, 1, 2)):
            a = AR_BASE + pos * YW
            Y[n] = arena[:, a:a + YW].bitcast(BF16).rearrange("p (a b) -> p a b", a=4)

        st = dict(ar=0, lim=AR_BASE, ps=0, psz=0, psh=0, wst=0, cast=0, ev=0, phase=0)
        tmpviews = {}

        def carve(name, free, dt=F32):
            n = int(np.prod(free))
            words = n if dt == F32 else (n + 1) // 2
            words = (words + 7) // 8 * 8
            a = st["ar"]
            assert a + words <= st["lim"], (name, a, words, st["lim"])
            st["ar"] = a + words
            v = arena[:, a:a + words]
            if dt != F32:
                v = v.bitcast(dt)
            v = v[:, 0:n]
            if len(free) == 2:
                v = v.rearrange("p (a b) -> p a b", a=free[0])
            elif len(free) == 3:
                v = v.rearrange("p (a b c) -> p a b c", a=free[0], b=free[1])
            return v

        def T(name, free, dt=F32):
            k = (name, st["phase"])
            if k not in tmpviews:
                tmpviews[k] = carve(name, free, dt)
            return tmpviews[k]

        def new_phase(extra=0):
            Sc.barrier()
            st["ar"] = 0
            st["lim"] = AR_BASE + extra * YW
            st["phase"] += 1

        def PS(pool="all"):
            if pool == "all":
                i = st["ps"] % NPS
                st["ps"] += 1
            else:
                i = st["psz"] % 4
                st["psz"] += 1
            return psb[i], ("ps", i)

        def PSH():
            i = st["psh"] % 2
            st["psh"] += 1
            return psh[i], ("psh", i)

        def ev_eng():
            st["ev"] += 1
            return "act" if st["ev"] % 2 else "dve"

        def onesrow(p, n):
            return onesf[0:p, 0:1].to_broadcast([p, n])

        MEMSET("pool", onesf[:], 1.0, ["onesf"])
        MEMSET("pool", onesD[:], 1.0 / D, ["onesD"])
        MEMSET("pool", onesE[:], 1.0 / E, ["onesE"])
        MEMSET("pool", epsc[:], EPS, ["epsc"])
        ASEL(ident[:], onesf[:], [[-1, 128]], ALU.is_equal, 1, ["onesf"], ["ident"])
        CP("pool", identb[:], ident[:], ["ident"], ["identb"])
        for h in range(4):
            ASEL(sel4[:, h, :], onesf[0:4, :], [[0, 128]], ALU.is_equal, 1, ["onesf"], ["sel4"], base=-h)
        DMA(fcols[:], fcols_d, [], ["fcols"])

        def col(name, i=0):
            o = COLS[name] + i
            return cols[:, o:o + 1]

        def load_w(src2d, K, ncols, dst, dkey):
            kc = K // 128
            c0 = 0
            while c0 < ncols:
                n = min(256, ncols - c0)
                i = st["wst"] % WST_N
                st["wst"] += 1
                stg = wst[i][:, 0:kc, 0:n]
                DMA(stg, src2d[:, c0:c0 + n].rearrange("(kc p) n -> p kc n", p=128), [], [("wst", i)])
                st["cast"] += 1
                eng = "pool" if st["cast"] % 2 else "act"
                CP(eng, dst[:, :, c0:c0 + n], stg, [("wst", i)], [dkey])
                c0 += n

        def rmsnorm_tile(xf, xkey, sq, sqkey, out_fn):
            ACT(sq, xf, AF.Square, [xkey], [sqkey])
            ps, pk = PS()
            for kc in range(8):
                MM(ps[:, 0:512], onesD[:], sq[:, kc, :], kc == 0, kc == 7, ["onesD", sqkey], [pk])
            rstd = T("rn_rstd", [512])
            ACT(rstd, ps[:, 0:512], AF.Sqrt, [pk, "epsc"], ["rn_rstd"], bias=epsc[:, 0:1])
            RECIP(rstd, rstd, ["rn_rstd"], ["rn_rstd"])
            for kc in range(8):
                out_fn(kc, rstd, "rn_rstd")

        def xskeys(seq, tt):
            return [("xs", seq, c, tt) for c in range(8)]

        def main_body():
          for seq in range(NSEQ):
            new_phase(5)
            for tb in range(NB):
                j = tb % 2
                xt = T(f"xin{j}", [D])
                xo = T(f"xout{j}", [8, 128])
                DMA(xt, x_d[seq, tb * 128:(tb + 1) * 128, :], [], [("xin", j)])
                for half in range(2):
                    ps, pk = PS()
                    for q in range(4):
                        kc = half * 4 + q
                        TR(ps[:, q * 128:(q + 1) * 128], xt[:, kc * 128:(kc + 1) * 128], ident[:], [("xin", j), "ident"], [pk])
                    CP(ev_eng(), xo[:, half * 4:(half + 1) * 4, :], ps[:, 0:512].rearrange("p (a b) -> p a b", a=4), [pk], [("xout", j)])
                DMA(xs_d[seq, :, :, tb * 128:(tb + 1) * 128], xo, [("xout", j)], xskeys(seq, tb // 4))

            for l in range(DEPTH):
                new_phase(5)
                DMA(cols[:], cols_d[l], [], ["cols"])
                for tt in range(NT):
                    j = tt % 2
                    xf = T(f"xf{j}", [8, 512])
                    sq = T("rn_sq", [8, 512])
                    DMA(xf, xs_d[seq, :, :, tt * 512:(tt + 1) * 512], xskeys(seq, tt), [("xf", j)])

                    def mk_h(kc, rstd, rkey, xf=xf, tt=tt, j=j):
                        STT(hT[:, kc, tt * 512:(tt + 1) * 512], xf[:, kc, :], col("ng", kc), rstd, ALU.mult, ALU.mult,
                            [("xf", j), "cols", rkey], ["hT"])
                    rmsnorm_tile(xf, ("xf", j), sq, "rn_sq", mk_h)
                debug(f"hT{l}", hT[:], [128, 8, S], "hT", BF16)
                if stop == "hT":
                    raise _Stop()

                def wsec(name):
                    return win_d[l, :, SEC[name] * 512:(SEC[name] + 1) * 512]

                def bcol(name, cc):
                    return col("bin", SEC[name] * 4 + cc)

                def proj(ps, wt, wkey, cc, tsl):
                    for kc in range(8):
                        MM(ps[:, 0:512], wt[:, kc, cc * 128:(cc + 1) * 128], hT[:, kc, tsl], kc == 0, kc == 7, [wkey, "hT"], [pskey[id(ps)]])

                new_phase(4)
                wC = [T(f"wC{i}", [8, 512], BF16) for i in range(2)]
                wci = T("wci", [8, 8], BF16)
                Vc = T("Vc", [NB, 4, 130], BF16)
                browc = T("browC", [512])
                Gt = T("Gt", [S])
                TSm = T("TSm", [NB, 96])
                expnm = T("expnm", [NB, 4])
                NGL = T("NGL", [4, NB + 1])
                PGL = T("PGL", [4, NB + 1])
                dec = T("decC", [4, NB])
                wint = T("wint", [NB, 4])
                wsta = T("wsta", [NB, 4])
                qC = T("qC", [4, S], BF16)
                kC = T("kC", [4, S], BF16)
                CTf = T("CTf", [4, 130])
                CTb = T("CTb", [4, 130], BF16)
                WT = [T(f"WTc{i}", [128]) for i in range(2)]
                STb = [T(f"STc{i}", [128], BF16) for i in range(2)]
                tmpi = [T(f"tmpiC{i}", [130]) for i in range(2)]
                nd = [T(f"ndC{i}", [130]) for i in range(2)]
                kw = [T(f"kwC{i}", [128], BF16) for i in range(2)]
                hn = [T(f"hnC{i}", [128]) for i in range(2)]
                sml = [T(f"smlC{i}", [16]) for i in range(2)]
                gtmp = [T(f"gtmpC{i}", [512]) for i in range(2)]
                mark = st["ar"]
                ibt = T("ibt", [S])
                Ft = T("Ft", [S])
                stk = T("stk", [S])

                DMA(browc, brow_d[l, 1, :].partition_broadcast(128), [], ["browC"])
                load_w(wsec("c_v"), D, 512, wC[0], "wC0")
                load_w(win_d[l, :, CIF0:CIF0 + 8], D, 8, wci, "wci")
                MEMSET("pool", Vc[:, :, :, 128:130], 1.0, ["Vc"])
                for tb in range(NB):
                    ps, pk = PS()
                    for kc in range(8):
                        MM(ps[:, 0:512], hT[:, kc, tb * 128:(tb + 1) * 128], wC[0][:, kc, :], kc == 0, kc == 7, ["wC0", "hT"], [pk])
                    TT("dve", Vc[:, tb, :, 0:128], ps[:, 0:512].rearrange("p (a b) -> p a b", a=4),
                       browc.rearrange("p (a b) -> p a b", a=4), ALU.add, [pk, "browC"], ["Vc"])
                MEMSET("pool", stk, 0.0, ["stk"])
                for tt in range(NT):
                    tsl = slice(tt * 512, (tt + 1) * 512)
                    ps, pk = PS()
                    for kc in range(8):
                        MM(ps[0:4, 0:512], wci[:, kc, 0:4], hT[:, kc, tsl], kc == 0, kc == 7, ["wci", "hT"], [pk])
                    ACT(ibt[0:4, tsl], ps[0:4, 0:512], AF.Identity, [pk, "cols"], ["ibt"], bias=cols[0:4, COLS["cib"]:COLS["cib"] + 1])
                    ps2, pk2 = PS()
                    for kc in range(8):
                        MM(ps2[0:4, 0:512], wci[:, kc, 4:8], hT[:, kc, tsl], kc == 0, kc == 7, ["wci", "hT"], [pk2])
                    ACT(Ft[0:4, tsl], ps2[0:4, 0:512], AF.Identity, [pk2, "cols"], ["Ft"], bias=cols[0:4, COLS["cfb"]:COLS["cfb"] + 1])
                    ACT(Ft[0:4, tsl], Ft[0:4, tsl], AF.Identity, ["Ft", "cols"], ["Ft"], bias=cols[0:4, COLS["cfb2"]:COLS["cfb2"] + 1])
                    ACT(Ft[0:4, tsl], Ft[0:4, tsl], AF.Exp, ["Ft"], ["Ft"], scale=-1.0)
                    ACT(Ft[0:4, tsl], Ft[0:4, tsl], AF.Ln, ["Ft"], ["Ft"], bias=1.0)
                SCAN(Gt[0:4, :], onesrow(4, S), Ft[0:4, :], 0.0, ALU.mult, ALU.subtract, ["Ft", "onesf"], ["Gt"])
                TT("dve", ibt[0:4, :], ibt[0:4, :], Gt[0:4, :], ALU.subtract, ["ibt", "Gt"], ["ibt"])
                SCAN(Ft[0:4, :], ibt[0:4, :], ibt[0:4, :], 0.0, ALU.max, ALU.max, ["ibt"], ["Ft"])
                TT("dve", Gt[0:4, :], Gt[0:4, :], Ft[0:4, :], ALU.add, ["Gt", "Ft"], ["Gt"])
                TS("dve", stk[32:36, :], Gt[0:4, :], -1.0, None, ALU.mult, None, ["Gt"], ["stk"])
                TS("dve", Gt[0:4, :], Ft[0:4, :], -1.0, None, ALU.mult, None, ["Ft"], ["Gt"])
                CP("dve", stk[64:68, :], Gt[0:4, :], ["Gt"], ["stk"])
                TS("dve", stk[0:4, :], ibt[0:4, :], math.log(128.0 ** -0.5), None, ALU.add, None, ["ibt"], ["stk"])
                for tb in range(NB):
                    ps, pk = PS()
                    TR(ps[:, 0:96], stk[0:96, tb * 128:(tb + 1) * 128], ident[0:96, 0:96], ["stk", "ident"], [pk])
                    CP(ev_eng(), TSm[:, tb, :], ps[:, 0:96], [pk], ["TSm"])
                ACT(expnm, TSm[:, :, 32:36], AF.Exp, ["TSm"], ["expnm"])
                MEMSET("pool", NGL, 0.0, ["NGL"])
                for h in range(4):
                    ps, pk = PS()
                    MM(ps[:, 0:NB], sel4[:, h, :], Gt[0:4, 127::128], True, True, ["sel4", "Gt"], [pk])
                    CP("dve", NGL[:, h, 1:NB + 1], ps[:, 0:NB], [pk], ["NGL"])
                TS("dve", PGL, NGL, -1.0, None, ALU.mult, None, ["NGL"], ["PGL"])
                TT("dve", dec, NGL[:, :, 1:NB + 1], NGL[:, :, 0:NB], ALU.subtract, ["NGL"], ["decC"])
                ACT(dec, dec, AF.Exp, ["decC"], ["decC"])
                for h in range(4):
                    for tb in range(NB):
                        ACT(wint[:, tb, h:h + 1], TSm[:, tb, 64 + h:65 + h], AF.Exp, ["TSm", "PGL"], ["wint"], bias=PGL[:, h, tb:tb + 1])
                        ACT(wsta[:, tb, h:h + 1], TSm[:, tb, h:h + 1], AF.Exp, ["TSm", "NGL"], ["wsta"], bias=NGL[:, h, tb + 1:tb + 2])
                debug(f"tsm{l}", TSm, [128, NB, 96], "TSm")
                debug(f"wint{l}", wint, [128, NB, 4], "wint")
                debug(f"wsta{l}", wsta, [128, NB, 4], "wsta")
                if stop == "Cprep":
                    raise _Stop()
                Sc.barrier()
                st["ar"] = mark
                cpad = T("cpad", [3 + S])
                cacc = [T(f"cacc{i}", [S]) for i in range(2)]
                load_w(wsec("c_q"), D, 512, wC[1], "wC1")
                load_w(wsec("c_k"), D, 512, wC[0], "wC0")
                MEMSET("pool", cpad[:, 0:3], 0.0, ["cpad"])
                it = 0
                for which, wt, wk, dst, sname in ((0, wC[1], "wC1", qC, "c_q"), (1, wC[0], "wC0", kC, "c_k")):
                    for h in range(4):
                        i = it % 2
                        it += 1
                        for tt in range(NT):
                            tsl = slice(tt * 512, (tt + 1) * 512)
                            ps, pk = PS()
                            proj(ps, wt, wk, h, tsl)
                            ACT(cpad[:, 3 + tt * 512:3 + (tt + 1) * 512], ps[:, 0:512], AF.Identity, [pk, "cols"], ["cpad"],
                                bias=bcol(sname, h))
                        ch = which * 4 + h
                        w0 = COLS["ccw"] + ch * 4
                        TS("dve", cacc[i], cpad[:, 0:S], cols[:, w0:w0 + 1], col("ccb", ch), ALU.mult, ALU.add,
                           ["cpad", "cols"], [("cacc", i)])
                        for jt in range(1, 4):
                            STT(cacc[i], cpad[:, jt:jt + S], cols[:, w0 + jt:w0 + jt + 1], cacc[i], ALU.mult, ALU.add,
                                ["cpad", "cols", ("cacc", i)], [("cacc", i)])
                        ACT(dst[:, h, :], cacc[i], AF.Silu, [("cacc", i)], [("qkC", which)])
                debug(f"qc{l}", qC, [128, 4, S], ("qkC", 0), BF16)
                debug(f"kc{l}", kC, [128, 4, S], ("qkC", 1), BF16)
                if stop == "Cqk":
                    raise _Stop()
                load_w(wsec("c_o"), D, 512, wC[1], "wC1")
                load_w(wsec("c_z"), D, 512, wC[0], "wC0")
                for h in range(4):
                    for tt in range(NT):
                        tsl = slice(tt * 512, (tt + 1) * 512)
                        j = tt % 2
                        ps, pk = PS()
                        proj(ps, wC[1], "wC1", h, tsl)
                        ACT(gtmp[j], ps[:, 0:512], AF.Sigmoid, [pk, "cols"], [("gtmpC", j)], bias=bcol("c_o", h))
                        ps2, pk2 = PS()
                        proj(ps2, wC[0], "wC0", h, tsl)
                        ACT(Y[2][:, h, tsl], ps2[:, 0:512], AF.Silu, [pk2, "cols"], [("Y", 2)], bias=bcol("c_z", h))
                        TT("dve", Y[2][:, h, tsl], Y[2][:, h, tsl], gtmp[j], ALU.mult, [("Y", 2), ("gtmpC", j)], [("Y", 2)])
                MEMSET("pool", CTf, 0.0, [("CTf", h) for h in range(4)])
                MEMSET("pool", CTb, 0.0, [("CT", h) for h in range(4)])
                it = 0
                for tb in range(NB):
                    bsl = slice(tb * 128, (tb + 1) * 128)
                    for h in range(4):
                        j = it % 2
                        it += 1
                        ck = ("CT", h)
                        ps, pk = PS()
                        MM(ps[:, 0:128], kC[:, h, bsl], qC[:, h, bsl], True, True, [("qkC", 0), ("qkC", 1)], [pk])
                        psg, pkg = PS()
                        MM(psg[:, 0:128], sel4[:, h, :], Gt[0:4, bsl], True, True, ["sel4", "Gt"], [pkg])
                        ACT(WT[j], psg[:, 0:128], AF.Exp, [pkg, "TSm"], [("WTc", j)], bias=TSm[:, tb, h:h + 1])
                        ASEL(WT[j], WT[j], [[1, 128]], ALU.is_ge, -1, [("WTc", j)], [("WTc", j)])
                        TT("dve", STb[j], ps[:, 0:128], WT[j], ALU.mult, [pk, ("WTc", j)], [("STc", j)])
                        psn, pkn = PS()
                        MM(psn[:, 0:129], STb[j], Vc[:, tb, h, 0:129], True, True, [("STc", j), "Vc"], [pkn])
                        psi, pki = PS()
                        MM(psi[:, 0:129], qC[:, h, bsl], CTb[:, h, 0:129], True, True, [("qkC", 0), ck], [pki])
                        ACT(tmpi[j][:, 0:129], psi[:, 0:129], AF.Copy, [pki, "wint"], [("tmpiC", j)], scale=wint[:, tb, h:h + 1])
                        TT("dve", nd[j][:, 0:129], psn[:, 0:129], tmpi[j][:, 0:129], ALU.add, [pkn, ("tmpiC", j)], [("ndC", j)])
                        ACT(sml[j][:, 12:13], nd[j][:, 128:129], AF.Abs, [("ndC", j)], [("smlC", j)])
                        TS("dve", sml[j][:, 0:1], sml[j][:, 12:13], expnm[:, tb, h:h + 1], None, ALU.max, None,
                           [("smlC", j), "expnm"], [("smlC", j)])
                        RECIP(sml[j][:, 1:2], sml[j][:, 0:1], [("smlC", j)], [("smlC", j)])
                        TS("dve", nd[j][:, 0:128], nd[j][:, 0:128], sml[j][:, 1:2], None, ALU.mult, None, [("ndC", j), ("smlC", j)], [("ndC", j)])
                        Sc.op("dve", lambda e, j=j: e.bn_stats(out=sml[j][:, 2:8], in_=nd[j][:, 0:128]), [("ndC", j)], [("smlC", j)])
                        Sc.op("dve", lambda e, j=j: e.bn_aggr(out=sml[j][:, 8:10], in_=sml[j][:, 2:8]), [("smlC", j)], [("smlC", j)])
                        ACT(sml[j][:, 10:11], sml[j][:, 9:10], AF.Sqrt, [("smlC", j), "epsc"], [("smlC", j)], bias=epsc[:, 0:1])
                        RECIP(sml[j][:, 11:12], sml[j][:, 10:11], [("smlC", j)], [("smlC", j)])
                        TS("dve", hn[j], nd[j][:, 0:128], sml[j][:, 8:9], sml[j][:, 11:12], ALU.subtract, ALU.mult,
                           [("ndC", j), ("smlC", j)], [("hnC", j)])
                        pst, pkt = PS()
                        TR(pst[:, 0:128], hn[j], ident[:], [("hnC", j), "ident"], [pkt])
                        STT(Y[2][:, h, bsl], pst[:, 0:128], col("chg", h), Y[2][:, h, bsl], ALU.mult, ALU.mult, [pkt, "cols", ("Y", 2)], [("Y", 2)])
                        if tb < NB - 1:
                            ph, phk = PSH()
                            TR(ph[:, 0:128], kC[:, h, bsl], identb[:], [("qkC", 1), "identb"], [phk])
                            TS("dve", kw[j], ph[:, 0:128], wsta[:, tb, h:h + 1], None, ALU.mult, None, [phk, "wsta"], [("kwC", j)])
                            psu, pku = PS()
                            MM(psu[:, 0:129], kw[j], Vc[:, tb, h, 0:129], True, True, [("kwC", j), "Vc"], [pku])
                            STT(CTf[:, h, 0:129], CTf[:, h, 0:129], dec[:, h, tb:tb + 1], psu[:, 0:129], ALU.mult, ALU.add,
                                [("CTf", h), "decC", pku], [("CTf", h)])
                            CP("act", CTb[:, h, 0:129], CTf[:, h, 0:129], [("CTf", h)], [ck])
                debug(f"yc{l}", Y[2], [128, 4, S], ("Y", 2), BF16)
                if stop == "C":
                    raise _Stop()

                new_phase(3)
                wB = [T(f"wB{i}", [8, 512], BF16) for i in range(2)]
                Vb = T("Vb", [NB, 512], BF16)
                brow = T("browB", [512])
                qB = T("qB", [S], BF16)
                kB = T("kB", [S], BF16)
                zs2 = [T(f"zsB{i}", [S]) for i in range(2)]
                spb2 = [T(f"spB{i}", [S + 1]) for i in range(2)]
                lat2 = [T(f"latB{i}", [S]) for i in range(2)]
                wbf2 = [T(f"wbfB{i}", [S], BF16) for i in range(2)]
                itb = 0
                wT = [T(f"wTB{i}", [4, 128], BF16) for i in range(2)]
                ob = T("obB", [128])
                DMA(brow, brow_d[l, 0, :].partition_broadcast(128), [], ["browB"])
                load_w(wsec("b_v"), D, 512, wB[0], "wB0")
                for tb in range(NB):
                    ps, pk = PS()
                    for kc in range(8):
                        MM(ps[:, 0:512], hT[:, kc, tb * 128:(tb + 1) * 128], wB[0][:, kc, :], kc == 0, kc == 7, ["wB0", "hT"], [pk])
                    TT("dve", Vb[:, tb, :], ps[:, 0:512], brow, ALU.add, [pk, "browB"], ["Vb"])
                load_w(wsec("b_q"), D, 512, wB[1], "wB1")
                load_w(wsec("b_k"), D, 512, wB[0], "wB0")
                sc_b = 64.0 ** -0.5
                itsB = [(pr, qb, hh) for pr in range(4) for qb in range(NB) for hh in range(2)]

                def projqk(pr):
                    for tt in range(NT):
                        tsl = slice(tt * 512, (tt + 1) * 512)
                        ps, pk = PS("z")
                        proj(ps, wB[1], "wB1", pr, tsl)
                        ACT(qB[:, tsl], ps[:, 0:512], AF.Identity, [pk, "cols"], ["qB"], bias=bcol("b_q", pr))
                        ps2, pk2 = PS("z")
                        proj(ps2, wB[0], "wB0", pr, tsl)
                        ACT(kB[:, tsl], ps2[:, 0:512], AF.Identity, [pk2, "cols"], ["kB"], bias=bcol("b_k", pr))

                def bufsB(idx):
                    jb = idx % 2
                    return (zs2[jb], spb2[jb], lat2[jb], wbf2[jb], ("zsB", jb), ("spB", jb), ("latB", jb), ("wbfB", jb))

                def S1(idx):
                    pr, qb, hh = itsB[idx]
                    zs, spb, lat, wbf, kz, ksp, kla, kwb = bufsB(idx)
                    L = (qb + 1) * 128
                    bsl = slice(qb * 128, (qb + 1) * 128)
                    pl = slice(hh * 64, (hh + 1) * 64)
                    nk = (L + 511) // 512
                    zps = []
                    for ki in range(nk):
                        n = min(512, L - ki * 512)
                        ps, pk = PS("z")
                        MM(ps[:, 0:n], qB[pl, bsl], kB[pl, ki * 512:ki * 512 + n], True, True, ["qB", "kB"], [pk])
                        zps.append((ps, pk, n))
                    for ki, (ps, pk, n) in enumerate(zps):
                        ksl = slice(ki * 512, ki * 512 + n)
                        ACT(spb[:, ksl], ps[:, 0:n], AF.Exp, [pk], [ksp], scale=sc_b)
                        ACT(zs[:, ksl], ps[:, 0:n], AF.Copy, [pk], [kz], scale=sc_b)
                    ACT(spb[:, 0:L], spb[:, 0:L], AF.Ln, [ksp], [ksp], bias=1.0)
                    ASEL(spb[:, qb * 128:qb * 128 + 129], spb[:, qb * 128:qb * 128 + 129], [[-1, 129]], ALU.is_gt, 1, [ksp], [ksp])
                    SCAN(lat[:, 0:L][:, ::-1], onesrow(128, L), spb[:, 1:L + 1][:, ::-1], 0.0, ALU.mult, ALU.add,
                         [ksp, "onesf"], [kla])
                    TT("pool", zs[:, 0:L], zs[:, 0:L], spb[:, 0:L], ALU.subtract, [kz, ksp], [kz])
                    TT("dve", zs[:, 0:L], zs[:, 0:L], lat[:, 0:L], ALU.subtract, [kz, kla], [kz])

                def S2(idx):
                    pr, qb, hh = itsB[idx]
                    zs, spb, lat, wbf, kz, ksp, kla, kwb = bufsB(idx)
                    L = (qb + 1) * 128
                    bsl = slice(qb * 128, (qb + 1) * 128)
                    pso, pko = psb[4 + qb % 2], ("ps", 4 + qb % 2)
                    ACT(wbf[:, 0:L], zs[:, 0:L], AF.Exp, [kz], [kwb])
                    ASEL(wbf[:, bsl], wbf[:, bsl], [[-1, 128]], ALU.is_gt, 1, [kwb], [kwb])
                    nkb = qb + 1
                    for g0 in range(0, nkb, 4):
                        gn = min(4, nkb - g0)
                        ph, phk = PSH()
                        j = (g0 // 4) % 2
                        for q in range(gn):
                            kb = g0 + q
                            TR(ph[:, q * 128:(q + 1) * 128], wbf[:, kb * 128:(kb + 1) * 128], identb[:], [kwb, "identb"], [phk])
                        CP(ev_eng(), wT[j][:, 0:gn, :], ph[:, 0:gn * 128].rearrange("p (a b) -> p a b", a=gn), [phk], [("wTB", j)])
                        for q in range(gn):
                            kb = g0 + q
                            hc = (pr * 2 + hh) * 64
                            MM(pso[:, hh * 64:(hh + 1) * 64], wT[j][:, q, :], Vb[:, kb, hc:hc + 64],
                               kb == 0, kb == nkb - 1, [("wTB", j), "Vb"], [pko])
                    if hh == 1:
                        CP("act", ob, pso[:, 0:128], [pko], ["obB"])
                        pst, pkt = PS("z")
                        TR(pst[:, 0:128], ob, ident[:], ["obB", "ident"], [pkt])
                        CP("dve", Y[1][:, pr, bsl], pst[:, 0:128], [pkt], [("Y", 1)])

                projqk(0)
                S1(0)
                for idx in range(len(itsB)):
                    if idx + 1 < len(itsB):
                        if itsB[idx + 1][0] != itsB[idx][0]:
                            projqk(itsB[idx + 1][0])
                        S1(idx + 1)
                    S2(idx)
                debug(f"yb_pre{l}", Y[1], [128, 4, S], ("Y", 1), BF16)
                load_w(wsec("b_z"), D, 512, wB[1], "wB1")
                for pr in range(4):
                    for tt in range(NT):
                        tsl = slice(tt * 512, (tt + 1) * 512)
                        ps, pk = PS()
                        proj(ps, wB[1], "wB1", pr, tsl)
                        ACT(qB[:, tsl], ps[:, 0:512], AF.Silu, [pk, "cols"], ["qB"], bias=bcol("b_z", pr))
                        TT("dve", Y[1][:, pr, tsl], Y[1][:, pr, tsl], qB[:, tsl], ALU.mult, [("Y", 1), "qB"], [("Y", 1)])
                debug(f"yb{l}", Y[1], [128, 4, S], ("Y", 1), BF16)
                if stop == "B":
                    raise _Stop()

                new_phase(2)
                wA = [T(f"wA{i}", [8, 512], BF16) for i in range(2)]
                yconv = T("yconv", [4, S])
                upad = [T("upad0", [30 + S])] * 2
                sg = [T(f"sgA{i}", [512]) for i in range(2)]
                ysq = T("ysq", [4, 512])
                mean_s = T("meanA", [512])
                rstd_s = T("rstdA", [512])
                tn = [T(f"tnA{i}", [512]) for i in range(2)]
                za = [T(f"zaA{i}", [512]) for i in range(2)]
                load_w(wsec("a_val"), D, 512, wA[0], "wA0")
                load_w(wsec("a_glu"), D, 512, wA[1], "wA1")
                MEMSET("pool", upad[0][:, 0:30], 0.0, [("upad", 0)])
                for cc in range(4):
                    i = 0
                    for tt in range(NT):
                        tsl = slice(tt * 512, (tt + 1) * 512)
                        j = tt % 2
                        ps, pk = PS()
                        proj(ps, wA[1], "wA1", cc, tsl)
                        ACT(sg[j], ps[:, 0:512], AF.Sigmoid, [pk, "cols"], [("sgA", j)], bias=bcol("a_glu", cc))
                        ps2, pk2 = PS()
                        proj(ps2, wA[0], "wA0", cc, tsl)
                        STT(upad[i][:, 30 + tt * 512:30 + (tt + 1) * 512], ps2[:, 0:512], bcol("a_val", cc), sg[j],
                            ALU.add, ALU.mult, [pk2, "cols", ("sgA", j)], [("upad", i)])
                    acw0 = COLS["acw"] + cc * 31
                    TS("dve", yconv[:, cc, :], upad[i][:, 0:S], cols[:, acw0:acw0 + 1], col("acb", cc), ALU.mult, ALU.add,
                       [("upad", i), "cols"], [("yconv", cc)])
                    for jt in range(1, 31):
                        STT(yconv[:, cc, :], upad[i][:, jt:jt + S], cols[:, acw0 + jt:acw0 + jt + 1], yconv[:, cc, :],
                            ALU.mult, ALU.add, [("upad", i), "cols", ("yconv", cc)], [("yconv", cc)])
                debug(f"yconv{l}", yconv, [128, 4, S], [("yconv", c) for c in range(4)])
                load_w(wsec("a_z"), D, 512, wA[0], "wA0")
                for tt in range(NT):
                    tsl = slice(tt * 512, (tt + 1) * 512)
                    ACT(ysq, yconv[:, :, tsl], AF.Square, [("yconv", c) for c in range(4)], ["ysq"])
                    psm, pkm = PS()
                    for cc in range(4):
                        MM(psm[:, 0:512], onesE[:], yconv[:, cc, tsl], cc == 0, cc == 3, ["onesE", ("yconv", cc)], [pkm])
                    pss, pks = PS()
                    for cc in range(4):
                        MM(pss[:, 0:512], onesE[:], ysq[:, cc, :], cc == 0, cc == 3, ["onesE", "ysq"], [pks])
                    CP("act", mean_s, psm[:, 0:512], [pkm], ["meanA"])
                    TT("dve", rstd_s, mean_s, mean_s, ALU.mult, ["meanA"], ["rstdA"])
                    TT("dve", rstd_s, pss[:, 0:512], rstd_s, ALU.subtract, [pks, "rstdA"], ["rstdA"])
                    ACT(rstd_s, rstd_s, AF.Sqrt, ["rstdA", "epsc"], ["rstdA"], bias=epsc[:, 0:1])
                    RECIP(rstd_s, rstd_s, ["rstdA"], ["rstdA"])
                    for cc in range(4):
                        j = cc % 2
                        TT("pool", tn[j], yconv[:, cc, tsl], mean_s, ALU.subtract, [("yconv", cc), "meanA"], [("tnA", j)])
                        TT("dve", tn[j], tn[j], rstd_s, ALU.mult, [("tnA", j), "rstdA"], [("tnA", j)])
                        ACT(tn[j], tn[j], AF.Silu, [("tnA", j), "cols"], [("tnA", j)], scale=col("alg", cc), bias=col("alb", cc))
                        ps, pk = PS()
                        proj(ps, wA[0], "wA0", cc, tsl)
                        ACT(za[j], ps[:, 0:512], AF.Silu, [pk, "cols"], [("zaA", j)], bias=bcol("a_z", cc))
                        TT("dve", Y[0][:, cc, tsl], tn[j], za[j], ALU.mult, [("tnA", j), ("zaA", j)], [("Y", 0)])
                debug(f"ya{l}", Y[0], [128, 4, S], ("Y", 0), BF16)
                if stop == "A":
                    raise _Stop()

                new_phase(1)
                wD = [T(f"wD{i}", [8, 512], BF16) for i in range(2)]
                dwall = T("dwall", [8, 128])
                c1 = T("c1", [4])
                dpad = T("dpad", [3 + S])
                xc = T("xcD", [S])
                av = T("avD", [S])
                uv = T("uvD", [S])
                gi = [T(f"giD{i}", [512]) for i in range(2)]
                zd = [T(f"zdD{i}", [512]) for i in range(2)]
                DMA(dwall, dw_d[l].rearrange("g c p d -> p (g c) d"), [], ["dwall"])
                load_w(wsec("d_x"), D, 512, wD[0], "wD0")
                load_w(wsec("d_z"), D, 512, wD[1], "wD1")
                ACT(c1, cols[:, COLS["dlam"]:COLS["dlam"] + 4], AF.Exp, ["cols"], ["c1"], scale=-1.0)
                ACT(c1, c1, AF.Ln, ["c1"], ["c1"], bias=1.0)
                TS("dve", c1, c1, -8.0, None, ALU.mult, None, ["c1"], ["c1"])
                MEMSET("pool", dpad[:, 0:3], 0.0, ["dpad"])
                for cc in range(4):
                    for tt in range(NT):
                        tsl = slice(tt * 512, (tt + 1) * 512)
                        ps, pk = PS()
                        proj(ps, wD[0], "wD0", cc, tsl)
                        ACT(dpad[:, 3 + tt * 512:3 + (tt + 1) * 512], ps[:, 0:512], AF.Identity, [pk, "cols"], ["dpad"],
                            bias=bcol("d_x", cc))
                    w0 = COLS["dcw"] + cc * 4
                    TS("dve", xc, dpad[:, 0:S], cols[:, w0:w0 + 1], col("dcb", cc), ALU.mult, ALU.add, ["dpad", "cols"], ["xcD"])
                    for jt in range(1, 4):
                        STT(xc, dpad[:, jt:jt + S], cols[:, w0 + jt:w0 + jt + 1], xc, ALU.mult, ALU.add, ["dpad", "cols", "xcD"], ["xcD"])
                    for tt in range(NT):
                        tsl = slice(tt * 512, (tt + 1) * 512)
                        j = tt % 2
                        psa, pka = PS()
                        MM(psa[:, 0:512], dwall[:, 0 * 4 + cc, :], xc[:, tsl], True, True, ["dwall", "xcD"], [pka])
                        psx, pkx = PS()
                        MM(psx[:, 0:512], dwall[:, 1 * 4 + cc, :], xc[:, tsl], True, True, ["dwall", "xcD"], [pkx])
                        ACT(av[:, tsl], psa[:, 0:512], AF.Sigmoid, [pka, "cols"], ["avD"], bias=col("dba", cc))
                        ACT(av[:, tsl], av[:, tsl], AF.Exp, ["avD", "c1"], ["avD"], scale=c1[:, cc:cc + 1])
                        ACT(gi[j], psx[:, 0:512], AF.Sigmoid, [pkx, "cols"], [("giD", j)], bias=col("dbx", cc))
                        TT("pool", uv[:, tsl], av[:, tsl], av[:, tsl], ALU.mult, ["avD"], ["uvD"])
                        ACT(uv[:, tsl], uv[:, tsl], AF.Sqrt, ["uvD"], ["uvD"], scale=-1.0, bias=1.0)
                        TT("pool", gi[j], gi[j], xc[:, tsl], ALU.mult, [("giD", j), "xcD"], [("giD", j)])
                        TT("dve", uv[:, tsl], uv[:, tsl], gi[j], ALU.mult, ["uvD", ("giD", j)], ["uvD"])
                    SCAN(xc, av, uv, 0.0, ALU.mult, ALU.add, ["avD", "uvD", "xcD"], ["xcD"])
                    for tt in range(NT):
                        tsl = slice(tt * 512, (tt + 1) * 512)
                        j = tt % 2
                        ps, pk = PS()
                        proj(ps, wD[1], "wD1", cc, tsl)
                        ACT(zd[j], ps[:, 0:512], AF.Silu, [pk, "cols"], [("zdD", j)], bias=bcol("d_z", cc))
                        TT("dve", Y[3][:, cc, tsl], xc[:, tsl], zd[j], ALU.mult, ["xcD", ("zdD", j)], [("Y", 3)])
                debug(f"yd{l}", Y[3], [128, 4, S], ("Y", 3), BF16)
                if stop == "D":
                    raise _Stop()

                new_phase(0)
                memT = T("memT", [8, NMEM], BF16)
                mkT = T("mkT", [4, NMEM], BF16)
                mv = T("mv", [2, 512], BF16)
                wM = [T(f"wM{i}", [8, 512], BF16) for i in range(2)]
                mrs = T("mrs", [2])
                mq = T("memsq", [D])
                mxs = [T(f"memx{mt}", [D]) for mt in range(2)]
                qm = T("qm", [S], BF16)
                zm = T("zm", [S], BF16)
                pbuf = [T(f"pm{i}", [NMEM], BF16) for i in range(2)]
                pT = [T(f"pTm{i}", [2, 128], BF16) for i in range(2)]
                on = [T(f"onm{i}", [128]) for i in range(2)]
                sm = [T(f"smm{i}", [4]) for i in range(2)]
                for mt in range(2):
                    mx = mxs[mt]
                    DMA(mx, mem_d[seq, mt * 128:(mt + 1) * 128, :], [], [("memx", mt)])
                    ACT(mq, mx, AF.Square, [("memx", mt)], ["memsq", ("mrs", mt)], accum_out=mrs[:, mt:mt + 1])
                    TS("dve", mrs[:, mt:mt + 1], mrs[:, mt:mt + 1], 1.0 / D, EPS, ALU.mult, ALU.add, [("mrs", mt)], [("mrs", mt)])
                    ACT(mrs[:, mt:mt + 1], mrs[:, mt:mt + 1], AF.Sqrt, [("mrs", mt)], [("mrs", mt)])
                    RECIP(mrs[:, mt:mt + 1], mrs[:, mt:mt + 1], [("mrs", mt)], [("mrs", mt)])
                    TS("dve", mx, mx, mrs[:, mt:mt + 1], None, ALU.mult, None, [("memx", mt), ("mrs", mt)], [("memx", mt)])
                    for half in range(2):
                        ps, pk = PS()
                        for q in range(4):
                            kc = half * 4 + q
                            TR(ps[:, q * 128:(q + 1) * 128], mx[:, kc * 128:(kc + 1) * 128], ident[:], [("memx", mt), "ident"], [pk])
                        for q in range(4):
                            kc = half * 4 + q
                            ACT(memT[:, kc, mt * 128:(mt + 1) * 128], ps[:, q * 128:(q + 1) * 128], AF.Copy, [pk, "cols"], ["memT"],
                                scale=col("mng", kc))
                load_w(wmkv_d[l, :, 0:512], D, 512, wM[0], "wM0")
                load_w(wmkv_d[l, :, 512:1024], D, 512, wM[1], "wM1")
                for h in range(4):
                    ps, pk = PS()
                    for kc in range(8):
                        MM(ps[:, 0:NMEM], wM[0][:, kc, h * 128:(h + 1) * 128], memT[:, kc, :], kc == 0, kc == 7, ["wM0", "memT"], [pk])
                    CP(ev_eng(), mkT[:, h, :], ps[:, 0:NMEM], [pk], ["mkT"])
                for mt in range(2):
                    ps, pk = PS()
                    for kc in range(8):
                        MM(ps[:, 0:512], memT[:, kc, mt * 128:(mt + 1) * 128], wM[1][:, kc, :], kc == 0, kc == 7, ["wM1", "memT"], [pk])
                    CP(ev_eng(), mv[:, mt, :], ps[:, 0:512], [pk], ["mv"])
                load_w(wsec("m_q"), D, 512, wM[0], "wM0")
                load_w(wsec("m_z"), D, 512, wM[1], "wM1")
                sc_m = 128.0 ** -0.5
                for h in range(4):
                    for tt in range(NT):
                        tsl = slice(tt * 512, (tt + 1) * 512)
                        ps, pk = PS()
                        proj(ps, wM[0], "wM0", h, tsl)
                        ACT(qm[:, tsl], ps[:, 0:512], AF.Identity, [pk, "cols"], ["qm"], bias=bcol("m_q", h))
                        ps2, pk2 = PS()
                        proj(ps2, wM[1], "wM1", h, tsl)
                        ACT(zm[:, tsl], ps2[:, 0:512], AF.Silu, [pk2, "cols"], ["zm"], bias=bcol("m_z", h))
                    for tb in range(NB):
                        bsl = slice(tb * 128, (tb + 1) * 128)
                        j = tb % 2
                        ps, pk = PS()
                        MM(ps[:, 0:NMEM], qm[:, bsl], mkT[:, h, :], True, True, ["qm", "mkT"], [pk])
                        Sc.op("dve", lambda e, ps=ps, j=j: e.reduce_max(out=sm[j][:, 0:1], in_=ps[:, 0:NMEM], axis=mybir.AxisListType.X),
                              [pk], [("smm", j)])
                        TS("dve", sm[j][:, 1:2], sm[j][:, 0:1], -sc_m, None, ALU.mult, None, [("smm", j)], [("smm", j)])
                        ACT(pbuf[j], ps[:, 0:NMEM], AF.Exp, [pk, ("smm", j)], [("pm", j), ("smm", j)], scale=sc_m, bias=sm[j][:, 1:2],
                            accum_out=sm[j][:, 2:3])
                        ph, phk = PSH()
                        for mt in range(2):
                            TR(ph[:, mt * 128:(mt + 1) * 128], pbuf[j][:, mt * 128:(mt + 1) * 128], identb[:], [("pm", j), "identb"], [phk])
                        CP(ev_eng(), pT[j], ph[:, 0:256].rearrange("p (a b) -> p a b", a=2), [phk], [("pTm", j)])
                        pso, pko = PS()
                        for mt in range(2):
                            MM(pso[:, 0:128], pT[j][:, mt, :], mv[:, mt, h * 128:(h + 1) * 128], mt == 0, mt == 1, [("pTm", j), "mv"], [pko])
                        RECIP(sm[j][:, 3:4], sm[j][:, 2:3], [("smm", j)], [("smm", j)])
                        TS("dve", on[j], pso[:, 0:128], sm[j][:, 3:4], None, ALU.mult, None, [pko, ("smm", j)], [("onm", j)])
                        pst, pkt = PS()
                        TR(pst[:, 0:128], on[j], ident[:], [("onm", j), "ident"], [pkt])
                        TT("dve", Y[4][:, h, bsl], pst[:, 0:128], zm[:, bsl], ALU.mult, [pkt, "zm"], [("Y", 4)])
                debug(f"ym{l}", Y[4], [128, 4, S], ("Y", 4), BF16)
                if stop == "M":
                    raise _Stop()

                new_phase(0)
                mg = T("mg", [8, S], BF16)
                mark = st["ar"]
                RING = 4
                wgn = [T(f"wgn{i}", [8, 128], BF16) for i in range(RING)]
                wun = [T(f"wun{i}", [4, 128], BF16) for i in range(RING)]
                sgm = [T(f"sgm{i}", [512]) for i in range(2)]
                acc = T("accm", [NT, 512])
                order = [(c, n) for c in range(8) for n in range(5)]

                def ldm(idx):
                    c, n = order[idx]
                    i = idx % RING
                    g0 = GATE0 + (c * 5 + n) * 128
                    load_w(win_d[l, :, g0:g0 + 128], D, 128, wgn[i], ("wgn", i))
                    load_w(wup_d[l, n, :, c * 128:(c + 1) * 128], E, 128, wun[i], ("wun", i))
                for idx in range(RING - 1):
                    ldm(idx)
                it = 0
                for idx, (c, n) in enumerate(order):
                    i = idx % RING
                    if idx + RING - 1 < len(order):
                        ldm(idx + RING - 1)
                    for tt in range(NT):
                        tsl = slice(tt * 512, (tt + 1) * 512)
                        j = it % 2
                        it += 1
                        psg, pkg = PS()
                        for kc in range(8):
                            MM(psg[:, 0:512], wgn[i][:, kc, :], hT[:, kc, tsl], kc == 0, kc == 7, [("wgn", i), "hT"], [pkg])
                        ACT(sgm[j], psg[:, 0:512], AF.Sigmoid, [pkg, "cols"], [("sgm", j)], bias=col("bin", 64 + c * 5 + n))
                        psu, pku = PS()
                        for kc in range(4):
                            MM(psu[:, 0:512], wun[i][:, kc, :], Y[n][:, kc, tsl], kc == 0, kc == 3, [("wun", i), ("Y", n)], [pku])
                        if n == 0:
                            TT("dve", acc[:, tt, :], sgm[j], psu[:, 0:512], ALU.mult, [("sgm", j), pku], [("accm", tt)])
                        else:
                            TT("dve", sgm[j], sgm[j], psu[:, 0:512], ALU.mult, [("sgm", j), pku], [("sgm", j)])
                            if n < 4:
                                TT("pool", acc[:, tt, :], acc[:, tt, :], sgm[j], ALU.add, [("accm", tt), ("sgm", j)], [("accm", tt)])
                            else:
                                TT("pool", mg[:, c, tsl], acc[:, tt, :], sgm[j], ALU.add, [("accm", tt), ("sgm", j)], ["mg"])
                debug(f"mg{l}", mg, [128, 8, S], "mg", BF16)
                if stop == "merge":
                    raise _Stop()
                Sc.barrier()
                st["ar"] = mark
                st["lim"] = AR_BASE + 5 * YW
                wo = [T(f"wo{i}", [8, 128], BF16) for i in range(2)]
                xbuf = T("xbuf", [8, S])
                allxs = [("xs", seq, c, tt) for c in range(8) for tt in range(NT)]
                DMA(xbuf, xs_d[seq], allxs, ["xbuf"])
                load_w(wout_d[l, :, 0:128], D, 128, wo[0], ("wo", 0))
                Sc.barrier()
                for c in range(8):
                    i = c % 2
                    if c + 1 < 8:
                        load_w(wout_d[l, :, (c + 1) * 128:(c + 2) * 128], D, 128, wo[(c + 1) % 2], ("wo", (c + 1) % 2))
                    for tt in range(NT):
                        tsl = slice(tt * 512, (tt + 1) * 512)
                        ps, pk = PS()
                        for kc in range(8):
                            MM(ps[:, 0:512], wo[i][:, kc, :], mg[:, kc, tsl], kc == 0, kc == 7, [("wo", i), "mg"], [pk])
                        TT("dve", xbuf[:, c, tsl], xbuf[:, c, tsl], ps[:, 0:512], ALU.add, ["xbuf", pk], ["xbuf"])
                Sc.barrier()
                DMA(xs_d[seq], xbuf, ["xbuf"], allxs)
                if stop == "resid":
                    raise _Stop()

            new_phase(5)
            ot = [T(f"ot{i}", [D]) for i in range(2)]
            it = 0
            for tt in range(NT):
                j = tt % 2
                xf = T(f"xf{j}", [8, 512])
                yo = T(f"yo{j}", [8, 512])
                DMA(xf, xs_d[seq, :, :, tt * 512:(tt + 1) * 512], xskeys(seq, tt), [("xf", j)])

                def mk_o(kc, rstd, rkey, xf=xf, yo=yo, j=j):
                    STT(yo[:, kc, :], xf[:, kc, :], fcols[:, kc:kc + 1], rstd, ALU.mult, ALU.mult,
                        [("xf", j), "fcols", rkey], [("yo", j)])
                rmsnorm_tile(xf, ("xf", j), yo, ("yo", j), mk_o)
                for q4 in range(4):
                    jo = it % 2
                    it += 1
                    for half in range(2):
                        ps, pk = PS()
                        for q in range(4):
                            kc = half * 4 + q
                            TR(ps[:, q * 128:(q + 1) * 128], yo[:, kc, q4 * 128:(q4 + 1) * 128], ident[:], [("yo", j), "ident"], [pk])
                        CP(ev_eng(), ot[jo][:, half * 512:(half + 1) * 512], ps[:, 0:512], [pk], [("ot", jo)])
                    tb = tt * 4 + q4
                    DMA(out_d[seq, tb * 128:(tb + 1) * 128, :], ot[jo], [("ot", jo)], [("outd", seq, tb)])

        try:
            main_body()
        except _Stop:
            pass
        Sc.barrier()
        Sc.emit()
    return nc, dbg_d


def prep_weights(inp):
    DEPTH = inp["w_in"].shape[0]
    perm = np.concatenate([np.arange(0, 5120), np.arange(5128, 8200)] +
                          [8200 + n * 1024 + c * 128 + np.arange(128) for c in range(8) for n in range(5)] +
                          [np.arange(5120, 5128)])
    w_in_r = np.ascontiguousarray(np.asarray(inp["w_in"], np.float32)[:, :, perm])
    b_in_r = np.asarray(inp["b_in"], np.float32)[:, perm]
    cols = np.zeros((DEPTH, 128, NCOL), np.float32)

    def put(l, name, arr):
        cols[l, :, COLS[name]:COLS[name] + arr.shape[1]] = arr

    def pc(v):
        return np.asarray(v, np.float32).reshape(-1, 128).T

    brow = np.zeros((DEPTH, 2, 512), np.float32)
    dw = np.zeros((DEPTH, 2, 4, 128, 128), np.float32)
    for l in range(DEPTH):
        put(l, "bin", pc(b_in_r[l, :13312]))
        put(l, "ng", pc(inp["norm_g"][l]))
        put(l, "mng", pc(inp["mem_norm_g"][l]))
        acw = np.asarray(inp["a_conv_w"][l], np.float32)
        put(l, "acw", acw.T.reshape(4, 128, 31).transpose(1, 0, 2).reshape(128, 124))
        put(l, "acb", pc(inp["a_conv_b"][l]))
        put(l, "alg", pc(inp["a_ln_g"][l]))
        put(l, "alb", pc(inp["a_ln_b"][l]))
        ccw = np.asarray(inp["c_conv_w"][l], np.float32)
        put(l, "ccw", ccw.T.reshape(8, 128, 4).transpose(1, 0, 2).reshape(128, 32))
        put(l, "ccb", pc(inp["c_conv_b"][l]))
        put(l, "chg", pc(inp["c_hn_g"][l]))
        dcw = np.asarray(inp["d_conv_w"][l], np.float32)
        put(l, "dcw", dcw.T.reshape(4, 128, 4).transpose(1, 0, 2).reshape(128, 16))
        put(l, "dcb", pc(inp["d_conv_b"][l]))
        put(l, "dba", pc(inp["d_ba"][l]))
        put(l, "dbx", pc(inp["d_bx"][l]))
        put(l, "dlam", pc(inp["d_lambda"][l]))
        cols[l, 0:4, COLS["cib"]] = b_in_r[l, 13312:13316]
        cols[l, 0:4, COLS["cfb"]] = b_in_r[l, 13316:13320]
        cols[l, 0:4, COLS["cfb2"]] = np.asarray(inp["c_f_bias"][l], np.float32)
        brow[l, 0] = b_in_r[l, SEC["b_v"] * 512:(SEC["b_v"] + 1) * 512]
        brow[l, 1] = b_in_r[l, SEC["c_v"] * 512:(SEC["c_v"] + 1) * 512]
        for g, nm in enumerate(("d_wa", "d_wx")):
            wgt = np.asarray(inp[nm][l], np.float32)
            for cc in range(4):
                dw[l, g, cc, 0:64, 0:64] = wgt[2 * cc]
                dw[l, g, cc, 64:128, 64:128] = wgt[2 * cc + 1]
    fcols = np.ascontiguousarray(pc(inp["final_norm_g"]))
    return dict(w_in_r=w_in_r, brow=brow, cols=cols, dw=dw,
                w_mkv=np.ascontiguousarray(np.asarray(inp["w_mkv"], np.float32)),
                w_up=np.ascontiguousarray(np.asarray(inp["w_up"], np.float32)),
                w_out=np.ascontiguousarray(np.asarray(inp["w_out"], np.float32)),
                fcols=fcols)


_NC_CACHE = {}


def kernel(**inputs):
    x = np.asarray(inputs["x"], np.float32)
    mem = np.asarray(inputs["mem"], np.float32)
    B, S, _ = x.shape
    DEPTH = inputs["w_in"].shape[0]
    ncores = 8
    nseq = B // ncores
    wts = prep_weights(inputs)
    key = (S, nseq, DEPTH)
    if key not in _NC_CACHE:
        _NC_CACHE[key] = build(S, nseq, DEPTH)[0]
    nc = _NC_CACHE[key]
    in_maps = []
    for c in range(ncores):
        m = dict(wts)
        m["x"] = np.ascontiguousarray(x[c * nseq:(c + 1) * nseq])
        m["mem"] = np.ascontiguousarray(mem[c * nseq:(c + 1) * nseq])
        in_maps.append(m)
    res = run_bass_kernel_spmd(nc, in_maps, core_ids=list(range(ncores)))
    out = np.concatenate([np.asarray(r["out"], np.float32) for r in res.results], axis=0)
    return out
```

```python
import math
from contextlib import ExitStack
import numpy as np
import concourse.bass as bass
import concourse.mybir as mybir
from concourse.bass_utils import run_bass_kernel_spmd

F32 = mybir.dt.float32
BF16 = mybir.dt.bfloat16
AF = mybir.ActivationFunctionType
ALU = mybir.AluOpType

ENGS = ("pe", "act", "dve", "pool", "sp")
EPOCH = 24000
NSLOT = {"sp": 8, "pool": 4}


class Sch:
    def __init__(self, nc, es):
        self.nc = nc
        self.es = es
        self.ops = {e: [] for e in ENGS}
        self.cnt = {e: 0 for e in ENGS}
        self.esem = {e: [] for e in ENGS}
        self.known = {e: {} for e in ENGS}
        self.lastw = {}
        self.readers = {}
        self.lasttok = {}
        self.dcnt = {q: 0 for q in NSLOT}
        self.dsem = {q: [self._newsem(f"d_{q}{i}") for i in range(n)] for q, n in NSLOT.items()}

    def _newsem(self, name):
        return self.es.enter_context(self.nc.semaphore(name))

    def _esem(self, e, ep):
        while len(self.esem[e]) <= ep:
            self.esem[e].append(self._newsem(f"e_{e}{len(self.esem[e])}"))
        return self.esem[e][ep]

    def _need(self, e, tok, waits):
        sem, val, src = tok
        if src == "pe" and e == "pe":
            return
        k = self.known[e]
        if k.get(id(sem), 0) >= val:
            return
        k[id(sem)] = val
        waits.append((sem, val))

    def _deps(self, e, r, w):
        waits = []
        for key in r:
            t = self.lastw.get(key)
            if t is not None:
                self._need(e, t, waits)
        for key in w:
            t = self.lastw.get(key)
            if t is not None:
                self._need(e, t, waits)
            for t in self.readers.get(key, ()):
                if t[2] == e:
                    continue
                self._need(e, t, waits)
        return waits

    def _commit(self, tok, r, w):
        for key in r:
            self.readers.setdefault(key, []).append(tok)
        for key in w:
            self.lastw[key] = tok
            self.readers[key] = []

    def op(self, e, fn, r=(), w=()):
        waits = self._deps(e, r, w)
        idx = self.cnt[e]
        self.cnt[e] += 1
        sem = self._esem(e, idx // EPOCH)
        val = idx % EPOCH + 1
        self.ops[e].append((waits, fn, (sem, 1)))
        tok = (sem, val, e)
        self.lasttok[e] = tok
        self._commit(tok, r, w)
        return tok

    def dma(self, q, out, in_, r=(), w=(), **kw):
        waits = self._deps(q, r, w)
        k = self.dcnt[q]
        self.dcnt[q] += 1
        n = NSLOT[q]
        sem = self.dsem[q][k % n]
        val = 16 * (k // n + 1)
        if k >= n:
            self._need(q, (sem, val - 16, "dma"), waits)
        fn = lambda eng: eng.dma_start(out=out, in_=in_, **kw)
        self.ops[q].append((waits, fn, (sem, 16)))
        tok = (sem, val, "dma")
        self._commit(tok, r, w)
        return tok

    def dma_tokens(self):
        toks = []
        for q, n in NSLOT.items():
            k = self.dcnt[q]
            for i in range(min(k, n)):
                j = ((k - 1 - i) // n) * n + i
                toks.append((self.dsem[q][i], 16 * (j // n + 1), "dma"))
        return toks

    def wait_all(self, e, toks):
        waits = []
        for t in toks:
            self._need(e, t, waits)
        if waits:
            self.ops[e].append((waits, None, None))

    def barrier(self):
        toks = list(self.lasttok.values()) + self.dma_tokens()
        for e in ENGS:
            self.wait_all(e, [t for t in toks if t[2] != e])

    def emit(self):
        nc = self.nc
        with nc.Block() as block:
            def run(e, eng):
                for waits, fn, inc in self.ops[e]:
                    for sem, val in waits:
                        eng.wait_ge(sem, val)
                    if fn is not None:
                        fn(eng).then_inc(inc[0], inc[1])

            @block.tensor
            def _(eng):
                run("pe", eng)

            @block.scalar
            def _(eng):
                run("act", eng)

            @block.vector
            def _(eng):
                run("dve", eng)

            @block.gpsimd
            def _(eng):
                run("pool", eng)

            @block.sync
            def _(eng):
                run("sp", eng)


D = 1024
E = 512
NMEM = 256
NIN = 13320
EPS = 1e-6
SEC = dict(a_val=0, a_glu=1, a_z=2, b_q=3, b_k=4, b_v=5, b_z=6, c_q=7, c_k=8, c_v=9,
           c_o=10, c_z=11, d_x=12, d_z=13, m_q=14, m_z=15)
GATE0 = 16 * 512
CIF0 = GATE0 + 5 * 1024


def col_layout():
    names = [("bin", 104), ("ng", 8), ("mng", 8), ("acw", 124), ("acb", 4), ("alg", 4), ("alb", 4),
             ("ccw", 32), ("ccb", 8), ("chg", 4), ("dcw", 16), ("dcb", 4), ("dba", 4), ("dbx", 4),
             ("dlam", 4), ("cib", 1), ("cfb", 1), ("cfb2", 1)]
    off = {}
    o = 0
    for n, w in names:
        off[n] = o
        o += w
    return off, o


COLS, NCOL = col_layout()
AR_BASE = 17152


class _Stop(Exception):
    pass


def build(S, NSEQ, DEPTH, dbg_names=(), stop=None):
    NT = S // 512
    NB = S // 128
    YW = 2 * S
    nc = bass.Bass("TRN2", target_bir_lowering=False)
    x_d = nc.dram_tensor("x", [NSEQ, S, D], F32, kind="ExternalInput").ap()
    mem_d = nc.dram_tensor("mem", [NSEQ, NMEM, D], F32, kind="ExternalInput").ap()
    win_d = nc.dram_tensor("w_in_r", [DEPTH, D, NIN], F32, kind="ExternalInput").ap()
    brow_d = nc.dram_tensor("brow", [DEPTH, 2, 512], F32, kind="ExternalInput").ap()
    cols_d = nc.dram_tensor("cols", [DEPTH, 128, NCOL], F32, kind="ExternalInput").ap()
    dw_d = nc.dram_tensor("dw", [DEPTH, 2, 4, 128, 128], F32, kind="ExternalInput").ap()
    wmkv_d = nc.dram_tensor("w_mkv", [DEPTH, D, 2 * E], F32, kind="ExternalInput").ap()
    wup_d = nc.dram_tensor("w_up", [DEPTH, 5, E, D], F32, kind="ExternalInput").ap()
    wout_d = nc.dram_tensor("w_out", [DEPTH, D, D], F32, kind="ExternalInput").ap()
    fcols_d = nc.dram_tensor("fcols", [128, 8], F32, kind="ExternalInput").ap()
    out_d = nc.dram_tensor("out", [NSEQ, S, D], F32, kind="ExternalOutput").ap()
    xs_d = nc.dram_tensor("xs", [NSEQ, 128, 8, S], F32, kind="Internal").ap()
    dbg_d = {}

    with ExitStack() as es:
        Sc = Sch(nc, es)

        def sb(name, shape, dt=F32):
            return es.enter_context(nc.sbuf_tensor("s_" + name, shape, dt))

        def ACT(out, in_, func, r, w, **kw):
            Sc.op("act", lambda e: e.activation(out=out, in_=in_, func=func, **kw), r, w)

        def TT(eng, out, in0, in1, op, r, w):
            Sc.op(eng, lambda e: e.tensor_tensor(out=out, in0=in0, in1=in1, op=op), r, w)

        def TS(eng, out, in0, s1, s2, op0, op1, r, w):
            if op1 is None:
                Sc.op(eng, lambda e: e.tensor_scalar(out=out, in0=in0, scalar1=s1, scalar2=None, op0=op0), r, w)
            else:
                Sc.op(eng, lambda e: e.tensor_scalar(out=out, in0=in0, scalar1=s1, scalar2=s2, op0=op0, op1=op1), r, w)

        def STT(out, in0, scalar, in1, op0, op1, r, w):
            Sc.op("dve", lambda e: e.scalar_tensor_tensor(out=out, in0=in0, scalar=scalar, in1=in1, op0=op0, op1=op1), r, w)

        def RECIP(out, in_, r, w):
            Sc.op("dve", lambda e: e.reciprocal(out=out, in_=in_), r, w)

        def MM(out, lhsT, rhs, start, stop, r, w):
            Sc.op("pe", lambda e: e.matmul(out, lhsT=lhsT, rhs=rhs, start=start, stop=stop), r, w)

        def TR(out, in_, idn, r, w):
            Sc.op("pe", lambda e: e.transpose(out, in_, idn), r, w)

        def CP(eng, out, in_, r, w):
            if eng == "act":
                Sc.op("act", lambda e: e.activation(out=out, in_=in_, func=AF.Copy), r, w)
            else:
                Sc.op(eng, lambda e: e.tensor_copy(out=out, in_=in_), r, w)

        def MEMSET(eng, ap, val, w):
            Sc.op(eng, lambda e: e.memset(ap, val), (), w)

        def ASEL(out, in_, pattern, cmp, cm, r, w, base=0):
            Sc.op("pool", lambda e: e.affine_select(out=out, in_=in_, pattern=pattern, compare_op=cmp, fill=0.0,
                                                    base=base, channel_multiplier=cm), r, w)

        def SCAN(out, d0, d1, init, op0, op1, r, w):
            Sc.op("dve", lambda e: e.tensor_tensor_scan(out=out, data0=d0, data1=d1, initial=init, op0=op0, op1=op1), r, w)

        def DMA(out, in_, r, w, q="sp", **kw):
            return Sc.dma(q, out, in_, r, w, **kw)

        def debug(name, ap, shape, key, dt=F32):
            if name not in dbg_names:
                return
            d = nc.dram_tensor("dbg_" + name, list(shape), dt, kind="ExternalOutput").ap()
            dbg_d[name] = d
            DMA(d, ap, [key] if not isinstance(key, list) else key, [("dbgd", name)])

        ident = sb("ident", [128, 128])
        identb = sb("identb", [128, 128], BF16)
        onesf = sb("onesf", [128, 128])
        onesD = sb("onesD", [128, 128])
        onesE = sb("onesE", [128, 128])
        sel4 = sb("sel4", [4, 4, 128])
        cols = sb("cols", [128, NCOL])
        fcols = sb("fcols", [128, 8])
        epsc = sb("epsc", [128, 1])
        hT = sb("hT", [128, 8, S], BF16)
        WST_N = 2
        wst = [sb(f"wst{i}", [128, 8, 256]) for i in range(WST_N)]
        AR = AR_BASE + 5 * YW
        arena = sb("arena", [128, AR])
        NPS = 6
        psb = [es.enter_context(nc.psum_tensor(f"ps{i}", [128, 512], F32)) for i in range(NPS)]
        psh = [es.enter_context(nc.psum_tensor(f"psh{i}", [128, 1024], BF16)) for i in range(2)]
        pskey = {id(t): ("ps", i) for i, t in enumerate(psb)}

        Y = [None] * 5
        for pos, n in enumerate((4, 3, 0, 1, 2)):
            a = AR_BASE + pos * YW
            Y[n] = arena[:, a:a + YW].bitcast(BF16).rearrange("p (a b) -> p a b", a=4)

        st = dict(ar=0, lim=AR_BASE, ps=0, psz=0, psh=0, wst=0, cast=0, ev=0, phase=0)
        tmpviews = {}

        def carve(name, free, dt=F32):
            n = int(np.prod(free))
            words = n if dt == F32 else (n + 1) // 2
            words = (words + 7) // 8 * 8
            a = st["ar"]
            assert a + words <= st["lim"], (name, a, words, st["lim"])
            st["ar"] = a + words
            v = arena[:, a:a + words]
            if dt != F32:
                v = v.bitcast(dt)
            v = v[:, 0:n]
            if len(free) == 2:
                v = v.rearrange("p (a b) -> p a b", a=free[0])
            elif len(free) == 3:
                v = v.rearrange("p (a b c) -> p a b c", a=free[0], b=free[1])
            return v

        def T(name, free, dt=F32):
            k = (name, st["phase"])
            if k not in tmpviews:
                tmpviews[k] = carve(name, free, dt)
            return tmpviews[k]

        def new_phase(extra=0):
            Sc.barrier()
            st["ar"] = 0
            st["lim"] = AR_BASE + extra * YW
            st["phase"] += 1

        def PS(pool="all"):
            if pool == "all":
                i = st["ps"] % NPS
                st["ps"] += 1
            else:
                i = st["psz"] % 4
                st["psz"] += 1
            return psb[i], ("ps", i)

        def PSH():
            i = st["psh"] % 2
            st["psh"] += 1
            return psh[i], ("psh", i)

        def ev_eng():
            st["ev"] += 1
            return "act" if st["ev"] % 2 else "dve"

        def onesrow(p, n):
            return onesf[0:p, 0:1].to_broadcast([p, n])

        MEMSET("pool", onesf[:], 1.0, ["onesf"])
        MEMSET("pool", onesD[:], 1.0 / D, ["onesD"])
        MEMSET("pool", onesE[:], 1.0 / E, ["onesE"])
        MEMSET("pool", epsc[:], EPS, ["epsc"])
        ASEL(ident[:], onesf[:], [[-1, 128]], ALU.is_equal, 1, ["onesf"], ["ident"])
        CP("pool", identb[:], ident[:], ["ident"], ["identb"])
        for h in range(4):
            ASEL(sel4[:, h, :], onesf[0:4, :], [[0, 128]], ALU.is_equal, 1, ["onesf"], ["sel4"], base=-h)
        DMA(fcols[:], fcols_d, [], ["fcols"])

        def col(name, i=0):
            o = COLS[name] + i
            return cols[:, o:o + 1]

        def load_w(src2d, K, ncols, dst, dkey):
            kc = K // 128
            c0 = 0
            while c0 < ncols:
                n = min(256, ncols - c0)
                i = st["wst"] % WST_N
                st["wst"] += 1
                stg = wst[i][:, 0:kc, 0:n]
                DMA(stg, src2d[:, c0:c0 + n].rearrange("(kc p) n -> p kc n", p=128), [], [("wst", i)])
                st["cast"] += 1
                eng = "pool" if st["cast"] % 2 else "act"
                CP(eng, dst[:, :, c0:c0 + n], stg, [("wst", i)], [dkey])
                c0 += n

        def rmsnorm_tile(xf, xkey, sq, sqkey, out_fn):
            ACT(sq, xf, AF.Square, [xkey], [sqkey])
            ps, pk = PS()
            for kc in range(8):
                MM(ps[:, 0:512], onesD[:], sq[:, kc, :], kc == 0, kc == 7, ["onesD", sqkey], [pk])
            rstd = T("rn_rstd", [512])
            ACT(rstd, ps[:, 0:512], AF.Sqrt, [pk, "epsc"], ["rn_rstd"], bias=epsc[:, 0:1])
            RECIP(rstd, rstd, ["rn_rstd"], ["rn_rstd"])
            for kc in range(8):
                out_fn(kc, rstd, "rn_rstd")

        def xskeys(seq, tt):
            return [("xs", seq, c, tt) for c in range(8)]

        def main_body():
          for seq in range(NSEQ):
            new_phase(5)
            for tb in range(NB):
                j = tb % 2
                xt = T(f"xin{j}", [D])
                xo = T(f"xout{j}", [8, 128])
                DMA(xt, x_d[seq, tb * 128:(tb + 1) * 128, :], [], [("xin", j)])
                for half in range(2):
                    ps, pk = PS()
                    for q in range(4):
                        kc = half * 4 + q
                        TR(ps[:, q * 128:(q + 1) * 128], xt[:, kc * 128:(kc + 1) * 128], ident[:], [("xin", j), "ident"], [pk])
                    CP(ev_eng(), xo[:, half * 4:(half + 1) * 4, :], ps[:, 0:512].rearrange("p (a b) -> p a b", a=4), [pk], [("xout", j)])
                DMA(xs_d[seq, :, :, tb * 128:(tb + 1) * 128], xo, [("xout", j)], xskeys(seq, tb // 4))

            for l in range(DEPTH):
                new_phase(5)
                DMA(cols[:], cols_d[l], [], ["cols"])
                for tt in range(NT):
                    j = tt % 2
                    xf = T(f"xf{j}", [8, 512])
                    sq = T("rn_sq", [8, 512])
                    DMA(xf, xs_d[seq, :, :, tt * 512:(tt + 1) * 512], xskeys(seq, tt), [("xf", j)])

                    def mk_h(kc, rstd, rkey, xf=xf, tt=tt, j=j):
                        STT(hT[:, kc, tt * 512:(tt + 1) * 512], xf[:, kc, :], col("ng", kc), rstd, ALU.mult, ALU.mult,
                            [("xf", j), "cols", rkey], ["hT"])
                    rmsnorm_tile(xf, ("xf", j), sq, "rn_sq", mk_h)
                debug(f"hT{l}", hT[:], [128, 8, S], "hT", BF16)
                if stop == "hT":
                    raise _Stop()

                def wsec(name):
                    return win_d[l, :, SEC[name] * 512:(SEC[name] + 1) * 512]

                def bcol(name, cc):
                    return col("bin", SEC[name] * 4 + cc)

                def proj(ps, wt, wkey, cc, tsl):
                    for kc in range(8):
                        MM(ps[:, 0:512], wt[:, kc, cc * 128:(cc + 1) * 128], hT[:, kc, tsl], kc == 0, kc == 7, [wkey, "hT"], [pskey[id(ps)]])

                new_phase(4)
                wC = [T(f"wC{i}", [8, 512], BF16) for i in range(2)]
                wci = T("wci", [8, 8], BF16)
                Vc = T("Vc", [NB, 4, 130], BF16)
                browc = T("browC", [512])
                Gt = T("Gt", [S])
                TSm = T("TSm", [NB, 96])
                expnm = T("expnm", [NB, 4])
                NGL = T("NGL", [4, NB + 1])
                PGL = T("PGL", [4, NB + 1])
                dec = T("decC", [4, NB])
                wint = T("wint", [NB, 4])
                wsta = T("wsta", [NB, 4])
                qC = T("qC", [4, S], BF16)
                kC = T("kC", [4, S], BF16)
                CTf = T("CTf", [4, 130])
                CTb = T("CTb", [4, 130], BF16)
                WT = [T(f"WTc{i}", [128]) for i in range(2)]
                STb = [T(f"STc{i}", [128], BF16) for i in range(2)]
                tmpi = [T(f"tmpiC{i}", [130]) for i in range(2)]
                nd = [T(f"ndC{i}", [130]) for i in range(2)]
                kw = [T(f"kwC{i}", [128], BF16) for i in range(2)]
                hn = [T(f"hnC{i}", [128]) for i in range(2)]
                sml = [T(f"smlC{i}", [16]) for i in range(2)]
                gtmp = [T(f"gtmpC{i}", [512]) for i in range(2)]
                mark = st["ar"]
                ibt = T("ibt", [S])
                Ft = T("Ft", [S])
                stk = T("stk", [S])

                DMA(browc, brow_d[l, 1, :].partition_broadcast(128), [], ["browC"])
                load_w(wsec("c_v"), D, 512, wC[0], "wC0")
                load_w(win_d[l, :, CIF0:CIF0 + 8], D, 8, wci, "wci")
                MEMSET("pool", Vc[:, :, :, 128:130], 1.0, ["Vc"])
                for tb in range(NB):
                    ps, pk = PS()
                    for kc in range(8):
                        MM(ps[:, 0:512], hT[:, kc, tb * 128:(tb + 1) * 128], wC[0][:, kc, :], kc == 0, kc == 7, ["wC0", "hT"], [pk])
                    TT("dve", Vc[:, tb, :, 0:128], ps[:, 0:512].rearrange("p (a b) -> p a b", a=4),
                       browc.rearrange("p (a b) -> p a b", a=4), ALU.add, [pk, "browC"], ["Vc"])
                MEMSET("pool", stk, 0.0, ["stk"])
                for tt in range(NT):
                    tsl = slice(tt * 512, (tt + 1) * 512)
                    ps, pk = PS()
                    for kc in range(8):
                        MM(ps[0:4, 0:512], wci[:, kc, 0:4], hT[:, kc, tsl], kc == 0, kc == 7, ["wci", "hT"], [pk])
                    ACT(ibt[0:4, tsl], ps[0:4, 0:512], AF.Identity, [pk, "cols"], ["ibt"], bias=cols[0:4, COLS["cib"]:COLS["cib"] + 1])
                    ps2, pk2 = PS()
                    for kc in range(8):
                        MM(ps2[0:4, 0:512], wci[:, kc, 4:8], hT[:, kc, tsl], kc == 0, kc == 7, ["wci", "hT"], [pk2])
                    ACT(Ft[0:4, tsl], ps2[0:4, 0:512], AF.Identity, [pk2, "cols"], ["Ft"], bias=cols[0:4, COLS["cfb"]:COLS["cfb"] + 1])
                    ACT(Ft[0:4, tsl], Ft[0:4, tsl], AF.Identity, ["Ft", "cols"], ["Ft"], bias=cols[0:4, COLS["cfb2"]:COLS["cfb2"] + 1])
                    ACT(Ft[0:4, tsl], Ft[0:4, tsl], AF.Exp, ["Ft"], ["Ft"], scale=-1.0)
                    ACT(Ft[0:4, tsl], Ft[0:4, tsl], AF.Ln, ["Ft"], ["Ft"], bias=1.0)
                SCAN(Gt[0:4, :], onesrow(4, S), Ft[0:4, :], 0.0, ALU.mult, ALU.subtract, ["Ft", "onesf"], ["Gt"])
                TT("dve", ibt[0:4, :], ibt[0:4, :], Gt[0:4, :], ALU.subtract, ["ibt", "Gt"], ["ibt"])
                SCAN(Ft[0:4, :], ibt[0:4, :], ibt[0:4, :], 0.0, ALU.max, ALU.max, ["ibt"], ["Ft"])
                TT("dve", Gt[0:4, :], Gt[0:4, :], Ft[0:4, :], ALU.add, ["Gt", "Ft"], ["Gt"])
                TS("dve", stk[32:36, :], Gt[0:4, :], -1.0, None, ALU.mult, None, ["Gt"], ["stk"])
                TS("dve", Gt[0:4, :], Ft[0:4, :], -1.0, None, ALU.mult, None, ["Ft"], ["Gt"])
                CP("dve", stk[64:68, :], Gt[0:4, :], ["Gt"], ["stk"])
                TS("dve", stk[0:4, :], ibt[0:4, :], math.log(128.0 ** -0.5), None, ALU.add, None, ["ibt"], ["stk"])
                for tb in range(NB):
                    ps, pk = PS()
                    TR(ps[:, 0:96], stk[0:96, tb * 128:(tb + 1) * 128], ident[0:96, 0:96], ["stk", "ident"], [pk])
                    CP(ev_eng(), TSm[:, tb, :], ps[:, 0:96], [pk], ["TSm"])
                ACT(expnm, TSm[:, :, 32:36], AF.Exp, ["TSm"], ["expnm"])
                MEMSET("pool", NGL, 0.0, ["NGL"])
                for h in range(4):
                    ps, pk = PS()
                    MM(ps[:, 0:NB], sel4[:, h, :], Gt[0:4, 127::128], True, True, ["sel4", "Gt"], [pk])
                    CP("dve", NGL[:, h, 1:NB + 1], ps[:, 0:NB], [pk], ["NGL"])
                TS("dve", PGL, NGL, -1.0, None, ALU.mult, None, ["NGL"], ["PGL"])
                TT("dve", dec, NGL[:, :, 1:NB + 1], NGL[:, :, 0:NB], ALU.subtract, ["NGL"], ["decC"])
                ACT(dec, dec, AF.Exp, ["decC"], ["decC"])
                for h in range(4):
                    for tb in range(NB):
                        ACT(wint[:, tb, h:h + 1], TSm[:, tb, 64 + h:65 + h], AF.Exp, ["TSm", "PGL"], ["wint"], bias=PGL[:, h, tb:tb + 1])
                        ACT(wsta[:, tb, h:h + 1], TSm[:, tb, h:h + 1], AF.Exp, ["TSm", "NGL"], ["wsta"], bias=NGL[:, h, tb + 1:tb + 2])
                debug(f"tsm{l}", TSm, [128, NB, 96], "TSm")
                debug(f"wint{l}", wint, [128, NB, 4], "wint")
                debug(f"wsta{l}", wsta, [128, NB, 4], "wsta")
                if stop == "Cprep":
                    raise _Stop()
                Sc.barrier()
                st["ar"] = mark
                cpad = T("cpad", [3 + S])
                cacc = [T(f"cacc{i}", [S]) for i in range(2)]
                load_w(wsec("c_q"), D, 512, wC[1], "wC1")
                load_w(wsec("c_k"), D, 512, wC[0], "wC0")
                MEMSET("pool", cpad[:, 0:3], 0.0, ["cpad"])
                it = 0
                for which, wt, wk, dst, sname in ((0, wC[1], "wC1", qC, "c_q"), (1, wC[0], "wC0", kC, "c_k")):
                    for h in range(4):
                        i = it % 2
                        it += 1
                        for tt in range(NT):
                            tsl = slice(tt * 512, (tt + 1) * 512)
                            ps, pk = PS()
                            proj(ps, wt, wk, h, tsl)
                            ACT(cpad[:, 3 + tt * 512:3 + (tt + 1) * 512], ps[:, 0:512], AF.Identity, [pk, "cols"], ["cpad"],
                                bias=bcol(sname, h))
                        ch = which * 4 + h
                        w0 = COLS["ccw"] + ch * 4
                        TS("dve", cacc[i], cpad[:, 0:S], cols[:, w0:w0 + 1], col("ccb", ch), ALU.mult, ALU.add,
                           ["cpad", "cols"], [("cacc", i)])
                        for jt in range(1, 4):
                            STT(cacc[i], cpad[:, jt:jt + S], cols[:, w0 + jt:w0 + jt + 1], cacc[i], ALU.mult, ALU.add,
                                ["cpad", "cols", ("cacc", i)], [("cacc", i)])
                        ACT(dst[:, h, :], cacc[i], AF.Silu, [("cacc", i)], [("qkC", which)])
                debug(f"qc{l}", qC, [128, 4, S], ("qkC", 0), BF16)
                debug(f"kc{l}", kC, [128, 4, S], ("qkC", 1), BF16)
                if stop == "Cqk":
                    raise _Stop()
                load_w(wsec("c_o"), D, 512, wC[1], "wC1")
                load_w(wsec("c_z"), D, 512, wC[0], "wC0")
                for h in range(4):
                    for tt in range(NT):
                        tsl = slice(tt * 512, (tt + 1) * 512)
                        j = tt % 2
                        ps, pk = PS()
                        proj(ps, wC[1], "wC1", h, tsl)
                        ACT(gtmp[j], ps[:, 0:512], AF.Sigmoid, [pk, "cols"], [("gtmpC", j)], bias=bcol("c_o", h))
                        ps2, pk2 = PS()
                        proj(ps2, wC[0], "wC0", h, tsl)
                        ACT(Y[2][:, h, tsl], ps2[:, 0:512], AF.Silu, [pk2, "cols"], [("Y", 2)], bias=bcol("c_z", h))
                        TT("dve", Y[2][:, h, tsl], Y[2][:, h, tsl], gtmp[j], ALU.mult, [("Y", 2), ("gtmpC", j)], [("Y", 2)])
                MEMSET("pool", CTf, 0.0, [("CTf", h) for h in range(4)])
                MEMSET("pool", CTb, 0.0, [("CT", h) for h in range(4)])
                it = 0
                for tb in range(NB):
                    bsl = slice(tb * 128, (tb + 1) * 128)
                    for h in range(4):
                        j = it % 2
                        it += 1
                        ck = ("CT", h)
                        ps, pk = PS()
                        MM(ps[:, 0:128], kC[:, h, bsl], qC[:, h, bsl], True, True, [("qkC", 0), ("qkC", 1)], [pk])
                        psg, pkg = PS()
                        MM(psg[:, 0:128], sel4[:, h, :], Gt[0:4, bsl], True, True, ["sel4", "Gt"], [pkg])
                        ACT(WT[j], psg[:, 0:128], AF.Exp, [pkg, "TSm"], [("WTc", j)], bias=TSm[:, tb, h:h + 1])
                        ASEL(WT[j], WT[j], [[1, 128]], ALU.is_ge, -1, [("WTc", j)], [("WTc", j)])
                        TT("dve", STb[j], ps[:, 0:128], WT[j], ALU.mult, [pk, ("WTc", j)], [("STc", j)])
                        psn, pkn = PS()
                        MM(psn[:, 0:129], STb[j], Vc[:, tb, h, 0:129], True, True, [("STc", j), "Vc"], [pkn])
                        psi, pki = PS()
                        MM(psi[:, 0:129], qC[:, h, bsl], CTb[:, h, 0:129], True, True, [("qkC", 0), ck], [pki])
                        ACT(tmpi[j][:, 0:129], psi[:, 0:129], AF.Copy, [pki, "wint"], [("tmpiC", j)], scale=wint[:, tb, h:h + 1])
                        TT("dve", nd[j][:, 0:129], psn[:, 0:129], tmpi[j][:, 0:129], ALU.add, [pkn, ("tmpiC", j)], [("ndC", j)])
                        ACT(sml[j][:, 12:13], nd[j][:, 128:129], AF.Abs, [("ndC", j)], [("smlC", j)])
                        TS("dve", sml[j][:, 0:1], sml[j][:, 12:13], expnm[:, tb, h:h + 1], None, ALU.max, None,
                           [("smlC", j), "expnm"], [("smlC", j)])
                        RECIP(sml[j][:, 1:2], sml[j][:, 0:1], [("smlC", j)], [("smlC", j)])
                        TS("dve", nd[j][:, 0:128], nd[j][:, 0:128], sml[j][:, 1:2], None, ALU.mult, None, [("ndC", j), ("smlC", j)], [("ndC", j)])
                        Sc.op("dve", lambda e, j=j: e.bn_stats(out=sml[j][:, 2:8], in_=nd[j][:, 0:128]), [("ndC", j)], [("smlC", j)])
                        Sc.op("dve", lambda e, j=j: e.bn_aggr(out=sml[j][:, 8:10], in_=sml[j][:, 2:8]), [("smlC", j)], [("smlC", j)])
                        ACT(sml[j][:, 10:11], sml[j][:, 9:10], AF.Sqrt, [("smlC", j), "epsc"], [("smlC", j)], bias=epsc[:, 0:1])
                        RECIP(sml[j][:, 11:12], sml[j][:, 10:11], [("smlC", j)], [("smlC", j)])
                        TS("dve", hn[j], nd[j][:, 0:128], sml[j][:, 8:9], sml[j][:, 11:12], ALU.subtract, ALU.mult,
                           [("ndC", j), ("smlC", j)], [("hnC", j)])
                        pst, pkt = PS()
                        TR(pst[:, 0:128], hn[j], ident[:], [("hnC", j), "ident"], [pkt])
                        STT(Y[2][:, h, bsl], pst[:, 0:128], col("chg", h), Y[2][:, h, bsl], ALU.mult, ALU.mult, [pkt, "cols", ("Y", 2)], [("Y", 2)])
                        if tb < NB - 1:
                            ph, phk = PSH()
                            TR(ph[:, 0:128], kC[:, h, bsl], identb[:], [("qkC", 1), "identb"], [phk])
                            TS("dve", kw[j], ph[:, 0:128], wsta[:, tb, h:h + 1], None, ALU.mult, None, [phk, "wsta"], [("kwC", j)])
                            psu, pku = PS()
                            MM(psu[:, 0:129], kw[j], Vc[:, tb, h, 0:129], True, True, [("kwC", j), "Vc"], [pku])
                            STT(CTf[:, h, 0:129], CTf[:, h, 0:129], dec[:, h, tb:tb + 1], psu[:, 0:129], ALU.mult, ALU.add,
                                [("CTf", h), "decC", pku], [("CTf", h)])
                            CP("act", CTb[:, h, 0:129], CTf[:, h, 0:129], [("CTf", h)], [ck])
                debug(f"yc{l}", Y[2], [128, 4, S], ("Y", 2), BF16)
                if stop == "C":
                    raise _Stop()

                new_phase(3)
                wB = [T(f"wB{i}", [8, 512], BF16) for i in range(2)]
                Vb = T("Vb", [NB, 512], BF16)
                brow = T("browB", [512])
                qB = T("qB", [S], BF16)
                kB = T("kB", [S], BF16)
                zs2 = [T(f"zsB{i}", [S]) for i in range(3)]
                spb2 = [T(f"spB{i}", [S + 1]) for i in range(3)]
                wbf2 = [T(f"wbfB{i}", [S], BF16) for i in range(3)]
                itb = 0
                wT = [T(f"wTB{i}", [4, 128], BF16) for i in range(2)]
                ob = T("obB", [128])
                DMA(brow, brow_d[l, 0, :].partition_broadcast(128), [], ["browB"])
                load_w(wsec("b_v"), D, 512, wB[0], "wB0")
                for tb in range(NB):
                    ps, pk = PS()
                    for kc in range(8):
                        MM(ps[:, 0:512], hT[:, kc, tb * 128:(tb + 1) * 128], wB[0][:, kc, :], kc == 0, kc == 7, ["wB0", "hT"], [pk])
                    TT("dve", Vb[:, tb, :], ps[:, 0:512], brow, ALU.add, [pk, "browB"], ["Vb"])
                load_w(wsec("b_q"), D, 512, wB[1], "wB1")
                load_w(wsec("b_k"), D, 512, wB[0], "wB0")
                sc_b = 64.0 ** -0.5
                itsB = [(pr, qb, hh) for pr in range(4) for qb in range(NB) for hh in range(2)]

                def projqk(pr):
                    for tt in range(NT):
                        tsl = slice(tt * 512, (tt + 1) * 512)
                        ps, pk = PS("z")
                        proj(ps, wB[1], "wB1", pr, tsl)
                        ACT(qB[:, tsl], ps[:, 0:512], AF.Identity, [pk, "cols"], ["qB"], bias=bcol("b_q", pr))
                        ps2, pk2 = PS("z")
                        proj(ps2, wB[0], "wB0", pr, tsl)
                        ACT(kB[:, tsl], ps2[:, 0:512], AF.Identity, [pk2, "cols"], ["kB"], bias=bcol("b_k", pr))

                def bufsB(idx):
                    jb = idx % 3
                    return (zs2[jb], spb2[jb], None, wbf2[jb], ("zsB", jb), ("spB", jb), ("latB", jb), ("wbfB", jb))

                def S1(idx):
                    pr, qb, hh = itsB[idx]
                    zs, spb, lat, wbf, kz, ksp, kla, kwb = bufsB(idx)
                    L = (qb + 1) * 128
                    bsl = slice(qb * 128, (qb + 1) * 128)
                    pl = slice(hh * 64, (hh + 1) * 64)
                    nk = (L + 511) // 512
                    zps = []
                    for ki in range(nk):
                        n = min(512, L - ki * 512)
                        ps, pk = PS("z")
                        MM(ps[:, 0:n], qB[pl, bsl], kB[pl, ki * 512:ki * 512 + n], True, True, ["qB", "kB"], [pk])
                        zps.append((ps, pk, n))
                    for ki, (ps, pk, n) in enumerate(zps):
                        ksl = slice(ki * 512, ki * 512 + n)
                        ACT(spb[:, ksl], ps[:, 0:n], AF.Exp, [pk], [ksp], scale=sc_b)
                        ACT(zs[:, ksl], ps[:, 0:n], AF.Copy, [pk], [kz], scale=sc_b)
                    ACT(spb[:, 0:L], spb[:, 0:L], AF.Ln, [ksp], [ksp], bias=1.0)
                    ASEL(spb[:, qb * 128:qb * 128 + 129], spb[:, qb * 128:qb * 128 + 129], [[-1, 129]], ALU.is_gt, 1, [ksp], [ksp])
                    TT("dve", zs[:, 0:L], zs[:, 0:L], spb[:, 0:L], ALU.subtract, [kz, ksp], [kz])
                    SCAN(spb[:, 1:L + 1][:, ::-1], onesrow(128, L), spb[:, 1:L + 1][:, ::-1], 0.0, ALU.mult, ALU.add,
                         [ksp, "onesf"], [ksp])
                    TT("dve", zs[:, 0:L], zs[:, 0:L], spb[:, 1:L + 1], ALU.subtract, [kz, ksp], [kz])

                def S2(idx):
                    pr, qb, hh = itsB[idx]
                    zs, spb, lat, wbf, kz, ksp, kla, kwb = bufsB(idx)
                    L = (qb + 1) * 128
                    bsl = slice(qb * 128, (qb + 1) * 128)
                    pso, pko = psb[4 + qb % 2], ("ps", 4 + qb % 2)
                    ACT(wbf[:, 0:L], zs[:, 0:L], AF.Exp, [kz], [kwb])
                    ASEL(wbf[:, bsl], wbf[:, bsl], [[-1, 128]], ALU.is_gt, 1, [kwb], [kwb])
                    nkb = qb + 1
                    for g0 in range(0, nkb, 4):
                        gn = min(4, nkb - g0)
                        ph, phk = PSH()
                        j = (g0 // 4) % 2
                        for q in range(gn):
                            kb = g0 + q
                            TR(ph[:, q * 128:(q + 1) * 128], wbf[:, kb * 128:(kb + 1) * 128], identb[:], [kwb, "identb"], [phk])
                        CP(ev_eng(), wT[j][:, 0:gn, :], ph[:, 0:gn * 128].rearrange("p (a b) -> p a b", a=gn), [phk], [("wTB", j)])
                        for q in range(gn):
                            kb = g0 + q
                            hc = (pr * 2 + hh) * 64
                            MM(pso[:, hh * 64:(hh + 1) * 64], wT[j][:, q, :], Vb[:, kb, hc:hc + 64],
                               kb == 0, kb == nkb - 1, [("wTB", j), "Vb"], [pko])
                    if hh == 1:
                        CP("act", ob, pso[:, 0:128], [pko], ["obB"])
                        pst, pkt = PS("z")
                        TR(pst[:, 0:128], ob, ident[:], ["obB", "ident"], [pkt])
                        CP("dve", Y[1][:, pr, bsl], pst[:, 0:128], [pkt], [("Y", 1)])

                projqk(0)
                S1(0)
                S1(1)
                for idx in range(len(itsB)):
                    if idx + 2 < len(itsB):
                        if itsB[idx + 2][0] != itsB[idx + 1][0]:
                            projqk(itsB[idx + 2][0])
                        S1(idx + 2)
                    S2(idx)
                debug(f"yb_pre{l}", Y[1], [128, 4, S], ("Y", 1), BF16)
                load_w(wsec("b_z"), D, 512, wB[1], "wB1")
                for pr in range(4):
                    for tt in range(NT):
                        tsl = slice(tt * 512, (tt + 1) * 512)
                        ps, pk = PS()
                        proj(ps, wB[1], "wB1", pr, tsl)
                        ACT(qB[:, tsl], ps[:, 0:512], AF.Silu, [pk, "cols"], ["qB"], bias=bcol("b_z", pr))
                        TT("dve", Y[1][:, pr, tsl], Y[1][:, pr, tsl], qB[:, tsl], ALU.mult, [("Y", 1), "qB"], [("Y", 1)])
                debug(f"yb{l}", Y[1], [128, 4, S], ("Y", 1), BF16)
                if stop == "B":
                    raise _Stop()

                new_phase(2)
                wA = [T(f"wA{i}", [8, 512], BF16) for i in range(2)]
                yconv = T("yconv", [4, S])
                upad = [T("upad0", [30 + S])] * 2
                sg = [T(f"sgA{i}", [512]) for i in range(2)]
                ysq = T("ysq", [4, 512])
                mean_s = T("meanA", [512])
                rstd_s = T("rstdA", [512])
                tn = [T(f"tnA{i}", [512]) for i in range(2)]
                za = [T(f"zaA{i}", [512]) for i in range(2)]
                load_w(wsec("a_val"), D, 512, wA[0], "wA0")
                load_w(wsec("a_glu"), D, 512, wA[1], "wA1")
                MEMSET("pool", upad[0][:, 0:30], 0.0, [("upad", 0)])
                for cc in range(4):
                    i = 0
                    for tt in range(NT):
                        tsl = slice(tt * 512, (tt + 1) * 512)
                        j = tt % 2
                        ps, pk = PS()
                        proj(ps, wA[1], "wA1", cc, tsl)
                        ACT(sg[j], ps[:, 0:512], AF.Sigmoid, [pk, "cols"], [("sgA", j)], bias=bcol("a_glu", cc))
                        ps2, pk2 = PS()
                        proj(ps2, wA[0], "wA0", cc, tsl)
                        STT(upad[i][:, 30 + tt * 512:30 + (tt + 1) * 512], ps2[:, 0:512], bcol("a_val", cc), sg[j],
                            ALU.add, ALU.mult, [pk2, "cols", ("sgA", j)], [("upad", i)])
                    acw0 = COLS["acw"] + cc * 31
                    TS("dve", yconv[:, cc, :], upad[i][:, 0:S], cols[:, acw0:acw0 + 1], col("acb", cc), ALU.mult, ALU.add,
                       [("upad", i), "cols"], [("yconv", cc)])
                    for jt in range(1, 31):
                        STT(yconv[:, cc, :], upad[i][:, jt:jt + S], cols[:, acw0 + jt:acw0 + jt + 1], yconv[:, cc, :],
                            ALU.mult, ALU.add, [("upad", i), "cols", ("yconv", cc)], [("yconv", cc)])
                debug(f"yconv{l}", yconv, [128, 4, S], [("yconv", c) for c in range(4)])
                load_w(wsec("a_z"), D, 512, wA[0], "wA0")
                for tt in range(NT):
                    tsl = slice(tt * 512, (tt + 1) * 512)
                    ACT(ysq, yconv[:, :, tsl], AF.Square, [("yconv", c) for c in range(4)], ["ysq"])
                    psm, pkm = PS()
                    for cc in range(4):
                        MM(psm[:, 0:512], onesE[:], yconv[:, cc, tsl], cc == 0, cc == 3, ["onesE", ("yconv", cc)], [pkm])
                    pss, pks = PS()
                    for cc in range(4):
                        MM(pss[:, 0:512], onesE[:], ysq[:, cc, :], cc == 0, cc == 3, ["onesE", "ysq"], [pks])
                    CP("act", mean_s, psm[:, 0:512], [pkm], ["meanA"])
                    TT("dve", rstd_s, mean_s, mean_s, ALU.mult, ["meanA"], ["rstdA"])
                    TT("dve", rstd_s, pss[:, 0:512], rstd_s, ALU.subtract, [pks, "rstdA"], ["rstdA"])
                    ACT(rstd_s, rstd_s, AF.Sqrt, ["rstdA", "epsc"], ["rstdA"], bias=epsc[:, 0:1])
                    RECIP(rstd_s, rstd_s, ["rstdA"], ["rstdA"])
                    for cc in range(4):
                        j = cc % 2
                        TT("pool", tn[j], yconv[:, cc, tsl], mean_s, ALU.subtract, [("yconv", cc), "meanA"], [("tnA", j)])
                        TT("dve", tn[j], tn[j], rstd_s, ALU.mult, [("tnA", j), "rstdA"], [("tnA", j)])
                        ACT(tn[j], tn[j], AF.Silu, [("tnA", j), "cols"], [("tnA", j)], scale=col("alg", cc), bias=col("alb", cc))
                        ps, pk = PS()
                        proj(ps, wA[0], "wA0", cc, tsl)
                        ACT(za[j], ps[:, 0:512], AF.Silu, [pk, "cols"], [("zaA", j)], bias=bcol("a_z", cc))
                        TT("dve", Y[0][:, cc, tsl], tn[j], za[j], ALU.mult, [("tnA", j), ("zaA", j)], [("Y", 0)])
                debug(f"ya{l}", Y[0], [128, 4, S], ("Y", 0), BF16)
                if stop == "A":
                    raise _Stop()

                new_phase(1)
                wD = [T(f"wD{i}", [8, 512], BF16) for i in range(2)]
                dwall = T("dwall", [8, 128])
                c1 = T("c1", [4])
                dpad = T("dpad", [3 + S])
                xc = T("xcD", [S])
                av = T("avD", [S])
                uv = T("uvD", [S])
                gi = [T(f"giD{i}", [512]) for i in range(2)]
                zd = [T(f"zdD{i}", [512]) for i in range(2)]
                DMA(dwall, dw_d[l].rearrange("g c p d -> p (g c) d"), [], ["dwall"])
                load_w(wsec("d_x"), D, 512, wD[0], "wD0")
                load_w(wsec("d_z"), D, 512, wD[1], "wD1")
                ACT(c1, cols[:, COLS["dlam"]:COLS["dlam"] + 4], AF.Exp, ["cols"], ["c1"], scale=-1.0)
                ACT(c1, c1, AF.Ln, ["c1"], ["c1"], bias=1.0)
                TS("dve", c1, c1, -8.0, None, ALU.mult, None, ["c1"], ["c1"])
                MEMSET("pool", dpad[:, 0:3], 0.0, ["dpad"])
                for cc in range(4):
                    for tt in range(NT):
                        tsl = slice(tt * 512, (tt + 1) * 512)
                        ps, pk = PS()
                        proj(ps, wD[0], "wD0", cc, tsl)
                        ACT(dpad[:, 3 + tt * 512:3 + (tt + 1) * 512], ps[:, 0:512], AF.Identity, [pk, "cols"], ["dpad"],
                            bias=bcol("d_x", cc))
                    w0 = COLS["dcw"] + cc * 4
                    TS("dve", xc, dpad[:, 0:S], cols[:, w0:w0 + 1], col("dcb", cc), ALU.mult, ALU.add, ["dpad", "cols"], ["xcD"])
                    for jt in range(1, 4):
                        STT(xc, dpad[:, jt:jt + S], cols[:, w0 + jt:w0 + jt + 1], xc, ALU.mult, ALU.add, ["dpad", "cols", "xcD"], ["xcD"])
                    for tt in range(NT):
                        tsl = slice(tt * 512, (tt + 1) * 512)
                        j = tt % 2
                        psa, pka = PS()
                        MM(psa[:, 0:512], dwall[:, 0 * 4 + cc, :], xc[:, tsl], True, True, ["dwall", "xcD"], [pka])
                        psx, pkx = PS()
                        MM(psx[:, 0:512], dwall[:, 1 * 4 + cc, :], xc[:, tsl], True, True, ["dwall", "xcD"], [pkx])
                        ACT(av[:, tsl], psa[:, 0:512], AF.Sigmoid, [pka, "cols"], ["avD"], bias=col("dba", cc))
                        ACT(av[:, tsl], av[:, tsl], AF.Exp, ["avD", "c1"], ["avD"], scale=c1[:, cc:cc + 1])
                        ACT(gi[j], psx[:, 0:512], AF.Sigmoid, [pkx, "cols"], [("giD", j)], bias=col("dbx", cc))
                        TT("pool", uv[:, tsl], av[:, tsl], av[:, tsl], ALU.mult, ["avD"], ["uvD"])
                        ACT(uv[:, tsl], uv[:, tsl], AF.Sqrt, ["uvD"], ["uvD"], scale=-1.0, bias=1.0)
                        TT("pool", gi[j], gi[j], xc[:, tsl], ALU.mult, [("giD", j), "xcD"], [("giD", j)])
                        TT("dve", uv[:, tsl], uv[:, tsl], gi[j], ALU.mult, ["uvD", ("giD", j)], ["uvD"])
                    SCAN(xc, av, uv, 0.0, ALU.mult, ALU.add, ["avD", "uvD", "xcD"], ["xcD"])
                    for tt in range(NT):
                        tsl = slice(tt * 512, (tt + 1) * 512)
                        j = tt % 2
                        ps, pk = PS()
                        proj(ps, wD[1], "wD1", cc, tsl)
                        ACT(zd[j], ps[:, 0:512], AF.Silu, [pk, "cols"], [("zdD", j)], bias=bcol("d_z", cc))
                        TT("dve", Y[3][:, cc, tsl], xc[:, tsl], zd[j], ALU.mult, ["xcD", ("zdD", j)], [("Y", 3)])
                debug(f"yd{l}", Y[3], [128, 4, S], ("Y", 3), BF16)
                if stop == "D":
                    raise _Stop()

                new_phase(0)
                memT = T("memT", [8, NMEM], BF16)
                mkT = T("mkT", [4, NMEM], BF16)
                mv = T("mv", [2, 512], BF16)
                wM = [T(f"wM{i}", [8, 512], BF16) for i in range(2)]
                mrs = T("mrs", [2])
                mq = T("memsq", [D])
                mxs = [T(f"memx{mt}", [D]) for mt in range(2)]
                qm = T("qm", [S], BF16)
                zm = T("zm", [S], BF16)
                pbuf = [T(f"pm{i}", [NMEM], BF16) for i in range(2)]
                pT = [T(f"pTm{i}", [2, 128], BF16) for i in range(2)]
                on = [T(f"onm{i}", [128]) for i in range(2)]
                sm = [T(f"smm{i}", [4]) for i in range(2)]
                for mt in range(2):
                    mx = mxs[mt]
                    DMA(mx, mem_d[seq, mt * 128:(mt + 1) * 128, :], [], [("memx", mt)])
                    ACT(mq, mx, AF.Square, [("memx", mt)], ["memsq", ("mrs", mt)], accum_out=mrs[:, mt:mt + 1])
                    TS("dve", mrs[:, mt:mt + 1], mrs[:, mt:mt + 1], 1.0 / D, EPS, ALU.mult, ALU.add, [("mrs", mt)], [("mrs", mt)])
                    ACT(mrs[:, mt:mt + 1], mrs[:, mt:mt + 1], AF.Sqrt, [("mrs", mt)], [("mrs", mt)])
                    RECIP(mrs[:, mt:mt + 1], mrs[:, mt:mt + 1], [("mrs", mt)], [("mrs", mt)])
                    TS("dve", mx, mx, mrs[:, mt:mt + 1], None, ALU.mult, None, [("memx", mt), ("mrs", mt)], [("memx", mt)])
                    for half in range(2):
                        ps, pk = PS()
                        for q in range(4):
                            kc = half * 4 + q
                            TR(ps[:, q * 128:(q + 1) * 128], mx[:, kc * 128:(kc + 1) * 128], ident[:], [("memx", mt), "ident"], [pk])
                        for q in range(4):
                            kc = half * 4 + q
                            ACT(memT[:, kc, mt * 128:(mt + 1) * 128], ps[:, q * 128:(q + 1) * 128], AF.Copy, [pk, "cols"], ["memT"],
                                scale=col("mng", kc))
                load_w(wmkv_d[l, :, 0:512], D, 512, wM[0], "wM0")
                load_w(wmkv_d[l, :, 512:1024], D, 512, wM[1], "wM1")
                for h in range(4):
                    ps, pk = PS()
                    for kc in range(8):
                        MM(ps[:, 0:NMEM], wM[0][:, kc, h * 128:(h + 1) * 128], memT[:, kc, :], kc == 0, kc == 7, ["wM0", "memT"], [pk])
                    CP(ev_eng(), mkT[:, h, :], ps[:, 0:NMEM], [pk], ["mkT"])
                for mt in range(2):
                    ps, pk = PS()
                    for kc in range(8):
                        MM(ps[:, 0:512], memT[:, kc, mt * 128:(mt + 1) * 128], wM[1][:, kc, :], kc == 0, kc == 7, ["wM1", "memT"], [pk])
                    CP(ev_eng(), mv[:, mt, :], ps[:, 0:512], [pk], ["mv"])
                load_w(wsec("m_q"), D, 512, wM[0], "wM0")
                load_w(wsec("m_z"), D, 512, wM[1], "wM1")
                sc_m = 128.0 ** -0.5
                for h in range(4):
                    for tt in range(NT):
                        tsl = slice(tt * 512, (tt + 1) * 512)
                        ps, pk = PS()
                        proj(ps, wM[0], "wM0", h, tsl)
                        ACT(qm[:, tsl], ps[:, 0:512], AF.Identity, [pk, "cols"], ["qm"], bias=bcol("m_q", h))
                        ps2, pk2 = PS()
                        proj(ps2, wM[1], "wM1", h, tsl)
                        ACT(zm[:, tsl], ps2[:, 0:512], AF.Silu, [pk2, "cols"], ["zm"], bias=bcol("m_z", h))
                    for tb in range(NB):
                        bsl = slice(tb * 128, (tb + 1) * 128)
                        j = tb % 2
                        ps, pk = PS()
                        MM(ps[:, 0:NMEM], qm[:, bsl], mkT[:, h, :], True, True, ["qm", "mkT"], [pk])
                        Sc.op("dve", lambda e, ps=ps, j=j: e.reduce_max(out=sm[j][:, 0:1], in_=ps[:, 0:NMEM], axis=mybir.AxisListType.X),
                              [pk], [("smm", j)])
                        TS("dve", sm[j][:, 1:2], sm[j][:, 0:1], -sc_m, None, ALU.mult, None, [("smm", j)], [("smm", j)])
                        ACT(pbuf[j], ps[:, 0:NMEM], AF.Exp, [pk, ("smm", j)], [("pm", j), ("smm", j)], scale=sc_m, bias=sm[j][:, 1:2],
                            accum_out=sm[j][:, 2:3])
                        ph, phk = PSH()
                        for mt in range(2):
                            TR(ph[:, mt * 128:(mt + 1) * 128], pbuf[j][:, mt * 128:(mt + 1) * 128], identb[:], [("pm", j), "identb"], [phk])
                        CP(ev_eng(), pT[j], ph[:, 0:256].rearrange("p (a b) -> p a b", a=2), [phk], [("pTm", j)])
                        pso, pko = PS()
                        for mt in range(2):
                            MM(pso[:, 0:128], pT[j][:, mt, :], mv[:, mt, h * 128:(h + 1) * 128], mt == 0, mt == 1, [("pTm", j), "mv"], [pko])
                        RECIP(sm[j][:, 3:4], sm[j][:, 2:3], [("smm", j)], [("smm", j)])
                        TS("dve", on[j], pso[:, 0:128], sm[j][:, 3:4], None, ALU.mult, None, [pko, ("smm", j)], [("onm", j)])
                        pst, pkt = PS()
                        TR(pst[:, 0:128], on[j], ident[:], [("onm", j), "ident"], [pkt])
                        TT("dve", Y[4][:, h, bsl], pst[:, 0:128], zm[:, bsl], ALU.mult, [pkt, "zm"], [("Y", 4)])
                debug(f"ym{l}", Y[4], [128, 4, S], ("Y", 4), BF16)
                if stop == "M":
                    raise _Stop()

                new_phase(0)
                mg = T("mg", [8, S], BF16)
                mark = st["ar"]
                RING = 4
                wgn = [T(f"wgn{i}", [8, 128], BF16) for i in range(RING)]
                wun = [T(f"wun{i}", [4, 128], BF16) for i in range(RING)]
                sgm = [T(f"sgm{i}", [512]) for i in range(2)]
                acc = T("accm", [NT, 512])
                order = [(c, n) for c in range(8) for n in range(5)]

                def ldm(idx):
                    c, n = order[idx]
                    i = idx % RING
                    g0 = GATE0 + (c * 5 + n) * 128
                    load_w(win_d[l, :, g0:g0 + 128], D, 128, wgn[i], ("wgn", i))
                    load_w(wup_d[l, n, :, c * 128:(c + 1) * 128], E, 128, wun[i], ("wun", i))
                for idx in range(RING - 1):
                    ldm(idx)
                it = 0
                for idx, (c, n) in enumerate(order):
                    i = idx % RING
                    if idx + RING - 1 < len(order):
                        ldm(idx + RING - 1)
                    for tt in range(NT):
                        tsl = slice(tt * 512, (tt + 1) * 512)
                        j = it % 2
                        it += 1
                        psg, pkg = PS()
                        for kc in range(8):
                            MM(psg[:, 0:512], wgn[i][:, kc, :], hT[:, kc, tsl], kc == 0, kc == 7, [("wgn", i), "hT"], [pkg])
                        ACT(sgm[j], psg[:, 0:512], AF.Sigmoid, [pkg, "cols"], [("sgm", j)], bias=col("bin", 64 + c * 5 + n))
                        psu, pku = PS()
                        for kc in range(4):
                            MM(psu[:, 0:512], wun[i][:, kc, :], Y[n][:, kc, tsl], kc == 0, kc == 3, [("wun", i), ("Y", n)], [pku])
                        if n == 0:
                            TT("dve", acc[:, tt, :], sgm[j], psu[:, 0:512], ALU.mult, [("sgm", j), pku], [("accm", tt)])
                        else:
                            TT("dve", sgm[j], sgm[j], psu[:, 0:512], ALU.mult, [("sgm", j), pku], [("sgm", j)])
                            if n < 4:
                                TT("pool", acc[:, tt, :], acc[:, tt, :], sgm[j], ALU.add, [("accm", tt), ("sgm", j)], [("accm", tt)])
                            else:
                                TT("pool", mg[:, c, tsl], acc[:, tt, :], sgm[j], ALU.add, [("accm", tt), ("sgm", j)], ["mg"])
                debug(f"mg{l}", mg, [128, 8, S], "mg", BF16)
                if stop == "merge":
                    raise _Stop()
                Sc.barrier()
                st["ar"] = mark
                st["lim"] = AR_BASE + 5 * YW
                wo = [T(f"wo{i}", [8, 128], BF16) for i in range(2)]
                xbuf = T("xbuf", [8, S])
                allxs = [("xs", seq, c, tt) for c in range(8) for tt in range(NT)]
                DMA(xbuf, xs_d[seq], allxs, ["xbuf"])
                load_w(wout_d[l, :, 0:128], D, 128, wo[0], ("wo", 0))
                Sc.barrier()
                for c in range(8):
                    i = c % 2
                    if c + 1 < 8:
                        load_w(wout_d[l, :, (c + 1) * 128:(c + 2) * 128], D, 128, wo[(c + 1) % 2], ("wo", (c + 1) % 2))
                    for tt in range(NT):
                        tsl = slice(tt * 512, (tt + 1) * 512)
                        ps, pk = PS()
                        for kc in range(8):
                            MM(ps[:, 0:512], wo[i][:, kc, :], mg[:, kc, tsl], kc == 0, kc == 7, [("wo", i), "mg"], [pk])
                        TT("dve", xbuf[:, c, tsl], xbuf[:, c, tsl], ps[:, 0:512], ALU.add, ["xbuf", pk], ["xbuf"])
                Sc.barrier()
                DMA(xs_d[seq], xbuf, ["xbuf"], allxs)
                if stop == "resid":
                    raise _Stop()

            new_phase(5)
            ot = [T(f"ot{i}", [D]) for i in range(2)]
            it = 0
            for tt in range(NT):
                j = tt % 2
                xf = T(f"xf{j}", [8, 512])
                yo = T(f"yo{j}", [8, 512])
                DMA(xf, xs_d[seq, :, :, tt * 512:(tt + 1) * 512], xskeys(seq, tt), [("xf", j)])

                def mk_o(kc, rstd, rkey, xf=xf, yo=yo, j=j):
                    STT(yo[:, kc, :], xf[:, kc, :], fcols[:, kc:kc + 1], rstd, ALU.mult, ALU.mult,
                        [("xf", j), "fcols", rkey], [("yo", j)])
                rmsnorm_tile(xf, ("xf", j), yo, ("yo", j), mk_o)
                for q4 in range(4):
                    jo = it % 2
                    it += 1
                    for half in range(2):
                        ps, pk = PS()
                        for q in range(4):
                            kc = half * 4 + q
                            TR(ps[:, q * 128:(q + 1) * 128], yo[:, kc, q4 * 128:(q4 + 1) * 128], ident[:], [("yo", j), "ident"], [pk])
                        CP(ev_eng(), ot[jo][:, half * 512:(half + 1) * 512], ps[:, 0:512], [pk], [("ot", jo)])
                    tb = tt * 4 + q4
                    DMA(out_d[seq, tb * 128:(tb + 1) * 128, :], ot[jo], [("ot", jo)], [("outd", seq, tb)])

        try:
            main_body()
        except _Stop:
            pass
        Sc.barrier()
        Sc.emit()
    return nc, dbg_d


def prep_weights(inp):
    DEPTH = inp["w_in"].shape[0]
    perm = np.concatenate([np.arange(0, 5120), np.arange(5128, 8200)] +
                          [8200 + n * 1024 + c * 128 + np.arange(128) for c in range(8) for n in range(5)] +
                          [np.arange(5120, 5128)])
    w_in_r = np.ascontiguousarray(np.asarray(inp["w_in"], np.float32)[:, :, perm])
    b_in_r = np.asarray(inp["b_in"], np.float32)[:, perm]
    cols = np.zeros((DEPTH, 128, NCOL), np.float32)

    def put(l, name, arr):
        cols[l, :, COLS[name]:COLS[name] + arr.shape[1]] = arr

    def pc(v):
        return np.asarray(v, np.float32).reshape(-1, 128).T

    brow = np.zeros((DEPTH, 2, 512), np.float32)
    dw = np.zeros((DEPTH, 2, 4, 128, 128), np.float32)
    for l in range(DEPTH):
        put(l, "bin", pc(b_in_r[l, :13312]))
        put(l, "ng", pc(inp["norm_g"][l]))
        put(l, "mng", pc(inp["mem_norm_g"][l]))
        acw = np.asarray(inp["a_conv_w"][l], np.float32)
        put(l, "acw", acw.T.reshape(4, 128, 31).transpose(1, 0, 2).reshape(128, 124))
        put(l, "acb", pc(inp["a_conv_b"][l]))
        put(l, "alg", pc(inp["a_ln_g"][l]))
        put(l, "alb", pc(inp["a_ln_b"][l]))
        ccw = np.asarray(inp["c_conv_w"][l], np.float32)
        put(l, "ccw", ccw.T.reshape(8, 128, 4).transpose(1, 0, 2).reshape(128, 32))
        put(l, "ccb", pc(inp["c_conv_b"][l]))
        put(l, "chg", pc(inp["c_hn_g"][l]))
        dcw = np.asarray(inp["d_conv_w"][l], np.float32)
        put(l, "dcw", dcw.T.reshape(4, 128, 4).transpose(1, 0, 2).reshape(128, 16))
        put(l, "dcb", pc(inp["d_conv_b"][l]))
        put(l, "dba", pc(inp["d_ba"][l]))
        put(l, "dbx", pc(inp["d_bx"][l]))
        put(l, "dlam", pc(inp["d_lambda"][l]))
        cols[l, 0:4, COLS["cib"]] = b_in_r[l, 13312:13316]
        cols[l, 0:4, COLS["cfb"]] = b_in_r[l, 13316:13320]
        cols[l, 0:4, COLS["cfb2"]] = np.asarray(inp["c_f_bias"][l], np.float32)
        brow[l, 0] = b_in_r[l, SEC["b_v"] * 512:(SEC["b_v"] + 1) * 512]
        brow[l, 1] = b_in_r[l, SEC["c_v"] * 512:(SEC["c_v"] + 1) * 512]
        for g, nm in enumerate(("d_wa", "d_wx")):
            wgt = np.asarray(inp[nm][l], np.float32)
            for cc in range(4):
                dw[l, g, cc, 0:64, 0:64] = wgt[2 * cc]
                dw[l, g, cc, 64:128, 64:128] = wgt[2 * cc + 1]
    fcols = np.ascontiguousarray(pc(inp["final_norm_g"]))
    return dict(w_in_r=w_in_r, brow=brow, cols=cols, dw=dw,
                w_mkv=np.ascontiguousarray(np.asarray(inp["w_mkv"], np.float32)),
                w_up=np.ascontiguousarray(np.asarray(inp["w_up"], np.float32)),
                w_out=np.ascontiguousarray(np.asarray(inp["w_out"], np.float32)),
                fcols=fcols)


_NC_CACHE = {}


def kernel(**inputs):
    x = np.asarray(inputs["x"], np.float32)
    mem = np.asarray(inputs["mem"], np.float32)
    B, S, _ = x.shape
    DEPTH = inputs["w_in"].shape[0]
    ncores = 8
    nseq = B // ncores
    wts = prep_weights(inputs)
    key = (S, nseq, DEPTH)
    if key not in _NC_CACHE:
        _NC_CACHE[key] = build(S, nseq, DEPTH)[0]
    nc = _NC_CACHE[key]
    in_maps = []
    for c in range(ncores):
        m = dict(wts)
        m["x"] = np.ascontiguousarray(x[c * nseq:(c + 1) * nseq])
        m["mem"] = np.ascontiguousarray(mem[c * nseq:(c + 1) * nseq])
        in_maps.append(m)
    res = run_bass_kernel_spmd(nc, in_maps, core_ids=list(range(ncores)))
    out = np.concatenate([np.asarray(r["out"], np.float32) for r in res.results], axis=0)
    return out
```

```python
import math
from contextlib import ExitStack
import numpy as np
import concourse.bass as bass
import concourse.mybir as mybir
from concourse.bass_utils import run_bass_kernel_spmd

F32 = mybir.dt.float32
BF16 = mybir.dt.bfloat16
AF = mybir.ActivationFunctionType
ALU = mybir.AluOpType

ENGS = ("pe", "act", "dve", "pool", "sp")
EPOCH = 24000
NSLOT = {"sp": 8, "pool": 4}


class Sch:
    def __init__(self, nc, es):
        self.nc = nc
        self.es = es
        self.ops = {e: [] for e in ENGS}
        self.cnt = {e: 0 for e in ENGS}
        self.esem = {e: [] for e in ENGS}
        self.known = {e: {} for e in ENGS}
        self.lastw = {}
        self.readers = {}
        self.lasttok = {}
        self.dcnt = {q: 0 for q in NSLOT}
        self.dsem = {q: [self._newsem(f"d_{q}{i}") for i in range(n)] for q, n in NSLOT.items()}

    def _newsem(self, name):
        return self.es.enter_context(self.nc.semaphore(name))

    def _esem(self, e, ep):
        while len(self.esem[e]) <= ep:
            self.esem[e].append(self._newsem(f"e_{e}{len(self.esem[e])}"))
        return self.esem[e][ep]

    def _need(self, e, tok, waits):
        sem, val, src = tok
        if src == "pe" and e == "pe":
            return
        k = self.known[e]
        if k.get(id(sem), 0) >= val:
            return
        k[id(sem)] = val
        waits.append((sem, val))

    def _deps(self, e, r, w):
        waits = []
        for key in r:
            t = self.lastw.get(key)
            if t is not None:
                self._need(e, t, waits)
        for key in w:
            t = self.lastw.get(key)
            if t is not None:
                self._need(e, t, waits)
            for t in self.readers.get(key, ()):
                if t[2] == e:
                    continue
                self._need(e, t, waits)
        return waits

    def _commit(self, tok, r, w):
        for key in r:
            self.readers.setdefault(key, []).append(tok)
        for key in w:
            self.lastw[key] = tok
            self.readers[key] = []

    def op(self, e, fn, r=(), w=()):
        waits = self._deps(e, r, w)
        idx = self.cnt[e]
        self.cnt[e] += 1
        sem = self._esem(e, idx // EPOCH)
        val = idx % EPOCH + 1
        self.ops[e].append((waits, fn, (sem, 1)))
        tok = (sem, val, e)
        self.lasttok[e] = tok
        self._commit(tok, r, w)
        return tok

    def dma(self, q, out, in_, r=(), w=(), **kw):
        waits = self._deps(q, r, w)
        k = self.dcnt[q]
        self.dcnt[q] += 1
        n = NSLOT[q]
        sem = self.dsem[q][k % n]
        val = 16 * (k // n + 1)
        if k >= n:
            self._need(q, (sem, val - 16, "dma"), waits)
        fn = lambda eng: eng.dma_start(out=out, in_=in_, **kw)
        self.ops[q].append((waits, fn, (sem, 16)))
        tok = (sem, val, "dma")
        self._commit(tok, r, w)
        return tok

    def dma_tokens(self):
        toks = []
        for q, n in NSLOT.items():
            k = self.dcnt[q]
            for i in range(min(k, n)):
                j = ((k - 1 - i) // n) * n + i
                toks.append((self.dsem[q][i], 16 * (j // n + 1), "dma"))
        return toks

    def wait_all(self, e, toks):
        waits = []
        for t in toks:
            self._need(e, t, waits)
        if waits:
            self.ops[e].append((waits, None, None))

    def barrier(self):
        toks = list(self.lasttok.values()) + self.dma_tokens()
        for e in ENGS:
            self.wait_all(e, [t for t in toks if t[2] != e])

    def emit(self):
        nc = self.nc
        with nc.Block() as block:
            def run(e, eng):
                for waits, fn, inc in self.ops[e]:
                    for sem, val in waits:
                        eng.wait_ge(sem, val)
                    if fn is not None:
                        fn(eng).then_inc(inc[0], inc[1])

            @block.tensor
            def _(eng):
                run("pe", eng)

            @block.scalar
            def _(eng):
                run("act", eng)

            @block.vector
            def _(eng):
                run("dve", eng)

            @block.gpsimd
            def _(eng):
                run("pool", eng)

            @block.sync
            def _(eng):
                run("sp", eng)


D = 1024
E = 512
NMEM = 256
NIN = 13320
EPS = 1e-6
SEC = dict(a_val=0, a_glu=1, a_z=2, b_q=3, b_k=4, b_v=5, b_z=6, c_q=7, c_k=8, c_v=9,
           c_o=10, c_z=11, d_x=12, d_z=13, m_q=14, m_z=15)
GATE0 = 16 * 512
CIF0 = GATE0 + 5 * 1024


def col_layout():
    names = [("bin", 104), ("ng", 8), ("mng", 8), ("acw", 124), ("acb", 4), ("alg", 4), ("alb", 4),
             ("ccw", 32), ("ccb", 8), ("chg", 4), ("dcw", 16), ("dcb", 4), ("dba", 4), ("dbx", 4),
             ("dlam", 4), ("cib", 1), ("cfb", 1), ("cfb2", 1)]
    off = {}
    o = 0
    for n, w in names:
        off[n] = o
        o += w
    return off, o


COLS, NCOL = col_layout()
AR_BASE = 17152


class _Stop(Exception):
    pass


def build(S, NSEQ, DEPTH, dbg_names=(), stop=None):
    NT = S // 512
    NB = S // 128
    YW = 2 * S
    nc = bass.Bass("TRN2", target_bir_lowering=False)
    x_d = nc.dram_tensor("x", [NSEQ, S, D], F32, kind="ExternalInput").ap()
    mem_d = nc.dram_tensor("mem", [NSEQ, NMEM, D], F32, kind="ExternalInput").ap()
    win_d = nc.dram_tensor("w_in_r", [DEPTH, D, NIN], F32, kind="ExternalInput").ap()
    brow_d = nc.dram_tensor("brow", [DEPTH, 2, 512], F32, kind="ExternalInput").ap()
    cols_d = nc.dram_tensor("cols", [DEPTH, 128, NCOL], F32, kind="ExternalInput").ap()
    dw_d = nc.dram_tensor("dw", [DEPTH, 2, 4, 128, 128], F32, kind="ExternalInput").ap()
    wmkv_d = nc.dram_tensor("w_mkv", [DEPTH, D, 2 * E], F32, kind="ExternalInput").ap()
    wup_d = nc.dram_tensor("w_up", [DEPTH, 5, E, D], F32, kind="ExternalInput").ap()
    wout_d = nc.dram_tensor("w_out", [DEPTH, D, D], F32, kind="ExternalInput").ap()
    fcols_d = nc.dram_tensor("fcols", [128, 8], F32, kind="ExternalInput").ap()
    out_d = nc.dram_tensor("out", [NSEQ, S, D], F32, kind="ExternalOutput").ap()
    xs_d = nc.dram_tensor("xs", [NSEQ, 128, 8, S], F32, kind="Internal").ap()
    dbg_d = {}

    with ExitStack() as es:
        Sc = Sch(nc, es)

        def sb(name, shape, dt=F32):
            return es.enter_context(nc.sbuf_tensor("s_" + name, shape, dt))

        def ACT(out, in_, func, r, w, **kw):
            Sc.op("act", lambda e: e.activation(out=out, in_=in_, func=func, **kw), r, w)

        def TT(eng, out, in0, in1, op, r, w):
            Sc.op(eng, lambda e: e.tensor_tensor(out=out, in0=in0, in1=in1, op=op), r, w)

        def TS(eng, out, in0, s1, s2, op0, op1, r, w):
            if op1 is None:
                Sc.op(eng, lambda e: e.tensor_scalar(out=out, in0=in0, scalar1=s1, scalar2=None, op0=op0), r, w)
            else:
                Sc.op(eng, lambda e: e.tensor_scalar(out=out, in0=in0, scalar1=s1, scalar2=s2, op0=op0, op1=op1), r, w)

        def STT(out, in0, scalar, in1, op0, op1, r, w):
            Sc.op("dve", lambda e: e.scalar_tensor_tensor(out=out, in0=in0, scalar=scalar, in1=in1, op0=op0, op1=op1), r, w)

        def RECIP(out, in_, r, w):
            Sc.op("dve", lambda e: e.reciprocal(out=out, in_=in_), r, w)

        def MM(out, lhsT, rhs, start, stop, r, w):
            Sc.op("pe", lambda e: e.matmul(out, lhsT=lhsT, rhs=rhs, start=start, stop=stop), r, w)

        def TR(out, in_, idn, r, w):
            Sc.op("pe", lambda e: e.transpose(out, in_, idn), r, w)

        def CP(eng, out, in_, r, w):
            if eng == "act":
                Sc.op("act", lambda e: e.activation(out=out, in_=in_, func=AF.Copy), r, w)
            else:
                Sc.op(eng, lambda e: e.tensor_copy(out=out, in_=in_), r, w)

        def MEMSET(eng, ap, val, w):
            Sc.op(eng, lambda e: e.memset(ap, val), (), w)

        def ASEL(out, in_, pattern, cmp, cm, r, w, base=0):
            Sc.op("pool", lambda e: e.affine_select(out=out, in_=in_, pattern=pattern, compare_op=cmp, fill=0.0,
                                                    base=base, channel_multiplier=cm), r, w)

        def SCAN(out, d0, d1, init, op0, op1, r, w):
            Sc.op("dve", lambda e: e.tensor_tensor_scan(out=out, data0=d0, data1=d1, initial=init, op0=op0, op1=op1), r, w)

        def DMA(out, in_, r, w, q="sp", **kw):
            return Sc.dma(q, out, in_, r, w, **kw)

        def debug(name, ap, shape, key, dt=F32):
            if name not in dbg_names:
                return
            d = nc.dram_tensor("dbg_" + name, list(shape), dt, kind="ExternalOutput").ap()
            dbg_d[name] = d
            DMA(d, ap, [key] if not isinstance(key, list) else key, [("dbgd", name)])

        ident = sb("ident", [128, 128])
        identb = sb("identb", [128, 128], BF16)
        onesf = sb("onesf", [128, 128])
        onesD = sb("onesD", [128, 128])
        onesE = sb("onesE", [128, 128])
        sel4 = sb("sel4", [4, 4, 128])
        cols = sb("cols", [128, NCOL])
        fcols = sb("fcols", [128, 8])
        epsc = sb("epsc", [128, 1])
        hT = sb("hT", [128, 8, S], BF16)
        WST_N = 2
        wst = [sb(f"wst{i}", [128, 8, 256]) for i in range(WST_N)]
        AR = AR_BASE + 5 * YW
        arena = sb("arena", [128, AR])
        NPS = 6
        psb = [es.enter_context(nc.psum_tensor(f"ps{i}", [128, 512], F32)) for i in range(NPS)]
        psh = [es.enter_context(nc.psum_tensor(f"psh{i}", [128, 1024], BF16)) for i in range(2)]
        pskey = {id(t): ("ps", i) for i, t in enumerate(psb)}

        Y = [None] * 5
        for pos, n in enumerate((4, 3, 0, 1, 2)):
            a = AR_BASE + pos * YW
            Y[n] = arena[:, a:a + YW].bitcast(BF16).rearrange("p (a b) -> p a b", a=4)

        st = dict(ar=0, lim=AR_BASE, ps=0, psz=0, psh=0, wst=0, cast=0, ev=0, phase=0)
        tmpviews = {}

        def carve(name, free, dt=F32):
            n = int(np.prod(free))
            words = n if dt == F32 else (n + 1) // 2
            words = (words + 7) // 8 * 8
            a = st["ar"]
            assert a + words <= st["lim"], (name, a, words, st["lim"])
            st["ar"] = a + words
            v = arena[:, a:a + words]
            if dt != F32:
                v = v.bitcast(dt)
            v = v[:, 0:n]
            if len(free) == 2:
                v = v.rearrange("p (a b) -> p a b", a=free[0])
            elif len(free) == 3:
                v = v.rearrange("p (a b c) -> p a b c", a=free[0], b=free[1])
            return v

        def T(name, free, dt=F32):
            k = (name, st["phase"])
            if k not in tmpviews:
                tmpviews[k] = carve(name, free, dt)
            return tmpviews[k]

        def new_phase(extra=0):
            Sc.barrier()
            st["ar"] = 0
            st["lim"] = AR_BASE + extra * YW
            st["phase"] += 1

        def PS(pool="all"):
            if pool == "all":
                i = st["ps"] % NPS
                st["ps"] += 1
            else:
                i = st["psz"] % 4
                st["psz"] += 1
            return psb[i], ("ps", i)

        def PSH():
            i = st["psh"] % 2
            st["psh"] += 1
            return psh[i], ("psh", i)

        def ev_eng():
            st["ev"] += 1
            return "act" if st["ev"] % 2 else "dve"

        def onesrow(p, n):
            return onesf[0:p, 0:1].to_broadcast([p, n])

        MEMSET("pool", onesf[:], 1.0, ["onesf"])
        MEMSET("pool", onesD[:], 1.0 / D, ["onesD"])
        MEMSET("pool", onesE[:], 1.0 / E, ["onesE"])
        MEMSET("pool", epsc[:], EPS, ["epsc"])
        ASEL(ident[:], onesf[:], [[-1, 128]], ALU.is_equal, 1, ["onesf"], ["ident"])
        CP("pool", identb[:], ident[:], ["ident"], ["identb"])
        for h in range(4):
            ASEL(sel4[:, h, :], onesf[0:4, :], [[0, 128]], ALU.is_equal, 1, ["onesf"], ["sel4"], base=-h)
        DMA(fcols[:], fcols_d, [], ["fcols"])

        def col(name, i=0):
            o = COLS[name] + i
            return cols[:, o:o + 1]

        def load_w(src2d, K, ncols, dst, dkey):
            kc = K // 128
            c0 = 0
            while c0 < ncols:
                n = min(256, ncols - c0)
                i = st["wst"] % WST_N
                st["wst"] += 1
                stg = wst[i][:, 0:kc, 0:n]
                DMA(stg, src2d[:, c0:c0 + n].rearrange("(kc p) n -> p kc n", p=128), [], [("wst", i)])
                st["cast"] += 1
                eng = "pool" if st["cast"] % 2 else "act"
                CP(eng, dst[:, :, c0:c0 + n], stg, [("wst", i)], [dkey])
                c0 += n

        def rmsnorm_tile(xf, xkey, sq, sqkey, out_fn):
            ACT(sq, xf, AF.Square, [xkey], [sqkey])
            ps, pk = PS()
            for kc in range(8):
                MM(ps[:, 0:512], onesD[:], sq[:, kc, :], kc == 0, kc == 7, ["onesD", sqkey], [pk])
            rstd = T("rn_rstd", [512])
            ACT(rstd, ps[:, 0:512], AF.Sqrt, [pk, "epsc"], ["rn_rstd"], bias=epsc[:, 0:1])
            RECIP(rstd, rstd, ["rn_rstd"], ["rn_rstd"])
            for kc in range(8):
                out_fn(kc, rstd, "rn_rstd")

        def xskeys(seq, tt):
            return [("xs", seq, c, tt) for c in range(8)]

        def main_body():
          for seq in range(NSEQ):
            new_phase(5)
            for tb in range(NB):
                j = tb % 2
                xt = T(f"xin{j}", [D])
                xo = T(f"xout{j}", [8, 128])
                DMA(xt, x_d[seq, tb * 128:(tb + 1) * 128, :], [], [("xin", j)])
                for half in range(2):
                    ps, pk = PS()
                    for q in range(4):
                        kc = half * 4 + q
                        TR(ps[:, q * 128:(q + 1) * 128], xt[:, kc * 128:(kc + 1) * 128], ident[:], [("xin", j), "ident"], [pk])
                    CP(ev_eng(), xo[:, half * 4:(half + 1) * 4, :], ps[:, 0:512].rearrange("p (a b) -> p a b", a=4), [pk], [("xout", j)])
                DMA(xs_d[seq, :, :, tb * 128:(tb + 1) * 128], xo, [("xout", j)], xskeys(seq, tb // 4))

            for l in range(DEPTH):
                new_phase(5)
                DMA(cols[:], cols_d[l], [], ["cols"])
                for tt in range(NT):
                    j = tt % 2
                    xf = T(f"xf{j}", [8, 512])
                    sq = T("rn_sq", [8, 512])
                    DMA(xf, xs_d[seq, :, :, tt * 512:(tt + 1) * 512], xskeys(seq, tt), [("xf", j)])

                    def mk_h(kc, rstd, rkey, xf=xf, tt=tt, j=j):
                        STT(hT[:, kc, tt * 512:(tt + 1) * 512], xf[:, kc, :], col("ng", kc), rstd, ALU.mult, ALU.mult,
                            [("xf", j), "cols", rkey], ["hT"])
                    rmsnorm_tile(xf, ("xf", j), sq, "rn_sq", mk_h)
                debug(f"hT{l}", hT[:], [128, 8, S], "hT", BF16)
                if stop == "hT":
                    raise _Stop()

                def wsec(name):
                    return win_d[l, :, SEC[name] * 512:(SEC[name] + 1) * 512]

                def bcol(name, cc):
                    return col("bin", SEC[name] * 4 + cc)

                def proj(ps, wt, wkey, cc, tsl):
                    for kc in range(8):
                        MM(ps[:, 0:512], wt[:, kc, cc * 128:(cc + 1) * 128], hT[:, kc, tsl], kc == 0, kc == 7, [wkey, "hT"], [pskey[id(ps)]])

                new_phase(4)
                wC = [T(f"wC{i}", [8, 512], BF16) for i in range(2)]
                wci = T("wci", [8, 8], BF16)
                Vc = T("Vc", [NB, 4, 130], BF16)
                browc = T("browC", [512])
                Gt = T("Gt", [S])
                TSm = T("TSm", [NB, 96])
                expnm = T("expnm", [NB, 4])
                NGL = T("NGL", [4, NB + 1])
                PGL = T("PGL", [4, NB + 1])
                dec = T("decC", [4, NB])
                wint = T("wint", [NB, 4])
                wsta = T("wsta", [NB, 4])
                qC = T("qC", [4, S], BF16)
                kC = T("kC", [4, S], BF16)
                CTf = T("CTf", [4, 130])
                CTb = T("CTb", [4, 130], BF16)
                WT = [T(f"WTc{i}", [128]) for i in range(2)]
                STb = [T(f"STc{i}", [128], BF16) for i in range(2)]
                tmpi = [T(f"tmpiC{i}", [130]) for i in range(2)]
                nd = [T(f"ndC{i}", [130]) for i in range(2)]
                kw = [T(f"kwC{i}", [128], BF16) for i in range(2)]
                hn = [T(f"hnC{i}", [128]) for i in range(2)]
                sml = [T(f"smlC{i}", [16]) for i in range(2)]
                gtmp = [T(f"gtmpC{i}", [512]) for i in range(2)]
                mark = st["ar"]
                ibt = T("ibt", [S])
                Ft = T("Ft", [S])
                stk = T("stk", [S])

                DMA(browc, brow_d[l, 1, :].partition_broadcast(128), [], ["browC"])
                load_w(wsec("c_v"), D, 512, wC[0], "wC0")
                load_w(win_d[l, :, CIF0:CIF0 + 8], D, 8, wci, "wci")
                MEMSET("pool", Vc[:, :, :, 128:130], 1.0, ["Vc"])
                for tb in range(NB):
                    ps, pk = PS()
                    for kc in range(8):
                        MM(ps[:, 0:512], hT[:, kc, tb * 128:(tb + 1) * 128], wC[0][:, kc, :], kc == 0, kc == 7, ["wC0", "hT"], [pk])
                    TT("dve", Vc[:, tb, :, 0:128], ps[:, 0:512].rearrange("p (a b) -> p a b", a=4),
                       browc.rearrange("p (a b) -> p a b", a=4), ALU.add, [pk, "browC"], ["Vc"])
                MEMSET("pool", stk, 0.0, ["stk"])
                for tt in range(NT):
                    tsl = slice(tt * 512, (tt + 1) * 512)
                    ps, pk = PS()
                    for kc in range(8):
                        MM(ps[0:4, 0:512], wci[:, kc, 0:4], hT[:, kc, tsl], kc == 0, kc == 7, ["wci", "hT"], [pk])
                    ACT(ibt[0:4, tsl], ps[0:4, 0:512], AF.Identity, [pk, "cols"], ["ibt"], bias=cols[0:4, COLS["cib"]:COLS["cib"] + 1])
                    ps2, pk2 = PS()
                    for kc in range(8):
                        MM(ps2[0:4, 0:512], wci[:, kc, 4:8], hT[:, kc, tsl], kc == 0, kc == 7, ["wci", "hT"], [pk2])
                    ACT(Ft[0:4, tsl], ps2[0:4, 0:512], AF.Identity, [pk2, "cols"], ["Ft"], bias=cols[0:4, COLS["cfb"]:COLS["cfb"] + 1])
                    ACT(Ft[0:4, tsl], Ft[0:4, tsl], AF.Identity, ["Ft", "cols"], ["Ft"], bias=cols[0:4, COLS["cfb2"]:COLS["cfb2"] + 1])
                    ACT(Ft[0:4, tsl], Ft[0:4, tsl], AF.Exp, ["Ft"], ["Ft"], scale=-1.0)
                    ACT(Ft[0:4, tsl], Ft[0:4, tsl], AF.Ln, ["Ft"], ["Ft"], bias=1.0)
                SCAN(Gt[0:4, :], onesrow(4, S), Ft[0:4, :], 0.0, ALU.mult, ALU.subtract, ["Ft", "onesf"], ["Gt"])
                TT("dve", ibt[0:4, :], ibt[0:4, :], Gt[0:4, :], ALU.subtract, ["ibt", "Gt"], ["ibt"])
                SCAN(Ft[0:4, :], ibt[0:4, :], ibt[0:4, :], 0.0, ALU.max, ALU.max, ["ibt"], ["Ft"])
                TT("dve", Gt[0:4, :], Gt[0:4, :], Ft[0:4, :], ALU.add, ["Gt", "Ft"], ["Gt"])
                TS("dve", stk[32:36, :], Gt[0:4, :], -1.0, None, ALU.mult, None, ["Gt"], ["stk"])
                TS("dve", Gt[0:4, :], Ft[0:4, :], -1.0, None, ALU.mult, None, ["Ft"], ["Gt"])
                CP("dve", stk[64:68, :], Gt[0:4, :], ["Gt"], ["stk"])
                TS("dve", stk[0:4, :], ibt[0:4, :], math.log(128.0 ** -0.5), None, ALU.add, None, ["ibt"], ["stk"])
                for tb in range(NB):
                    ps, pk = PS()
                    TR(ps[:, 0:96], stk[0:96, tb * 128:(tb + 1) * 128], ident[0:96, 0:96], ["stk", "ident"], [pk])
                    CP(ev_eng(), TSm[:, tb, :], ps[:, 0:96], [pk], ["TSm"])
                ACT(expnm, TSm[:, :, 32:36], AF.Exp, ["TSm"], ["expnm"])
                MEMSET("pool", NGL, 0.0, ["NGL"])
                for h in range(4):
                    ps, pk = PS()
                    MM(ps[:, 0:NB], sel4[:, h, :], Gt[0:4, 127::128], True, True, ["sel4", "Gt"], [pk])
                    CP("dve", NGL[:, h, 1:NB + 1], ps[:, 0:NB], [pk], ["NGL"])
                TS("dve", PGL, NGL, -1.0, None, ALU.mult, None, ["NGL"], ["PGL"])
                TT("dve", dec, NGL[:, :, 1:NB + 1], NGL[:, :, 0:NB], ALU.subtract, ["NGL"], ["decC"])
                ACT(dec, dec, AF.Exp, ["decC"], ["decC"])
                for h in range(4):
                    for tb in range(NB):
                        ACT(wint[:, tb, h:h + 1], TSm[:, tb, 64 + h:65 + h], AF.Exp, ["TSm", "PGL"], ["wint"], bias=PGL[:, h, tb:tb + 1])
                        ACT(wsta[:, tb, h:h + 1], TSm[:, tb, h:h + 1], AF.Exp, ["TSm", "NGL"], ["wsta"], bias=NGL[:, h, tb + 1:tb + 2])
                debug(f"tsm{l}", TSm, [128, NB, 96], "TSm")
                debug(f"wint{l}", wint, [128, NB, 4], "wint")
                debug(f"wsta{l}", wsta, [128, NB, 4], "wsta")
                if stop == "Cprep":
                    raise _Stop()
                Sc.barrier()
                st["ar"] = mark
                cpad = T("cpad", [3 + S])
                cacc = [T(f"cacc{i}", [S]) for i in range(2)]
                load_w(wsec("c_q"), D, 512, wC[1], "wC1")
                load_w(wsec("c_k"), D, 512, wC[0], "wC0")
                MEMSET("pool", cpad[:, 0:3], 0.0, ["cpad"])
                it = 0
                for which, wt, wk, dst, sname in ((0, wC[1], "wC1", qC, "c_q"), (1, wC[0], "wC0", kC, "c_k")):
                    for h in range(4):
                        i = it % 2
                        it += 1
                        for tt in range(NT):
                            tsl = slice(tt * 512, (tt + 1) * 512)
                            ps, pk = PS()
                            proj(ps, wt, wk, h, tsl)
                            ACT(cpad[:, 3 + tt * 512:3 + (tt + 1) * 512], ps[:, 0:512], AF.Identity, [pk, "cols"], ["cpad"],
                                bias=bcol(sname, h))
                        ch = which * 4 + h
                        w0 = COLS["ccw"] + ch * 4
                        TS("dve", cacc[i], cpad[:, 0:S], cols[:, w0:w0 + 1], col("ccb", ch), ALU.mult, ALU.add,
                           ["cpad", "cols"], [("cacc", i)])
                        for jt in range(1, 4):
                            STT(cacc[i], cpad[:, jt:jt + S], cols[:, w0 + jt:w0 + jt + 1], cacc[i], ALU.mult, ALU.add,
                                ["cpad", "cols", ("cacc", i)], [("cacc", i)])
                        ACT(dst[:, h, :], cacc[i], AF.Silu, [("cacc", i)], [("qkC", which)])
                debug(f"qc{l}", qC, [128, 4, S], ("qkC", 0), BF16)
                debug(f"kc{l}", kC, [128, 4, S], ("qkC", 1), BF16)
                if stop == "Cqk":
                    raise _Stop()
                load_w(wsec("c_o"), D, 512, wC[1], "wC1")
                load_w(wsec("c_z"), D, 512, wC[0], "wC0")
                for h in range(4):
                    for tt in range(NT):
                        tsl = slice(tt * 512, (tt + 1) * 512)
                        j = tt % 2
                        ps, pk = PS()
                        proj(ps, wC[1], "wC1", h, tsl)
                        ACT(gtmp[j], ps[:, 0:512], AF.Sigmoid, [pk, "cols"], [("gtmpC", j)], bias=bcol("c_o", h))
                        ps2, pk2 = PS()
                        proj(ps2, wC[0], "wC0", h, tsl)
                        ACT(Y[2][:, h, tsl], ps2[:, 0:512], AF.Silu, [pk2, "cols"], [("Y", 2)], bias=bcol("c_z", h))
                        TT("dve", Y[2][:, h, tsl], Y[2][:, h, tsl], gtmp[j], ALU.mult, [("Y", 2), ("gtmpC", j)], [("Y", 2)])
                MEMSET("pool", CTf, 0.0, [("CTf", h) for h in range(4)])
                MEMSET("pool", CTb, 0.0, [("CT", h) for h in range(4)])
                it = 0
                for tb in range(NB):
                    bsl = slice(tb * 128, (tb + 1) * 128)
                    for h in range(4):
                        j = it % 2
                        it += 1
                        ck = ("CT", h)
                        ps, pk = PS()
                        MM(ps[:, 0:128], kC[:, h, bsl], qC[:, h, bsl], True, True, [("qkC", 0), ("qkC", 1)], [pk])
                        psg, pkg = PS()
                        MM(psg[:, 0:128], sel4[:, h, :], Gt[0:4, bsl], True, True, ["sel4", "Gt"], [pkg])
                        ACT(WT[j], psg[:, 0:128], AF.Exp, [pkg, "TSm"], [("WTc", j)], bias=TSm[:, tb, h:h + 1])
                        ASEL(WT[j], WT[j], [[1, 128]], ALU.is_ge, -1, [("WTc", j)], [("WTc", j)])
                        TT("dve", STb[j], ps[:, 0:128], WT[j], ALU.mult, [pk, ("WTc", j)], [("STc", j)])
                        psn, pkn = PS()
                        MM(psn[:, 0:129], STb[j], Vc[:, tb, h, 0:129], True, True, [("STc", j), "Vc"], [pkn])
                        psi, pki = PS()
                        MM(psi[:, 0:129], qC[:, h, bsl], CTb[:, h, 0:129], True, True, [("qkC", 0), ck], [pki])
                        ACT(tmpi[j][:, 0:129], psi[:, 0:129], AF.Copy, [pki, "wint"], [("tmpiC", j)], scale=wint[:, tb, h:h + 1])
                        TT("dve", nd[j][:, 0:129], psn[:, 0:129], tmpi[j][:, 0:129], ALU.add, [pkn, ("tmpiC", j)], [("ndC", j)])
                        ACT(sml[j][:, 12:13], nd[j][:, 128:129], AF.Abs, [("ndC", j)], [("smlC", j)])
                        TS("dve", sml[j][:, 0:1], sml[j][:, 12:13], expnm[:, tb, h:h + 1], None, ALU.max, None,
                           [("smlC", j), "expnm"], [("smlC", j)])
                        RECIP(sml[j][:, 1:2], sml[j][:, 0:1], [("smlC", j)], [("smlC", j)])
                        TS("dve", nd[j][:, 0:128], nd[j][:, 0:128], sml[j][:, 1:2], None, ALU.mult, None, [("ndC", j), ("smlC", j)], [("ndC", j)])
                        Sc.op("dve", lambda e, j=j: e.bn_stats(out=sml[j][:, 2:8], in_=nd[j][:, 0:128]), [("ndC", j)], [("smlC", j)])
                        Sc.op("dve", lambda e, j=j: e.bn_aggr(out=sml[j][:, 8:10], in_=sml[j][:, 2:8]), [("smlC", j)], [("smlC", j)])
                        ACT(sml[j][:, 10:11], sml[j][:, 9:10], AF.Sqrt, [("smlC", j), "epsc"], [("smlC", j)], bias=epsc[:, 0:1])
                        RECIP(sml[j][:, 11:12], sml[j][:, 10:11], [("smlC", j)], [("smlC", j)])
                        TS("dve", hn[j], nd[j][:, 0:128], sml[j][:, 8:9], sml[j][:, 11:12], ALU.subtract, ALU.mult,
                           [("ndC", j), ("smlC", j)], [("hnC", j)])
                        pst, pkt = PS()
                        TR(pst[:, 0:128], hn[j], ident[:], [("hnC", j), "ident"], [pkt])
                        STT(Y[2][:, h, bsl], pst[:, 0:128], col("chg", h), Y[2][:, h, bsl], ALU.mult, ALU.mult, [pkt, "cols", ("Y", 2)], [("Y", 2)])
                        if tb < NB - 1:
                            ph, phk = PSH()
                            TR(ph[:, 0:128], kC[:, h, bsl], identb[:], [("qkC", 1), "identb"], [phk])
                            TS("dve", kw[j], ph[:, 0:128], wsta[:, tb, h:h + 1], None, ALU.mult, None, [phk, "wsta"], [("kwC", j)])
                            psu, pku = PS()
                            MM(psu[:, 0:129], kw[j], Vc[:, tb, h, 0:129], True, True, [("kwC", j), "Vc"], [pku])
                            STT(CTf[:, h, 0:129], CTf[:, h, 0:129], dec[:, h, tb:tb + 1], psu[:, 0:129], ALU.mult, ALU.add,
                                [("CTf", h), "decC", pku], [("CTf", h)])
                            CP("act", CTb[:, h, 0:129], CTf[:, h, 0:129], [("CTf", h)], [ck])
                debug(f"yc{l}", Y[2], [128, 4, S], ("Y", 2), BF16)
                if stop == "C":
                    raise _Stop()

                new_phase(3)
                wB = [T(f"wB{i}", [8, 512], BF16) for i in range(2)]
                Vb = T("Vb", [NB, 512], BF16)
                brow = T("browB", [512])
                qB = T("qB", [S], BF16)
                kB = T("kB", [S], BF16)
                zs2 = [T(f"zsB{i}", [S]) for i in range(3)]
                spb2 = [T(f"spB{i}", [S + 1]) for i in range(3)]
                wbf2 = [T(f"wbfB{i}", [S], BF16) for i in range(3)]
                itb = 0
                wTall = T("wTall", [NB, 128], BF16)
                ob = T("obB", [128])
                DMA(brow, brow_d[l, 0, :].partition_broadcast(128), [], ["browB"])
                load_w(wsec("b_v"), D, 512, wB[0], "wB0")
                for tb in range(NB):
                    ps, pk = PS()
                    for kc in range(8):
                        MM(ps[:, 0:512], hT[:, kc, tb * 128:(tb + 1) * 128], wB[0][:, kc, :], kc == 0, kc == 7, ["wB0", "hT"], [pk])
                    TT("dve", Vb[:, tb, :], ps[:, 0:512], brow, ALU.add, [pk, "browB"], ["Vb"])
                load_w(wsec("b_q"), D, 512, wB[1], "wB1")
                load_w(wsec("b_k"), D, 512, wB[0], "wB0")
                sc_b = 64.0 ** -0.5
                itsB = [(pr, qb, hh) for pr in range(4) for qb in range(NB) for hh in range(2)]

                def projqk(pr):
                    for tt in range(NT):
                        tsl = slice(tt * 512, (tt + 1) * 512)
                        ps, pk = PS("z")
                        proj(ps, wB[1], "wB1", pr, tsl)
                        ACT(qB[:, tsl], ps[:, 0:512], AF.Identity, [pk, "cols"], ["qB"], bias=bcol("b_q", pr))
                        ps2, pk2 = PS("z")
                        proj(ps2, wB[0], "wB0", pr, tsl)
                        ACT(kB[:, tsl], ps2[:, 0:512], AF.Identity, [pk2, "cols"], ["kB"], bias=bcol("b_k", pr))

                def bufsB(idx):
                    jb = idx % 3
                    return (zs2[jb], spb2[jb], None, wbf2[jb], ("zsB", jb), ("spB", jb), ("latB", jb), ("wbfB", jb))

                def S1(idx):
                    pr, qb, hh = itsB[idx]
                    zs, spb, lat, wbf, kz, ksp, kla, kwb = bufsB(idx)
                    L = (qb + 1) * 128
                    bsl = slice(qb * 128, (qb + 1) * 128)
                    pl = slice(hh * 64, (hh + 1) * 64)
                    nk = (L + 511) // 512
                    zps = []
                    for ki in range(nk):
                        n = min(512, L - ki * 512)
                        ps, pk = PS("z")
                        MM(ps[:, 0:n], qB[pl, bsl], kB[pl, ki * 512:ki * 512 + n], True, True, ["qB", "kB"], [pk])
                        zps.append((ps, pk, n))
                    for ki, (ps, pk, n) in enumerate(zps):
                        ksl = slice(ki * 512, ki * 512 + n)
                        ACT(spb[:, ksl], ps[:, 0:n], AF.Exp, [pk], [ksp], scale=sc_b)
                        ACT(zs[:, ksl], ps[:, 0:n], AF.Copy, [pk], [kz], scale=sc_b)
                    ACT(spb[:, 0:L], spb[:, 0:L], AF.Ln, [ksp], [ksp], bias=1.0)
                    ASEL(spb[:, qb * 128:qb * 128 + 129], spb[:, qb * 128:qb * 128 + 129], [[-1, 129]], ALU.is_gt, 1, [ksp], [ksp])
                    TT("dve", zs[:, 0:L], zs[:, 0:L], spb[:, 0:L], ALU.subtract, [kz, ksp], [kz])
                    SCAN(spb[:, 1:L + 1][:, ::-1], onesrow(128, L), spb[:, 1:L + 1][:, ::-1], 0.0, ALU.mult, ALU.add,
                         [ksp, "onesf"], [ksp])
                    TT("dve", zs[:, 0:L], zs[:, 0:L], spb[:, 1:L + 1], ALU.subtract, [kz, ksp], [kz])

                def S2a(idx):
                    pr, qb, hh = itsB[idx]
                    zs, spb, lat, wbf, kz, ksp, kla, kwb = bufsB(idx)
                    L = (qb + 1) * 128
                    bsl = slice(qb * 128, (qb + 1) * 128)
                    ACT(wbf[:, 0:L], zs[:, 0:L], AF.Exp, [kz], [kwb])
                    ASEL(wbf[:, bsl], wbf[:, bsl], [[-1, 128]], ALU.is_gt, 1, [kwb], [kwb])

                def S2tr(idx):
                    pr, qb, hh = itsB[idx]
                    zs, spb, lat, wbf, kz, ksp, kla, kwb = bufsB(idx)
                    nkb = qb + 1
                    for kb in range(nkb):
                        bk = kb // 8
                        q = kb % 8
                        TR(psh[bk][:, q * 128:(q + 1) * 128], wbf[:, kb * 128:(kb + 1) * 128], identb[:], [kwb, "identb"], [("psh", bk)])

                def S2ev(idx):
                    pr, qb, hh = itsB[idx]
                    nkb = qb + 1
                    for bk in range((nkb + 7) // 8):
                        gn = min(8, nkb - bk * 8)
                        CP("act", wTall[:, bk * 8:bk * 8 + gn, :], psh[bk][:, 0:gn * 128].rearrange("p (a b) -> p a b", a=gn),
                           [("psh", bk)], ["wTall"])

                def S2pv(idx):
                    pr, qb, hh = itsB[idx]
                    bsl = slice(qb * 128, (qb + 1) * 128)
                    pso, pko = psb[4 + qb % 2], ("ps", 4 + qb % 2)
                    nkb = qb + 1
                    hc = (pr * 2 + hh) * 64
                    for kb in range(nkb):
                        MM(pso[:, hh * 64:(hh + 1) * 64], wTall[:, kb, :], Vb[:, kb, hc:hc + 64],
                           kb == 0, kb == nkb - 1, ["wTall", "Vb"], [pko])
                    if hh == 1:
                        CP("dve", ob, pso[:, 0:128], [pko], ["obB"])
                        pst, pkt = PS("z")
                        TR(pst[:, 0:128], ob, ident[:], ["obB", "ident"], [pkt])
                        CP("dve", Y[1][:, pr, bsl], pst[:, 0:128], [pkt], [("Y", 1)])

                def S1pe(idx):
                    pr, qb, hh = itsB[idx]
                    L = (qb + 1) * 128
                    bsl = slice(qb * 128, (qb + 1) * 128)
                    pl = slice(hh * 64, (hh + 1) * 64)
                    nk = (L + 511) // 512
                    zps = []
                    for ki in range(nk):
                        n = min(512, L - ki * 512)
                        ps, pk = PS("z")
                        MM(ps[:, 0:n], qB[pl, bsl], kB[pl, ki * 512:ki * 512 + n], True, True, ["qB", "kB"], [pk])
                        zps.append((ps, pk, n))
                    return zps

                def S1rest(idx, zps):
                    pr, qb, hh = itsB[idx]
                    zs, spb, lat, wbf, kz, ksp, kla, kwb = bufsB(idx)
                    L = (qb + 1) * 128
                    for ki, (ps, pk, n) in enumerate(zps):
                        ksl = slice(ki * 512, ki * 512 + n)
                        ACT(spb[:, ksl], ps[:, 0:n], AF.Exp, [pk], [ksp], scale=sc_b)
                        ACT(zs[:, ksl], ps[:, 0:n], AF.Copy, [pk], [kz], scale=sc_b)
                    ACT(spb[:, 0:L], spb[:, 0:L], AF.Ln, [ksp], [ksp], bias=1.0)
                    ASEL(spb[:, qb * 128:qb * 128 + 129], spb[:, qb * 128:qb * 128 + 129], [[-1, 129]], ALU.is_gt, 1, [ksp], [ksp])
                    TT("dve", zs[:, 0:L], zs[:, 0:L], spb[:, 0:L], ALU.subtract, [kz, ksp], [kz])
                    SCAN(spb[:, 1:L + 1][:, ::-1], onesrow(128, L), spb[:, 1:L + 1][:, ::-1], 0.0, ALU.mult, ALU.add,
                         [ksp, "onesf"], [ksp])
                    TT("dve", zs[:, 0:L], zs[:, 0:L], spb[:, 1:L + 1], ALU.subtract, [kz, ksp], [kz])

                projqk(0)
                S1rest(0, S1pe(0))
                S1rest(1, S1pe(1))
                for idx in range(len(itsB)):
                    S2a(idx)
                    zps = None
                    if idx + 2 < len(itsB):
                        if itsB[idx + 2][0] != itsB[idx + 1][0]:
                            projqk(itsB[idx + 2][0])
                        zps = S1pe(idx + 2)
                    S2tr(idx)
                    if zps is not None:
                        S1rest(idx + 2, zps)
                    S2ev(idx)
                    S2pv(idx)
                debug(f"yb_pre{l}", Y[1], [128, 4, S], ("Y", 1), BF16)
                load_w(wsec("b_z"), D, 512, wB[1], "wB1")
                for pr in range(4):
                    for tt in range(NT):
                        tsl = slice(tt * 512, (tt + 1) * 512)
                        ps, pk = PS()
                        proj(ps, wB[1], "wB1", pr, tsl)
                        ACT(qB[:, tsl], ps[:, 0:512], AF.Silu, [pk, "cols"], ["qB"], bias=bcol("b_z", pr))
                        TT("dve", Y[1][:, pr, tsl], Y[1][:, pr, tsl], qB[:, tsl], ALU.mult, [("Y", 1), "qB"], [("Y", 1)])
                debug(f"yb{l}", Y[1], [128, 4, S], ("Y", 1), BF16)
                if stop == "B":
                    raise _Stop()

                new_phase(2)
                wA = [T(f"wA{i}", [8, 512], BF16) for i in range(2)]
                yconv = T("yconv", [4, S])
                upad = [T("upad0", [30 + S])] * 2
                sg = [T(f"sgA{i}", [512]) for i in range(2)]
                ysq = T("ysq", [4, 512])
                mean_s = T("meanA", [512])
                rstd_s = T("rstdA", [512])
                tn = [T(f"tnA{i}", [512]) for i in range(2)]
                za = [T(f"zaA{i}", [512]) for i in range(2)]
                load_w(wsec("a_val"), D, 512, wA[0], "wA0")
                load_w(wsec("a_glu"), D, 512, wA[1], "wA1")
                MEMSET("pool", upad[0][:, 0:30], 0.0, [("upad", 0)])
                for cc in range(4):
                    i = 0
                    for tt in range(NT):
                        tsl = slice(tt * 512, (tt + 1) * 512)
                        j = tt % 2
                        ps, pk = PS()
                        proj(ps, wA[1], "wA1", cc, tsl)
                        ACT(sg[j], ps[:, 0:512], AF.Sigmoid, [pk, "cols"], [("sgA", j)], bias=bcol("a_glu", cc))
                        ps2, pk2 = PS()
                        proj(ps2, wA[0], "wA0", cc, tsl)
                        STT(upad[i][:, 30 + tt * 512:30 + (tt + 1) * 512], ps2[:, 0:512], bcol("a_val", cc), sg[j],
                            ALU.add, ALU.mult, [pk2, "cols", ("sgA", j)], [("upad", i)])
                    acw0 = COLS["acw"] + cc * 31
                    TS("dve", yconv[:, cc, :], upad[i][:, 0:S], cols[:, acw0:acw0 + 1], col("acb", cc), ALU.mult, ALU.add,
                       [("upad", i), "cols"], [("yconv", cc)])
                    for jt in range(1, 31):
                        STT(yconv[:, cc, :], upad[i][:, jt:jt + S], cols[:, acw0 + jt:acw0 + jt + 1], yconv[:, cc, :],
                            ALU.mult, ALU.add, [("upad", i), "cols", ("yconv", cc)], [("yconv", cc)])
                debug(f"yconv{l}", yconv, [128, 4, S], [("yconv", c) for c in range(4)])
                load_w(wsec("a_z"), D, 512, wA[0], "wA0")
                for tt in range(NT):
                    tsl = slice(tt * 512, (tt + 1) * 512)
                    ACT(ysq, yconv[:, :, tsl], AF.Square, [("yconv", c) for c in range(4)], ["ysq"])
                    psm, pkm = PS()
                    for cc in range(4):
                        MM(psm[:, 0:512], onesE[:], yconv[:, cc, tsl], cc == 0, cc == 3, ["onesE", ("yconv", cc)], [pkm])
                    pss, pks = PS()
                    for cc in range(4):
                        MM(pss[:, 0:512], onesE[:], ysq[:, cc, :], cc == 0, cc == 3, ["onesE", "ysq"], [pks])
                    CP("act", mean_s, psm[:, 0:512], [pkm], ["meanA"])
                    TT("dve", rstd_s, mean_s, mean_s, ALU.mult, ["meanA"], ["rstdA"])
                    TT("dve", rstd_s, pss[:, 0:512], rstd_s, ALU.subtract, [pks, "rstdA"], ["rstdA"])
                    ACT(rstd_s, rstd_s, AF.Sqrt, ["rstdA", "epsc"], ["rstdA"], bias=epsc[:, 0:1])
                    RECIP(rstd_s, rstd_s, ["rstdA"], ["rstdA"])
                    for cc in range(4):
                        j = cc % 2
                        TT("pool", tn[j], yconv[:, cc, tsl], mean_s, ALU.subtract, [("yconv", cc), "meanA"], [("tnA", j)])
                        TT("dve", tn[j], tn[j], rstd_s, ALU.mult, [("tnA", j), "rstdA"], [("tnA", j)])
                        ACT(tn[j], tn[j], AF.Silu, [("tnA", j), "cols"], [("tnA", j)], scale=col("alg", cc), bias=col("alb", cc))
                        ps, pk = PS()
                        proj(ps, wA[0], "wA0", cc, tsl)
                        ACT(za[j], ps[:, 0:512], AF.Silu, [pk, "cols"], [("zaA", j)], bias=bcol("a_z", cc))
                        TT("dve", Y[0][:, cc, tsl], tn[j], za[j], ALU.mult, [("tnA", j), ("zaA", j)], [("Y", 0)])
                debug(f"ya{l}", Y[0], [128, 4, S], ("Y", 0), BF16)
                if stop == "A":
                    raise _Stop()

                new_phase(1)
                wD = [T(f"wD{i}", [8, 512], BF16) for i in range(2)]
                dwall = T("dwall", [8, 128])
                c1 = T("c1", [4])
                dpad = T("dpad", [3 + S])
                xc = T("xcD", [S])
                av = T("avD", [S])
                uv = T("uvD", [S])
                gi = [T(f"giD{i}", [512]) for i in range(2)]
                zd = [T(f"zdD{i}", [512]) for i in range(2)]
                DMA(dwall, dw_d[l].rearrange("g c p d -> p (g c) d"), [], ["dwall"])
                load_w(wsec("d_x"), D, 512, wD[0], "wD0")
                load_w(wsec("d_z"), D, 512, wD[1], "wD1")
                ACT(c1, cols[:, COLS["dlam"]:COLS["dlam"] + 4], AF.Exp, ["cols"], ["c1"], scale=-1.0)
                ACT(c1, c1, AF.Ln, ["c1"], ["c1"], bias=1.0)
                TS("dve", c1, c1, -8.0, None, ALU.mult, None, ["c1"], ["c1"])
                MEMSET("pool", dpad[:, 0:3], 0.0, ["dpad"])
                for cc in range(4):
                    for tt in range(NT):
                        tsl = slice(tt * 512, (tt + 1) * 512)
                        ps, pk = PS()
                        proj(ps, wD[0], "wD0", cc, tsl)
                        ACT(dpad[:, 3 + tt * 512:3 + (tt + 1) * 512], ps[:, 0:512], AF.Identity, [pk, "cols"], ["dpad"],
                            bias=bcol("d_x", cc))
                    w0 = COLS["dcw"] + cc * 4
                    TS("dve", xc, dpad[:, 0:S], cols[:, w0:w0 + 1], col("dcb", cc), ALU.mult, ALU.add, ["dpad", "cols"], ["xcD"])
                    for jt in range(1, 4):
                        STT(xc, dpad[:, jt:jt + S], cols[:, w0 + jt:w0 + jt + 1], xc, ALU.mult, ALU.add, ["dpad", "cols", "xcD"], ["xcD"])
                    for tt in range(NT):
                        tsl = slice(tt * 512, (tt + 1) * 512)
                        j = tt % 2
                        psa, pka = PS()
                        MM(psa[:, 0:512], dwall[:, 0 * 4 + cc, :], xc[:, tsl], True, True, ["dwall", "xcD"], [pka])
                        psx, pkx = PS()
                        MM(psx[:, 0:512], dwall[:, 1 * 4 + cc, :], xc[:, tsl], True, True, ["dwall", "xcD"], [pkx])
                        ACT(av[:, tsl], psa[:, 0:512], AF.Sigmoid, [pka, "cols"], ["avD"], bias=col("dba", cc))
                        ACT(av[:, tsl], av[:, tsl], AF.Exp, ["avD", "c1"], ["avD"], scale=c1[:, cc:cc + 1])
                        ACT(gi[j], psx[:, 0:512], AF.Sigmoid, [pkx, "cols"], [("giD", j)], bias=col("dbx", cc))
                        TT("pool", uv[:, tsl], av[:, tsl], av[:, tsl], ALU.mult, ["avD"], ["uvD"])
                        ACT(uv[:, tsl], uv[:, tsl], AF.Sqrt, ["uvD"], ["uvD"], scale=-1.0, bias=1.0)
                        TT("pool", gi[j], gi[j], xc[:, tsl], ALU.mult, [("giD", j), "xcD"], [("giD", j)])
                        TT("dve", uv[:, tsl], uv[:, tsl], gi[j], ALU.mult, ["uvD", ("giD", j)], ["uvD"])
                    SCAN(xc, av, uv, 0.0, ALU.mult, ALU.add, ["avD", "uvD", "xcD"], ["xcD"])
                    for tt in range(NT):
                        tsl = slice(tt * 512, (tt + 1) * 512)
                        j = tt % 2
                        ps, pk = PS()
                        proj(ps, wD[1], "wD1", cc, tsl)
                        ACT(zd[j], ps[:, 0:512], AF.Silu, [pk, "cols"], [("zdD", j)], bias=bcol("d_z", cc))
                        TT("dve", Y[3][:, cc, tsl], xc[:, tsl], zd[j], ALU.mult, ["xcD", ("zdD", j)], [("Y", 3)])
                debug(f"yd{l}", Y[3], [128, 4, S], ("Y", 3), BF16)
                if stop == "D":
                    raise _Stop()

                new_phase(0)
                memT = T("memT", [8, NMEM], BF16)
                mkT = T("mkT", [4, NMEM], BF16)
                mv = T("mv", [2, 512], BF16)
                wM = [T(f"wM{i}", [8, 512], BF16) for i in range(2)]
                mrs = T("mrs", [2])
                mq = T("memsq", [D])
                mxs = [T(f"memx{mt}", [D]) for mt in range(2)]
                qm = T("qm", [S], BF16)
                zm = T("zm", [S], BF16)
                pbuf = [T(f"pm{i}", [NMEM], BF16) for i in range(2)]
                pT = [T(f"pTm{i}", [2, 128], BF16) for i in range(2)]
                on = [T(f"onm{i}", [128]) for i in range(2)]
                sm = [T(f"smm{i}", [4]) for i in range(2)]
                for mt in range(2):
                    mx = mxs[mt]
                    DMA(mx, mem_d[seq, mt * 128:(mt + 1) * 128, :], [], [("memx", mt)])
                    ACT(mq, mx, AF.Square, [("memx", mt)], ["memsq", ("mrs", mt)], accum_out=mrs[:, mt:mt + 1])
                    TS("dve", mrs[:, mt:mt + 1], mrs[:, mt:mt + 1], 1.0 / D, EPS, ALU.mult, ALU.add, [("mrs", mt)], [("mrs", mt)])
                    ACT(mrs[:, mt:mt + 1], mrs[:, mt:mt + 1], AF.Sqrt, [("mrs", mt)], [("mrs", mt)])
                    RECIP(mrs[:, mt:mt + 1], mrs[:, mt:mt + 1], [("mrs", mt)], [("mrs", mt)])
                    TS("dve", mx, mx, mrs[:, mt:mt + 1], None, ALU.mult, None, [("memx", mt), ("mrs", mt)], [("memx", mt)])
                    for half in range(2):
                        ps, pk = PS()
                        for q in range(4):
                            kc = half * 4 + q
                            TR(ps[:, q * 128:(q + 1) * 128], mx[:, kc * 128:(kc + 1) * 128], ident[:], [("memx", mt), "ident"], [pk])
                        for q in range(4):
                            kc = half * 4 + q
                            ACT(memT[:, kc, mt * 128:(mt + 1) * 128], ps[:, q * 128:(q + 1) * 128], AF.Copy, [pk, "cols"], ["memT"],
                                scale=col("mng", kc))
                load_w(wmkv_d[l, :, 0:512], D, 512, wM[0], "wM0")
                load_w(wmkv_d[l, :, 512:1024], D, 512, wM[1], "wM1")
                for h in range(4):
                    ps, pk = PS()
                    for kc in range(8):
                        MM(ps[:, 0:NMEM], wM[0][:, kc, h * 128:(h + 1) * 128], memT[:, kc, :], kc == 0, kc == 7, ["wM0", "memT"], [pk])
                    CP(ev_eng(), mkT[:, h, :], ps[:, 0:NMEM], [pk], ["mkT"])
                for mt in range(2):
                    ps, pk = PS()
                    for kc in range(8):
                        MM(ps[:, 0:512], memT[:, kc, mt * 128:(mt + 1) * 128], wM[1][:, kc, :], kc == 0, kc == 7, ["wM1", "memT"], [pk])
                    CP(ev_eng(), mv[:, mt, :], ps[:, 0:512], [pk], ["mv"])
                load_w(wsec("m_q"), D, 512, wM[0], "wM0")
                load_w(wsec("m_z"), D, 512, wM[1], "wM1")
                sc_m = 128.0 ** -0.5
                for h in range(4):
                    for tt in range(NT):
                        tsl = slice(tt * 512, (tt + 1) * 512)
                        ps, pk = PS()
                        proj(ps, wM[0], "wM0", h, tsl)
                        ACT(qm[:, tsl], ps[:, 0:512], AF.Identity, [pk, "cols"], ["qm"], bias=bcol("m_q", h))
                        ps2, pk2 = PS()
                        proj(ps2, wM[1], "wM1", h, tsl)
                        ACT(zm[:, tsl], ps2[:, 0:512], AF.Silu, [pk2, "cols"], ["zm"], bias=bcol("m_z", h))
                    for tb in range(NB):
                        bsl = slice(tb * 128, (tb + 1) * 128)
                        j = tb % 2
                        ps, pk = PS()
                        MM(ps[:, 0:NMEM], qm[:, bsl], mkT[:, h, :], True, True, ["qm", "mkT"], [pk])
                        Sc.op("dve", lambda e, ps=ps, j=j: e.reduce_max(out=sm[j][:, 0:1], in_=ps[:, 0:NMEM], axis=mybir.AxisListType.X),
                              [pk], [("smm", j)])
                        TS("dve", sm[j][:, 1:2], sm[j][:, 0:1], -sc_m, None, ALU.mult, None, [("smm", j)], [("smm", j)])
                        ACT(pbuf[j], ps[:, 0:NMEM], AF.Exp, [pk, ("smm", j)], [("pm", j), ("smm", j)], scale=sc_m, bias=sm[j][:, 1:2],
                            accum_out=sm[j][:, 2:3])
                        ph, phk = PSH()
                        for mt in range(2):
                            TR(ph[:, mt * 128:(mt + 1) * 128], pbuf[j][:, mt * 128:(mt + 1) * 128], identb[:], [("pm", j), "identb"], [phk])
                        CP(ev_eng(), pT[j], ph[:, 0:256].rearrange("p (a b) -> p a b", a=2), [phk], [("pTm", j)])
                        pso, pko = PS()
                        for mt in range(2):
                            MM(pso[:, 0:128], pT[j][:, mt, :], mv[:, mt, h * 128:(h + 1) * 128], mt == 0, mt == 1, [("pTm", j), "mv"], [pko])
                        RECIP(sm[j][:, 3:4], sm[j][:, 2:3], [("smm", j)], [("smm", j)])
                        TS("dve", on[j], pso[:, 0:128], sm[j][:, 3:4], None, ALU.mult, None, [pko, ("smm", j)], [("onm", j)])
                        pst, pkt = PS()
                        TR(pst[:, 0:128], on[j], ident[:], [("onm", j), "ident"], [pkt])
                        TT("dve", Y[4][:, h, bsl], pst[:, 0:128], zm[:, bsl], ALU.mult, [pkt, "zm"], [("Y", 4)])
                debug(f"ym{l}", Y[4], [128, 4, S], ("Y", 4), BF16)
                if stop == "M":
                    raise _Stop()

                new_phase(0)
                mg = T("mg", [8, S], BF16)
                mark = st["ar"]
                RING = 4
                wgn = [T(f"wgn{i}", [8, 128], BF16) for i in range(RING)]
                wun = [T(f"wun{i}", [4, 128], BF16) for i in range(RING)]
                sgm = [T(f"sgm{i}", [512]) for i in range(2)]
                acc = T("accm", [NT, 512])
                order = [(c, n) for c in range(8) for n in range(5)]

                def ldm(idx):
                    c, n = order[idx]
                    i = idx % RING
                    g0 = GATE0 + (c * 5 + n) * 128
                    load_w(win_d[l, :, g0:g0 + 128], D, 128, wgn[i], ("wgn", i))
                    load_w(wup_d[l, n, :, c * 128:(c + 1) * 128], E, 128, wun[i], ("wun", i))
                for idx in range(RING - 1):
                    ldm(idx)
                it = 0
                for idx, (c, n) in enumerate(order):
                    i = idx % RING
                    if idx + RING - 1 < len(order):
                        ldm(idx + RING - 1)
                    for tt in range(NT):
                        tsl = slice(tt * 512, (tt + 1) * 512)
                        j = it % 2
                        it += 1
                        psg, pkg = PS()
                        for kc in range(8):
                            MM(psg[:, 0:512], wgn[i][:, kc, :], hT[:, kc, tsl], kc == 0, kc == 7, [("wgn", i), "hT"], [pkg])
                        ACT(sgm[j], psg[:, 0:512], AF.Sigmoid, [pkg, "cols"], [("sgm", j)], bias=col("bin", 64 + c * 5 + n))
                        psu, pku = PS()
                        for kc in range(4):
                            MM(psu[:, 0:512], wun[i][:, kc, :], Y[n][:, kc, tsl], kc == 0, kc == 3, [("wun", i), ("Y", n)], [pku])
                        if n == 0:
                            TT("dve", acc[:, tt, :], sgm[j], psu[:, 0:512], ALU.mult, [("sgm", j), pku], [("accm", tt)])
                        else:
                            TT("dve", sgm[j], sgm[j], psu[:, 0:512], ALU.mult, [("sgm", j), pku], [("sgm", j)])
                            if n < 4:
                                TT("pool", acc[:, tt, :], acc[:, tt, :], sgm[j], ALU.add, [("accm", tt), ("sgm", j)], [("accm", tt)])
                            else:
                                TT("pool", mg[:, c, tsl], acc[:, tt, :], sgm[j], ALU.add, [("accm", tt), ("sgm", j)], ["mg"])
                debug(f"mg{l}", mg, [128, 8, S], "mg", BF16)
                if stop == "merge":
                    raise _Stop()
                Sc.barrier()
                st["ar"] = mark
                st["lim"] = AR_BASE + 5 * YW
                wo = [T(f"wo{i}", [8, 128], BF16) for i in range(2)]
                xbuf = T("xbuf", [8, S])
                allxs = [("xs", seq, c, tt) for c in range(8) for tt in range(NT)]
                DMA(xbuf, xs_d[seq], allxs, ["xbuf"])
                load_w(wout_d[l, :, 0:128], D, 128, wo[0], ("wo", 0))
                Sc.barrier()
                for c in range(8):
                    i = c % 2
                    if c + 1 < 8:
                        load_w(wout_d[l, :, (c + 1) * 128:(c + 2) * 128], D, 128, wo[(c + 1) % 2], ("wo", (c + 1) % 2))
                    for tt in range(NT):
                        tsl = slice(tt * 512, (tt + 1) * 512)
                        ps, pk = PS()
                        for kc in range(8):
                            MM(ps[:, 0:512], wo[i][:, kc, :], mg[:, kc, tsl], kc == 0, kc == 7, [("wo", i), "mg"], [pk])
                        TT("dve", xbuf[:, c, tsl], xbuf[:, c, tsl], ps[:, 0:512], ALU.add, ["xbuf", pk], ["xbuf"])
                Sc.barrier()
                DMA(xs_d[seq], xbuf, ["xbuf"], allxs)
                if stop == "resid":
                    raise _Stop()

            new_phase(5)
            ot = [T(f"ot{i}", [D]) for i in range(2)]
            it = 0
            for tt in range(NT):
                j = tt % 2
                xf = T(f"xf{j}", [8, 512])
                yo = T(f"yo{j}", [8, 512])
                DMA(xf, xs_d[seq, :, :, tt * 512:(tt + 1) * 512], xskeys(seq, tt), [("xf", j)])

                def mk_o(kc, rstd, rkey, xf=xf, yo=yo, j=j):
                    STT(yo[:, kc, :], xf[:, kc, :], fcols[:, kc:kc + 1], rstd, ALU.mult, ALU.mult,
                        [("xf", j), "fcols", rkey], [("yo", j)])
                rmsnorm_tile(xf, ("xf", j), yo, ("yo", j), mk_o)
                for q4 in range(4):
                    jo = it % 2
                    it += 1
                    for half in range(2):
                        ps, pk = PS()
                        for q in range(4):
                            kc = half * 4 + q
                            TR(ps[:, q * 128:(q + 1) * 128], yo[:, kc, q4 * 128:(q4 + 1) * 128], ident[:], [("yo", j), "ident"], [pk])
                        CP(ev_eng(), ot[jo][:, half * 512:(half + 1) * 512], ps[:, 0:512], [pk], [("ot", jo)])
                    tb = tt * 4 + q4
                    DMA(out_d[seq, tb * 128:(tb + 1) * 128, :], ot[jo], [("ot", jo)], [("outd", seq, tb)])

        try:
            main_body()
        except _Stop:
            pass
        Sc.barrier()
        Sc.emit()
    return nc, dbg_d


def prep_weights(inp):
    DEPTH = inp["w_in"].shape[0]
    perm = np.concatenate([np.arange(0, 5120), np.arange(5128, 8200)] +
                          [8200 + n * 1024 + c * 128 + np.arange(128) for c in range(8) for n in range(5)] +
                          [np.arange(5120, 5128)])
    w_in_r = np.ascontiguousarray(np.asarray(inp["w_in"], np.float32)[:, :, perm])
    b_in_r = np.asarray(inp["b_in"], np.float32)[:, perm]
    cols = np.zeros((DEPTH, 128, NCOL), np.float32)

    def put(l, name, arr):
        cols[l, :, COLS[name]:COLS[name] + arr.shape[1]] = arr

    def pc(v):
        return np.asarray(v, np.float32).reshape(-1, 128).T

    brow = np.zeros((DEPTH, 2, 512), np.float32)
    dw = np.zeros((DEPTH, 2, 4, 128, 128), np.float32)
    for l in range(DEPTH):
        put(l, "bin", pc(b_in_r[l, :13312]))
        put(l, "ng", pc(inp["norm_g"][l]))
        put(l, "mng", pc(inp["mem_norm_g"][l]))
        acw = np.asarray(inp["a_conv_w"][l], np.float32)
        put(l, "acw", acw.T.reshape(4, 128, 31).transpose(1, 0, 2).reshape(128, 124))
        put(l, "acb", pc(inp["a_conv_b"][l]))
        put(l, "alg", pc(inp["a_ln_g"][l]))
        put(l, "alb", pc(inp["a_ln_b"][l]))
        ccw = np.asarray(inp["c_conv_w"][l], np.float32)
        put(l, "ccw", ccw.T.reshape(8, 128, 4).transpose(1, 0, 2).reshape(128, 32))
        put(l, "ccb", pc(inp["c_conv_b"][l]))
        put(l, "chg", pc(inp["c_hn_g"][l]))
        dcw = np.asarray(inp["d_conv_w"][l], np.float32)
        put(l, "dcw", dcw.T.reshape(4, 128, 4).transpose(1, 0, 2).reshape(128, 16))
        put(l, "dcb", pc(inp["d_conv_b"][l]))
        put(l, "dba", pc(inp["d_ba"][l]))
        put(l, "dbx", pc(inp["d_bx"][l]))
        put(l, "dlam", pc(inp["d_lambda"][l]))
        cols[l, 0:4, COLS["cib"]] = b_in_r[l, 13312:13316]
        cols[l, 0:4, COLS["cfb"]] = b_in_r[l, 13316:13320]
        cols[l, 0:4, COLS["cfb2"]] = np.asarray(inp["c_f_bias"][l], np.float32)
        brow[l, 0] = b_in_r[l, SEC["b_v"] * 512:(SEC["b_v"] + 1) * 512]
        brow[l, 1] = b_in_r[l, SEC["c_v"] * 512:(SEC["c_v"] + 1) * 512]
        for g, nm in enumerate(("d_wa", "d_wx")):
            wgt = np.asarray(inp[nm][l], np.float32)
            for cc in range(4):
                dw[l, g, cc, 0:64, 0:64] = wgt[2 * cc]
                dw[l, g, cc, 64:128, 64:128] = wgt[2 * cc + 1]
    fcols = np.ascontiguousarray(pc(inp["final_norm_g"]))
    return dict(w_in_r=w_in_r, brow=brow, cols=cols, dw=dw,
                w_mkv=np.ascontiguousarray(np.asarray(inp["w_mkv"], np.float32)),
                w_up=np.ascontiguousarray(np.asarray(inp["w_up"], np.float32)),
                w_out=np.ascontiguousarray(np.asarray(inp["w_out"], np.float32)),
                fcols=fcols)


_NC_CACHE = {}


def kernel(**inputs):
    x = np.asarray(inputs["x"], np.float32)
    mem = np.asarray(inputs["mem"], np.float32)
    B, S, _ = x.shape
    DEPTH = inputs["w_in"].shape[0]
    ncores = 8
    nseq = B // ncores
    wts = prep_weights(inputs)
    key = (S, nseq, DEPTH)
    if key not in _NC_CACHE:
        _NC_CACHE[key] = build(S, nseq, DEPTH)[0]
    nc = _NC_CACHE[key]
    in_maps = []
    for c in range(ncores):
        m = dict(wts)
        m["x"] = np.ascontiguousarray(x[c * nseq:(c + 1) * nseq])
        m["mem"] = np.ascontiguousarray(mem[c * nseq:(c + 1) * nseq])
        in_maps.append(m)
    res = run_bass_kernel_spmd(nc, in_maps, core_ids=list(range(ncores)))
    out = np.concatenate([np.asarray(r["out"], np.float32) for r in res.results], axis=0)
    return out
```

```python
import math
from contextlib import ExitStack
import numpy as np
import concourse.bass as bass
import concourse.mybir as mybir
from concourse.bass_utils import run_bass_kernel_spmd

F32 = mybir.dt.float32
BF16 = mybir.dt.bfloat16
AF = mybir.ActivationFunctionType
ALU = mybir.AluOpType

ENGS = ("pe", "act", "dve", "pool", "sp")
EPOCH = 24000
NSLOT = {"sp": 8, "pool": 4}


class Sch:
    def __init__(self, nc, es):
        self.nc = nc
        self.es = es
        self.ops = {e: [] for e in ENGS}
        self.cnt = {e: 0 for e in ENGS}
        self.esem = {e: [] for e in ENGS}
        self.known = {e: {} for e in ENGS}
        self.lastw = {}
        self.readers = {}
        self.lasttok = {}
        self.dcnt = {q: 0 for q in NSLOT}
        self.dsem = {q: [self._newsem(f"d_{q}{i}") for i in range(n)] for q, n in NSLOT.items()}

    def _newsem(self, name):
        return self.es.enter_context(self.nc.semaphore(name))

    def _esem(self, e, ep):
        while len(self.esem[e]) <= ep:
            self.esem[e].append(self._newsem(f"e_{e}{len(self.esem[e])}"))
        return self.esem[e][ep]

    def _need(self, e, tok, waits):
        sem, val, src = tok
        if src == "pe" and e == "pe":
            return
        k = self.known[e]
        if k.get(id(sem), 0) >= val:
            return
        k[id(sem)] = val
        waits.append((sem, val))

    def _deps(self, e, r, w):
        waits = []
        for key in r:
            t = self.lastw.get(key)
            if t is not None:
                self._need(e, t, waits)
        for key in w:
            t = self.lastw.get(key)
            if t is not None:
                self._need(e, t, waits)
            for t in self.readers.get(key, ()):
                if t[2] == e:
                    continue
                self._need(e, t, waits)
        return waits

    def _commit(self, tok, r, w):
        for key in r:
            self.readers.setdefault(key, []).append(tok)
        for key in w:
            self.lastw[key] = tok
            self.readers[key] = []

    def op(self, e, fn, r=(), w=()):
        waits = self._deps(e, r, w)
        idx = self.cnt[e]
        self.cnt[e] += 1
        sem = self._esem(e, idx // EPOCH)
        val = idx % EPOCH + 1
        self.ops[e].append((waits, fn, (sem, 1)))
        tok = (sem, val, e)
        self.lasttok[e] = tok
        self._commit(tok, r, w)
        return tok

    def dma(self, q, out, in_, r=(), w=(), **kw):
        waits = self._deps(q, r, w)
        k = self.dcnt[q]
        self.dcnt[q] += 1
        n = NSLOT[q]
        sem = self.dsem[q][k % n]
        val = 16 * (k // n + 1)
        if k >= n:
            self._need(q, (sem, val - 16, "dma"), waits)
        fn = lambda eng: eng.dma_start(out=out, in_=in_, **kw)
        self.ops[q].append((waits, fn, (sem, 16)))
        tok = (sem, val, "dma")
        self._commit(tok, r, w)
        return tok

    def dma_tokens(self):
        toks = []
        for q, n in NSLOT.items():
            k = self.dcnt[q]
            for i in range(min(k, n)):
                j = ((k - 1 - i) // n) * n + i
                toks.append((self.dsem[q][i], 16 * (j // n + 1), "dma"))
        return toks

    def wait_all(self, e, toks):
        waits = []
        for t in toks:
            self._need(e, t, waits)
        if waits:
            self.ops[e].append((waits, None, None))

    def barrier(self):
        toks = list(self.lasttok.values()) + self.dma_tokens()
        for e in ENGS:
            self.wait_all(e, [t for t in toks if t[2] != e])

    def emit(self):
        nc = self.nc
        with nc.Block() as block:
            def run(e, eng):
                for waits, fn, inc in self.ops[e]:
                    for sem, val in waits:
                        eng.wait_ge(sem, val)
                    if fn is not None:
                        fn(eng).then_inc(inc[0], inc[1])

            @block.tensor
            def _(eng):
                run("pe", eng)

            @block.scalar
            def _(eng):
                run("act", eng)

            @block.vector
            def _(eng):
                run("dve", eng)

            @block.gpsimd
            def _(eng):
                run("pool", eng)

            @block.sync
            def _(eng):
                run("sp", eng)


D = 1024
E = 512
NMEM = 256
NIN = 13320
EPS = 1e-6
SEC = dict(a_val=0, a_glu=1, a_z=2, b_q=3, b_k=4, b_v=5, b_z=6, c_q=7, c_k=8, c_v=9,
           c_o=10, c_z=11, d_x=12, d_z=13, m_q=14, m_z=15)
GATE0 = 16 * 512
CIF0 = GATE0 + 5 * 1024


def col_layout():
    names = [("bin", 104), ("ng", 8), ("mng", 8), ("acw", 124), ("acb", 4), ("alg", 4), ("alb", 4),
             ("ccw", 32), ("ccb", 8), ("chg", 4), ("dcw", 16), ("dcb", 4), ("dba", 4), ("dbx", 4),
             ("dlam", 4), ("cib", 1), ("cfb", 1), ("cfb2", 1)]
    off = {}
    o = 0
    for n, w in names:
        off[n] = o
        o += w
    return off, o


COLS, NCOL = col_layout()
AR_BASE = 17152


class _Stop(Exception):
    pass


def build(S, NSEQ, DEPTH, dbg_names=(), stop=None):
    NT = S // 512
    NB = S // 128
    YW = 2 * S
    nc = bass.Bass("TRN2", target_bir_lowering=False)
    x_d = nc.dram_tensor("x", [NSEQ, S, D], F32, kind="ExternalInput").ap()
    mem_d = nc.dram_tensor("mem", [NSEQ, NMEM, D], F32, kind="ExternalInput").ap()
    win_d = nc.dram_tensor("w_in_r", [DEPTH, D, NIN], F32, kind="ExternalInput").ap()
    brow_d = nc.dram_tensor("brow", [DEPTH, 2, 512], F32, kind="ExternalInput").ap()
    cols_d = nc.dram_tensor("cols", [DEPTH, 128, NCOL], F32, kind="ExternalInput").ap()
    dw_d = nc.dram_tensor("dw", [DEPTH, 2, 4, 128, 128], F32, kind="ExternalInput").ap()
    wmkv_d = nc.dram_tensor("w_mkv", [DEPTH, D, 2 * E], F32, kind="ExternalInput").ap()
    wup_d = nc.dram_tensor("w_up", [DEPTH, 5, E, D], F32, kind="ExternalInput").ap()
    wout_d = nc.dram_tensor("w_out", [DEPTH, D, D], F32, kind="ExternalInput").ap()
    fcols_d = nc.dram_tensor("fcols", [128, 8], F32, kind="ExternalInput").ap()
    out_d = nc.dram_tensor("out", [NSEQ, S, D], F32, kind="ExternalOutput").ap()
    xs_d = nc.dram_tensor("xs", [NSEQ, 128, 8, S], F32, kind="Internal").ap()
    dbg_d = {}

    with ExitStack() as es:
        Sc = Sch(nc, es)

        def sb(name, shape, dt=F32):
            return es.enter_context(nc.sbuf_tensor("s_" + name, shape, dt))

        def ACT(out, in_, func, r, w, **kw):
            Sc.op("act", lambda e: e.activation(out=out, in_=in_, func=func, **kw), r, w)

        def TT(eng, out, in0, in1, op, r, w):
            Sc.op(eng, lambda e: e.tensor_tensor(out=out, in0=in0, in1=in1, op=op), r, w)

        def TS(eng, out, in0, s1, s2, op0, op1, r, w):
            if op1 is None:
                Sc.op(eng, lambda e: e.tensor_scalar(out=out, in0=in0, scalar1=s1, scalar2=None, op0=op0), r, w)
            else:
                Sc.op(eng, lambda e: e.tensor_scalar(out=out, in0=in0, scalar1=s1, scalar2=s2, op0=op0, op1=op1), r, w)

        def STT(out, in0, scalar, in1, op0, op1, r, w):
            Sc.op("dve", lambda e: e.scalar_tensor_tensor(out=out, in0=in0, scalar=scalar, in1=in1, op0=op0, op1=op1), r, w)

        def RECIP(out, in_, r, w):
            Sc.op("dve", lambda e: e.reciprocal(out=out, in_=in_), r, w)

        def MM(out, lhsT, rhs, start, stop, r, w):
            Sc.op("pe", lambda e: e.matmul(out, lhsT=lhsT, rhs=rhs, start=start, stop=stop), r, w)

        def TR(out, in_, idn, r, w):
            Sc.op("pe", lambda e: e.transpose(out, in_, idn), r, w)

        def CP(eng, out, in_, r, w):
            if eng == "act":
                Sc.op("act", lambda e: e.activation(out=out, in_=in_, func=AF.Copy), r, w)
            else:
                Sc.op(eng, lambda e: e.tensor_copy(out=out, in_=in_), r, w)

        def MEMSET(eng, ap, val, w):
            Sc.op(eng, lambda e: e.memset(ap, val), (), w)

        def ASEL(out, in_, pattern, cmp, cm, r, w, base=0):
            Sc.op("pool", lambda e: e.affine_select(out=out, in_=in_, pattern=pattern, compare_op=cmp, fill=0.0,
                                                    base=base, channel_multiplier=cm), r, w)

        def SCAN(out, d0, d1, init, op0, op1, r, w):
            Sc.op("dve", lambda e: e.tensor_tensor_scan(out=out, data0=d0, data1=d1, initial=init, op0=op0, op1=op1), r, w)

        def DMA(out, in_, r, w, q="sp", **kw):
            return Sc.dma(q, out, in_, r, w, **kw)

        def debug(name, ap, shape, key, dt=F32):
            if name not in dbg_names:
                return
            d = nc.dram_tensor("dbg_" + name, list(shape), dt, kind="ExternalOutput").ap()
            dbg_d[name] = d
            DMA(d, ap, [key] if not isinstance(key, list) else key, [("dbgd", name)])

        ident = sb("ident", [128, 128])
        identb = sb("identb", [128, 128], BF16)
        onesf = sb("onesf", [128, 128])
        onesD = sb("onesD", [128, 128])
        onesE = sb("onesE", [128, 128])
        sel4 = sb("sel4", [4, 4, 128])
        cols = sb("cols", [128, NCOL])
        fcols = sb("fcols", [128, 8])
        epsc = sb("epsc", [128, 1])
        hT = sb("hT", [128, 8, S], BF16)
        WST_N = 2
        wst = [sb(f"wst{i}", [128, 8, 256]) for i in range(WST_N)]
        AR = AR_BASE + 5 * YW
        arena = sb("arena", [128, AR])
        NPS = 6
        psb = [es.enter_context(nc.psum_tensor(f"ps{i}", [128, 512], F32)) for i in range(NPS)]
        psh = [es.enter_context(nc.psum_tensor(f"psh{i}", [128, 1024], BF16)) for i in range(2)]
        pskey = {id(t): ("ps", i) for i, t in enumerate(psb)}

        Y = [None] * 5
        for pos, n in enumerate((4, 3, 0, 1, 2)):
            a = AR_BASE + pos * YW
            Y[n] = arena[:, a:a + YW].bitcast(BF16).rearrange("p (a b) -> p a b", a=4)

        st = dict(ar=0, lim=AR_BASE, ps=0, psz=0, psh=0, wst=0, cast=0, ev=0, phase=0)
        tmpviews = {}

        def carve(name, free, dt=F32):
            n = int(np.prod(free))
            words = n if dt == F32 else (n + 1) // 2
            words = (words + 7) // 8 * 8
            a = st["ar"]
            assert a + words <= st["lim"], (name, a, words, st["lim"])
            st["ar"] = a + words
            v = arena[:, a:a + words]
            if dt != F32:
                v = v.bitcast(dt)
            v = v[:, 0:n]
            if len(free) == 2:
                v = v.rearrange("p (a b) -> p a b", a=free[0])
            elif len(free) == 3:
                v = v.rearrange("p (a b c) -> p a b c", a=free[0], b=free[1])
            return v

        def T(name, free, dt=F32):
            k = (name, st["phase"])
            if k not in tmpviews:
                tmpviews[k] = carve(name, free, dt)
            return tmpviews[k]

        def new_phase(extra=0):
            Sc.barrier()
            st["ar"] = 0
            st["lim"] = AR_BASE + extra * YW
            st["phase"] += 1

        def PS(pool="all"):
            if pool == "all":
                i = st["ps"] % NPS
                st["ps"] += 1
            else:
                i = st["psz"] % 4
                st["psz"] += 1
            return psb[i], ("ps", i)

        def PSH():
            i = st["psh"] % 2
            st["psh"] += 1
            return psh[i], ("psh", i)

        def ev_eng():
            st["ev"] += 1
            return "act" if st["ev"] % 2 else "dve"

        def onesrow(p, n):
            return onesf[0:p, 0:1].to_broadcast([p, n])

        MEMSET("pool", onesf[:], 1.0, ["onesf"])
        MEMSET("pool", onesD[:], 1.0 / D, ["onesD"])
        MEMSET("pool", onesE[:], 1.0 / E, ["onesE"])
        MEMSET("pool", epsc[:], EPS, ["epsc"])
        ASEL(ident[:], onesf[:], [[-1, 128]], ALU.is_equal, 1, ["onesf"], ["ident"])
        CP("pool", identb[:], ident[:], ["ident"], ["identb"])
        for h in range(4):
            ASEL(sel4[:, h, :], onesf[0:4, :], [[0, 128]], ALU.is_equal, 1, ["onesf"], ["sel4"], base=-h)
        DMA(fcols[:], fcols_d, [], ["fcols"])

        def col(name, i=0):
            o = COLS[name] + i
            return cols[:, o:o + 1]

        def load_w(src2d, K, ncols, dst, dkey):
            kc = K // 128
            c0 = 0
            while c0 < ncols:
                n = min(256, ncols - c0)
                i = st["wst"] % WST_N
                st["wst"] += 1
                stg = wst[i][:, 0:kc, 0:n]
                DMA(stg, src2d[:, c0:c0 + n].rearrange("(kc p) n -> p kc n", p=128), [], [("wst", i)])
                st["cast"] += 1
                eng = "pool" if st["cast"] % 2 else "act"
                CP(eng, dst[:, :, c0:c0 + n], stg, [("wst", i)], [dkey])
                c0 += n

        def rmsnorm_tile(xf, xkey, sq, sqkey, out_fn):
            ACT(sq, xf, AF.Square, [xkey], [sqkey])
            ps, pk = PS()
            for kc in range(8):
                MM(ps[:, 0:512], onesD[:], sq[:, kc, :], kc == 0, kc == 7, ["onesD", sqkey], [pk])
            rstd = T("rn_rstd", [512])
            ACT(rstd, ps[:, 0:512], AF.Sqrt, [pk, "epsc"], ["rn_rstd"], bias=epsc[:, 0:1])
            RECIP(rstd, rstd, ["rn_rstd"], ["rn_rstd"])
            for kc in range(8):
                out_fn(kc, rstd, "rn_rstd")

        def xskeys(seq, tt):
            return [("xs", seq, c, tt) for c in range(8)]

        def main_body():
          for seq in range(NSEQ):
            new_phase(5)
            for tb in range(NB):
                j = tb % 2
                xt = T(f"xin{j}", [D])
                xo = T(f"xout{j}", [8, 128])
                DMA(xt, x_d[seq, tb * 128:(tb + 1) * 128, :], [], [("xin", j)])
                for half in range(2):
                    ps, pk = PS()
                    for q in range(4):
                        kc = half * 4 + q
                        TR(ps[:, q * 128:(q + 1) * 128], xt[:, kc * 128:(kc + 1) * 128], ident[:], [("xin", j), "ident"], [pk])
                    CP(ev_eng(), xo[:, half * 4:(half + 1) * 4, :], ps[:, 0:512].rearrange("p (a b) -> p a b", a=4), [pk], [("xout", j)])
                DMA(xs_d[seq, :, :, tb * 128:(tb + 1) * 128], xo, [("xout", j)], xskeys(seq, tb // 4))

            for l in range(DEPTH):
                new_phase(5)
                DMA(cols[:], cols_d[l], [], ["cols"])
                for tt in range(NT):
                    j = tt % 2
                    xf = T(f"xf{j}", [8, 512])
                    sq = T("rn_sq", [8, 512])
                    DMA(xf, xs_d[seq, :, :, tt * 512:(tt + 1) * 512], xskeys(seq, tt), [("xf", j)])

                    def mk_h(kc, rstd, rkey, xf=xf, tt=tt, j=j):
                        STT(hT[:, kc, tt * 512:(tt + 1) * 512], xf[:, kc, :], col("ng", kc), rstd, ALU.mult, ALU.mult,
                            [("xf", j), "cols", rkey], ["hT"])
                    rmsnorm_tile(xf, ("xf", j), sq, "rn_sq", mk_h)
                debug(f"hT{l}", hT[:], [128, 8, S], "hT", BF16)
                if stop == "hT":
                    raise _Stop()

                def wsec(name):
                    return win_d[l, :, SEC[name] * 512:(SEC[name] + 1) * 512]

                def bcol(name, cc):
                    return col("bin", SEC[name] * 4 + cc)

                def proj(ps, wt, wkey, cc, tsl):
                    for kc in range(8):
                        MM(ps[:, 0:512], wt[:, kc, cc * 128:(cc + 1) * 128], hT[:, kc, tsl], kc == 0, kc == 7, [wkey, "hT"], [pskey[id(ps)]])

                new_phase(4)
                wC = [T(f"wC{i}", [8, 512], BF16) for i in range(2)]
                wci = T("wci", [8, 8], BF16)
                Vc = T("Vc", [NB, 4, 130], BF16)
                browc = T("browC", [512])
                Gt = T("Gt", [S])
                TSm = T("TSm", [NB, 96])
                expnm = T("expnm", [NB, 4])
                NGL = T("NGL", [4, NB + 1])
                PGL = T("PGL", [4, NB + 1])
                dec = T("decC", [4, NB])
                wint = T("wint", [NB, 4])
                wsta = T("wsta", [NB, 4])
                qC = T("qC", [4, S], BF16)
                kC = T("kC", [4, S], BF16)
                CTf = T("CTf", [4, 130])
                CTb = T("CTb", [4, 130], BF16)
                WT = [T(f"WTc{i}", [128]) for i in range(2)]
                STb = [T(f"STc{i}", [128], BF16) for i in range(2)]
                tmpi = [T(f"tmpiC{i}", [130]) for i in range(2)]
                nd = [T(f"ndC{i}", [130]) for i in range(2)]
                kw = [T(f"kwC{i}", [128], BF16) for i in range(2)]
                hn = [T(f"hnC{i}", [128]) for i in range(2)]
                sml = [T(f"smlC{i}", [16]) for i in range(2)]
                gtmp = [T(f"gtmpC{i}", [512]) for i in range(2)]
                mark = st["ar"]
                ibt = T("ibt", [S])
                Ft = T("Ft", [S])
                stk = T("stk", [S])

                DMA(browc, brow_d[l, 1, :].partition_broadcast(128), [], ["browC"])
                load_w(wsec("c_v"), D, 512, wC[0], "wC0")
                load_w(win_d[l, :, CIF0:CIF0 + 8], D, 8, wci, "wci")
                MEMSET("pool", Vc[:, :, :, 128:130], 1.0, ["Vc"])
                for tb in range(NB):
                    ps, pk = PS()
                    for kc in range(8):
                        MM(ps[:, 0:512], hT[:, kc, tb * 128:(tb + 1) * 128], wC[0][:, kc, :], kc == 0, kc == 7, ["wC0", "hT"], [pk])
                    TT("dve", Vc[:, tb, :, 0:128], ps[:, 0:512].rearrange("p (a b) -> p a b", a=4),
                       browc.rearrange("p (a b) -> p a b", a=4), ALU.add, [pk, "browC"], ["Vc"])
                MEMSET("pool", stk, 0.0, ["stk"])
                for tt in range(NT):
                    tsl = slice(tt * 512, (tt + 1) * 512)
                    ps, pk = PS()
                    for kc in range(8):
                        MM(ps[0:4, 0:512], wci[:, kc, 0:4], hT[:, kc, tsl], kc == 0, kc == 7, ["wci", "hT"], [pk])
                    ACT(ibt[0:4, tsl], ps[0:4, 0:512], AF.Identity, [pk, "cols"], ["ibt"], bias=cols[0:4, COLS["cib"]:COLS["cib"] + 1])
                    ps2, pk2 = PS()
                    for kc in range(8):
                        MM(ps2[0:4, 0:512], wci[:, kc, 4:8], hT[:, kc, tsl], kc == 0, kc == 7, ["wci", "hT"], [pk2])
                    ACT(Ft[0:4, tsl], ps2[0:4, 0:512], AF.Identity, [pk2, "cols"], ["Ft"], bias=cols[0:4, COLS["cfb"]:COLS["cfb"] + 1])
                    ACT(Ft[0:4, tsl], Ft[0:4, tsl], AF.Identity, ["Ft", "cols"], ["Ft"], bias=cols[0:4, COLS["cfb2"]:COLS["cfb2"] + 1])
                    ACT(Ft[0:4, tsl], Ft[0:4, tsl], AF.Exp, ["Ft"], ["Ft"], scale=-1.0)
                    ACT(Ft[0:4, tsl], Ft[0:4, tsl], AF.Ln, ["Ft"], ["Ft"], bias=1.0)
                SCAN(Gt[0:4, :], onesrow(4, S), Ft[0:4, :], 0.0, ALU.mult, ALU.subtract, ["Ft", "onesf"], ["Gt"])
                TT("dve", ibt[0:4, :], ibt[0:4, :], Gt[0:4, :], ALU.subtract, ["ibt", "Gt"], ["ibt"])
                SCAN(Ft[0:4, :], ibt[0:4, :], ibt[0:4, :], 0.0, ALU.max, ALU.max, ["ibt"], ["Ft"])
                TT("dve", Gt[0:4, :], Gt[0:4, :], Ft[0:4, :], ALU.add, ["Gt", "Ft"], ["Gt"])
                TS("dve", stk[32:36, :], Gt[0:4, :], -1.0, None, ALU.mult, None, ["Gt"], ["stk"])
                TS("dve", Gt[0:4, :], Ft[0:4, :], -1.0, None, ALU.mult, None, ["Ft"], ["Gt"])
                CP("dve", stk[64:68, :], Gt[0:4, :], ["Gt"], ["stk"])
                TS("dve", stk[0:4, :], ibt[0:4, :], math.log(128.0 ** -0.5), None, ALU.add, None, ["ibt"], ["stk"])
                for tb in range(NB):
                    ps, pk = PS()
                    TR(ps[:, 0:96], stk[0:96, tb * 128:(tb + 1) * 128], ident[0:96, 0:96], ["stk", "ident"], [pk])
                    CP(ev_eng(), TSm[:, tb, :], ps[:, 0:96], [pk], ["TSm"])
                ACT(expnm, TSm[:, :, 32:36], AF.Exp, ["TSm"], ["expnm"])
                MEMSET("pool", NGL, 0.0, ["NGL"])
                for h in range(4):
                    ps, pk = PS()
                    MM(ps[:, 0:NB], sel4[:, h, :], Gt[0:4, 127::128], True, True, ["sel4", "Gt"], [pk])
                    CP("dve", NGL[:, h, 1:NB + 1], ps[:, 0:NB], [pk], ["NGL"])
                TS("dve", PGL, NGL, -1.0, None, ALU.mult, None, ["NGL"], ["PGL"])
                TT("dve", dec, NGL[:, :, 1:NB + 1], NGL[:, :, 0:NB], ALU.subtract, ["NGL"], ["decC"])
                ACT(dec, dec, AF.Exp, ["decC"], ["decC"])
                for h in range(4):
                    for tb in range(NB):
                        ACT(wint[:, tb, h:h + 1], TSm[:, tb, 64 + h:65 + h], AF.Exp, ["TSm", "PGL"], ["wint"], bias=PGL[:, h, tb:tb + 1])
                        ACT(wsta[:, tb, h:h + 1], TSm[:, tb, h:h + 1], AF.Exp, ["TSm", "NGL"], ["wsta"], bias=NGL[:, h, tb + 1:tb + 2])
                debug(f"tsm{l}", TSm, [128, NB, 96], "TSm")
                debug(f"wint{l}", wint, [128, NB, 4], "wint")
                debug(f"wsta{l}", wsta, [128, NB, 4], "wsta")
                if stop == "Cprep":
                    raise _Stop()
                Sc.barrier()
                st["ar"] = mark
                cpad = T("cpad", [3 + S])
                cacc = [T(f"cacc{i}", [S]) for i in range(2)]
                load_w(wsec("c_q"), D, 512, wC[1], "wC1")
                load_w(wsec("c_k"), D, 512, wC[0], "wC0")
                MEMSET("pool", cpad[:, 0:3], 0.0, ["cpad"])
                it = 0
                for which, wt, wk, dst, sname in ((0, wC[1], "wC1", qC, "c_q"), (1, wC[0], "wC0", kC, "c_k")):
                    for h in range(4):
                        i = it % 2
                        it += 1
                        for tt in range(NT):
                            tsl = slice(tt * 512, (tt + 1) * 512)
                            ps, pk = PS()
                            proj(ps, wt, wk, h, tsl)
                            ACT(cpad[:, 3 + tt * 512:3 + (tt + 1) * 512], ps[:, 0:512], AF.Identity, [pk, "cols"], ["cpad"],
                                bias=bcol(sname, h))
                        ch = which * 4 + h
                        w0 = COLS["ccw"] + ch * 4
                        TS("dve", cacc[i], cpad[:, 0:S], cols[:, w0:w0 + 1], col("ccb", ch), ALU.mult, ALU.add,
                           ["cpad", "cols"], [("cacc", i)])
                        for jt in range(1, 4):
                            STT(cacc[i], cpad[:, jt:jt + S], cols[:, w0 + jt:w0 + jt + 1], cacc[i], ALU.mult, ALU.add,
                                ["cpad", "cols", ("cacc", i)], [("cacc", i)])
                        ACT(dst[:, h, :], cacc[i], AF.Silu, [("cacc", i)], [("qkC", which)])
                debug(f"qc{l}", qC, [128, 4, S], ("qkC", 0), BF16)
                debug(f"kc{l}", kC, [128, 4, S], ("qkC", 1), BF16)
                if stop == "Cqk":
                    raise _Stop()
                load_w(wsec("c_o"), D, 512, wC[1], "wC1")
                load_w(wsec("c_z"), D, 512, wC[0], "wC0")
                for h in range(4):
                    for tt in range(NT):
                        tsl = slice(tt * 512, (tt + 1) * 512)
                        j = tt % 2
                        ps, pk = PS()
                        proj(ps, wC[1], "wC1", h, tsl)
                        ACT(gtmp[j], ps[:, 0:512], AF.Sigmoid, [pk, "cols"], [("gtmpC", j)], bias=bcol("c_o", h))
                        ps2, pk2 = PS()
                        proj(ps2, wC[0], "wC0", h, tsl)
                        ACT(Y[2][:, h, tsl], ps2[:, 0:512], AF.Silu, [pk2, "cols"], [("Y", 2)], bias=bcol("c_z", h))
                        TT("dve", Y[2][:, h, tsl], Y[2][:, h, tsl], gtmp[j], ALU.mult, [("Y", 2), ("gtmpC", j)], [("Y", 2)])
                MEMSET("pool", CTf, 0.0, [("CTf", h) for h in range(4)])
                MEMSET("pool", CTb, 0.0, [("CT", h) for h in range(4)])
                itsC = [(tb, h) for tb in range(NB) for h in range(4)]

                def P1(i):
                    tb, h = itsC[i]
                    j = i % 2
                    bsl = slice(tb * 128, (tb + 1) * 128)
                    ps, pk = PS()
                    MM(ps[:, 0:128], kC[:, h, bsl], qC[:, h, bsl], True, True, [("qkC", 0), ("qkC", 1)], [pk])
                    psg, pkg = PS()
                    MM(psg[:, 0:128], sel4[:, h, :], Gt[0:4, bsl], True, True, ["sel4", "Gt"], [pkg])
                    ACT(WT[j], psg[:, 0:128], AF.Exp, [pkg, "TSm"], [("WTc", j)], bias=TSm[:, tb, h:h + 1])
                    ASEL(WT[j], WT[j], [[1, 128]], ALU.is_ge, -1, [("WTc", j)], [("WTc", j)])
                    TT("dve", STb[j], ps[:, 0:128], WT[j], ALU.mult, [pk, ("WTc", j)], [("STc", j)])

                def P23(i):
                    tb, h = itsC[i]
                    j = i % 2
                    bsl = slice(tb * 128, (tb + 1) * 128)
                    ck = ("CT", h)
                    psn, pkn = PS()
                    MM(psn[:, 0:129], STb[j], Vc[:, tb, h, 0:129], True, True, [("STc", j), "Vc"], [pkn])
                    psi, pki = PS()
                    MM(psi[:, 0:129], qC[:, h, bsl], CTb[:, h, 0:129], True, True, [("qkC", 0), ck], [pki])
                    ACT(tmpi[j][:, 0:129], psi[:, 0:129], AF.Copy, [pki, "wint"], [("tmpiC", j)], scale=wint[:, tb, h:h + 1])
                    TT("dve", nd[j][:, 0:129], psn[:, 0:129], tmpi[j][:, 0:129], ALU.add, [pkn, ("tmpiC", j)], [("ndC", j)])
                    ACT(sml[j][:, 12:13], nd[j][:, 128:129], AF.Abs, [("ndC", j)], [("smlC", j)])
                    TS("dve", sml[j][:, 0:1], sml[j][:, 12:13], expnm[:, tb, h:h + 1], None, ALU.max, None,
                       [("smlC", j), "expnm"], [("smlC", j)])
                    RECIP(sml[j][:, 1:2], sml[j][:, 0:1], [("smlC", j)], [("smlC", j)])
                    TS("dve", nd[j][:, 0:128], nd[j][:, 0:128], sml[j][:, 1:2], None, ALU.mult, None, [("ndC", j), ("smlC", j)], [("ndC", j)])
                    Sc.op("dve", lambda e, j=j: e.bn_stats(out=sml[j][:, 2:8], in_=nd[j][:, 0:128]), [("ndC", j)], [("smlC", j)])
                    Sc.op("dve", lambda e, j=j: e.bn_aggr(out=sml[j][:, 8:10], in_=sml[j][:, 2:8]), [("smlC", j)], [("smlC", j)])
                    ACT(sml[j][:, 10:11], sml[j][:, 9:10], AF.Ln, [("smlC", j), "epsc"], [("smlC", j)], bias=epsc[:, 0:1])
                    ACT(sml[j][:, 11:12], sml[j][:, 10:11], AF.Exp, [("smlC", j)], [("smlC", j)], scale=-0.5)
                    TS("dve", hn[j], nd[j][:, 0:128], sml[j][:, 8:9], sml[j][:, 11:12], ALU.subtract, ALU.mult,
                       [("ndC", j), ("smlC", j)], [("hnC", j)])
                    pst, pkt = PS()
                    TR(pst[:, 0:128], hn[j], ident[:], [("hnC", j), "ident"], [pkt])
                    STT(Y[2][:, h, bsl], pst[:, 0:128], col("chg", h), Y[2][:, h, bsl], ALU.mult, ALU.mult, [pkt, "cols", ("Y", 2)], [("Y", 2)])
                    if tb < NB - 1:
                        ph, phk = PSH()
                        TR(ph[:, 0:128], kC[:, h, bsl], identb[:], [("qkC", 1), "identb"], [phk])
                        TS("dve", kw[j], ph[:, 0:128], wsta[:, tb, h:h + 1], None, ALU.mult, None, [phk, "wsta"], [("kwC", j)])
                        psu, pku = PS()
                        MM(psu[:, 0:129], kw[j], Vc[:, tb, h, 0:129], True, True, [("kwC", j), "Vc"], [pku])
                        STT(CTf[:, h, 0:129], CTf[:, h, 0:129], dec[:, h, tb:tb + 1], psu[:, 0:129], ALU.mult, ALU.add,
                            [("CTf", h), "decC", pku], [("CTf", h)])
                        CP("act", CTb[:, h, 0:129], CTf[:, h, 0:129], [("CTf", h)], [ck])

                P1(0)
                for i in range(len(itsC)):
                    if i + 1 < len(itsC):
                        P1(i + 1)
                    P23(i)
                debug(f"yc{l}", Y[2], [128, 4, S], ("Y", 2), BF16)
                if stop == "C":
                    raise _Stop()

                new_phase(3)
                wB = [T(f"wB{i}", [8, 512], BF16) for i in range(2)]
                Vb = T("Vb", [NB, 512], BF16)
                brow = T("browB", [512])
                qB = T("qB", [S], BF16)
                kB = T("kB", [S], BF16)
                zs2 = [T(f"zsB{i}", [S]) for i in range(3)]
                spb2 = [T(f"spB{i}", [S + 1]) for i in range(3)]
                wbf2 = [T(f"wbfB{i}", [S], BF16) for i in range(3)]
                itb = 0
                wTall = T("wTall", [NB, 128], BF16)
                ob = T("obB", [128])
                DMA(brow, brow_d[l, 0, :].partition_broadcast(128), [], ["browB"])
                load_w(wsec("b_v"), D, 512, wB[0], "wB0")
                for tb in range(NB):
                    ps, pk = PS()
                    for kc in range(8):
                        MM(ps[:, 0:512], hT[:, kc, tb * 128:(tb + 1) * 128], wB[0][:, kc, :], kc == 0, kc == 7, ["wB0", "hT"], [pk])
                    TT("dve", Vb[:, tb, :], ps[:, 0:512], brow, ALU.add, [pk, "browB"], ["Vb"])
                load_w(wsec("b_q"), D, 512, wB[1], "wB1")
                load_w(wsec("b_k"), D, 512, wB[0], "wB0")
                sc_b = 64.0 ** -0.5
                itsB = [(pr, qb, hh) for pr in range(4) for qb in range(NB) for hh in range(2)]

                def projqk(pr):
                    for tt in range(NT):
                        tsl = slice(tt * 512, (tt + 1) * 512)
                        ps, pk = PS("z")
                        proj(ps, wB[1], "wB1", pr, tsl)
                        ACT(qB[:, tsl], ps[:, 0:512], AF.Identity, [pk, "cols"], ["qB"], bias=bcol("b_q", pr))
                        ps2, pk2 = PS("z")
                        proj(ps2, wB[0], "wB0", pr, tsl)
                        ACT(kB[:, tsl], ps2[:, 0:512], AF.Identity, [pk2, "cols"], ["kB"], bias=bcol("b_k", pr))

                def bufsB(idx):
                    jb = idx % 3
                    return (zs2[jb], spb2[jb], None, wbf2[jb], ("zsB", jb), ("spB", jb), ("latB", jb), ("wbfB", jb))

                def S1(idx):
                    pr, qb, hh = itsB[idx]
                    zs, spb, lat, wbf, kz, ksp, kla, kwb = bufsB(idx)
                    L = (qb + 1) * 128
                    bsl = slice(qb * 128, (qb + 1) * 128)
                    pl = slice(hh * 64, (hh + 1) * 64)
                    nk = (L + 511) // 512
                    zps = []
                    for ki in range(nk):
                        n = min(512, L - ki * 512)
                        ps, pk = PS("z")
                        MM(ps[:, 0:n], qB[pl, bsl], kB[pl, ki * 512:ki * 512 + n], True, True, ["qB", "kB"], [pk])
                        zps.append((ps, pk, n))
                    for ki, (ps, pk, n) in enumerate(zps):
                        ksl = slice(ki * 512, ki * 512 + n)
                        ACT(spb[:, ksl], ps[:, 0:n], AF.Exp, [pk], [ksp], scale=sc_b)
                        ACT(zs[:, ksl], ps[:, 0:n], AF.Copy, [pk], [kz], scale=sc_b)
                    ACT(spb[:, 0:L], spb[:, 0:L], AF.Ln, [ksp], [ksp], bias=1.0)
                    ASEL(spb[:, qb * 128:qb * 128 + 129], spb[:, qb * 128:qb * 128 + 129], [[-1, 129]], ALU.is_gt, 1, [ksp], [ksp])
                    TT("dve", zs[:, 0:L], zs[:, 0:L], spb[:, 0:L], ALU.subtract, [kz, ksp], [kz])
                    SCAN(spb[:, 1:L + 1][:, ::-1], onesrow(128, L), spb[:, 1:L + 1][:, ::-1], 0.0, ALU.mult, ALU.add,
                         [ksp, "onesf"], [ksp])
                    TT("dve", zs[:, 0:L], zs[:, 0:L], spb[:, 1:L + 1], ALU.subtract, [kz, ksp], [kz])

                def S2a(idx):
                    pr, qb, hh = itsB[idx]
                    zs, spb, lat, wbf, kz, ksp, kla, kwb = bufsB(idx)
                    L = (qb + 1) * 128
                    bsl = slice(qb * 128, (qb + 1) * 128)
                    ACT(wbf[:, 0:L], zs[:, 0:L], AF.Exp, [kz], [kwb])
                    ASEL(wbf[:, bsl], wbf[:, bsl], [[-1, 128]], ALU.is_gt, 1, [kwb], [kwb])

                def S2tr(idx):
                    pr, qb, hh = itsB[idx]
                    zs, spb, lat, wbf, kz, ksp, kla, kwb = bufsB(idx)
                    nkb = qb + 1
                    for kb in range(nkb):
                        bk = kb // 8
                        q = kb % 8
                        TR(psh[bk][:, q * 128:(q + 1) * 128], wbf[:, kb * 128:(kb + 1) * 128], identb[:], [kwb, "identb"], [("psh", bk)])

                def S2ev(idx):
                    pr, qb, hh = itsB[idx]
                    nkb = qb + 1
                    for bk in range((nkb + 7) // 8):
                        gn = min(8, nkb - bk * 8)
                        CP("act", wTall[:, bk * 8:bk * 8 + gn, :], psh[bk][:, 0:gn * 128].rearrange("p (a b) -> p a b", a=gn),
                           [("psh", bk)], ["wTall"])

                def S2pv(idx):
                    pr, qb, hh = itsB[idx]
                    bsl = slice(qb * 128, (qb + 1) * 128)
                    pso, pko = psb[4 + qb % 2], ("ps", 4 + qb % 2)
                    nkb = qb + 1
                    hc = (pr * 2 + hh) * 64
                    for kb in range(nkb):
                        MM(pso[:, hh * 64:(hh + 1) * 64], wTall[:, kb, :], Vb[:, kb, hc:hc + 64],
                           kb == 0, kb == nkb - 1, ["wTall", "Vb"], [pko])
                    if hh == 1:
                        CP("dve", ob, pso[:, 0:128], [pko], ["obB"])
                        pst, pkt = PS("z")
                        TR(pst[:, 0:128], ob, ident[:], ["obB", "ident"], [pkt])
                        CP("dve", Y[1][:, pr, bsl], pst[:, 0:128], [pkt], [("Y", 1)])

                def S1pe(idx):
                    pr, qb, hh = itsB[idx]
                    L = (qb + 1) * 128
                    bsl = slice(qb * 128, (qb + 1) * 128)
                    pl = slice(hh * 64, (hh + 1) * 64)
                    nk = (L + 511) // 512
                    zps = []
                    for ki in range(nk):
                        n = min(512, L - ki * 512)
                        ps, pk = PS("z")
                        MM(ps[:, 0:n], qB[pl, bsl], kB[pl, ki * 512:ki * 512 + n], True, True, ["qB", "kB"], [pk])
                        zps.append((ps, pk, n))
                    return zps

                def S1rest(idx, zps):
                    pr, qb, hh = itsB[idx]
                    zs, spb, lat, wbf, kz, ksp, kla, kwb = bufsB(idx)
                    L = (qb + 1) * 128
                    for ki, (ps, pk, n) in enumerate(zps):
                        ksl = slice(ki * 512, ki * 512 + n)
                        ACT(spb[:, ksl], ps[:, 0:n], AF.Exp, [pk], [ksp], scale=sc_b)
                        ACT(zs[:, ksl], ps[:, 0:n], AF.Copy, [pk], [kz], scale=sc_b)
                    ACT(spb[:, 0:L], spb[:, 0:L], AF.Ln, [ksp], [ksp], bias=1.0)
                    ASEL(spb[:, qb * 128:qb * 128 + 129], spb[:, qb * 128:qb * 128 + 129], [[-1, 129]], ALU.is_gt, 1, [ksp], [ksp])
                    TT("dve", zs[:, 0:L], zs[:, 0:L], spb[:, 0:L], ALU.subtract, [kz, ksp], [kz])
                    SCAN(spb[:, 1:L + 1][:, ::-1], onesrow(128, L), spb[:, 1:L + 1][:, ::-1], 0.0, ALU.mult, ALU.add,
                         [ksp, "onesf"], [ksp])
                    TT("dve", zs[:, 0:L], zs[:, 0:L], spb[:, 1:L + 1], ALU.subtract, [kz, ksp], [kz])

                projqk(0)
                S1rest(0, S1pe(0))
                S1rest(1, S1pe(1))
                for idx in range(len(itsB)):
                    S2a(idx)
                    zps = None
                    if idx + 2 < len(itsB):
                        if itsB[idx + 2][0] != itsB[idx + 1][0]:
                            projqk(itsB[idx + 2][0])
                        zps = S1pe(idx + 2)
                    S2tr(idx)
                    if zps is not None:
                        S1rest(idx + 2, zps)
                    S2ev(idx)
                    S2pv(idx)
                debug(f"yb_pre{l}", Y[1], [128, 4, S], ("Y", 1), BF16)
                load_w(wsec("b_z"), D, 512, wB[1], "wB1")
                for pr in range(4):
                    for tt in range(NT):
                        tsl = slice(tt * 512, (tt + 1) * 512)
                        ps, pk = PS()
                        proj(ps, wB[1], "wB1", pr, tsl)
                        ACT(qB[:, tsl], ps[:, 0:512], AF.Silu, [pk, "cols"], ["qB"], bias=bcol("b_z", pr))
                        TT("dve", Y[1][:, pr, tsl], Y[1][:, pr, tsl], qB[:, tsl], ALU.mult, [("Y", 1), "qB"], [("Y", 1)])
                debug(f"yb{l}", Y[1], [128, 4, S], ("Y", 1), BF16)
                if stop == "B":
                    raise _Stop()

                new_phase(2)
                wA = [T(f"wA{i}", [8, 512], BF16) for i in range(2)]
                yconv = T("yconv", [4, S])
                upad = [T("upad0", [30 + S])] * 2
                sg = [T(f"sgA{i}", [512]) for i in range(2)]
                ysq = T("ysq", [4, 512])
                mean_s = T("meanA", [512])
                rstd_s = T("rstdA", [512])
                tn = [T(f"tnA{i}", [512]) for i in range(2)]
                za = [T(f"zaA{i}", [512]) for i in range(2)]
                load_w(wsec("a_val"), D, 512, wA[0], "wA0")
                load_w(wsec("a_glu"), D, 512, wA[1], "wA1")
                MEMSET("pool", upad[0][:, 0:30], 0.0, [("upad", 0)])
                for cc in range(4):
                    i = 0
                    for tt in range(NT):
                        tsl = slice(tt * 512, (tt + 1) * 512)
                        j = tt % 2
                        ps, pk = PS()
                        proj(ps, wA[1], "wA1", cc, tsl)
                        ACT(sg[j], ps[:, 0:512], AF.Sigmoid, [pk, "cols"], [("sgA", j)], bias=bcol("a_glu", cc))
                        ps2, pk2 = PS()
                        proj(ps2, wA[0], "wA0", cc, tsl)
                        STT(upad[i][:, 30 + tt * 512:30 + (tt + 1) * 512], ps2[:, 0:512], bcol("a_val", cc), sg[j],
                            ALU.add, ALU.mult, [pk2, "cols", ("sgA", j)], [("upad", i)])
                    acw0 = COLS["acw"] + cc * 31
                    TS("dve", yconv[:, cc, :], upad[i][:, 0:S], cols[:, acw0:acw0 + 1], col("acb", cc), ALU.mult, ALU.add,
                       [("upad", i), "cols"], [("yconv", cc)])
                    for jt in range(1, 31):
                        STT(yconv[:, cc, :], upad[i][:, jt:jt + S], cols[:, acw0 + jt:acw0 + jt + 1], yconv[:, cc, :],
                            ALU.mult, ALU.add, [("upad", i), "cols", ("yconv", cc)], [("yconv", cc)])
                debug(f"yconv{l}", yconv, [128, 4, S], [("yconv", c) for c in range(4)])
                load_w(wsec("a_z"), D, 512, wA[0], "wA0")
                for tt in range(NT):
                    tsl = slice(tt * 512, (tt + 1) * 512)
                    ACT(ysq, yconv[:, :, tsl], AF.Square, [("yconv", c) for c in range(4)], ["ysq"])
                    psm, pkm = PS()
                    for cc in range(4):
                        MM(psm[:, 0:512], onesE[:], yconv[:, cc, tsl], cc == 0, cc == 3, ["onesE", ("yconv", cc)], [pkm])
                    pss, pks = PS()
                    for cc in range(4):
                        MM(pss[:, 0:512], onesE[:], ysq[:, cc, :], cc == 0, cc == 3, ["onesE", "ysq"], [pks])
                    CP("act", mean_s, psm[:, 0:512], [pkm], ["meanA"])
                    TT("dve", rstd_s, mean_s, mean_s, ALU.mult, ["meanA"], ["rstdA"])
                    TT("dve", rstd_s, pss[:, 0:512], rstd_s, ALU.subtract, [pks, "rstdA"], ["rstdA"])
                    ACT(rstd_s, rstd_s, AF.Sqrt, ["rstdA", "epsc"], ["rstdA"], bias=epsc[:, 0:1])
                    RECIP(rstd_s, rstd_s, ["rstdA"], ["rstdA"])
                    for cc in range(4):
                        j = cc % 2
                        TT("pool", tn[j], yconv[:, cc, tsl], mean_s, ALU.subtract, [("yconv", cc), "meanA"], [("tnA", j)])
                        TT("dve", tn[j], tn[j], rstd_s, ALU.mult, [("tnA", j), "rstdA"], [("tnA", j)])
                        ACT(tn[j], tn[j], AF.Silu, [("tnA", j), "cols"], [("tnA", j)], scale=col("alg", cc), bias=col("alb", cc))
                        ps, pk = PS()
                        proj(ps, wA[0], "wA0", cc, tsl)
                        ACT(za[j], ps[:, 0:512], AF.Silu, [pk, "cols"], [("zaA", j)], bias=bcol("a_z", cc))
                        TT("dve", Y[0][:, cc, tsl], tn[j], za[j], ALU.mult, [("tnA", j), ("zaA", j)], [("Y", 0)])
                debug(f"ya{l}", Y[0], [128, 4, S], ("Y", 0), BF16)
                if stop == "A":
                    raise _Stop()

                new_phase(1)
                wD = [T(f"wD{i}", [8, 512], BF16) for i in range(2)]
                dwall = T("dwall", [8, 128])
                c1 = T("c1", [4])
                dpad = T("dpad", [3 + S])
                xc = T("xcD", [S])
                av = T("avD", [S])
                uv = T("uvD", [S])
                gi = [T(f"giD{i}", [512]) for i in range(2)]
                zd = [T(f"zdD{i}", [512]) for i in range(2)]
                DMA(dwall, dw_d[l].rearrange("g c p d -> p (g c) d"), [], ["dwall"])
                load_w(wsec("d_x"), D, 512, wD[0], "wD0")
                load_w(wsec("d_z"), D, 512, wD[1], "wD1")
                ACT(c1, cols[:, COLS["dlam"]:COLS["dlam"] + 4], AF.Exp, ["cols"], ["c1"], scale=-1.0)
                ACT(c1, c1, AF.Ln, ["c1"], ["c1"], bias=1.0)
                TS("dve", c1, c1, -8.0, None, ALU.mult, None, ["c1"], ["c1"])
                MEMSET("pool", dpad[:, 0:3], 0.0, ["dpad"])
                for cc in range(4):
                    for tt in range(NT):
                        tsl = slice(tt * 512, (tt + 1) * 512)
                        ps, pk = PS()
                        proj(ps, wD[0], "wD0", cc, tsl)
                        ACT(dpad[:, 3 + tt * 512:3 + (tt + 1) * 512], ps[:, 0:512], AF.Identity, [pk, "cols"], ["dpad"],
                            bias=bcol("d_x", cc))
                    w0 = COLS["dcw"] + cc * 4
                    TS("dve", xc, dpad[:, 0:S], cols[:, w0:w0 + 1], col("dcb", cc), ALU.mult, ALU.add, ["dpad", "cols"], ["xcD"])
                    for jt in range(1, 4):
                        STT(xc, dpad[:, jt:jt + S], cols[:, w0 + jt:w0 + jt + 1], xc, ALU.mult, ALU.add, ["dpad", "cols", "xcD"], ["xcD"])
                    for tt in range(NT):
                        tsl = slice(tt * 512, (tt + 1) * 512)
                        j = tt % 2
                        psa, pka = PS()
                        MM(psa[:, 0:512], dwall[:, 0 * 4 + cc, :], xc[:, tsl], True, True, ["dwall", "xcD"], [pka])
                        psx, pkx = PS()
                        MM(psx[:, 0:512], dwall[:, 1 * 4 + cc, :], xc[:, tsl], True, True, ["dwall", "xcD"], [pkx])
                        ACT(av[:, tsl], psa[:, 0:512], AF.Sigmoid, [pka, "cols"], ["avD"], bias=col("dba", cc))
                        ACT(av[:, tsl], av[:, tsl], AF.Exp, ["avD", "c1"], ["avD"], scale=c1[:, cc:cc + 1])
                        ACT(gi[j], psx[:, 0:512], AF.Sigmoid, [pkx, "cols"], [("giD", j)], bias=col("dbx", cc))
                        TT("pool", uv[:, tsl], av[:, tsl], av[:, tsl], ALU.mult, ["avD"], ["uvD"])
                        ACT(uv[:, tsl], uv[:, tsl], AF.Sqrt, ["uvD"], ["uvD"], scale=-1.0, bias=1.0)
                        TT("pool", gi[j], gi[j], xc[:, tsl], ALU.mult, [("giD", j), "xcD"], [("giD", j)])
                        TT("dve", uv[:, tsl], uv[:, tsl], gi[j], ALU.mult, ["uvD", ("giD", j)], ["uvD"])
                    SCAN(xc, av, uv, 0.0, ALU.mult, ALU.add, ["avD", "uvD", "xcD"], ["xcD"])
                    for tt in range(NT):
                        tsl = slice(tt * 512, (tt + 1) * 512)
                        j = tt % 2
                        ps, pk = PS()
                        proj(ps, wD[1], "wD1", cc, tsl)
                        ACT(zd[j], ps[:, 0:512], AF.Silu, [pk, "cols"], [("zdD", j)], bias=bcol("d_z", cc))
                        TT("dve", Y[3][:, cc, tsl], xc[:, tsl], zd[j], ALU.mult, ["xcD", ("zdD", j)], [("Y", 3)])
                debug(f"yd{l}", Y[3], [128, 4, S], ("Y", 3), BF16)
                if stop == "D":
                    raise _Stop()

                new_phase(0)
                memT = T("memT", [8, NMEM], BF16)
                mkT = T("mkT", [4, NMEM], BF16)
                mv = T("mv", [2, 512], BF16)
                wM = [T(f"wM{i}", [8, 512], BF16) for i in range(2)]
                mrs = T("mrs", [2])
                mq = T("memsq", [D])
                mxs = [T(f"memx{mt}", [D]) for mt in range(2)]
                qm = T("qm", [S], BF16)
                zm = T("zm", [S], BF16)
                pbuf = [T(f"pm{i}", [NMEM], BF16) for i in range(2)]
                pT = [T(f"pTm{i}", [2, 128], BF16) for i in range(2)]
                on = [T(f"onm{i}", [128]) for i in range(2)]
                sm = [T(f"smm{i}", [4]) for i in range(2)]
                for mt in range(2):
                    mx = mxs[mt]
                    DMA(mx, mem_d[seq, mt * 128:(mt + 1) * 128, :], [], [("memx", mt)])
                    ACT(mq, mx, AF.Square, [("memx", mt)], ["memsq", ("mrs", mt)], accum_out=mrs[:, mt:mt + 1])
                    TS("dve", mrs[:, mt:mt + 1], mrs[:, mt:mt + 1], 1.0 / D, EPS, ALU.mult, ALU.add, [("mrs", mt)], [("mrs", mt)])
                    ACT(mrs[:, mt:mt + 1], mrs[:, mt:mt + 1], AF.Sqrt, [("mrs", mt)], [("mrs", mt)])
                    RECIP(mrs[:, mt:mt + 1], mrs[:, mt:mt + 1], [("mrs", mt)], [("mrs", mt)])
                    TS("dve", mx, mx, mrs[:, mt:mt + 1], None, ALU.mult, None, [("memx", mt), ("mrs", mt)], [("memx", mt)])
                    for half in range(2):
                        ps, pk = PS()
                        for q in range(4):
                            kc = half * 4 + q
                            TR(ps[:, q * 128:(q + 1) * 128], mx[:, kc * 128:(kc + 1) * 128], ident[:], [("memx", mt), "ident"], [pk])
                        for q in range(4):
                            kc = half * 4 + q
                            ACT(memT[:, kc, mt * 128:(mt + 1) * 128], ps[:, q * 128:(q + 1) * 128], AF.Copy, [pk, "cols"], ["memT"],
                                scale=col("mng", kc))
                load_w(wmkv_d[l, :, 0:512], D, 512, wM[0], "wM0")
                load_w(wmkv_d[l, :, 512:1024], D, 512, wM[1], "wM1")
                for h in range(4):
                    ps, pk = PS()
                    for kc in range(8):
                        MM(ps[:, 0:NMEM], wM[0][:, kc, h * 128:(h + 1) * 128], memT[:, kc, :], kc == 0, kc == 7, ["wM0", "memT"], [pk])
                    CP(ev_eng(), mkT[:, h, :], ps[:, 0:NMEM], [pk], ["mkT"])
                for mt in range(2):
                    ps, pk = PS()
                    for kc in range(8):
                        MM(ps[:, 0:512], memT[:, kc, mt * 128:(mt + 1) * 128], wM[1][:, kc, :], kc == 0, kc == 7, ["wM1", "memT"], [pk])
                    CP(ev_eng(), mv[:, mt, :], ps[:, 0:512], [pk], ["mv"])
                load_w(wsec("m_q"), D, 512, wM[0], "wM0")
                load_w(wsec("m_z"), D, 512, wM[1], "wM1")
                sc_m = 128.0 ** -0.5
                for h in range(4):
                    for tt in range(NT):
                        tsl = slice(tt * 512, (tt + 1) * 512)
                        ps, pk = PS()
                        proj(ps, wM[0], "wM0", h, tsl)
                        ACT(qm[:, tsl], ps[:, 0:512], AF.Identity, [pk, "cols"], ["qm"], bias=bcol("m_q", h))
                        ps2, pk2 = PS()
                        proj(ps2, wM[1], "wM1", h, tsl)
                        ACT(zm[:, tsl], ps2[:, 0:512], AF.Silu, [pk2, "cols"], ["zm"], bias=bcol("m_z", h))
                    for tb in range(NB):
                        bsl = slice(tb * 128, (tb + 1) * 128)
                        j = tb % 2
                        ps, pk = PS()
                        MM(ps[:, 0:NMEM], qm[:, bsl], mkT[:, h, :], True, True, ["qm", "mkT"], [pk])
                        Sc.op("dve", lambda e, ps=ps, j=j: e.reduce_max(out=sm[j][:, 0:1], in_=ps[:, 0:NMEM], axis=mybir.AxisListType.X),
                              [pk], [("smm", j)])
                        TS("dve", sm[j][:, 1:2], sm[j][:, 0:1], -sc_m, None, ALU.mult, None, [("smm", j)], [("smm", j)])
                        ACT(pbuf[j], ps[:, 0:NMEM], AF.Exp, [pk, ("smm", j)], [("pm", j), ("smm", j)], scale=sc_m, bias=sm[j][:, 1:2],
                            accum_out=sm[j][:, 2:3])
                        ph, phk = PSH()
                        for mt in range(2):
                            TR(ph[:, mt * 128:(mt + 1) * 128], pbuf[j][:, mt * 128:(mt + 1) * 128], identb[:], [("pm", j), "identb"], [phk])
                        CP(ev_eng(), pT[j], ph[:, 0:256].rearrange("p (a b) -> p a b", a=2), [phk], [("pTm", j)])
                        pso, pko = PS()
                        for mt in range(2):
                            MM(pso[:, 0:128], pT[j][:, mt, :], mv[:, mt, h * 128:(h + 1) * 128], mt == 0, mt == 1, [("pTm", j), "mv"], [pko])
                        RECIP(sm[j][:, 3:4], sm[j][:, 2:3], [("smm", j)], [("smm", j)])
                        TS("dve", on[j], pso[:, 0:128], sm[j][:, 3:4], None, ALU.mult, None, [pko, ("smm", j)], [("onm", j)])
                        pst, pkt = PS()
                        TR(pst[:, 0:128], on[j], ident[:], [("onm", j), "ident"], [pkt])
                        TT("dve", Y[4][:, h, bsl], pst[:, 0:128], zm[:, bsl], ALU.mult, [pkt, "zm"], [("Y", 4)])
                debug(f"ym{l}", Y[4], [128, 4, S], ("Y", 4), BF16)
                if stop == "M":
                    raise _Stop()

                new_phase(0)
                mg = T("mg", [8, S], BF16)
                mark = st["ar"]
                RING = 4
                wgn = [T(f"wgn{i}", [8, 128], BF16) for i in range(RING)]
                wun = [T(f"wun{i}", [4, 128], BF16) for i in range(RING)]
                sgm = [T(f"sgm{i}", [512]) for i in range(2)]
                acc = T("accm", [NT, 512])
                order = [(c, n) for c in range(8) for n in range(5)]

                def ldm(idx):
                    c, n = order[idx]
                    i = idx % RING
                    g0 = GATE0 + (c * 5 + n) * 128
                    load_w(win_d[l, :, g0:g0 + 128], D, 128, wgn[i], ("wgn", i))
                    load_w(wup_d[l, n, :, c * 128:(c + 1) * 128], E, 128, wun[i], ("wun", i))
                for idx in range(RING - 1):
                    ldm(idx)
                it = 0
                for idx, (c, n) in enumerate(order):
                    i = idx % RING
                    if idx + RING - 1 < len(order):
                        ldm(idx + RING - 1)
                    for tt in range(NT):
                        tsl = slice(tt * 512, (tt + 1) * 512)
                        j = it % 2
                        it += 1
                        psg, pkg = PS()
                        for kc in range(8):
                            MM(psg[:, 0:512], wgn[i][:, kc, :], hT[:, kc, tsl], kc == 0, kc == 7, [("wgn", i), "hT"], [pkg])
                        ACT(sgm[j], psg[:, 0:512], AF.Sigmoid, [pkg, "cols"], [("sgm", j)], bias=col("bin", 64 + c * 5 + n))
                        psu, pku = PS()
                        for kc in range(4):
                            MM(psu[:, 0:512], wun[i][:, kc, :], Y[n][:, kc, tsl], kc == 0, kc == 3, [("wun", i), ("Y", n)], [pku])
                        if n == 0:
                            TT("dve", acc[:, tt, :], sgm[j], psu[:, 0:512], ALU.mult, [("sgm", j), pku], [("accm", tt)])
                        else:
                            TT("dve", sgm[j], sgm[j], psu[:, 0:512], ALU.mult, [("sgm", j), pku], [("sgm", j)])
                            if n < 4:
                                TT("pool", acc[:, tt, :], acc[:, tt, :], sgm[j], ALU.add, [("accm", tt), ("sgm", j)], [("accm", tt)])
                            else:
                                TT("pool", mg[:, c, tsl], acc[:, tt, :], sgm[j], ALU.add, [("accm", tt), ("sgm", j)], ["mg"])
                debug(f"mg{l}", mg, [128, 8, S], "mg", BF16)
                if stop == "merge":
                    raise _Stop()
                Sc.barrier()
                st["ar"] = mark
                st["lim"] = AR_BASE + 5 * YW
                wo = [T(f"wo{i}", [8, 128], BF16) for i in range(2)]
                xbuf = T("xbuf", [8, S])
                allxs = [("xs", seq, c, tt) for c in range(8) for tt in range(NT)]
                DMA(xbuf, xs_d[seq], allxs, ["xbuf"])
                load_w(wout_d[l, :, 0:128], D, 128, wo[0], ("wo", 0))
                Sc.barrier()
                for c in range(8):
                    i = c % 2
                    if c + 1 < 8:
                        load_w(wout_d[l, :, (c + 1) * 128:(c + 2) * 128], D, 128, wo[(c + 1) % 2], ("wo", (c + 1) % 2))
                    for tt in range(NT):
                        tsl = slice(tt * 512, (tt + 1) * 512)
                        ps, pk = PS()
                        for kc in range(8):
                            MM(ps[:, 0:512], wo[i][:, kc, :], mg[:, kc, tsl], kc == 0, kc == 7, [("wo", i), "mg"], [pk])
                        TT("dve", xbuf[:, c, tsl], xbuf[:, c, tsl], ps[:, 0:512], ALU.add, ["xbuf", pk], ["xbuf"])
                Sc.barrier()
                DMA(xs_d[seq], xbuf, ["xbuf"], allxs)
                if stop == "resid":
                    raise _Stop()

            new_phase(5)
            ot = [T(f"ot{i}", [D]) for i in range(2)]
            it = 0
            for tt in range(NT):
                j = tt % 2
                xf = T(f"xf{j}", [8, 512])
                yo = T(f"yo{j}", [8, 512])
                DMA(xf, xs_d[seq, :, :, tt * 512:(tt + 1) * 512], xskeys(seq, tt), [("xf", j)])

                def mk_o(kc, rstd, rkey, xf=xf, yo=yo, j=j):
                    STT(yo[:, kc, :], xf[:, kc, :], fcols[:, kc:kc + 1], rstd, ALU.mult, ALU.mult,
                        [("xf", j), "fcols", rkey], [("yo", j)])
                rmsnorm_tile(xf, ("xf", j), yo, ("yo", j), mk_o)
                for q4 in range(4):
                    jo = it % 2
                    it += 1
                    for half in range(2):
                        ps, pk = PS()
                        for q in range(4):
                            kc = half * 4 + q
                            TR(ps[:, q * 128:(q + 1) * 128], yo[:, kc, q4 * 128:(q4 + 1) * 128], ident[:], [("yo", j), "ident"], [pk])
                        CP(ev_eng(), ot[jo][:, half * 512:(half + 1) * 512], ps[:, 0:512], [pk], [("ot", jo)])
                    tb = tt * 4 + q4
                    DMA(out_d[seq, tb * 128:(tb + 1) * 128, :], ot[jo], [("ot", jo)], [("outd", seq, tb)])

        try:
            main_body()
        except _Stop:
            pass
        Sc.barrier()
        Sc.emit()
    return nc, dbg_d


def prep_weights(inp):
    DEPTH = inp["w_in"].shape[0]
    perm = np.concatenate([np.arange(0, 5120), np.arange(5128, 8200)] +
                          [8200 + n * 1024 + c * 128 + np.arange(128) for c in range(8) for n in range(5)] +
                          [np.arange(5120, 5128)])
    w_in_r = np.ascontiguousarray(np.asarray(inp["w_in"], np.float32)[:, :, perm])
    b_in_r = np.asarray(inp["b_in"], np.float32)[:, perm]
    cols = np.zeros((DEPTH, 128, NCOL), np.float32)

    def put(l, name, arr):
        cols[l, :, COLS[name]:COLS[name] + arr.shape[1]] = arr

    def pc(v):
        return np.asarray(v, np.float32).reshape(-1, 128).T

    brow = np.zeros((DEPTH, 2, 512), np.float32)
    dw = np.zeros((DEPTH, 2, 4, 128, 128), np.float32)
    for l in range(DEPTH):
        put(l, "bin", pc(b_in_r[l, :13312]))
        put(l, "ng", pc(inp["norm_g"][l]))
        put(l, "mng", pc(inp["mem_norm_g"][l]))
        acw = np.asarray(inp["a_conv_w"][l], np.float32)
        put(l, "acw", acw.T.reshape(4, 128, 31).transpose(1, 0, 2).reshape(128, 124))
        put(l, "acb", pc(inp["a_conv_b"][l]))
        put(l, "alg", pc(inp["a_ln_g"][l]))
        put(l, "alb", pc(inp["a_ln_b"][l]))
        ccw = np.asarray(inp["c_conv_w"][l], np.float32)
        put(l, "ccw", ccw.T.reshape(8, 128, 4).transpose(1, 0, 2).reshape(128, 32))
        put(l, "ccb", pc(inp["c_conv_b"][l]))
        put(l, "chg", pc(inp["c_hn_g"][l]))
        dcw = np.asarray(inp["d_conv_w"][l], np.float32)
        put(l, "dcw", dcw.T.reshape(4, 128, 4).transpose(1, 0, 2).reshape(128, 16))
        put(l, "dcb", pc(inp["d_conv_b"][l]))
        put(l, "dba", pc(inp["d_ba"][l]))
        put(l, "dbx", pc(inp["d_bx"][l]))
        put(l, "dlam", pc(inp["d_lambda"][l]))
        cols[l, 0:4, COLS["cib"]] = b_in_r[l, 13312:13316]
        cols[l, 0:4, COLS["cfb"]] = b_in_r[l, 13316:13320]
        cols[l, 0:4, COLS["cfb2"]] = np.asarray(inp["c_f_bias"][l], np.float32)
        brow[l, 0] = b_in_r[l, SEC["b_v"] * 512:(SEC["b_v"] + 1) * 512]
        brow[l, 1] = b_in_r[l, SEC["c_v"] * 512:(SEC["c_v"] + 1) * 512]
        for g, nm in enumerate(("d_wa", "d_wx")):
            wgt = np.asarray(inp[nm][l], np.float32)
            for cc in range(4):
                dw[l, g, cc, 0:64, 0:64] = wgt[2 * cc]
                dw[l, g, cc, 64:128, 64:128] = wgt[2 * cc + 1]
    fcols = np.ascontiguousarray(pc(inp["final_norm_g"]))
    return dict(w_in_r=w_in_r, brow=brow, cols=cols, dw=dw,
                w_mkv=np.ascontiguousarray(np.asarray(inp["w_mkv"], np.float32)),
                w_up=np.ascontiguousarray(np.asarray(inp["w_up"], np.float32)),
                w_out=np.ascontiguousarray(np.asarray(inp["w_out"], np.float32)),
                fcols=fcols)


_NC_CACHE = {}


def kernel(**inputs):
    x = np.asarray(inputs["x"], np.float32)
    mem = np.asarray(inputs["mem"], np.float32)
    B, S, _ = x.shape
    DEPTH = inputs["w_in"].shape[0]
    ncores = 8
    nseq = B // ncores
    wts = prep_weights(inputs)
    key = (S, nseq, DEPTH)
    if key not in _NC_CACHE:
        _NC_CACHE[key] = build(S, nseq, DEPTH)[0]
    nc = _NC_CACHE[key]
    in_maps = []
    for c in range(ncores):
        m = dict(wts)
        m["x"] = np.ascontiguousarray(x[c * nseq:(c + 1) * nseq])
        m["mem"] = np.ascontiguousarray(mem[c * nseq:(c + 1) * nseq])
        in_maps.append(m)
    res = run_bass_kernel_spmd(nc, in_maps, core_ids=list(range(ncores)))
    out = np.concatenate([np.asarray(r["out"], np.float32) for r in res.results], axis=0)
    return out
```

```python
import math
from contextlib import ExitStack
import numpy as np
import concourse.bass as bass
import concourse.mybir as mybir
from concourse.bass_utils import run_bass_kernel_spmd

F32 = mybir.dt.float32
BF16 = mybir.dt.bfloat16
AF = mybir.ActivationFunctionType
ALU = mybir.AluOpType

ENGS = ("pe", "act", "dve", "pool", "sp")
EPOCH = 24000
NSLOT = {"sp": 8, "pool": 4}


class Sch:
    def __init__(self, nc, es):
        self.nc = nc
        self.es = es
        self.ops = {e: [] for e in ENGS}
        self.cnt = {e: 0 for e in ENGS}
        self.esem = {e: [] for e in ENGS}
        self.known = {e: {} for e in ENGS}
        self.lastw = {}
        self.readers = {}
        self.lasttok = {}
        self.dcnt = {q: 0 for q in NSLOT}
        self.dsem = {q: [self._newsem(f"d_{q}{i}") for i in range(n)] for q, n in NSLOT.items()}

    def _newsem(self, name):
        return self.es.enter_context(self.nc.semaphore(name))

    def _esem(self, e, ep):
        while len(self.esem[e]) <= ep:
            self.esem[e].append(self._newsem(f"e_{e}{len(self.esem[e])}"))
        return self.esem[e][ep]

    def _need(self, e, tok, waits):
        sem, val, src = tok
        if src == "pe" and e == "pe":
            return
        k = self.known[e]
        if k.get(id(sem), 0) >= val:
            return
        k[id(sem)] = val
        waits.append((sem, val))

    def _deps(self, e, r, w):
        waits = []
        for key in r:
            t = self.lastw.get(key)
            if t is not None:
                self._need(e, t, waits)
        for key in w:
            t = self.lastw.get(key)
            if t is not None:
                self._need(e, t, waits)
            for t in self.readers.get(key, ()):
                if t[2] == e:
                    continue
                self._need(e, t, waits)
        return waits

    def _commit(self, tok, r, w):
        for key in r:
            self.readers.setdefault(key, []).append(tok)
        for key in w:
            self.lastw[key] = tok
            self.readers[key] = []

    def op(self, e, fn, r=(), w=()):
        waits = self._deps(e, r, w)
        idx = self.cnt[e]
        self.cnt[e] += 1
        sem = self._esem(e, idx // EPOCH)
        val = idx % EPOCH + 1
        self.ops[e].append((waits, fn, (sem, 1)))
        tok = (sem, val, e)
        self.lasttok[e] = tok
        self._commit(tok, r, w)
        return tok

    def dma(self, q, out, in_, r=(), w=(), **kw):
        waits = self._deps(q, r, w)
        k = self.dcnt[q]
        self.dcnt[q] += 1
        n = NSLOT[q]
        sem = self.dsem[q][k % n]
        val = 16 * (k // n + 1)
        if k >= n:
            self._need(q, (sem, val - 16, "dma"), waits)
        fn = lambda eng: eng.dma_start(out=out, in_=in_, **kw)
        self.ops[q].append((waits, fn, (sem, 16)))
        tok = (sem, val, "dma")
        self._commit(tok, r, w)
        return tok

    def dma_tokens(self):
        toks = []
        for q, n in NSLOT.items():
            k = self.dcnt[q]
            for i in range(min(k, n)):
                j = ((k - 1 - i) // n) * n + i
                toks.append((self.dsem[q][i], 16 * (j // n + 1), "dma"))
        return toks

    def wait_all(self, e, toks):
        waits = []
        for t in toks:
            self._need(e, t, waits)
        if waits:
            self.ops[e].append((waits, None, None))

    def barrier(self):
        toks = list(self.lasttok.values()) + self.dma_tokens()
        for e in ENGS:
            self.wait_all(e, [t for t in toks if t[2] != e])

    def emit(self):
        nc = self.nc
        with nc.Block() as block:
            def run(e, eng):
                for waits, fn, inc in self.ops[e]:
                    for sem, val in waits:
                        eng.wait_ge(sem, val)
                    if fn is not None:
                        fn(eng).then_inc(inc[0], inc[1])

            @block.tensor
            def _(eng):
                run("pe", eng)

            @block.scalar
            def _(eng):
                run("act", eng)

            @block.vector
            def _(eng):
                run("dve", eng)

            @block.gpsimd
            def _(eng):
                run("pool", eng)

            @block.sync
            def _(eng):
                run("sp", eng)


D = 1024
E = 512
NMEM = 256
NIN = 13320
EPS = 1e-6
SEC = dict(a_val=0, a_glu=1, a_z=2, b_q=3, b_k=4, b_v=5, b_z=6, c_q=7, c_k=8, c_v=9,
           c_o=10, c_z=11, d_x=12, d_z=13, m_q=14, m_z=15)
GATE0 = 16 * 512
CIF0 = GATE0 + 5 * 1024


def col_layout():
    names = [("bin", 104), ("ng", 8), ("mng", 8), ("acw", 124), ("acb", 4), ("alg", 4), ("alb", 4),
             ("ccw", 32), ("ccb", 8), ("chg", 4), ("dcw", 16), ("dcb", 4), ("dba", 4), ("dbx", 4),
             ("dlam", 4), ("cib", 1), ("cfb", 1), ("cfb2", 1)]
    off = {}
    o = 0
    for n, w in names:
        off[n] = o
        o += w
    return off, o


COLS, NCOL = col_layout()
AR_BASE = 17152


class _Stop(Exception):
    pass


def build(S, NSEQ, DEPTH, dbg_names=(), stop=None):
    NT = S // 512
    NB = S // 128
    YW = 2 * S
    nc = bass.Bass("TRN2", target_bir_lowering=False)
    x_d = nc.dram_tensor("x", [NSEQ, S, D], F32, kind="ExternalInput").ap()
    mem_d = nc.dram_tensor("mem", [NSEQ, NMEM, D], F32, kind="ExternalInput").ap()
    win_d = nc.dram_tensor("w_in_r", [DEPTH, D, NIN], F32, kind="ExternalInput").ap()
    brow_d = nc.dram_tensor("brow", [DEPTH, 2, 512], F32, kind="ExternalInput").ap()
    cols_d = nc.dram_tensor("cols", [DEPTH, 128, NCOL], F32, kind="ExternalInput").ap()
    dw_d = nc.dram_tensor("dw", [DEPTH, 2, 4, 128, 128], F32, kind="ExternalInput").ap()
    wmkv_d = nc.dram_tensor("w_mkv", [DEPTH, D, 2 * E], F32, kind="ExternalInput").ap()
    wup_d = nc.dram_tensor("w_up", [DEPTH, 5, E, D], F32, kind="ExternalInput").ap()
    wout_d = nc.dram_tensor("w_out", [DEPTH, D, D], F32, kind="ExternalInput").ap()
    fcols_d = nc.dram_tensor("fcols", [128, 8], F32, kind="ExternalInput").ap()
    out_d = nc.dram_tensor("out", [NSEQ, S, D], F32, kind="ExternalOutput").ap()
    xs_d = nc.dram_tensor("xs", [NSEQ, 128, 8, S], F32, kind="Internal").ap()
    dbg_d = {}

    with ExitStack() as es:
        Sc = Sch(nc, es)

        def sb(name, shape, dt=F32):
            return es.enter_context(nc.sbuf_tensor("s_" + name, shape, dt))

        def ACT(out, in_, func, r, w, **kw):
            Sc.op("act", lambda e: e.activation(out=out, in_=in_, func=func, **kw), r, w)

        def TT(eng, out, in0, in1, op, r, w):
            Sc.op(eng, lambda e: e.tensor_tensor(out=out, in0=in0, in1=in1, op=op), r, w)

        def TS(eng, out, in0, s1, s2, op0, op1, r, w):
            if op1 is None:
                Sc.op(eng, lambda e: e.tensor_scalar(out=out, in0=in0, scalar1=s1, scalar2=None, op0=op0), r, w)
            else:
                Sc.op(eng, lambda e: e.tensor_scalar(out=out, in0=in0, scalar1=s1, scalar2=s2, op0=op0, op1=op1), r, w)

        def STT(out, in0, scalar, in1, op0, op1, r, w):
            Sc.op("dve", lambda e: e.scalar_tensor_tensor(out=out, in0=in0, scalar=scalar, in1=in1, op0=op0, op1=op1), r, w)

        def RECIP(out, in_, r, w):
            Sc.op("dve", lambda e: e.reciprocal(out=out, in_=in_), r, w)

        def MM(out, lhsT, rhs, start, stop, r, w):
            Sc.op("pe", lambda e: e.matmul(out, lhsT=lhsT, rhs=rhs, start=start, stop=stop), r, w)

        def TR(out, in_, idn, r, w):
            Sc.op("pe", lambda e: e.transpose(out, in_, idn), r, w)

        def CP(eng, out, in_, r, w):
            if eng == "act":
                Sc.op("act", lambda e: e.activation(out=out, in_=in_, func=AF.Copy), r, w)
            else:
                Sc.op(eng, lambda e: e.tensor_copy(out=out, in_=in_), r, w)

        def MEMSET(eng, ap, val, w):
            Sc.op(eng, lambda e: e.memset(ap, val), (), w)

        def ASEL(out, in_, pattern, cmp, cm, r, w, base=0):
            Sc.op("pool", lambda e: e.affine_select(out=out, in_=in_, pattern=pattern, compare_op=cmp, fill=0.0,
                                                    base=base, channel_multiplier=cm), r, w)

        def SCAN(out, d0, d1, init, op0, op1, r, w):
            Sc.op("dve", lambda e: e.tensor_tensor_scan(out=out, data0=d0, data1=d1, initial=init, op0=op0, op1=op1), r, w)

        def DMA(out, in_, r, w, q="sp", **kw):
            return Sc.dma(q, out, in_, r, w, **kw)

        def debug(name, ap, shape, key, dt=F32):
            if name not in dbg_names:
                return
            d = nc.dram_tensor("dbg_" + name, list(shape), dt, kind="ExternalOutput").ap()
            dbg_d[name] = d
            DMA(d, ap, [key] if not isinstance(key, list) else key, [("dbgd", name)])

        ident = sb("ident", [128, 128])
        identb = sb("identb", [128, 128], BF16)
        onesf = sb("onesf", [128, 128])
        onesD = sb("onesD", [128, 128])
        onesE = sb("onesE", [128, 128])
        sel4 = sb("sel4", [4, 4, 128])
        cols = sb("cols", [128, NCOL])
        fcols = sb("fcols", [128, 8])
        epsc = sb("epsc", [128, 1])
        hT = sb("hT", [128, 8, S], BF16)
        WST_N = 2
        wst = [sb(f"wst{i}", [128, 8, 256]) for i in range(WST_N)]
        AR = AR_BASE + 5 * YW
        arena = sb("arena", [128, AR])
        NPS = 6
        psb = [es.enter_context(nc.psum_tensor(f"ps{i}", [128, 512], F32)) for i in range(NPS)]
        psh = [es.enter_context(nc.psum_tensor(f"psh{i}", [128, 1024], BF16)) for i in range(2)]
        pskey = {id(t): ("ps", i) for i, t in enumerate(psb)}

        Y = [None] * 5
        for pos, n in enumerate((4, 3, 0, 1, 2)):
            a = AR_BASE + pos * YW
            Y[n] = arena[:, a:a + YW].bitcast(BF16).rearrange("p (a b) -> p a b", a=4)

        st = dict(ar=0, lim=AR_BASE, ps=0, psz=0, psh=0, wst=0, cast=0, ev=0, phase=0)
        tmpviews = {}

        def carve(name, free, dt=F32):
            n = int(np.prod(free))
            words = n if dt == F32 else (n + 1) // 2
            words = (words + 7) // 8 * 8
            a = st["ar"]
            assert a + words <= st["lim"], (name, a, words, st["lim"])
            st["ar"] = a + words
            v = arena[:, a:a + words]
            if dt != F32:
                v = v.bitcast(dt)
            v = v[:, 0:n]
            if len(free) == 2:
                v = v.rearrange("p (a b) -> p a b", a=free[0])
            elif len(free) == 3:
                v = v.rearrange("p (a b c) -> p a b c", a=free[0], b=free[1])
            return v

        def T(name, free, dt=F32):
            k = (name, st["phase"])
            if k not in tmpviews:
                tmpviews[k] = carve(name, free, dt)
            return tmpviews[k]

        def new_phase(extra=0):
            Sc.barrier()
            st["ar"] = 0
            st["lim"] = AR_BASE + extra * YW
            st["phase"] += 1

        def PS(pool="all"):
            if pool == "all":
                i = st["ps"] % NPS
                st["ps"] += 1
            else:
                i = st["psz"] % 4
                st["psz"] += 1
            return psb[i], ("ps", i)

        def PSH():
            i = st["psh"] % 2
            st["psh"] += 1
            return psh[i], ("psh", i)

        def ev_eng():
            st["ev"] += 1
            return "act" if st["ev"] % 2 else "dve"

        def onesrow(p, n):
            return onesf[0:p, 0:1].to_broadcast([p, n])

        MEMSET("pool", onesf[:], 1.0, ["onesf"])
        MEMSET("pool", onesD[:], 1.0 / D, ["onesD"])
        MEMSET("pool", onesE[:], 1.0 / E, ["onesE"])
        MEMSET("pool", epsc[:], EPS, ["epsc"])
        ASEL(ident[:], onesf[:], [[-1, 128]], ALU.is_equal, 1, ["onesf"], ["ident"])
        CP("pool", identb[:], ident[:], ["ident"], ["identb"])
        for h in range(4):
            ASEL(sel4[:, h, :], onesf[0:4, :], [[0, 128]], ALU.is_equal, 1, ["onesf"], ["sel4"], base=-h)
        DMA(fcols[:], fcols_d, [], ["fcols"])

        def col(name, i=0):
            o = COLS[name] + i
            return cols[:, o:o + 1]

        def load_w(src2d, K, ncols, dst, dkey):
            kc = K // 128
            c0 = 0
            while c0 < ncols:
                n = min(256, ncols - c0)
                i = st["wst"] % WST_N
                st["wst"] += 1
                stg = wst[i][:, 0:kc, 0:n]
                DMA(stg, src2d[:, c0:c0 + n].rearrange("(kc p) n -> p kc n", p=128), [], [("wst", i)])
                st["cast"] += 1
                eng = "pool" if st["cast"] % 2 else "act"
                CP(eng, dst[:, :, c0:c0 + n], stg, [("wst", i)], [dkey])
                c0 += n

        def rmsnorm_tile(xf, xkey, sq, sqkey, out_fn):
            ACT(sq, xf, AF.Square, [xkey], [sqkey])
            ps, pk = PS()
            for kc in range(8):
                MM(ps[:, 0:512], onesD[:], sq[:, kc, :], kc == 0, kc == 7, ["onesD", sqkey], [pk])
            rstd = T("rn_rstd", [512])
            ACT(rstd, ps[:, 0:512], AF.Sqrt, [pk, "epsc"], ["rn_rstd"], bias=epsc[:, 0:1])
            RECIP(rstd, rstd, ["rn_rstd"], ["rn_rstd"])
            for kc in range(8):
                out_fn(kc, rstd, "rn_rstd")

        def xskeys(seq, tt):
            return [("xs", seq, c, tt) for c in range(8)]

        def main_body():
          for seq in range(NSEQ):
            new_phase(5)
            for tb in range(NB):
                j = tb % 2
                xt = T(f"xin{j}", [D])
                xo = T(f"xout{j}", [8, 128])
                DMA(xt, x_d[seq, tb * 128:(tb + 1) * 128, :], [], [("xin", j)])
                for half in range(2):
                    ps, pk = PS()
                    for q in range(4):
                        kc = half * 4 + q
                        TR(ps[:, q * 128:(q + 1) * 128], xt[:, kc * 128:(kc + 1) * 128], ident[:], [("xin", j), "ident"], [pk])
                    CP(ev_eng(), xo[:, half * 4:(half + 1) * 4, :], ps[:, 0:512].rearrange("p (a b) -> p a b", a=4), [pk], [("xout", j)])
                DMA(xs_d[seq, :, :, tb * 128:(tb + 1) * 128], xo, [("xout", j)], xskeys(seq, tb // 4))

            for l in range(DEPTH):
                new_phase(5)
                DMA(cols[:], cols_d[l], [], ["cols"])
                for tt in range(NT):
                    j = tt % 2
                    xf = T(f"xf{j}", [8, 512])
                    sq = T("rn_sq", [8, 512])
                    DMA(xf, xs_d[seq, :, :, tt * 512:(tt + 1) * 512], xskeys(seq, tt), [("xf", j)])

                    def mk_h(kc, rstd, rkey, xf=xf, tt=tt, j=j):
                        STT(hT[:, kc, tt * 512:(tt + 1) * 512], xf[:, kc, :], col("ng", kc), rstd, ALU.mult, ALU.mult,
                            [("xf", j), "cols", rkey], ["hT"])
                    rmsnorm_tile(xf, ("xf", j), sq, "rn_sq", mk_h)
                debug(f"hT{l}", hT[:], [128, 8, S], "hT", BF16)
                if stop == "hT":
                    raise _Stop()

                def wsec(name):
                    return win_d[l, :, SEC[name] * 512:(SEC[name] + 1) * 512]

                def bcol(name, cc):
                    return col("bin", SEC[name] * 4 + cc)

                def proj(ps, wt, wkey, cc, tsl):
                    for kc in range(8):
                        MM(ps[:, 0:512], wt[:, kc, cc * 128:(cc + 1) * 128], hT[:, kc, tsl], kc == 0, kc == 7, [wkey, "hT"], [pskey[id(ps)]])

                new_phase(4)
                wC = [T(f"wC{i}", [8, 512], BF16) for i in range(2)]
                wci = T("wci", [8, 8], BF16)
                Vc = T("Vc", [NB, 4, 130], BF16)
                browc = T("browC", [512])
                Gt = T("Gt", [S])
                TSm = T("TSm", [NB, 96])
                expnm = T("expnm", [NB, 4])
                NGL = T("NGL", [4, NB + 1])
                PGL = T("PGL", [4, NB + 1])
                dec = T("decC", [4, NB])
                wint = T("wint", [NB, 4])
                wsta = T("wsta", [NB, 4])
                qC = T("qC", [4, S], BF16)
                kC = T("kC", [4, S], BF16)
                CTf = T("CTf", [4, 130])
                CTb = T("CTb", [4, 130], BF16)
                WT = [T(f"WTc{i}", [128]) for i in range(2)]
                STb = [T(f"STc{i}", [128], BF16) for i in range(2)]
                tmpi = [T(f"tmpiC{i}", [130]) for i in range(2)]
                nd = [T(f"ndC{i}", [130]) for i in range(2)]
                kw = [T(f"kwC{i}", [128], BF16) for i in range(2)]
                hn = [T(f"hnC{i}", [128]) for i in range(2)]
                sml = [T(f"smlC{i}", [16]) for i in range(2)]
                gtmp = [T(f"gtmpC{i}", [512]) for i in range(2)]
                mark = st["ar"]
                ibt = T("ibt", [S])
                Ft = T("Ft", [S])
                stk = T("stk", [S])

                DMA(browc, brow_d[l, 1, :].partition_broadcast(128), [], ["browC"])
                load_w(wsec("c_v"), D, 512, wC[0], "wC0")
                load_w(win_d[l, :, CIF0:CIF0 + 8], D, 8, wci, "wci")
                MEMSET("pool", Vc[:, :, :, 128:130], 1.0, ["Vc"])
                for tb in range(NB):
                    ps, pk = PS()
                    for kc in range(8):
                        MM(ps[:, 0:512], hT[:, kc, tb * 128:(tb + 1) * 128], wC[0][:, kc, :], kc == 0, kc == 7, ["wC0", "hT"], [pk])
                    TT("dve", Vc[:, tb, :, 0:128], ps[:, 0:512].rearrange("p (a b) -> p a b", a=4),
                       browc.rearrange("p (a b) -> p a b", a=4), ALU.add, [pk, "browC"], ["Vc"])
                MEMSET("pool", stk, 0.0, ["stk"])
                for tt in range(NT):
                    tsl = slice(tt * 512, (tt + 1) * 512)
                    ps, pk = PS()
                    for kc in range(8):
                        MM(ps[0:4, 0:512], wci[:, kc, 0:4], hT[:, kc, tsl], kc == 0, kc == 7, ["wci", "hT"], [pk])
                    ACT(ibt[0:4, tsl], ps[0:4, 0:512], AF.Identity, [pk, "cols"], ["ibt"], bias=cols[0:4, COLS["cib"]:COLS["cib"] + 1])
                    ps2, pk2 = PS()
                    for kc in range(8):
                        MM(ps2[0:4, 0:512], wci[:, kc, 4:8], hT[:, kc, tsl], kc == 0, kc == 7, ["wci", "hT"], [pk2])
                    ACT(Ft[0:4, tsl], ps2[0:4, 0:512], AF.Identity, [pk2, "cols"], ["Ft"], bias=cols[0:4, COLS["cfb"]:COLS["cfb"] + 1])
                    ACT(Ft[0:4, tsl], Ft[0:4, tsl], AF.Identity, ["Ft", "cols"], ["Ft"], bias=cols[0:4, COLS["cfb2"]:COLS["cfb2"] + 1])
                    ACT(Ft[0:4, tsl], Ft[0:4, tsl], AF.Exp, ["Ft"], ["Ft"], scale=-1.0)
                    ACT(Ft[0:4, tsl], Ft[0:4, tsl], AF.Ln, ["Ft"], ["Ft"], bias=1.0)
                SCAN(Gt[0:4, :], onesrow(4, S), Ft[0:4, :], 0.0, ALU.mult, ALU.subtract, ["Ft", "onesf"], ["Gt"])
                TT("dve", ibt[0:4, :], ibt[0:4, :], Gt[0:4, :], ALU.subtract, ["ibt", "Gt"], ["ibt"])
                SCAN(Ft[0:4, :], ibt[0:4, :], ibt[0:4, :], 0.0, ALU.max, ALU.max, ["ibt"], ["Ft"])
                TT("dve", Gt[0:4, :], Gt[0:4, :], Ft[0:4, :], ALU.add, ["Gt", "Ft"], ["Gt"])
                TS("dve", stk[32:36, :], Gt[0:4, :], -1.0, None, ALU.mult, None, ["Gt"], ["stk"])
                TS("dve", Gt[0:4, :], Ft[0:4, :], -1.0, None, ALU.mult, None, ["Ft"], ["Gt"])
                CP("dve", stk[64:68, :], Gt[0:4, :], ["Gt"], ["stk"])
                TS("dve", stk[0:4, :], ibt[0:4, :], math.log(128.0 ** -0.5), None, ALU.add, None, ["ibt"], ["stk"])
                for tb in range(NB):
                    ps, pk = PS()
                    TR(ps[:, 0:96], stk[0:96, tb * 128:(tb + 1) * 128], ident[0:96, 0:96], ["stk", "ident"], [pk])
                    CP(ev_eng(), TSm[:, tb, :], ps[:, 0:96], [pk], ["TSm"])
                ACT(expnm, TSm[:, :, 32:36], AF.Exp, ["TSm"], ["expnm"])
                MEMSET("pool", NGL, 0.0, ["NGL"])
                for h in range(4):
                    ps, pk = PS()
                    MM(ps[:, 0:NB], sel4[:, h, :], Gt[0:4, 127::128], True, True, ["sel4", "Gt"], [pk])
                    CP("dve", NGL[:, h, 1:NB + 1], ps[:, 0:NB], [pk], ["NGL"])
                TS("dve", PGL, NGL, -1.0, None, ALU.mult, None, ["NGL"], ["PGL"])
                TT("dve", dec, NGL[:, :, 1:NB + 1], NGL[:, :, 0:NB], ALU.subtract, ["NGL"], ["decC"])
                ACT(dec, dec, AF.Exp, ["decC"], ["decC"])
                for h in range(4):
                    for tb in range(NB):
                        ACT(wint[:, tb, h:h + 1], TSm[:, tb, 64 + h:65 + h], AF.Exp, ["TSm", "PGL"], ["wint"], bias=PGL[:, h, tb:tb + 1])
                        ACT(wsta[:, tb, h:h + 1], TSm[:, tb, h:h + 1], AF.Exp, ["TSm", "NGL"], ["wsta"], bias=NGL[:, h, tb + 1:tb + 2])
                debug(f"tsm{l}", TSm, [128, NB, 96], "TSm")
                debug(f"wint{l}", wint, [128, NB, 4], "wint")
                debug(f"wsta{l}", wsta, [128, NB, 4], "wsta")
                if stop == "Cprep":
                    raise _Stop()
                Sc.barrier()
                st["ar"] = mark
                cpad = T("cpad", [3 + S])
                cacc = [T(f"cacc{i}", [S]) for i in range(2)]
                load_w(wsec("c_q"), D, 512, wC[1], "wC1")
                load_w(wsec("c_k"), D, 512, wC[0], "wC0")
                MEMSET("pool", cpad[:, 0:3], 0.0, ["cpad"])
                it = 0
                for which, wt, wk, dst, sname in ((0, wC[1], "wC1", qC, "c_q"), (1, wC[0], "wC0", kC, "c_k")):
                    for h in range(4):
                        i = it % 2
                        it += 1
                        for tt in range(NT):
                            tsl = slice(tt * 512, (tt + 1) * 512)
                            ps, pk = PS()
                            proj(ps, wt, wk, h, tsl)
                            ACT(cpad[:, 3 + tt * 512:3 + (tt + 1) * 512], ps[:, 0:512], AF.Identity, [pk, "cols"], ["cpad"],
                                bias=bcol(sname, h))
                        ch = which * 4 + h
                        w0 = COLS["ccw"] + ch * 4
                        TS("dve", cacc[i], cpad[:, 0:S], cols[:, w0:w0 + 1], col("ccb", ch), ALU.mult, ALU.add,
                           ["cpad", "cols"], [("cacc", i)])
                        for jt in range(1, 4):
                            STT(cacc[i], cpad[:, jt:jt + S], cols[:, w0 + jt:w0 + jt + 1], cacc[i], ALU.mult, ALU.add,
                                ["cpad", "cols", ("cacc", i)], [("cacc", i)])
                        ACT(dst[:, h, :], cacc[i], AF.Silu, [("cacc", i)], [("qkC", which)])
                debug(f"qc{l}", qC, [128, 4, S], ("qkC", 0), BF16)
                debug(f"kc{l}", kC, [128, 4, S], ("qkC", 1), BF16)
                if stop == "Cqk":
                    raise _Stop()
                load_w(wsec("c_o"), D, 512, wC[1], "wC1")
                load_w(wsec("c_z"), D, 512, wC[0], "wC0")
                for h in range(4):
                    for tt in range(NT):
                        tsl = slice(tt * 512, (tt + 1) * 512)
                        j = tt % 2
                        ps, pk = PS()
                        proj(ps, wC[1], "wC1", h, tsl)
                        ACT(gtmp[j], ps[:, 0:512], AF.Sigmoid, [pk, "cols"], [("gtmpC", j)], bias=bcol("c_o", h))
                        ps2, pk2 = PS()
                        proj(ps2, wC[0], "wC0", h, tsl)
                        ACT(Y[2][:, h, tsl], ps2[:, 0:512], AF.Silu, [pk2, "cols"], [("Y", 2)], bias=bcol("c_z", h))
                        TT("dve", Y[2][:, h, tsl], Y[2][:, h, tsl], gtmp[j], ALU.mult, [("Y", 2), ("gtmpC", j)], [("Y", 2)])
                MEMSET("pool", CTf, 0.0, [("CTf", h) for h in range(4)])
                MEMSET("pool", CTb, 0.0, [("CT", h) for h in range(4)])
                itsC = [(tb, h) for tb in range(NB) for h in range(4)]

                def P1(i):
                    tb, h = itsC[i]
                    j = i % 2
                    bsl = slice(tb * 128, (tb + 1) * 128)
                    ps, pk = PS()
                    MM(ps[:, 0:128], kC[:, h, bsl], qC[:, h, bsl], True, True, [("qkC", 0), ("qkC", 1)], [pk])
                    psg, pkg = PS()
                    MM(psg[:, 0:128], sel4[:, h, :], Gt[0:4, bsl], True, True, ["sel4", "Gt"], [pkg])
                    ACT(WT[j], psg[:, 0:128], AF.Exp, [pkg, "TSm"], [("WTc", j)], bias=TSm[:, tb, h:h + 1])
                    ASEL(WT[j], WT[j], [[1, 128]], ALU.is_ge, -1, [("WTc", j)], [("WTc", j)])
                    TT("dve", STb[j], ps[:, 0:128], WT[j], ALU.mult, [pk, ("WTc", j)], [("STc", j)])

                def P23(i):
                    tb, h = itsC[i]
                    j = i % 2
                    bsl = slice(tb * 128, (tb + 1) * 128)
                    ck = ("CT", h)
                    psn, pkn = PS()
                    MM(psn[:, 0:129], STb[j], Vc[:, tb, h, 0:129], True, True, [("STc", j), "Vc"], [pkn])
                    psi, pki = PS()
                    MM(psi[:, 0:129], qC[:, h, bsl], CTb[:, h, 0:129], True, True, [("qkC", 0), ck], [pki])
                    ACT(tmpi[j][:, 0:129], psi[:, 0:129], AF.Copy, [pki, "wint"], [("tmpiC", j)], scale=wint[:, tb, h:h + 1])
                    TT("dve", nd[j][:, 0:129], psn[:, 0:129], tmpi[j][:, 0:129], ALU.add, [pkn, ("tmpiC", j)], [("ndC", j)])
                    ACT(sml[j][:, 12:13], nd[j][:, 128:129], AF.Abs, [("ndC", j)], [("smlC", j)])
                    TS("dve", sml[j][:, 0:1], sml[j][:, 12:13], expnm[:, tb, h:h + 1], None, ALU.max, None,
                       [("smlC", j), "expnm"], [("smlC", j)])
                    RECIP(sml[j][:, 1:2], sml[j][:, 0:1], [("smlC", j)], [("smlC", j)])
                    TS("dve", nd[j][:, 0:128], nd[j][:, 0:128], sml[j][:, 1:2], None, ALU.mult, None, [("ndC", j), ("smlC", j)], [("ndC", j)])
                    Sc.op("dve", lambda e, j=j: e.bn_stats(out=sml[j][:, 2:8], in_=nd[j][:, 0:128]), [("ndC", j)], [("smlC", j)])
                    Sc.op("dve", lambda e, j=j: e.bn_aggr(out=sml[j][:, 8:10], in_=sml[j][:, 2:8]), [("smlC", j)], [("smlC", j)])
                    ACT(sml[j][:, 10:11], sml[j][:, 9:10], AF.Ln, [("smlC", j), "epsc"], [("smlC", j)], bias=epsc[:, 0:1])
                    ACT(sml[j][:, 11:12], sml[j][:, 10:11], AF.Exp, [("smlC", j)], [("smlC", j)], scale=-0.5)
                    TS("dve", hn[j], nd[j][:, 0:128], sml[j][:, 8:9], sml[j][:, 11:12], ALU.subtract, ALU.mult,
                       [("ndC", j), ("smlC", j)], [("hnC", j)])
                    pst, pkt = PS()
                    TR(pst[:, 0:128], hn[j], ident[:], [("hnC", j), "ident"], [pkt])
                    STT(Y[2][:, h, bsl], pst[:, 0:128], col("chg", h), Y[2][:, h, bsl], ALU.mult, ALU.mult, [pkt, "cols", ("Y", 2)], [("Y", 2)])
                    if tb < NB - 1:
                        ph, phk = PSH()
                        TR(ph[:, 0:128], kC[:, h, bsl], identb[:], [("qkC", 1), "identb"], [phk])
                        TS("dve", kw[j], ph[:, 0:128], wsta[:, tb, h:h + 1], None, ALU.mult, None, [phk, "wsta"], [("kwC", j)])
                        psu, pku = PS()
                        MM(psu[:, 0:129], kw[j], Vc[:, tb, h, 0:129], True, True, [("kwC", j), "Vc"], [pku])
                        STT(CTf[:, h, 0:129], CTf[:, h, 0:129], dec[:, h, tb:tb + 1], psu[:, 0:129], ALU.mult, ALU.add,
                            [("CTf", h), "decC", pku], [("CTf", h)])
                        CP("act", CTb[:, h, 0:129], CTf[:, h, 0:129], [("CTf", h)], [ck])

                P1(0)
                for i in range(len(itsC)):
                    if i + 1 < len(itsC):
                        P1(i + 1)
                    P23(i)
                debug(f"yc{l}", Y[2], [128, 4, S], ("Y", 2), BF16)
                if stop == "C":
                    raise _Stop()

                new_phase(3)
                wB = [T(f"wB{i}", [8, 512], BF16) for i in range(2)]
                Vb = T("Vb", [NB, 512], BF16)
                brow = T("browB", [512])
                qB = T("qB", [S], BF16)
                kB = T("kB", [S], BF16)
                zs2 = [T(f"zsB{i}", [S]) for i in range(3)]
                spb2 = [T(f"spB{i}", [S + 1]) for i in range(3)]
                wbf2 = [T(f"wbfB{i}", [S], BF16) for i in range(3)]
                itb = 0
                wTall = T("wTall", [NB, 128], BF16)
                ob = T("obB", [128])
                DMA(brow, brow_d[l, 0, :].partition_broadcast(128), [], ["browB"])
                load_w(wsec("b_v"), D, 512, wB[0], "wB0")
                for tb in range(NB):
                    ps, pk = PS()
                    for kc in range(8):
                        MM(ps[:, 0:512], hT[:, kc, tb * 128:(tb + 1) * 128], wB[0][:, kc, :], kc == 0, kc == 7, ["wB0", "hT"], [pk])
                    TT("dve", Vb[:, tb, :], ps[:, 0:512], brow, ALU.add, [pk, "browB"], ["Vb"])
                load_w(wsec("b_q"), D, 512, wB[1], "wB1")
                load_w(wsec("b_k"), D, 512, wB[0], "wB0")
                sc_b = 64.0 ** -0.5
                itsB = [(pr, qb, hh) for pr in range(4) for qb in range(NB) for hh in range(2)]

                def projqk(pr):
                    for tt in range(NT):
                        tsl = slice(tt * 512, (tt + 1) * 512)
                        ps, pk = PS("z")
                        proj(ps, wB[1], "wB1", pr, tsl)
                        ACT(qB[:, tsl], ps[:, 0:512], AF.Identity, [pk, "cols"], ["qB"], bias=bcol("b_q", pr))
                        ps2, pk2 = PS("z")
                        proj(ps2, wB[0], "wB0", pr, tsl)
                        ACT(kB[:, tsl], ps2[:, 0:512], AF.Identity, [pk2, "cols"], ["kB"], bias=bcol("b_k", pr))

                def bufsB(idx):
                    jb = idx % 3
                    return (zs2[jb], spb2[jb], None, wbf2[jb], ("zsB", jb), ("spB", jb), ("latB", jb), ("wbfB", jb))

                def S1(idx):
                    pr, qb, hh = itsB[idx]
                    zs, spb, lat, wbf, kz, ksp, kla, kwb = bufsB(idx)
                    L = (qb + 1) * 128
                    bsl = slice(qb * 128, (qb + 1) * 128)
                    pl = slice(hh * 64, (hh + 1) * 64)
                    nk = (L + 511) // 512
                    zps = []
                    for ki in range(nk):
                        n = min(512, L - ki * 512)
                        ps, pk = PS("z")
                        MM(ps[:, 0:n], qB[pl, bsl], kB[pl, ki * 512:ki * 512 + n], True, True, ["qB", "kB"], [pk])
                        zps.append((ps, pk, n))
                    for ki, (ps, pk, n) in enumerate(zps):
                        ksl = slice(ki * 512, ki * 512 + n)
                        ACT(spb[:, ksl], ps[:, 0:n], AF.Exp, [pk], [ksp], scale=sc_b)
                        ACT(zs[:, ksl], ps[:, 0:n], AF.Copy, [pk], [kz], scale=sc_b)
                    ACT(spb[:, 0:L], spb[:, 0:L], AF.Ln, [ksp], [ksp], bias=1.0)
                    ASEL(spb[:, qb * 128:qb * 128 + 129], spb[:, qb * 128:qb * 128 + 129], [[-1, 129]], ALU.is_gt, 1, [ksp], [ksp])
                    TT("dve", zs[:, 0:L], zs[:, 0:L], spb[:, 0:L], ALU.subtract, [kz, ksp], [kz])
                    SCAN(spb[:, 1:L + 1][:, ::-1], onesrow(128, L), spb[:, 1:L + 1][:, ::-1], 0.0, ALU.mult, ALU.add,
                         [ksp, "onesf"], [ksp])
                    TT("dve", zs[:, 0:L], zs[:, 0:L], spb[:, 1:L + 1], ALU.subtract, [kz, ksp], [kz])

                def S2a(idx):
                    pr, qb, hh = itsB[idx]
                    zs, spb, lat, wbf, kz, ksp, kla, kwb = bufsB(idx)
                    L = (qb + 1) * 128
                    bsl = slice(qb * 128, (qb + 1) * 128)
                    ACT(wbf[:, 0:L], zs[:, 0:L], AF.Exp, [kz], [kwb])
                    ASEL(wbf[:, bsl], wbf[:, bsl], [[-1, 128]], ALU.is_gt, 1, [kwb], [kwb])

                def S2tr(idx):
                    pr, qb, hh = itsB[idx]
                    zs, spb, lat, wbf, kz, ksp, kla, kwb = bufsB(idx)
                    nkb = qb + 1
                    for kb in range(nkb):
                        bk = kb // 8
                        q = kb % 8
                        TR(psh[bk][:, q * 128:(q + 1) * 128], wbf[:, kb * 128:(kb + 1) * 128], identb[:], [kwb, "identb"], [("psh", bk)])

                def S2ev(idx):
                    pr, qb, hh = itsB[idx]
                    nkb = qb + 1
                    for bk in range((nkb + 7) // 8):
                        gn = min(8, nkb - bk * 8)
                        CP("act", wTall[:, bk * 8:bk * 8 + gn, :], psh[bk][:, 0:gn * 128].rearrange("p (a b) -> p a b", a=gn),
                           [("psh", bk)], ["wTall"])

                def S2pv(idx):
                    pr, qb, hh = itsB[idx]
                    bsl = slice(qb * 128, (qb + 1) * 128)
                    pso, pko = psb[4 + qb % 2], ("ps", 4 + qb % 2)
                    nkb = qb + 1
                    hc = (pr * 2 + hh) * 64
                    for kb in range(nkb):
                        MM(pso[:, hh * 64:(hh + 1) * 64], wTall[:, kb, :], Vb[:, kb, hc:hc + 64],
                           kb == 0, kb == nkb - 1, ["wTall", "Vb"], [pko])
                    if hh == 1:
                        CP("dve", ob, pso[:, 0:128], [pko], ["obB"])
                        pst, pkt = PS("z")
                        TR(pst[:, 0:128], ob, ident[:], ["obB", "ident"], [pkt])
                        CP("dve", Y[1][:, pr, bsl], pst[:, 0:128], [pkt], [("Y", 1)])

                def S1pe(idx):
                    pr, qb, hh = itsB[idx]
                    L = (qb + 1) * 128
                    bsl = slice(qb * 128, (qb + 1) * 128)
                    pl = slice(hh * 64, (hh + 1) * 64)
                    nk = (L + 511) // 512
                    zps = []
                    for ki in range(nk):
                        n = min(512, L - ki * 512)
                        ps, pk = PS("z")
                        MM(ps[:, 0:n], qB[pl, bsl], kB[pl, ki * 512:ki * 512 + n], True, True, ["qB", "kB"], [pk])
                        zps.append((ps, pk, n))
                    return zps

                def S1rest(idx, zps):
                    pr, qb, hh = itsB[idx]
                    zs, spb, lat, wbf, kz, ksp, kla, kwb = bufsB(idx)
                    L = (qb + 1) * 128
                    for ki, (ps, pk, n) in enumerate(zps):
                        ksl = slice(ki * 512, ki * 512 + n)
                        ACT(spb[:, ksl], ps[:, 0:n], AF.Exp, [pk], [ksp], scale=sc_b)
                        ACT(zs[:, ksl], ps[:, 0:n], AF.Copy, [pk], [kz], scale=sc_b)
                    ACT(spb[:, 0:L], spb[:, 0:L], AF.Ln, [ksp], [ksp], bias=1.0)
                    ASEL(spb[:, qb * 128:qb * 128 + 129], spb[:, qb * 128:qb * 128 + 129], [[-1, 129]], ALU.is_gt, 1, [ksp], [ksp])
                    TT("dve", zs[:, 0:L], zs[:, 0:L], spb[:, 0:L], ALU.subtract, [kz, ksp], [kz])
                    SCAN(spb[:, 1:L + 1][:, ::-1], onesrow(128, L), spb[:, 1:L + 1][:, ::-1], 0.0, ALU.mult, ALU.add,
                         [ksp, "onesf"], [ksp])
                    TT("dve", zs[:, 0:L], zs[:, 0:L], spb[:, 1:L + 1], ALU.subtract, [kz, ksp], [kz])

                projqk(0)
                S1rest(0, S1pe(0))
                S1rest(1, S1pe(1))
                for idx in range(len(itsB)):
                    S2a(idx)
                    zps = None
                    if idx + 2 < len(itsB):
                        if itsB[idx + 2][0] != itsB[idx + 1][0]:
                            projqk(itsB[idx + 2][0])
                        zps = S1pe(idx + 2)
                    S2tr(idx)
                    if zps is not None:
                        S1rest(idx + 2, zps)
                    S2ev(idx)
                    S2pv(idx)
                debug(f"yb_pre{l}", Y[1], [128, 4, S], ("Y", 1), BF16)
                load_w(wsec("b_z"), D, 512, wB[1], "wB1")
                for pr in range(4):
                    for tt in range(NT):
                        tsl = slice(tt * 512, (tt + 1) * 512)
                        ps, pk = PS()
                        proj(ps, wB[1], "wB1", pr, tsl)
                        ACT(qB[:, tsl], ps[:, 0:512], AF.Silu, [pk, "cols"], ["qB"], bias=bcol("b_z", pr))
                        TT("dve", Y[1][:, pr, tsl], Y[1][:, pr, tsl], qB[:, tsl], ALU.mult, [("Y", 1), "qB"], [("Y", 1)])
                debug(f"yb{l}", Y[1], [128, 4, S], ("Y", 1), BF16)
                if stop == "B":
                    raise _Stop()

                new_phase(2)
                wA = [T(f"wA{i}", [8, 512], BF16) for i in range(2)]
                yconv = T("yconv", [4, S])
                upad = [T("upad0", [30 + S])] * 2
                sg = [T(f"sgA{i}", [512]) for i in range(2)]
                ysq = T("ysq", [4, 512])
                mean_s = T("meanA", [512])
                rstd_s = T("rstdA", [512])
                tn = [T(f"tnA{i}", [512]) for i in range(2)]
                za = [T(f"zaA{i}", [512]) for i in range(2)]
                load_w(wsec("a_val"), D, 512, wA[0], "wA0")
                load_w(wsec("a_glu"), D, 512, wA[1], "wA1")
                MEMSET("pool", upad[0][:, 0:30], 0.0, [("upad", 0)])
                for cc in range(4):
                    i = 0
                    for tt in range(NT):
                        tsl = slice(tt * 512, (tt + 1) * 512)
                        j = tt % 2
                        ps, pk = PS()
                        proj(ps, wA[1], "wA1", cc, tsl)
                        ACT(sg[j], ps[:, 0:512], AF.Sigmoid, [pk, "cols"], [("sgA", j)], bias=bcol("a_glu", cc))
                        ps2, pk2 = PS()
                        proj(ps2, wA[0], "wA0", cc, tsl)
                        STT(upad[i][:, 30 + tt * 512:30 + (tt + 1) * 512], ps2[:, 0:512], bcol("a_val", cc), sg[j],
                            ALU.add, ALU.mult, [pk2, "cols", ("sgA", j)], [("upad", i)])
                    acw0 = COLS["acw"] + cc * 31
                    TS("dve", yconv[:, cc, :], upad[i][:, 0:S], cols[:, acw0:acw0 + 1], col("acb", cc), ALU.mult, ALU.add,
                       [("upad", i), "cols"], [("yconv", cc)])
                    for jt in range(1, 31):
                        STT(yconv[:, cc, :], upad[i][:, jt:jt + S], cols[:, acw0 + jt:acw0 + jt + 1], yconv[:, cc, :],
                            ALU.mult, ALU.add, [("upad", i), "cols", ("yconv", cc)], [("yconv", cc)])
                debug(f"yconv{l}", yconv, [128, 4, S], [("yconv", c) for c in range(4)])
                load_w(wsec("a_z"), D, 512, wA[0], "wA0")
                for tt in range(NT):
                    tsl = slice(tt * 512, (tt + 1) * 512)
                    ACT(ysq, yconv[:, :, tsl], AF.Square, [("yconv", c) for c in range(4)], ["ysq"])
                    psm, pkm = PS()
                    for cc in range(4):
                        MM(psm[:, 0:512], onesE[:], yconv[:, cc, tsl], cc == 0, cc == 3, ["onesE", ("yconv", cc)], [pkm])
                    pss, pks = PS()
                    for cc in range(4):
                        MM(pss[:, 0:512], onesE[:], ysq[:, cc, :], cc == 0, cc == 3, ["onesE", "ysq"], [pks])
                    CP("act", mean_s, psm[:, 0:512], [pkm], ["meanA"])
                    TT("dve", rstd_s, mean_s, mean_s, ALU.mult, ["meanA"], ["rstdA"])
                    TT("dve", rstd_s, pss[:, 0:512], rstd_s, ALU.subtract, [pks, "rstdA"], ["rstdA"])
                    ACT(rstd_s, rstd_s, AF.Sqrt, ["rstdA", "epsc"], ["rstdA"], bias=epsc[:, 0:1])
                    RECIP(rstd_s, rstd_s, ["rstdA"], ["rstdA"])
                    for cc in range(4):
                        j = cc % 2
                        TT("pool", tn[j], yconv[:, cc, tsl], mean_s, ALU.subtract, [("yconv", cc), "meanA"], [("tnA", j)])
                        TT("dve", tn[j], tn[j], rstd_s, ALU.mult, [("tnA", j), "rstdA"], [("tnA", j)])
                        ACT(tn[j], tn[j], AF.Silu, [("tnA", j), "cols"], [("tnA", j)], scale=col("alg", cc), bias=col("alb", cc))
                        ps, pk = PS()
                        proj(ps, wA[0], "wA0", cc, tsl)
                        ACT(za[j], ps[:, 0:512], AF.Silu, [pk, "cols"], [("zaA", j)], bias=bcol("a_z", cc))
                        TT("dve", Y[0][:, cc, tsl], tn[j], za[j], ALU.mult, [("tnA", j), ("zaA", j)], [("Y", 0)])
                debug(f"ya{l}", Y[0], [128, 4, S], ("Y", 0), BF16)
                if stop == "A":
                    raise _Stop()

                new_phase(1)
                wD = [T(f"wD{i}", [8, 512], BF16) for i in range(2)]
                dwall = T("dwall", [8, 128])
                c1 = T("c1", [4])
                dpad = T("dpad", [3 + S])
                xc = T("xcD", [S])
                av = T("avD", [S])
                uv = T("uvD", [S])
                gi = [T(f"giD{i}", [512]) for i in range(2)]
                zd = [T(f"zdD{i}", [512]) for i in range(2)]
                DMA(dwall, dw_d[l].rearrange("g c p d -> p (g c) d"), [], ["dwall"])
                load_w(wsec("d_x"), D, 512, wD[0], "wD0")
                load_w(wsec("d_z"), D, 512, wD[1], "wD1")
                ACT(c1, cols[:, COLS["dlam"]:COLS["dlam"] + 4], AF.Exp, ["cols"], ["c1"], scale=-1.0)
                ACT(c1, c1, AF.Ln, ["c1"], ["c1"], bias=1.0)
                TS("dve", c1, c1, -8.0, None, ALU.mult, None, ["c1"], ["c1"])
                MEMSET("pool", dpad[:, 0:3], 0.0, ["dpad"])
                for cc in range(4):
                    for tt in range(NT):
                        tsl = slice(tt * 512, (tt + 1) * 512)
                        ps, pk = PS()
                        proj(ps, wD[0], "wD0", cc, tsl)
                        ACT(dpad[:, 3 + tt * 512:3 + (tt + 1) * 512], ps[:, 0:512], AF.Identity, [pk, "cols"], ["dpad"],
                            bias=bcol("d_x", cc))
                    w0 = COLS["dcw"] + cc * 4
                    TS("dve", xc, dpad[:, 0:S], cols[:, w0:w0 + 1], col("dcb", cc), ALU.mult, ALU.add, ["dpad", "cols"], ["xcD"])
                    for jt in range(1, 4):
                        STT(xc, dpad[:, jt:jt + S], cols[:, w0 + jt:w0 + jt + 1], xc, ALU.mult, ALU.add, ["dpad", "cols", "xcD"], ["xcD"])
                    for tt in range(NT):
                        tsl = slice(tt * 512, (tt + 1) * 512)
                        j = tt % 2
                        psa, pka = PS()
                        MM(psa[:, 0:512], dwall[:, 0 * 4 + cc, :], xc[:, tsl], True, True, ["dwall", "xcD"], [pka])
                        psx, pkx = PS()
                        MM(psx[:, 0:512], dwall[:, 1 * 4 + cc, :], xc[:, tsl], True, True, ["dwall", "xcD"], [pkx])
                        ACT(av[:, tsl], psa[:, 0:512], AF.Sigmoid, [pka, "cols"], ["avD"], bias=col("dba", cc))
                        ACT(av[:, tsl], av[:, tsl], AF.Exp, ["avD", "c1"], ["avD"], scale=c1[:, cc:cc + 1])
                        ACT(gi[j], psx[:, 0:512], AF.Sigmoid, [pkx, "cols"], [("giD", j)], bias=col("dbx", cc))
                        TT("pool", uv[:, tsl], av[:, tsl], av[:, tsl], ALU.mult, ["avD"], ["uvD"])
                        ACT(uv[:, tsl], uv[:, tsl], AF.Sqrt, ["uvD"], ["uvD"], scale=-1.0, bias=1.0)
                        TT("pool", gi[j], gi[j], xc[:, tsl], ALU.mult, [("giD", j), "xcD"], [("giD", j)])
                        TT("dve", uv[:, tsl], uv[:, tsl], gi[j], ALU.mult, ["uvD", ("giD", j)], ["uvD"])
                    SCAN(xc, av, uv, 0.0, ALU.mult, ALU.add, ["avD", "uvD", "xcD"], ["xcD"])
                    for tt in range(NT):
                        tsl = slice(tt * 512, (tt + 1) * 512)
                        j = tt % 2
                        ps, pk = PS()
                        proj(ps, wD[1], "wD1", cc, tsl)
                        ACT(zd[j], ps[:, 0:512], AF.Silu, [pk, "cols"], [("zdD", j)], bias=bcol("d_z", cc))
                        TT("dve", Y[3][:, cc, tsl], xc[:, tsl], zd[j], ALU.mult, ["xcD", ("zdD", j)], [("Y", 3)])
                debug(f"yd{l}", Y[3], [128, 4, S], ("Y", 3), BF16)
                if stop == "D":
                    raise _Stop()

                new_phase(0)
                memT = T("memT", [8, NMEM], BF16)
                mkT = T("mkT", [4, NMEM], BF16)
                mv = T("mv", [2, 512], BF16)
                wM = [T(f"wM{i}", [8, 512], BF16) for i in range(2)]
                mrs = T("mrs", [2])
                mq = T("memsq", [D])
                mxs = [T(f"memx{mt}", [D]) for mt in range(2)]
                qm = T("qm", [S], BF16)
                zm = T("zm", [S], BF16)
                pbuf = [T(f"pm{i}", [NMEM], BF16) for i in range(2)]
                pT = [T(f"pTm{i}", [2, 128], BF16) for i in range(2)]
                on = [T(f"onm{i}", [128]) for i in range(2)]
                sm = [T(f"smm{i}", [4]) for i in range(2)]
                for mt in range(2):
                    mx = mxs[mt]
                    DMA(mx, mem_d[seq, mt * 128:(mt + 1) * 128, :], [], [("memx", mt)])
                    ACT(mq, mx, AF.Square, [("memx", mt)], ["memsq", ("mrs", mt)], accum_out=mrs[:, mt:mt + 1])
                    TS("dve", mrs[:, mt:mt + 1], mrs[:, mt:mt + 1], 1.0 / D, EPS, ALU.mult, ALU.add, [("mrs", mt)], [("mrs", mt)])
                    ACT(mrs[:, mt:mt + 1], mrs[:, mt:mt + 1], AF.Sqrt, [("mrs", mt)], [("mrs", mt)])
                    RECIP(mrs[:, mt:mt + 1], mrs[:, mt:mt + 1], [("mrs", mt)], [("mrs", mt)])
                    TS("dve", mx, mx, mrs[:, mt:mt + 1], None, ALU.mult, None, [("memx", mt), ("mrs", mt)], [("memx", mt)])
                    for half in range(2):
                        ps, pk = PS()
                        for q in range(4):
                            kc = half * 4 + q
                            TR(ps[:, q * 128:(q + 1) * 128], mx[:, kc * 128:(kc + 1) * 128], ident[:], [("memx", mt), "ident"], [pk])
                        for q in range(4):
                            kc = half * 4 + q
                            ACT(memT[:, kc, mt * 128:(mt + 1) * 128], ps[:, q * 128:(q + 1) * 128], AF.Copy, [pk, "cols"], ["memT"],
                                scale=col("mng", kc))
                load_w(wmkv_d[l, :, 0:512], D, 512, wM[0], "wM0")
                load_w(wmkv_d[l, :, 512:1024], D, 512, wM[1], "wM1")
                for h in range(4):
                    ps, pk = PS()
                    for kc in range(8):
                        MM(ps[:, 0:NMEM], wM[0][:, kc, h * 128:(h + 1) * 128], memT[:, kc, :], kc == 0, kc == 7, ["wM0", "memT"], [pk])
                    CP(ev_eng(), mkT[:, h, :], ps[:, 0:NMEM], [pk], ["mkT"])
                for mt in range(2):
                    ps, pk = PS()
                    for kc in range(8):
                        MM(ps[:, 0:512], memT[:, kc, mt * 128:(mt + 1) * 128], wM[1][:, kc, :], kc == 0, kc == 7, ["wM1", "memT"], [pk])
                    CP(ev_eng(), mv[:, mt, :], ps[:, 0:512], [pk], ["mv"])
                load_w(wsec("m_q"), D, 512, wM[0], "wM0")
                load_w(wsec("m_z"), D, 512, wM[1], "wM1")
                sc_m = 128.0 ** -0.5
                qm2 = [qm, T("qm1", [S], BF16)]
                zm2 = [zm, T("zm1", [S], BF16)]
                itsM = [(h, tb) for h in range(4) for tb in range(NB)]

                def projM(h):
                    hp = h % 2
                    for tt in range(NT):
                        tsl = slice(tt * 512, (tt + 1) * 512)
                        ps, pk = PS()
                        proj(ps, wM[0], "wM0", h, tsl)
                        ACT(qm2[hp][:, tsl], ps[:, 0:512], AF.Identity, [pk, "cols"], [("qm", hp)], bias=bcol("m_q", h))
                        ps2, pk2 = PS()
                        proj(ps2, wM[1], "wM1", h, tsl)
                        ACT(zm2[hp][:, tsl], ps2[:, 0:512], AF.Silu, [pk2, "cols"], [("zm", hp)], bias=bcol("m_z", h))

                def M1(i):
                    h, tb = itsM[i]
                    hp = h % 2
                    j = i % 2
                    bsl = slice(tb * 128, (tb + 1) * 128)
                    ps, pk = PS()
                    MM(ps[:, 0:NMEM], qm2[hp][:, bsl], mkT[:, h, :], True, True, [("qm", hp), "mkT"], [pk])
                    Sc.op("dve", lambda e, ps=ps, j=j: e.reduce_max(out=sm[j][:, 0:1], in_=ps[:, 0:NMEM], axis=mybir.AxisListType.X),
                          [pk], [("smm", j)])
                    TS("dve", sm[j][:, 1:2], sm[j][:, 0:1], -sc_m, None, ALU.mult, None, [("smm", j)], [("smm", j)])
                    ACT(pbuf[j], ps[:, 0:NMEM], AF.Exp, [pk, ("smm", j)], [("pm", j), ("smm", j)], scale=sc_m, bias=sm[j][:, 1:2],
                        accum_out=sm[j][:, 2:3])

                def M2(i):
                    h, tb = itsM[i]
                    hp = h % 2
                    j = i % 2
                    bsl = slice(tb * 128, (tb + 1) * 128)
                    ph, phk = PSH()
                    for mt in range(2):
                        TR(ph[:, mt * 128:(mt + 1) * 128], pbuf[j][:, mt * 128:(mt + 1) * 128], identb[:], [("pm", j), "identb"], [phk])
                    CP(ev_eng(), pT[j], ph[:, 0:256].rearrange("p (a b) -> p a b", a=2), [phk], [("pTm", j)])
                    pso, pko = PS()
                    for mt in range(2):
                        MM(pso[:, 0:128], pT[j][:, mt, :], mv[:, mt, h * 128:(h + 1) * 128], mt == 0, mt == 1, [("pTm", j), "mv"], [pko])
                    RECIP(sm[j][:, 3:4], sm[j][:, 2:3], [("smm", j)], [("smm", j)])
                    TS("dve", on[j], pso[:, 0:128], sm[j][:, 3:4], None, ALU.mult, None, [pko, ("smm", j)], [("onm", j)])
                    pst, pkt = PS()
                    TR(pst[:, 0:128], on[j], ident[:], [("onm", j), "ident"], [pkt])
                    TT("dve", Y[4][:, h, bsl], pst[:, 0:128], zm2[hp][:, bsl], ALU.mult, [pkt, ("zm", hp)], [("Y", 4)])

                projM(0)
                M1(0)
                for i in range(len(itsM)):
                    if i + 1 < len(itsM):
                        if itsM[i + 1][0] != itsM[i][0]:
                            projM(itsM[i + 1][0])
                        M1(i + 1)
                    M2(i)
                debug(f"ym{l}", Y[4], [128, 4, S], ("Y", 4), BF16)
                if stop == "M":
                    raise _Stop()

                new_phase(0)
                mg = T("mg", [8, S], BF16)
                mark = st["ar"]
                RING = 4
                wgn = [T(f"wgn{i}", [8, 128], BF16) for i in range(RING)]
                wun = [T(f"wun{i}", [4, 128], BF16) for i in range(RING)]
                sgm = [T(f"sgm{i}", [512]) for i in range(2)]
                acc = T("accm", [NT, 512])
                order = [(c, n) for c in range(8) for n in range(5)]

                def ldm(idx):
                    c, n = order[idx]
                    i = idx % RING
                    g0 = GATE0 + (c * 5 + n) * 128
                    load_w(win_d[l, :, g0:g0 + 128], D, 128, wgn[i], ("wgn", i))
                    load_w(wup_d[l, n, :, c * 128:(c + 1) * 128], E, 128, wun[i], ("wun", i))
                for idx in range(RING - 1):
                    ldm(idx)
                it = 0
                for idx, (c, n) in enumerate(order):
                    i = idx % RING
                    if idx + RING - 1 < len(order):
                        ldm(idx + RING - 1)
                    for tt in range(NT):
                        tsl = slice(tt * 512, (tt + 1) * 512)
                        j = it % 2
                        it += 1
                        psg, pkg = PS()
                        for kc in range(8):
                            MM(psg[:, 0:512], wgn[i][:, kc, :], hT[:, kc, tsl], kc == 0, kc == 7, [("wgn", i), "hT"], [pkg])
                        ACT(sgm[j], psg[:, 0:512], AF.Sigmoid, [pkg, "cols"], [("sgm", j)], bias=col("bin", 64 + c * 5 + n))
                        psu, pku = PS()
                        for kc in range(4):
                            MM(psu[:, 0:512], wun[i][:, kc, :], Y[n][:, kc, tsl], kc == 0, kc == 3, [("wun", i), ("Y", n)], [pku])
                        if n == 0:
                            TT("dve", acc[:, tt, :], sgm[j], psu[:, 0:512], ALU.mult, [("sgm", j), pku], [("accm", tt)])
                        else:
                            TT("dve", sgm[j], sgm[j], psu[:, 0:512], ALU.mult, [("sgm", j), pku], [("sgm", j)])
                            if n < 4:
                                TT("pool", acc[:, tt, :], acc[:, tt, :], sgm[j], ALU.add, [("accm", tt), ("sgm", j)], [("accm", tt)])
                            else:
                                TT("pool", mg[:, c, tsl], acc[:, tt, :], sgm[j], ALU.add, [("accm", tt), ("sgm", j)], ["mg"])
                debug(f"mg{l}", mg, [128, 8, S], "mg", BF16)
                if stop == "merge":
                    raise _Stop()
                Sc.barrier()
                st["ar"] = mark
                st["lim"] = AR_BASE + 5 * YW
                wo = [T(f"wo{i}", [8, 128], BF16) for i in range(2)]
                xbuf = T("xbuf", [8, S])
                allxs = [("xs", seq, c, tt) for c in range(8) for tt in range(NT)]
                DMA(xbuf, xs_d[seq], allxs, ["xbuf"])
                load_w(wout_d[l, :, 0:128], D, 128, wo[0], ("wo", 0))
                Sc.barrier()
                for c in range(8):
                    i = c % 2
                    if c + 1 < 8:
                        load_w(wout_d[l, :, (c + 1) * 128:(c + 2) * 128], D, 128, wo[(c + 1) % 2], ("wo", (c + 1) % 2))
                    for tt in range(NT):
                        tsl = slice(tt * 512, (tt + 1) * 512)
                        ps, pk = PS()
                        for kc in range(8):
                            MM(ps[:, 0:512], wo[i][:, kc, :], mg[:, kc, tsl], kc == 0, kc == 7, [("wo", i), "mg"], [pk])
                        TT("dve", xbuf[:, c, tsl], xbuf[:, c, tsl], ps[:, 0:512], ALU.add, ["xbuf", pk], ["xbuf"])
                Sc.barrier()
                DMA(xs_d[seq], xbuf, ["xbuf"], allxs)
                if stop == "resid":
                    raise _Stop()

            new_phase(5)
            ot = [T(f"ot{i}", [D]) for i in range(2)]
            it = 0
            for tt in range(NT):
                j = tt % 2
                xf = T(f"xf{j}", [8, 512])
                yo = T(f"yo{j}", [8, 512])
                DMA(xf, xs_d[seq, :, :, tt * 512:(tt + 1) * 512], xskeys(seq, tt), [("xf", j)])

                def mk_o(kc, rstd, rkey, xf=xf, yo=yo, j=j):
                    STT(yo[:, kc, :], xf[:, kc, :], fcols[:, kc:kc + 1], rstd, ALU.mult, ALU.mult,
                        [("xf", j), "fcols", rkey], [("yo", j)])
                rmsnorm_tile(xf, ("xf", j), yo, ("yo", j), mk_o)
                for q4 in range(4):
                    jo = it % 2
                    it += 1
                    for half in range(2):
                        ps, pk = PS()
                        for q in range(4):
                            kc = half * 4 + q
                            TR(ps[:, q * 128:(q + 1) * 128], yo[:, kc, q4 * 128:(q4 + 1) * 128], ident[:], [("yo", j), "ident"], [pk])
                        CP(ev_eng(), ot[jo][:, half * 512:(half + 1) * 512], ps[:, 0:512], [pk], [("ot", jo)])
                    tb = tt * 4 + q4
                    DMA(out_d[seq, tb * 128:(tb + 1) * 128, :], ot[jo], [("ot", jo)], [("outd", seq, tb)])

        try:
            main_body()
        except _Stop:
            pass
        Sc.barrier()
        Sc.emit()
    return nc, dbg_d


def prep_weights(inp):
    DEPTH = inp["w_in"].shape[0]
    perm = np.concatenate([np.arange(0, 5120), np.arange(5128, 8200)] +
                          [8200 + n * 1024 + c * 128 + np.arange(128) for c in range(8) for n in range(5)] +
                          [np.arange(5120, 5128)])
    w_in_r = np.ascontiguousarray(np.asarray(inp["w_in"], np.float32)[:, :, perm])
    b_in_r = np.asarray(inp["b_in"], np.float32)[:, perm]
    cols = np.zeros((DEPTH, 128, NCOL), np.float32)

    def put(l, name, arr):
        cols[l, :, COLS[name]:COLS[name] + arr.shape[1]] = arr

    def pc(v):
        return np.asarray(v, np.float32).reshape(-1, 128).T

    brow = np.zeros((DEPTH, 2, 512), np.float32)
    dw = np.zeros((DEPTH, 2, 4, 128, 128), np.float32)
    for l in range(DEPTH):
        put(l, "bin", pc(b_in_r[l, :13312]))
        put(l, "ng", pc(inp["norm_g"][l]))
        put(l, "mng", pc(inp["mem_norm_g"][l]))
        acw = np.asarray(inp["a_conv_w"][l], np.float32)
        put(l, "acw", acw.T.reshape(4, 128, 31).transpose(1, 0, 2).reshape(128, 124))
        put(l, "acb", pc(inp["a_conv_b"][l]))
        put(l, "alg", pc(inp["a_ln_g"][l]))
        put(l, "alb", pc(inp["a_ln_b"][l]))
        ccw = np.asarray(inp["c_conv_w"][l], np.float32)
        put(l, "ccw", ccw.T.reshape(8, 128, 4).transpose(1, 0, 2).reshape(128, 32))
        put(l, "ccb", pc(inp["c_conv_b"][l]))
        put(l, "chg", pc(inp["c_hn_g"][l]))
        dcw = np.asarray(inp["d_conv_w"][l], np.float32)
        put(l, "dcw", dcw.T.reshape(4, 128, 4).transpose(1, 0, 2).reshape(128, 16))
        put(l, "dcb", pc(inp["d_conv_b"][l]))
        put(l, "dba", pc(inp["d_ba"][l]))
        put(l, "dbx", pc(inp["d_bx"][l]))
        put(l, "dlam", pc(inp["d_lambda"][l]))
        cols[l, 0:4, COLS["cib"]] = b_in_r[l, 13312:13316]
        cols[l, 0:4, COLS["cfb"]] = b_in_r[l, 13316:13320]
        cols[l, 0:4, COLS["cfb2"]] = np.asarray(inp["c_f_bias"][l], np.float32)
        brow[l, 0] = b_in_r[l, SEC["b_v"] * 512:(SEC["b_v"] + 1) * 512]
        brow[l, 1] = b_in_r[l, SEC["c_v"] * 512:(SEC["c_v"] + 1) * 512]
        for g, nm in enumerate(("d_wa", "d_wx")):
            wgt = np.asarray(inp[nm][l], np.float32)
            for cc in range(4):
                dw[l, g, cc, 0:64, 0:64] = wgt[2 * cc]
                dw[l, g, cc, 64:128, 64:128] = wgt[2 * cc + 1]
    fcols = np.ascontiguousarray(pc(inp["final_norm_g"]))
    return dict(w_in_r=w_in_r, brow=brow, cols=cols, dw=dw,
                w_mkv=np.ascontiguousarray(np.asarray(inp["w_mkv"], np.float32)),
                w_up=np.ascontiguousarray(np.asarray(inp["w_up"], np.float32)),
                w_out=np.ascontiguousarray(np.asarray(inp["w_out"], np.float32)),
                fcols=fcols)


_NC_CACHE = {}


def kernel(**inputs):
    x = np.asarray(inputs["x"], np.float32)
    mem = np.asarray(inputs["mem"], np.float32)
    B, S, _ = x.shape
    DEPTH = inputs["w_in"].shape[0]
    ncores = 8
    nseq = B // ncores
    wts = prep_weights(inputs)
    key = (S, nseq, DEPTH)
    if key not in _NC_CACHE:
        _NC_CACHE[key] = build(S, nseq, DEPTH)[0]
    nc = _NC_CACHE[key]
    in_maps = []
    for c in range(ncores):
        m = dict(wts)
        m["x"] = np.ascontiguousarray(x[c * nseq:(c + 1) * nseq])
        m["mem"] = np.ascontiguousarray(mem[c * nseq:(c + 1) * nseq])
        in_maps.append(m)
    res = run_bass_kernel_spmd(nc, in_maps, core_ids=list(range(ncores)))
    out = np.concatenate([np.asarray(r["out"], np.float32) for r in res.results], axis=0)
    return out
```
